# Optimizing a Trainium2 kernel written in Bass

```python
import jax, jax.numpy as jnp
from jax import lax
import numpy as np

D_MODEL = 2048
BATCH = 2
SEQ = 4096
DEPTH = 2
DEC_BATCH = 8
DEC_SEQ = 4
PAST_LEN = 16384
PAGE_SIZE = 128

POOL_WIDTH = D_MODEL // 2
POOL_WINDOWS = (2, 4, 8, 16)
POOL_GROUPS = len(POOL_WINDOWS)
POOL_GROUP_DIM = POOL_WIDTH // POOL_GROUPS
POOL_STATE = max(POOL_WINDOWS) - 1
N_HEADS = 16
HEAD_DIM = 64
N_KV_HEADS = 4
GQA = N_HEADS // N_KV_HEADS
ATT_WIDTH = N_HEADS * HEAD_DIM
KV_WIDTH = N_KV_HEADS * HEAD_DIM
N_KV_ROWS = 4
N_KV_PROJ = 6
BLOCK = 64
TOP_N = 16
WINDOW = 512
Q_BLOCK = 64
N_ATT_GATES = 3
SSM_WIDTH = D_MODEL // 2
SSM_GROUP_DIM = 16
SSM_GROUPS = SSM_WIDTH // SSM_GROUP_DIM
SSM_STATE = 64
DT_MIN = 0.001
DT_MAX = 0.1
N_BRANCH = 3
EPS = 1e-6
NEG = -1e30
FORCE = 1e4
IN_SPLITS = (POOL_WIDTH, POOL_WIDTH, ATT_WIDTH, N_KV_PROJ * KV_WIDTH, N_HEADS * N_ATT_GATES,
             ATT_WIDTH, SSM_WIDTH, SSM_WIDTH, N_BRANCH * D_MODEL)
D_IN = sum(IN_SPLITS)
IN_OFFSETS = tuple(int(v) for v in np.cumsum(IN_SPLITS)[:-1])

kernel_name = 'hybrid_pool_nsa_s5_decode_step'


def _rmsnorm(x, g):
    x32 = x.astype(jnp.float32)
    y = x32 * lax.rsqrt(jnp.mean(x32 * x32, axis=-1, keepdims=True) + EPS)
    return (y * g.astype(jnp.float32)).astype(x.dtype)


def _pool_mixer(u, prefix, q0, w_pool, pool_scale):
    b, s, _ = u.shape
    ext = jnp.concatenate([prefix.astype(u.dtype), u], axis=1)
    e32 = ext.astype(jnp.float32)
    cs = jnp.cumsum(e32, axis=1)
    cs = jnp.concatenate([jnp.zeros_like(cs[:, :1]), cs], axis=1)
    pos = q0 + jnp.arange(s, dtype=jnp.int32)
    end = POOL_STATE + 1
    means = []
    for gi, w in enumerate(POOL_WINDOWS):
        lo, hi = gi * POOL_GROUP_DIM, (gi + 1) * POOL_GROUP_DIM
        tot = cs[:, end:end + s, lo:hi] - cs[:, end - w:end - w + s, lo:hi]
        cnt = jnp.minimum(pos + 1, w).astype(jnp.float32)
        means.append(tot / cnt[None, :, None])
    diff = jnp.concatenate(means, axis=-1) - e32[:, POOL_STATE:]
    diff = diff.reshape(b, s, POOL_GROUPS, POOL_GROUP_DIM)
    y = jnp.einsum('bsgc,gcd->bsgd', diff, w_pool.astype(jnp.float32)).reshape(b, s, POOL_WIDTH)
    y = y * pool_scale.astype(jnp.float32)
    return y.astype(u.dtype), ext[:, -POOL_STATE:]


def _block_gather(kb, ix):
    return kb[ix]


_gather_bk = jax.vmap(jax.vmap(_block_gather))


def _nsa_mixer(q, kv_new, gate_logits, kv_past, win_prefix, win_pos0, win_keep, pe_cmp, w_phi):
    b, s, _ = q.shape
    f32 = jnp.float32
    t0 = kv_past.shape[1]
    scale = HEAD_DIM ** -0.5
    q = q.reshape(b, s, N_KV_HEADS, GQA, HEAD_DIM)
    kvn = kv_new.reshape(b, s, N_KV_PROJ, N_KV_HEADS, HEAD_DIM)
    full = jnp.concatenate([kv_past.astype(kvn.dtype), kvn[:, :, :N_KV_ROWS]], axis=1)
    t = t0 + s
    nb = -(-t // BLOCK)
    full = jnp.pad(full, ((0, 0), (0, nb * BLOCK - t), (0, 0), (0, 0), (0, 0)))
    blk = full.reshape(b, nb, BLOCK, N_KV_ROWS, N_KV_HEADS, HEAD_DIM)
    kc = jnp.einsum('bnlkd,lde->bnke', blk[:, :, :, 0] + pe_cmp[0][:, None, :], w_phi[0])
    vc = jnp.einsum('bnlkd,lde->bnke', blk[:, :, :, 1] + pe_cmp[1][:, None, :], w_phi[1])
    pos = t0 + jnp.arange(s, dtype=jnp.int32)
    nidx = jnp.arange(nb, dtype=jnp.int32)
    sc = jnp.einsum('bskgd,bnkd->bkgsn', q, kc).astype(f32) * scale
    cvalid = nidx[None, :] < ((pos + 1) // BLOCK)[:, None]
    pc = jax.nn.softmax(jnp.where(cvalid, sc, NEG), axis=-1) * cvalid
    o_cmp = jnp.einsum('bkgsn,bnkd->bskgd', pc.astype(vc.dtype), vc)
    imp = pc.sum(axis=2)
    cur = (pos // BLOCK)[:, None]
    forced = (nidx[None] == 0) | (nidx[None] == cur) | (nidx[None] == cur - 1)
    score = jnp.where(forced, FORCE, jnp.where(nidx[None] < cur, imp, NEG))
    n_sel = min(TOP_N, nb)
    top_val, top_idx = lax.top_k(score, n_sel)
    sel_ok = top_val > NEG / 2
    ks_b = jnp.moveaxis(blk[:, :, :, 2], 3, 1)
    vs_b = jnp.moveaxis(blk[:, :, :, 3], 3, 1)
    kw_full = jnp.concatenate([win_prefix[:, :, 0].astype(kvn.dtype), kvn[:, :, 4]], axis=1)
    vw_full = jnp.concatenate([win_prefix[:, :, 1].astype(kvn.dtype), kvn[:, :, 5]], axis=1)
    lp = win_prefix.shape[1]
    qb = Q_BLOCK if s % Q_BLOCK == 0 else s
    nq = s // qb
    q_blocks = jnp.moveaxis(q.reshape(b, nq, qb, N_KV_HEADS, GQA, HEAD_DIM), 1, 0)
    idx_blocks = jnp.moveaxis(top_idx.reshape(b, N_KV_HEADS, nq, qb, n_sel), 2, 0)
    ok_blocks = jnp.moveaxis(sel_ok.reshape(b, N_KV_HEADS, nq, qb, n_sel), 2, 0)
    pos_blocks = pos.reshape(nq, qb)
    starts = jnp.arange(nq, dtype=jnp.int32) * qb
    coff = jnp.arange(BLOCK, dtype=jnp.int32)
    woff = jnp.arange(lp + qb, dtype=jnp.int32)

    def sparse_block(args):
        qx, ix, ok, pq, st = args
        kg = _gather_bk(ks_b, ix)
        vg = _gather_bk(vs_b, ix)
        ss = jnp.einsum('bqkgd,bkqjcd->bkgqjc', qx, kg).astype(f32) * scale
        kpos = ix[..., None] * BLOCK + coff
        m = ok[..., None] & (kpos <= pq[None, None, :, None, None])
        ss = jnp.where(m[:, :, None], ss, NEG)
        ps = jax.nn.softmax(ss.reshape(ss.shape[:4] + (-1,)), axis=-1).reshape(ss.shape)
        o_sel = jnp.einsum('bkgqjc,bkqjcd->bqkgd', ps.astype(vg.dtype), vg)
        kw = lax.dynamic_slice_in_dim(kw_full, st, lp + qb, axis=1)
        vw = lax.dynamic_slice_in_dim(vw_full, st, lp + qb, axis=1)
        kp = win_pos0 + st + woff
        sw = jnp.einsum('bqkgd,btkd->bkgqt', qx, kw).astype(f32) * scale
        mw = (kp[None] >= 0) & (kp[None] <= pq[:, None]) & (kp[None] > pq[:, None] - WINDOW)
        pw = jax.nn.softmax(jnp.where(mw, sw, NEG), axis=-1)
        o_win = jnp.einsum('bkgqt,btkd->bqkgd', pw.astype(vw.dtype), vw)
        return o_sel, o_win

    o_sel, o_win = lax.map(sparse_block, (q_blocks, idx_blocks, ok_blocks, pos_blocks, starts))
    o_sel = jnp.moveaxis(o_sel, 0, 1).reshape(b, s, N_KV_HEADS, GQA, HEAD_DIM)
    o_win = jnp.moveaxis(o_win, 0, 1).reshape(b, s, N_KV_HEADS, GQA, HEAD_DIM)
    g = jax.nn.sigmoid(gate_logits.astype(f32)).reshape(b, s, N_KV_HEADS, GQA, N_ATT_GATES)
    o = (g[..., 0:1] * o_cmp.astype(f32) + g[..., 1:2] * o_sel.astype(f32)
         + g[..., 2:3] * o_win.astype(f32))
    win_state = jnp.stack([kw_full[:, -win_keep:], vw_full[:, -win_keep:]], axis=2)
    return o.reshape(b, s, ATT_WIDTH).astype(kvn.dtype), kvn[:, :, :N_KV_ROWS], win_state


def _complex_affine_combine(e1, e2):
    a1r, a1i, b1r, b1i = e1
    a2r, a2i, b2r, b2i = e2
    return (a2r * a1r - a2i * a1i, a2r * a1i + a2i * a1r,
            a2r * b1r - a2i * b1i + b2r, a2r * b1i + a2i * b1r + b2i)


def _ssm_mixer(u, h0, lam_re, lam_im, log_step, b_re, b_im, c_re, c_im, d_skip, w_glu):
    b, s, _ = u.shape
    f32 = jnp.float32
    u32 = u.astype(f32).reshape(b, s, SSM_GROUPS, SSM_GROUP_DIM)
    lr, li = lam_re.astype(f32), lam_im.astype(f32)
    dt = jnp.exp(log_step.astype(f32))[:, None]
    mag = jnp.exp(lr * dt)
    ab_re, ab_im = mag * jnp.cos(li * dt), mag * jnp.sin(li * dt)
    den = lr * lr + li * li
    co_re = ((ab_re - 1.0) * lr + ab_im * li) / den
    co_im = (ab_im * lr - (ab_re - 1.0) * li) / den
    br, bi = b_re.astype(f32), b_im.astype(f32)
    bb_re = co_re[..., None] * br - co_im[..., None] * bi
    bb_im = co_re[..., None] * bi + co_im[..., None] * br
    bu_re = jnp.einsum('bsgc,gnc->bsgn', u32, bb_re)
    bu_im = jnp.einsum('bsgc,gnc->bsgn', u32, bb_im)
    h0r, h0i = h0[:, 0].astype(f32), h0[:, 1].astype(f32)
    bu_re = bu_re.at[:, 0].add(ab_re * h0r - ab_im * h0i)
    bu_im = bu_im.at[:, 0].add(ab_re * h0i + ab_im * h0r)
    a_re = jnp.broadcast_to(ab_re, bu_re.shape)
    a_im = jnp.broadcast_to(ab_im, bu_im.shape)
    _, _, hr, hi = lax.associative_scan(_complex_affine_combine, (a_re, a_im, bu_re, bu_im), axis=1)
    y = (jnp.einsum('gcn,bsgn->bsgc', c_re.astype(f32), hr)
         - jnp.einsum('gcn,bsgn->bsgc', c_im.astype(f32), hi))
    y = y.reshape(b, s, SSM_WIDTH) + d_skip.astype(f32) * u.astype(f32)
    z = jax.nn.gelu(y)
    out = z * jax.nn.sigmoid(z @ w_glu.astype(f32))
    new_state = jnp.stack([hr[:, -1], hi[:, -1]], axis=1)
    return out.astype(u.dtype), new_state.astype(h0.dtype)


def _layer(x, kv_past, win_prefix, win_pos0, win_keep, pool_prefix, ssm_h0, lw):
    (g_pre, g_post, w_in, w_pool, pool_scale, pe_cmp, w_phi, lam_re, lam_im, log_step,
     b_re, b_im, c_re, c_im, d_skip, w_glu, w_br_pool, w_br_nsa, w_br_ssm, w_out) = lw
    b, s, _ = x.shape
    q0 = kv_past.shape[1]
    h = _rmsnorm(x, g_pre)
    proj = h @ w_in
    pu, pz, q, kv, ag, az, su, sz, mg = jnp.split(proj, IN_OFFSETS, axis=-1)
    y_pool, pool_state = _pool_mixer(pu, pool_prefix, q0, w_pool, pool_scale)
    y_att, kv_rows, win_state = _nsa_mixer(q, kv, ag, kv_past, win_prefix, win_pos0, win_keep,
                                           pe_cmp, w_phi)
    y_ssm, ssm_state = _ssm_mixer(su, ssm_h0, lam_re, lam_im, log_step, b_re, b_im, c_re, c_im,
                                  d_skip, w_glu)
    br_pool = (y_pool * jax.nn.silu(pz)) @ w_br_pool
    br_att = (y_att * jax.nn.silu(az)) @ w_br_nsa
    br_ssm = (y_ssm * jax.nn.silu(sz)) @ w_br_ssm
    gm = jax.nn.sigmoid(mg.reshape(b, s, N_BRANCH, D_MODEL))
    merged = gm[:, :, 0] * br_pool + gm[:, :, 1] * br_att + gm[:, :, 2] * br_ssm
    out = merged @ w_out
    return x + _rmsnorm(out, g_post), kv_rows, win_state, pool_state, ssm_state


def setup_inputs(seed: int = 0) -> dict:
    key = jax.random.key(seed)
    ks = jax.random.split(key, 32)
    f32 = jnp.float32
    n_pages = PAST_LEN // PAGE_SIZE
    n_pool = (DEC_BATCH * n_pages * 5 + 3) // 4
    win_buf = min(WINDOW, PAST_LEN)

    def nrm(k, shape, sc):
        return jax.random.normal(k, shape, f32) * sc

    perm = jax.random.permutation(ks[3], n_pool)[:DEC_BATCH * n_pages]
    lam_im = jnp.broadcast_to(jnp.pi * jnp.arange(SSM_STATE, dtype=f32), (DEPTH, SSM_GROUPS, SSM_STATE))
    return {
        'x_prompt': nrm(ks[0], (BATCH, SEQ, D_MODEL), 1.0),
        'x_sample': nrm(ks[1], (DEC_BATCH, DEC_SEQ, D_MODEL), 1.0),
        'cache_kv': nrm(ks[2], (DEPTH, n_pool, PAGE_SIZE, N_KV_ROWS, N_KV_HEADS, HEAD_DIM), 1.0),
        'page_table': perm.reshape(DEC_BATCH, n_pages).astype(jnp.int32),
        'state_win_kv': nrm(ks[4], (DEPTH, DEC_BATCH, win_buf, 2, N_KV_HEADS, HEAD_DIM), 1.0),
        'state_pool': nrm(ks[5], (DEPTH, DEC_BATCH, POOL_STATE, POOL_WIDTH), 1.0),
        'state_ssm': nrm(ks[6], (DEPTH, DEC_BATCH, 2, SSM_GROUPS, SSM_STATE), 0.5),
        'g_pre': 1.0 + nrm(ks[7], (DEPTH, D_MODEL), 0.02),
        'g_post': 1.0 + nrm(ks[8], (DEPTH, D_MODEL), 0.02),
        'w_in': nrm(ks[9], (DEPTH, D_MODEL, D_IN), D_MODEL ** -0.5),
        'w_pool': nrm(ks[10], (DEPTH, POOL_GROUPS, POOL_GROUP_DIM, POOL_GROUP_DIM), POOL_GROUP_DIM ** -0.5),
        'pool_scale': 1.0 + nrm(ks[11], (DEPTH, POOL_WIDTH), 0.1),
        'pe_cmp': nrm(ks[12], (DEPTH, 2, BLOCK, HEAD_DIM), 0.02),
        'w_phi': nrm(ks[13], (DEPTH, 2, BLOCK, HEAD_DIM, HEAD_DIM), (BLOCK * HEAD_DIM) ** -0.5),
        'lam_re': -0.5 + nrm(ks[14], (DEPTH, SSM_GROUPS, SSM_STATE), 0.01),
        'lam_im': lam_im,
        'log_step': jax.random.uniform(ks[15], (DEPTH, SSM_GROUPS), f32,
                                       minval=float(np.log(DT_MIN)), maxval=float(np.log(DT_MAX))),
        'b_re': nrm(ks[16], (DEPTH, SSM_GROUPS, SSM_STATE, SSM_GROUP_DIM), (2 * SSM_GROUP_DIM) ** -0.5),
        'b_im': nrm(ks[17], (DEPTH, SSM_GROUPS, SSM_STATE, SSM_GROUP_DIM), (2 * SSM_GROUP_DIM) ** -0.5),
        'c_re': nrm(ks[18], (DEPTH, SSM_GROUPS, SSM_GROUP_DIM, SSM_STATE), (2 * SSM_STATE) ** -0.5),
        'c_im': nrm(ks[19], (DEPTH, SSM_GROUPS, SSM_GROUP_DIM, SSM_STATE), (2 * SSM_STATE) ** -0.5),
        'd_skip': nrm(ks[20], (DEPTH, SSM_WIDTH), 1.0),
        'w_glu': nrm(ks[21], (DEPTH, SSM_WIDTH, SSM_WIDTH), SSM_WIDTH ** -0.5),
        'w_br_pool': nrm(ks[22], (DEPTH, POOL_WIDTH, D_MODEL), POOL_WIDTH ** -0.5),
        'w_br_nsa': nrm(ks[23], (DEPTH, ATT_WIDTH, D_MODEL), ATT_WIDTH ** -0.5),
        'w_br_ssm': nrm(ks[24], (DEPTH, SSM_WIDTH, D_MODEL), SSM_WIDTH ** -0.5),
        'w_out': nrm(ks[25], (DEPTH, D_MODEL, D_MODEL), D_MODEL ** -0.5),
    }


def reference(x_prompt, x_sample, cache_kv, page_table, state_win_kv, state_pool, state_ssm,
              g_pre, g_post, w_in, w_pool, pool_scale, pe_cmp, w_phi, lam_re, lam_im, log_step,
              b_re, b_im, c_re, c_im, d_skip, w_glu, w_br_pool, w_br_nsa, w_br_ssm, w_out):
    bp, sp, _ = x_prompt.shape
    bd = x_sample.shape[0]
    past_len = page_table.shape[1] * cache_kv.shape[2]
    win_keep = state_win_kv.shape[2]
    dt = x_prompt.dtype
    yp, ys = x_prompt, x_sample
    kvp, kvs, wpr, wsa, ppr, psa, hpr, hsa = [], [], [], [], [], [], [], []
    for l in range(DEPTH):
        lw = (g_pre[l], g_post[l], w_in[l], w_pool[l], pool_scale[l], pe_cmp[l], w_phi[l],
              lam_re[l], lam_im[l], log_step[l], b_re[l], b_im[l], c_re[l], c_im[l],
              d_skip[l], w_glu[l], w_br_pool[l], w_br_nsa[l], w_br_ssm[l], w_out[l])
        yp, r_kv, r_win, r_pool, r_ssm = _layer(
            yp, jnp.zeros((bp, 0, N_KV_ROWS, N_KV_HEADS, HEAD_DIM), dt),
            jnp.zeros((bp, WINDOW, 2, N_KV_HEADS, HEAD_DIM), dt), -WINDOW, min(WINDOW, sp),
            jnp.zeros((bp, POOL_STATE, POOL_WIDTH), dt),
            jnp.zeros((bp, 2, SSM_GROUPS, SSM_STATE), dt), lw)
        past = cache_kv[l][page_table].reshape(bd, past_len, N_KV_ROWS, N_KV_HEADS, HEAD_DIM)
        ys, s_kv, s_win, s_pool, s_ssm = _layer(
            ys, past, state_win_kv[l], past_len - win_keep, win_keep,
            state_pool[l], state_ssm[l], lw)
        kvp.append(r_kv); kvs.append(s_kv); wpr.append(r_win); wsa.append(s_win)
        ppr.append(r_pool); psa.append(s_pool); hpr.append(r_ssm); hsa.append(s_ssm)
    return (yp, ys, jnp.stack(kvp), jnp.stack(kvs), jnp.stack(wpr), jnp.stack(wsa),
            jnp.stack(ppr), jnp.stack(psa), jnp.stack(hpr), jnp.stack(hsa))
```

```python
import math
import os
import numpy as np
KSTOP = int(os.environ.get('KSTOP', '99'))
KSUB = int(os.environ.get('KSUB', '99'))
import concourse.bass as bass
import concourse.mybir as mybir
from concourse.bass_utils import run_bass_kernel_spmd
from contextlib import ExitStack

F32 = mybir.dt.float32
BF16 = mybir.dt.bfloat16
I32 = mybir.dt.int32
ALU = mybir.AluOpType
AF = mybir.ActivationFunctionType
AX = mybir.AxisListType

EPOCH = 8192
NDMASEM = 12

D = 2048
DIN = 13872
SEQ = 4096
PAST = 16384
NPOOL = 1280
EPS = 1e-6
NEG = -1e30
O_PU, O_PZ, O_Q, O_KV, O_AG, O_AZ, O_SU, O_SZ, O_MG = 0, 1024, 2048, 3072, 4608, 4656, 5680, 6704, 7728
A_SEGS = [O_PU, O_PZ, O_Q, O_AZ, O_SU, O_SZ]
NKVB = 1584
TC = 32


class Buf:
    __slots__ = ("t", "w", "r", "name")

    def __init__(self, t, name="", carry=None):
        self.t = t
        self.w = dict(carry) if carry else {}
        self.r = {}
        self.name = name

    def __getitem__(self, idx):
        return self.t[idx]


class Eng:
    def __init__(self, P, name, obj):
        self.P = P
        self.name = name
        self.obj = obj
        self.count = 0
        self.sems = []
        self.waited = {}
        self.dsems = []
        self.dnext = 0

    def sem_for(self, seq):
        k = (seq - 1) // EPOCH
        while len(self.sems) <= k:
            self.sems.append(self.P.new_sem(f"{self.name}_e{len(self.sems)}"))
        return self.sems[k], (seq - 1) % EPOCH + 1, (self.name, k)


def _dep_rank(d):
    return d[2]


class Prog:
    def __init__(self, nc):
        self.nc = nc
        self.es = ExitStack()
        self.nsem = 0
        self.nname = 0
        self.pe = Eng(self, "pe", nc.tensor)
        self.act = Eng(self, "act", nc.scalar)
        self.dve = Eng(self, "dve", nc.vector)
        self.pool = Eng(self, "pool", nc.gpsimd)
        self.sp = Eng(self, "sp", nc.sync)
        self.engs = [self.pe, self.act, self.dve, self.pool, self.sp]
        self.carry = {}
        self.ninstr = 0

    def new_sem(self, name):
        self.nsem += 1
        return self.es.enter_context(self.nc.semaphore(f"s_{name}_{self.nsem}"))

    def uname(self, name):
        self.nname += 1
        return f"{name}_{self.nname}"

    def sbuf(self, name, shape, dtype, es=None):
        t = (es or self.es).enter_context(self.nc.sbuf_tensor(self.uname(name), list(shape), dtype))
        return Buf(t, name, self.carry)

    def psum(self, name, shape, dtype):
        t = self.es.enter_context(self.nc.psum_tensor(self.uname(name), list(shape), dtype))
        return Buf(t, name)

    def dram_in(self, name, shape, dtype):
        return self.nc.dram_tensor(name, list(shape), dtype, kind="ExternalInput").ap()

    def dram_out(self, name, shape, dtype):
        return self.nc.dram_tensor(name, list(shape), dtype, kind="ExternalOutput").ap()

    def dram_scratch(self, name, shape, dtype):
        t = self.nc.dram_tensor(name, list(shape), dtype, kind="Internal").ap()
        return Buf(t, name)

    def retire(self, bufs):
        for b in bufs:
            for dd in (b.w, b.r):
                for k, d in dd.items():
                    kk = k[:3] if k[0] == "d" else k[:2]
                    o = self.carry.get(kk)
                    if o is None or _dep_rank(o) < _dep_rank(d):
                        self.carry[kk] = d

    def _wait(self, eng, dep, force=False):
        if dep[0] == "e":
            _, src, seq = dep
            if src is eng and eng is self.pe and not force:
                return
            sem, val, key = src.sem_for(seq)
        else:
            _, sem, val, key = dep
        if eng.waited.get(key, 0) >= val:
            return
        eng.waited[key] = val
        eng.obj.wait_ge(sem, val)
        self.ninstr += 1

    def _collect(self, eng, reads, writes):
        for b in reads:
            for d in b.w.values():
                self._wait(eng, d)
        for b in writes:
            for d in b.w.values():
                self._wait(eng, d)
            for d in b.r.values():
                self._wait(eng, d)

    def _commit(self, me, mekey, reads, writes):
        for b in writes:
            b.w = {mekey: me}
            b.r = {}
        for b in reads:
            if b not in writes:
                b.r[mekey] = me

    def op(self, eng, fn, reads=(), writes=()):
        self._collect(eng, reads, writes)
        ins = fn(eng.obj)
        eng.count += 1
        sem, val, key = eng.sem_for(eng.count)
        ins.then_inc(sem, 1)
        self.ninstr += 1
        me = ("e", eng, eng.count)
        self._commit(me, ("e", eng.name), reads, writes)
        return me

    def _dma_common(self, q, reads, writes, issue):
        self._collect(q, reads, writes)
        if len(q.dsems) < NDMASEM:
            q.dsems.append([self.new_sem(f"{q.name}_d{len(q.dsems)}"), 0])
        i = q.dnext % NDMASEM
        q.dnext += 1
        ent = q.dsems[i]
        key = ("d", q.name, i)
        if ent[1] > 0:
            self._wait(q, ("d", ent[0], 16 * ent[1], key))
        ent[1] += 1
        ins = issue(q.obj)
        ins.then_inc(ent[0], 16)
        self.ninstr += 1
        me = ("d", ent[0], 16 * ent[1], key)
        self._commit(me, key, reads, writes)
        return me

    def dma(self, q, out, in_, reads=(), writes=()):
        return self._dma_common(q, reads, writes, lambda e: e.dma_start(out=out, in_=in_))

    def gather(self, out, in_, idx_ap, reads=(), writes=()):
        return self._dma_common(
            self.pool, reads, writes,
            lambda e: e.indirect_dma_start(out=out, out_offset=None, in_=in_,
                                           in_offset=bass.IndirectOffsetOnAxis(ap=idx_ap, axis=0)))

    def finish(self):
        for e in self.engs:
            if e is not self.sp and e.count > 0:
                self._wait(self.sp, ("e", e, e.count))
        for q in self.engs:
            for i, ent in enumerate(q.dsems):
                if ent[1] > 0:
                    self._wait(self.sp, ("d", ent[0], 16 * ent[1], ("d", q.name, i)))

    def barrier(self):
        for tgt in self.engs:
            for e in self.engs:
                if e is not tgt and e.count > 0:
                    self._wait(tgt, ("e", e, e.count))
            for q in self.engs:
                for i, ent in enumerate(q.dsems):
                    if ent[1] > 0:
                        self._wait(tgt, ("d", ent[0], 16 * ent[1], ("d", q.name, i)))

    def close(self):
        self.es.close()


class Arena:
    def __init__(self, P):
        self.P = P
        self.es = ExitStack()
        self.bufs = []

    def sbuf(self, name, shape, dtype):
        b = self.P.sbuf(name, shape, dtype, es=self.es)
        self.bufs.append(b)
        return b

    def close(self):
        self.P.retire(self.bufs)
        self.es.close()

    def __enter__(self):
        return self

    def __exit__(self, *a):
        self.close()


class KB:
    def __init__(self, S=SEQ, NS=4, C=256, debug=False, parts=("pool", "ssm", "nsa")):
        self.S, self.NS, self.C, self.debug, self.parts = S, NS, C, debug, set(parts)
        self.NCH = S // C
        self.nc = bass.Bass("TRN2", target_bir_lowering=False)
        self.P = Prog(self.nc)
        self.dbg_outs = []

    def TT(self, eng, out, a, b, op, R, W):
        return self.P.op(eng, lambda e: e.tensor_tensor(out=out, in0=a, in1=b, op=op), R, W)

    def TS(self, eng, out, a, s1, s2, op0, op1, R, W):
        if s2 is None:
            return self.P.op(eng, lambda e: e.tensor_scalar(out=out, in0=a, scalar1=s1, scalar2=None, op0=op0), R, W)
        return self.P.op(eng, lambda e: e.tensor_scalar(out=out, in0=a, scalar1=s1, scalar2=s2, op0=op0, op1=op1), R, W)

    def STT(self, eng, out, a, s, b, op0, op1, R, W):
        return self.P.op(eng, lambda e: e.scalar_tensor_tensor(out=out, in0=a, scalar=s, in1=b, op0=op0, op1=op1), R, W)

    def AC(self, out, in_, func, R, W, **kw):
        return self.P.op(self.P.act, lambda e: e.activation(out=out, in_=in_, func=func, **kw), R, W)

    def CP(self, eng, out, in_, R, W):
        if eng is self.P.act:
            return self.P.op(eng, lambda e: e.copy(out=out, in_=in_), R, W)
        return self.P.op(eng, lambda e: e.tensor_copy(out=out, in_=in_), R, W)

    def MS(self, eng, ap, val, W):
        return self.P.op(eng, lambda e: e.memset(ap, val), (), W)

    def _rowgroup(self, ap, out):
        rg = (ap.base_partition(), ap.shape[0], out.base_partition(), out.shape[0])
        last = getattr(self, "_last_rg", (0, 128, 0, 128))
        if rg != last and (rg[1] < 128 or last[1] < 128 or rg[3] < 128 or last[3] < 128):
            pe = self.P.pe
            if pe.count > 0:
                self.P._wait(pe, ("e", pe, pe.count), force=True)
        self._last_rg = rg

    def MM(self, out, lhsT, rhs, start, stop, R, W):
        self._rowgroup(lhsT, out)
        return self.P.op(self.P.pe, lambda e: e.matmul(out, lhsT=lhsT, rhs=rhs, start=start, stop=stop), R, W)

    def TR(self, out, in_, ident, R, W):
        self._rowgroup(in_, out)
        return self.P.op(self.P.pe, lambda e: e.transpose(out=out, in_=in_, identity=ident), R, W)

    def MT(self, out, in_, R, W):
        kp = in_.shape[0]
        idn = self.identf if in_.dtype == F32 else self.ident
        return self.MM(out, in_, idn[0:kp, 0:kp], True, True, R + [idn], W)

    def DMA(self, out, in_, R=(), W=(), q=None):
        return self.P.dma(q or self.P.sp, out, in_, R, W)

    def dump(self, name, ap, shape, R):
        if not self.debug:
            return
        o = self.P.dram_out("dbg_" + name, shape, F32)
        self.dbg_outs.append("dbg_" + name)
        if ap.dtype != F32:
            with Arena(self.P) as A:
                t = A.sbuf("dbgt", list(ap.shape), F32)
                self.CP(self.P.dve, t[:], ap, R, [t])
                self.DMA(o, t[:], [t], [])
        else:
            self.DMA(o, ap, R, [])

    def declare(self):
        P, S, NS = self.P, self.S, self.NS
        din = P.dram_in
        self.xp = din("xp", [S, D], F32)
        if NS:
            self.xs = din("xs", [4 * NS, D], F32)
            self.cache = [din(f"cache{i}", [NPOOL * 128 * 2, 512], F32) for i in range(2)]
            self.ptab = din("ptab", [NS, 128], I32)
            self.swin = din("swin", [2, NS, 512, 512], F32)
            self.spool = din("spool", [2, NS, 15, 1024], F32)
            self.sssm = din("sssm", [2, NS, 2, 32, 128], F32)
            self.t_cm = din("t_cm", [128, NS, 4 * NS], F32)
            self.t_wm = din("t_wm", [128, NS, 4 * NS], F32)
            self.t_m16 = din("t_m16", [4 * NS, 4 * NS], F32)
            self.t_ssel = din("t_ssel", [4 * NS, 256], F32)
        self.g_pre = din("g_pre", [2, D], F32)
        self.g_post = din("g_post", [2, D], F32)
        self.w_in = din("w_in", [2, D, DIN], F32)
        self.w_pool = din("w_pool", [2, 4, 256, 256], F32)
        self.pool_scale = din("pool_scale", [2, 1024], F32)
        self.pe_cmp = din("pe_cmp", [2, 2, 64, 64], F32)
        self.w_phi = din("w_phi", [2, 2, 64, 64, 64], F32)
        self.lam_re = din("lam_re", [2, 32, 128], F32)
        self.lam_im = din("lam_im", [2, 32, 128], F32)
        self.log_step = din("log_step", [2, 32, 2], F32)
        self.b_re = din("b_re", [2, 64, 64, 16], F32)
        self.b_im = din("b_im", [2, 64, 64, 16], F32)
        self.c_re = din("c_re", [2, 32, 2, 16, 64], F32)
        self.c_im = din("c_im", [2, 32, 2, 16, 64], F32)
        self.d_skip = din("d_skip", [2, 1024], F32)
        self.w_glu = din("w_glu", [2, 1024, 1024], F32)
        self.w_br = [din(n, [2, 1024, D], F32) for n in ("w_br_pool", "w_br_nsa", "w_br_ssm")]
        self.w_out = din("w_out", [2, D, D], F32)
        self.t_cbq = din("t_cbq", [SEQ // 128, 128, 64], F32)
        self.t_sel = din("t_sel", [SEQ // 128, 128, 64], F32)
        self.t_cbT = din("t_cbT", [64, SEQ], F32)
        self.t_rc = din("t_rc", [128, 60], F32)
        dout = P.dram_out
        self.yp = dout("yp", [S, D], F32)
        self.kvp = dout("kvp", [2, S, 1024], F32)
        self.winp = dout("winp", [2, 512, 512], F32)
        self.poolp = dout("poolp", [2, 15, 1024], F32)
        self.ssmp = dout("ssmp", [2, 2, 32, 128], F32)
        if NS:
            self.ys = dout("ys", [4 * NS, D], F32)
            self.kvs = dout("kvs", [2, 4 * NS, 1024], F32)
            self.wins = dout("wins", [2, NS, 512, 512], F32)
            self.pools = dout("pools", [2, NS, 15, 1024], F32)
            self.ssms = dout("ssms", [2, NS, 2, 32, 128], F32)
        sc = P.dram_scratch
        self.wA = sc("wA", [2, 96, 128, 16, 128], BF16)
        self.wB = sc("wB", [2, 128, 16, NKVB], BF16)
        self.wG = sc("wG", [2, 8, 128, 8, 128], BF16)
        self.wR = sc("wR", [2, 3, 16, 128, 8, 128], BF16)
        self.wO = sc("wO", [2, 128, 16, D], BF16)
        self.x1 = sc("x1", [S, D], F32)
        if NS:
            self.xs1 = sc("xs1", [4 * NS, D], F32)

    def consts(self):
        P = self.P
        pl, dv = P.pool, P.dve
        self.identf = P.sbuf("identf", [128, 128], F32)
        self.ident = P.sbuf("ident", [128, 128], BF16)
        self.shu = P.sbuf("shu", [128, 128], BF16)
        self.tri = P.sbuf("tri", [128, 128], BF16)
        self.tris = P.sbuf("tris", [128, 128], BF16)
        self.G = P.sbuf("G", [64, 64, 64], BF16)
        self.rc = P.sbuf("rc", [128, 4, 15], F32)
        CA_ = Arena(P)
        tmp = CA_.sbuf("ctmp", [128, 128], F32)
        P.op(pl, lambda e: e.iota(tmp[:], pattern=[[1, 128]], base=0, channel_multiplier=-1,
                                  allow_small_or_imprecise_dtypes=True), (), [tmp])
        self.TS(dv, self.identf[:], tmp[:], 0.0, None, ALU.is_equal, None, [tmp], [self.identf])
        self.CP(dv, self.ident[:], self.identf[:], [self.identf], [self.ident])
        self.TS(dv, self.tri[:], tmp[:], 0.0, None, ALU.is_ge, None, [tmp], [self.tri])
        self.TS(dv, self.tris[:], tmp[:], 0.0, None, ALU.is_lt, None, [tmp], [self.tris])
        self.TS(dv, self.shu[:], tmp[:], 64.0, None, ALU.is_equal, None, [tmp], [self.shu])
        gt = CA_.sbuf("gtmp", [64, 64, 64], F32)
        P.op(pl, lambda e: e.iota(gt[:], pattern=[[1, 64], [0, 64]], base=0, channel_multiplier=-1,
                                  allow_small_or_imprecise_dtypes=True), (), [gt])
        self.TS(dv, self.G[:], gt[:], 0.0, None, ALU.is_equal, None, [gt], [self.G])
        self.DMA(self.rc[:], self.t_rc.rearrange("p (w c) -> p w c", w=4), [], [self.rc])
        CA_.close()
        self.pA = [P.psum(f"pA{i}", [128, 512], F32) for i in range(2)]
        self.pK = [P.psum(f"pK{i}", [128, 512], F32) for i in range(3)]
        self.pS = [P.psum(f"pS{i}", [128, 512], F32) for i in range(2)]
        self.pT = P.psum("pT", [128, 1024], BF16)
        self.pa_i = 0

    def nextpA(self):
        self.pa_i += 1
        return self.pA[self.pa_i % 2]

    def prepass(self):
        P = self.P
        with Arena(P) as A:
            stg = [A.sbuf(f"stg{i}", [128, 16, 512], BF16) for i in range(2)]
            si = [0]

            def stage():
                si[0] += 1
                return stg[si[0] % 2]

            for l in range(2):
                tid = 0
                segs = [(o, 1024) for o in A_SEGS] + [(O_MG, 6144)]
                for (o, n) in segs:
                    for c0 in range(0, n, 512):
                        st = stage()
                        self.DMA(st[:], self.w_in[l][:, o + c0:o + c0 + 512].rearrange("(k p) c -> p k c", p=128),
                                 [], [st], q=P.pool)
                        for t in range(4):
                            self.DMA(self.wA[l, tid], st[:, :, t * 128:(t + 1) * 128], [st], [])
                            tid += 1
                assert tid == 96
                for c0 in range(0, NKVB, 512):
                    n = min(512, NKVB - c0)
                    st = stage()
                    self.DMA(st[:, :, :n], self.w_in[l][:, O_KV + c0:O_KV + c0 + n].rearrange("(k p) c -> p k c", p=128),
                             [], [st], q=P.pool)
                    self.DMA(self.wB[l][:, :, c0:c0 + n], st[:, :, :n], [st], [])
                for c0 in range(0, D, 512):
                    st = stage()
                    self.DMA(st[:], self.w_out[l][:, c0:c0 + 512].rearrange("(k p) c -> p k c", p=128), [], [st], q=P.pool)
                    self.DMA(self.wO[l][:, :, c0:c0 + 512], st[:], [st], [])
                for c0 in range(0, 1024, 512):
                    st = stage()
                    self.DMA(st[:, 0:8, :], self.w_glu[l][:, c0:c0 + 512].rearrange("(k p) c -> p k c", p=128), [], [st], q=P.pool)
                    for t in range(4):
                        self.DMA(self.wG[l, c0 // 128 + t], st[:, 0:8, t * 128:(t + 1) * 128], [st], [])
                for i in range(3):
                    for c0 in range(0, D, 512):
                        st = stage()
                        self.DMA(st[:, 0:8, :], self.w_br[i][l][:, c0:c0 + 512].rearrange("(k p) c -> p k c", p=128), [], [st], q=P.pool)
                        for t in range(4):
                            self.DMA(self.wR[l, i, c0 // 128 + t], st[:, 0:8, t * 128:(t + 1) * 128], [st], [])
        P.barrier()

    def load_T(self, A, name, src_ap, rows, dst=None):
        P = self.P
        t = A.sbuf(name + "_ld", [rows, 128], F32)
        self.DMA(t[:], src_ap, [], [t])
        ps = self.nextpA()
        self.TR(ps[:, 0:rows], t[:], self.identf[0:rows, 0:rows], [t, self.identf], [ps])
        o = dst if dst is not None else A.sbuf(name, [128, rows], F32)
        self.CP(P.act, o[:, 0:rows], ps[:, 0:rows], [ps], [o])
        return o

    def layer_prep(self, l, LA):
        P = self.P
        dv, pl, ac = P.dve, P.pool, P.act
        L = type("L", (), {})()
        self.L = L
        L.gpre = LA.sbuf("gpre", [128, 16], F32)
        L.pscale = LA.sbuf("pscale", [128, 8], F32)
        L.dskip = LA.sbuf("dskip", [128, 8], F32)
        L.wpool = LA.sbuf("wpool", [128, 4, 2, 256], BF16)
        L.wphi2 = LA.sbuf("wphi2", [128, 2, 64, 64], BF16)
        L.pe2 = LA.sbuf("pe2", [128, 2, 64], BF16)
        L.cosT = LA.sbuf("cosT", [128, 32, TC], F32)
        L.sinT = LA.sbuf("sinT", [128, 32, TC], F32)
        L.rhoz = LA.sbuf("rhoz", [128, 32, TC], F32)
        L.abr = LA.sbuf("abr", [128, 32], F32)
        L.abi = LA.sbuf("abi", [128, 32], F32)
        L.BBr = LA.sbuf("BBr", [128, 8, 2, 128], BF16)
        L.BBi = LA.sbuf("BBi", [128, 8, 2, 128], BF16)
        L.CTr = LA.sbuf("CTr", [128, 32, 128], BF16)
        L.CTi = LA.sbuf("CTi", [128, 32, 128], BF16)
        with Arena(P) as A:
            self.load_T(A, "gpre", self.g_pre[l].rearrange("(k p) -> k p", p=128), 16, dst=L.gpre)
            self.load_T(A, "pscale", self.pool_scale[l].rearrange("(k p) -> k p", p=128), 8, dst=L.pscale)
            self.load_T(A, "dskip", self.d_skip[l].rearrange("(k p) -> k p", p=128), 8, dst=L.dskip)
            self.DMA(L.wpool[:], self.w_pool[l].rearrange("g (k p) d -> p g k d", p=128), [], [L.wpool], q=pl)
            for h in range(2):
                self.DMA(L.wphi2[64 * h:64 * h + 64], self.w_phi[l].rearrange("a l d e -> d a l e"), [], [L.wphi2], q=pl)
            pel = A.sbuf("pel", [128, 64], F32)
            self.DMA(pel[:], self.pe_cmp[l].rearrange("a l d -> (a l) d"), [], [pel])
            pel2 = A.sbuf("pel2", [128, 2, 64], F32)
            for h in range(2):
                self.CP(dv, pel2[:, h, :], pel[:], [pel], [pel2])
            ps = self.nextpA()
            self.TR(ps[:, 0:128], pel2[:].rearrange("p a d -> p (a d)"), self.identf[:], [pel2, self.identf], [ps])
            self.CP(ac, L.pe2[:].rearrange("p a l -> p (a l)"), ps[:, 0:128], [ps], [L.pe2])

            lr = self.load_T(A, "lr", self.lam_re[l], 32)
            li = self.load_T(A, "li", self.lam_im[l], 32)
            lsl = A.sbuf("lsl", [32, 2], F32)
            self.DMA(lsl[:], self.log_step[l], [], [lsl])
            lsx = A.sbuf("lsx", [32, 2, 64], F32)
            self.CP(dv, lsx[:], lsl[:].unsqueeze(2).to_broadcast([32, 2, 64]), [lsl], [lsx])
            ps = self.nextpA()
            self.TR(ps[:, 0:32], lsx[:].rearrange("p a n -> p (a n)"), self.identf[0:32, 0:32], [lsx, self.identf], [ps])
            dt = A.sbuf("dt", [128, 32], F32)
            self.AC(dt[:], ps[:, 0:32], AF.Exp, [ps], [dt])

            def T(name):
                return A.sbuf(name, [128, 32], F32)

            def mul(o, a, b):
                self.TT(dv, o[:], a[:], b[:], ALU.mult, [a, b], [o])

            def sub(o, a, b):
                self.TT(dv, o[:], a[:], b[:], ALU.subtract, [a, b], [o])

            def add(o, a, b):
                self.TT(dv, o[:], a[:], b[:], ALU.add, [a, b], [o])

            x, mag, th, c, s_, t1, t2, t3 = T("x"), T("mag"), T("th"), T("c"), T("s"), T("t1"), T("t2"), T("t3")
            mul(x, lr, dt)
            self.AC(mag[:], x[:], AF.Exp, [x], [mag])
            mul(th, li, dt)
            hp = A.sbuf("hp", [128, 1], F32)
            self.MS(dv, hp[:], math.pi / 2, [hp])
            self.AC(s_[:], th[:], AF.Sin, [th], [s_], scale=1.0 / 16)
            self.AC(c[:], th[:], AF.Sin, [th, hp], [c], scale=1.0 / 16, bias=hp[:, 0:1])

            def csq(cc, ss):
                mul(t1, cc, cc)
                mul(t2, ss, ss)
                mul(t3, cc, ss)
                sub(cc, t1, t2)
                self.TS(dv, ss[:], t3[:], 2.0, None, ALU.mult, None, [t3], [ss])

            for _ in range(4):
                csq(c, s_)
            mul(L.abr, mag, c)
            mul(L.abi, mag, s_)
            den, rden, a1, cor, coi = T("den"), T("rden"), T("a1"), T("cor"), T("coi")
            mul(t1, lr, lr)
            mul(t2, li, li)
            add(den, t1, t2)
            self.P.op(dv, lambda e: e.reciprocal(out=rden[:], in_=den[:]), [den], [rden])
            self.TS(dv, a1[:], L.abr[:], -1.0, None, ALU.add, None, [L.abr], [a1])
            mul(t1, a1, lr)
            mul(t2, L.abi, li)
            add(t3, t1, t2)
            mul(cor, t3, rden)
            mul(t1, L.abi, lr)
            mul(t2, a1, li)
            sub(t3, t1, t2)
            mul(coi, t3, rden)
            self.MS(dv, L.cosT[:, :, 0:1], 1.0, [L.cosT])
            self.MS(dv, L.sinT[:, :, 0:1], 0.0, [L.sinT])
            self.CP(dv, L.cosT[:, :, 1:2], c[:].unsqueeze(2), [c], [L.cosT])
            self.CP(dv, L.sinT[:, :, 1:2], s_[:].unsqueeze(2), [s_], [L.sinT])
            Ck, Sk = T("Ck"), T("Sk")
            self.CP(dv, Ck[:], c[:], [c], [Ck])
            self.CP(dv, Sk[:], s_[:], [s_], [Sk])
            tb1 = A.sbuf("tb1", [128, 32, TC // 2], F32)
            tb2 = A.sbuf("tb2", [128, 32, TC // 2], F32)
            csq(Ck, Sk)
            w = 2
            while w < TC:
                Cb = Ck[:].unsqueeze(2).to_broadcast([128, 32, w])
                Sb = Sk[:].unsqueeze(2).to_broadcast([128, 32, w])
                self.TT(dv, tb1[:, :, 0:w], L.cosT[:, :, 0:w], Cb, ALU.mult, [L.cosT, Ck], [tb1])
                self.TT(dv, tb2[:, :, 0:w], L.sinT[:, :, 0:w], Sb, ALU.mult, [L.sinT, Sk], [tb2])
                self.TT(dv, L.cosT[:, :, w:2 * w], tb1[:, :, 0:w], tb2[:, :, 0:w], ALU.subtract, [tb1, tb2], [L.cosT])
                self.TT(dv, tb1[:, :, 0:w], L.sinT[:, :, 0:w], Cb, ALU.mult, [L.sinT, Ck], [tb1])
                self.TT(dv, tb2[:, :, 0:w], L.cosT[:, :, 0:w], Sb, ALU.mult, [L.cosT, Sk], [tb2])
                self.TT(dv, L.sinT[:, :, w:2 * w], tb1[:, :, 0:w], tb2[:, :, 0:w], ALU.add, [tb1, tb2], [L.sinT])
                csq(Ck, Sk)
                w *= 2
            self.MS(dv, L.rhoz[:, :, 0:1], 0.0, [L.rhoz])
            self.CP(dv, L.rhoz[:, :, 1:TC], mag[:].unsqueeze(2).to_broadcast([128, 32, TC - 1]), [mag], [L.rhoz])

            bre = A.sbuf("bre", [128, 32, 16], F32)
            bim = A.sbuf("bim", [128, 32, 16], F32)
            self.DMA(bre[:], self.b_re[l].rearrange("g n c -> (g n) c").rearrange("(i p) c -> p i c", p=128), [], [bre])
            self.DMA(bim[:], self.b_im[l].rearrange("g n c -> (g n) c").rearrange("(i p) c -> p i c", p=128), [], [bim])
            u1 = A.sbuf("u1", [128, 32, 16], F32)
            u2 = A.sbuf("u2", [128, 32, 16], F32)
            bbr = A.sbuf("bbr", [128, 32, 16], F32)
            bbi = A.sbuf("bbi", [128, 32, 16], F32)
            corb = cor[:].unsqueeze(2).to_broadcast([128, 32, 16])
            coib = coi[:].unsqueeze(2).to_broadcast([128, 32, 16])
            self.TT(dv, u1[:], bre[:], corb, ALU.mult, [bre, cor], [u1])
            self.TT(dv, u2[:], bim[:], coib, ALU.mult, [bim, coi], [u2])
            self.TT(dv, bbr[:], u1[:], u2[:], ALU.subtract, [u1, u2], [bbr])
            self.TT(dv, u1[:], bim[:], corb, ALU.mult, [bim, cor], [u1])
            self.TT(dv, u2[:], bre[:], coib, ALU.mult, [bre, coi], [u2])
            self.TT(dv, bbi[:], u1[:], u2[:], ALU.add, [u1, u2], [bbi])
            spad = A.sbuf("spad", [128, 32, 128], BF16)
            for (bb, dst) in ((bbr, L.BBr), (bbi, L.BBi)):
                self.MS(pl, spad[:], 0.0, [spad])
                sv = spad[:].rearrange("p (k j) m -> p k j m", j=4)
                bv = bb[:].rearrange("p (k j) c -> p k j c", j=4)
                for il in range(4):
                    for g2 in range(2):
                        o = 32 * il + 16 * g2
                        self.CP(dv, sv[64 * g2:64 * g2 + 64, :, il, o:o + 16], bv[64 * g2:64 * g2 + 64, :, il, :], [bb], [spad])
                for i0 in range(0, 32, 8):
                    for j in range(8):
                        self.TR(self.pT[:, j * 128:(j + 1) * 128], spad[:, i0 + j, :], self.ident[:], [spad, self.ident], [self.pT])
                    for j in range(8):
                        i = i0 + j
                        hb = (i % 4) // 2
                        self.CP(ac, dst[64 * hb:64 * hb + 64, i // 4, i % 2, :], self.pT[64 * hb:64 * hb + 64, j * 128:(j + 1) * 128], [self.pT], [dst])
            for (csrc, dst, sgn) in ((self.c_re, L.CTr, 1.0), (self.c_im, L.CTi, -1.0)):
                cn = A.sbuf("cn", [32, 2, 16, 64], F32)
                self.DMA(cn[:], csrc[l], [], [cn])
                cn2 = A.sbuf("cn2", [32, 16, 2, 64], F32)
                self.CP(dv, cn2[:].rearrange("p c a n -> p a c n"), cn[:], [cn], [cn2])
                ps = self.nextpA()
                for cc in range(16):
                    self.TR(ps[:, cc * 32:(cc + 1) * 32], cn2[:, cc, :, :].rearrange("p a n -> p (a n)"), self.identf[0:32, 0:32], [cn2, self.identf], [ps])
                cst = A.sbuf("cst", [128, 32, 16], F32)
                self.TS(dv, cst[:].rearrange("p i c -> p c i"), ps[:, 0:512].rearrange("p (c i) -> p c i", c=16), sgn, None, ALU.mult, None, [ps], [cst])
                self.MS(pl, dst[:], 0.0, [dst])
                dv4 = dst[:].rearrange("p (k j) m -> p k j m", j=4)
                cv = cst[:].rearrange("p (k j) c -> p k j c", j=4)
                for il in range(4):
                    for g2 in range(2):
                        o = 32 * il + 16 * g2
                        self.CP(dv, dv4[64 * g2:64 * g2 + 64, :, il, o:o + 16], cv[64 * g2:64 * g2 + 64, :, il, :], [cst], [dst])

    def next_wt(self):
        self.wt_i += 1
        return self.wts[self.wt_i % 3]

    def projA(self, src, N, rhs_of_k, nk, consumer, R):
        for idx, wsrc in enumerate(src):
            wt = self.next_wt()
            self.DMA(wt[:, 0:nk, :], wsrc, [], [wt])
            ps = self.nextpA()
            for k in range(nk):
                self.MM(ps[:, 0:N], wt[:, k, :], rhs_of_k(k), k == 0, k == nk - 1, [wt] + R, [ps])
            consumer(idx, ps)

    def norm_phase(self, src_rows, tiles, hT):
        P, L = self.P, self.L
        with Arena(P) as A:
            xt = A.sbuf("xt", [128, D], F32)
            junk = A.sbuf("junk", [128, D], BF16)
            ss = A.sbuf("ss", [128, 1], F32)
            hb = A.sbuf("hb", [128, D], BF16)
            for (r0, nr) in tiles:
                self.DMA(xt[0:nr], src_rows(r0, nr), [], [xt])
                self.AC(junk[0:nr], xt[0:nr], AF.Square, [xt], [junk, ss], accum_out=ss[0:nr, 0:1])
                self.TS(P.dve, ss[0:nr], ss[0:nr], 1.0 / D, EPS, ALU.mult, ALU.add, [ss], [ss])
                self.AC(ss[0:nr], ss[0:nr], AF.Sqrt, [ss], [ss])
                P.op(P.dve, lambda e: e.reciprocal(out=ss[0:nr], in_=ss[0:nr]), [ss], [ss])
                self.TS(P.dve, hb[0:nr], xt[0:nr], ss[0:nr, 0:1], None, ALU.mult, None, [xt, ss], [hb])
                for k0 in range(0, 16, 8):
                    for kk in range(8):
                        k = k0 + kk
                        self.TR(self.pT[:, kk * 128:kk * 128 + nr], hb[0:nr, k * 128:(k + 1) * 128],
                                self.ident[0:nr, 0:nr], [hb, self.ident], [self.pT])
                    for kk in range(8):
                        k = k0 + kk
                        self.AC(hT[:, k, r0:r0 + nr], self.pT[:, kk * 128:kk * 128 + nr], AF.Copy,
                                [self.pT, L.gpre], [hT], scale=L.gpre[:, k:k + 1])

    def merge_out_phase(self, l, N, tiles, hT, yz, src_rows, dst_rows, dst_bufs):
        P, L = self.P, self.L
        dv, ac = P.dve, P.act
        with Arena(P) as A:
            mT = A.sbuf("mT", [128, 16, N], BF16)
            gs = [A.sbuf(f"gs{i}", [128, N], F32) for i in range(3)]
            tm = [A.sbuf(f"tm{i}", [128, N], F32) for i in range(3)]
            for dt in range(16):
                brp = []
                for i in range(3):
                    wt = self.next_wt()
                    self.DMA(wt[:, 0:8, :], self.wR[l, i, dt], [], [wt])
                    ps = self.pK[i]
                    for k in range(8):
                        self.MM(ps[:, 0:N], wt[:, k, :], yz[i][:, k, 0:N], k == 0, k == 7, [wt, yz[i]], [ps])
                    brp.append(ps)
                for i in range(3):
                    wt = self.next_wt()
                    self.DMA(wt[:], self.wA[l, 48 + 16 * i + dt], [], [wt])
                    ps = self.nextpA()
                    for k in range(16):
                        self.MM(ps[:, 0:N], wt[:, k, :], hT[:, k, 0:N], k == 0, k == 15, [wt, hT], [ps])
                    self.AC(gs[i][:], ps[:, 0:N], AF.Sigmoid, [ps], [gs[i]])
                for i in range(3):
                    self.TT(dv, tm[i][:], gs[i][:], brp[i][:, 0:N], ALU.mult, [gs[i], brp[i]], [tm[i]])
                self.TT(dv, tm[0][:], tm[0][:], tm[1][:], ALU.add, [tm[0], tm[1]], [tm[0]])
                self.TT(dv, mT[:, dt, :], tm[0][:], tm[2][:], ALU.add, [tm[0], tm[2]], [mT])
            if self.debug:
                self.dump(f"mT{l}", mT[:], [128, 16, N], [mT])
            gpb = A.sbuf("gpb", [128, D], F32)
            self.DMA(gpb[:], self.g_post[l:l + 1, :].to_broadcast([128, D]), [], [gpb])
            of = [A.sbuf(f"of{i}", [128, D], F32) for i in range(len(tiles))]
            for ci, c0 in enumerate(range(0, D, 128)):
                w = self.next_wt()
                self.DMA(w[:], self.wO[l][:, :, c0:c0 + 128], [], [w])
                for ti, (r0, nr) in enumerate(tiles):
                    ps = self.pK[ti % 3]
                    for k in range(16):
                        self.MM(ps[0:nr, 0:128], mT[:, k, r0:r0 + nr], w[:, k, :], k == 0, k == 15, [mT, w], [ps])
                    self.CP(ac, of[ti][0:nr, c0:c0 + 128], ps[0:nr, 0:128], [ps], [of[ti]])
            junk = A.sbuf("junk2", [128, D], BF16)
            ss = A.sbuf("ss2", [128, 1], F32)
            xt = A.sbuf("xt2", [128, D], F32)
            for ti, (r0, nr) in enumerate(tiles):
                o = of[ti]
                self.DMA(xt[0:nr], src_rows(r0, nr), [], [xt])
                self.AC(junk[0:nr], o[0:nr], AF.Square, [o], [junk, ss], accum_out=ss[0:nr, 0:1])
                self.TS(dv, ss[0:nr], ss[0:nr], 1.0 / D, EPS, ALU.mult, ALU.add, [ss], [ss])
                self.AC(ss[0:nr], ss[0:nr], AF.Sqrt, [ss], [ss])
                P.op(dv, lambda e: e.reciprocal(out=ss[0:nr], in_=ss[0:nr]), [ss], [ss])
                self.STT(dv, o[0:nr], o[0:nr], ss[0:nr, 0:1], gpb[0:nr], ALU.mult, ALU.mult, [o, ss, gpb], [o])
                self.TT(dv, o[0:nr], o[0:nr], xt[0:nr], ALU.add, [o, xt], [o])
                self.DMA(dst_rows(r0, nr), o[0:nr], [o], [dst_bufs[ti]] if dst_bufs else [])

    def pool_phase(self, l, N, hT, yzp, halo, first, nseg=1):
        P, L = self.P, self.L
        dv, ac, pl = P.dve, P.act, P.pool
        n1 = N // nseg
        W = 15 + n1
        with Arena(P) as A:
            pu = A.sbuf("pu", [128, 8, nseg, W], F32)
            spz = A.sbuf("spz", [128, 8, N], BF16)
            sA = A.sbuf("sA", [128, 8, nseg, W], F32)
            sB = A.sbuf("sB", [128, 8, nseg, W], F32)
            df = A.sbuf("df", [128, 8, N], BF16)
            self.CP(dv, pu[:, :, :, 0:15], halo[:], [halo], [pu])
            self.projA([self.wA[l, t] for t in range(0, 8)], N, lambda k: hT[:, k, 0:N], 16,
                       lambda i, ps: self.CP(ac, pu[:, i, :, 15:W], ps[:, 0:N].rearrange("p (s c) -> p s c", s=nseg), [ps], [pu]), [hT])
            self.projA([self.wA[l, t] for t in range(8, 16)], N, lambda k: hT[:, k, 0:N], 16,
                       lambda i, ps: self.AC(spz[:, i, :], ps[:, 0:N], AF.Silu, [ps], [spz]), [hT])
            if self.debug and nseg == 1:
                self.dump(f"pu{l}", pu[:, :, 0, 15:W], [128, 8, n1], [pu])
            self.TT(dv, sA[:, :, :, 1:W], pu[:, :, :, 1:W], pu[:, :, :, 0:W - 1], ALU.add, [pu], [sA])
            self.TT(dv, sB[:, 2:8, :, 3:W], sA[:, 2:8, :, 3:W], sA[:, 2:8, :, 1:W - 2], ALU.add, [sA], [sB])
            self.TT(dv, sA[:, 4:8, :, 7:W], sB[:, 4:8, :, 7:W], sB[:, 4:8, :, 3:W - 4], ALU.add, [sB], [sA])
            self.TT(dv, sB[:, 6:8, :, 15:W], sA[:, 6:8, :, 15:W], sA[:, 6:8, :, 7:W - 8], ALU.add, [sA], [sB])
            dfv = df[:].rearrange("p t (s c) -> p t s c", s=nseg)
            for gi, (src, w) in enumerate(((sA, 2), (sB, 4), (sA, 8), (sB, 16))):
                t0 = 2 * gi
                self.STT(dv, dfv[:, t0:t0 + 2], src[:, t0:t0 + 2, :, 15:W], 1.0 / w, pu[:, t0:t0 + 2, :, 15:W],
                         ALU.mult, ALU.subtract, [src, pu], [df])
                if first:
                    if not hasattr(A, "_pt"):
                        A._pt = A.sbuf("ptmp", [128, 2, 15], F32)
                    tmp = A._pt
                    self.TT(dv, tmp[:], src[:, t0:t0 + 2, 0, 15:30], self.rc[:, gi:gi + 1, :].to_broadcast([128, 2, 15]),
                            ALU.mult, [src, self.rc], [tmp])
                    self.TT(dv, df[:, t0:t0 + 2, 0:15], tmp[:], pu[:, t0:t0 + 2, 0, 15:30], ALU.subtract, [tmp, pu], [df])
            for t in range(8):
                g = t // 2
                ps = self.nextpA()
                for kc in range(2):
                    self.MM(ps[:, 0:N], L.wpool[:, g, kc, (t % 2) * 128:(t % 2) * 128 + 128], df[:, 2 * g + kc, :],
                            kc == 0, kc == 1, [L.wpool, df], [ps])
                self.STT(dv, yzp[:, t, 0:N], ps[:, 0:N], L.pscale[:, t:t + 1], spz[:, t, :], ALU.mult, ALU.mult,
                         [ps, L.pscale, spz], [yzp])
            self.CP(dv, halo[:], pu[:, :, :, n1:W], [pu], [halo])
            return None

    def pool_state_out(self, halo_seg_ap, halo_buf, dst_ap):
        P = self.P
        with Arena(P) as A:
            po = A.sbuf("po", [15, 1024], F32)
            for half in range(2):
                ps = self.nextpA()
                for t in range(4):
                    self.TR(ps[0:15, t * 128:(t + 1) * 128], halo_seg_ap(4 * half + t), self.identf[:], [halo_buf, self.identf], [ps])
                self.CP(P.act, po[0:15, half * 512:(half + 1) * 512], ps[0:15, 0:512], [ps], [po])
            self.DMA(dst_ap, po[:], [po], [])

    def ssm_phase(self, l, N, hT, yzs, st, nseg=1):
        P, L = self.P, self.L
        dv, ac, pl = P.dve, P.act, P.pool
        n1 = N // nseg
        tc = min(TC, n1)
        nsub = n1 // tc
        F = 16 * nseg * tc
        with Arena(P) as A:
            su = A.sbuf("su", [128, 8, N], BF16)
            ssz = A.sbuf("ssz", [128, 8, N], BF16)
            zT = A.sbuf("zT", [128, 8, N], BF16)
            self.projA([self.wA[l, t] for t in range(32, 40)], N, lambda k: hT[:, k, 0:N], 16,
                       lambda i, ps: self.CP(ac, su[:, i, :], ps[:, 0:N], [ps], [su]), [hT])
            self.projA([self.wA[l, t] for t in range(40, 48)], N, lambda k: hT[:, k, 0:N], 16,
                       lambda i, ps: self.AC(ssz[:, i, :], ps[:, 0:N], AF.Silu, [ps], [ssz]), [hT])
            if self.debug and nseg == 1:
                self.dump(f"su{l}", su[:], [128, 8, N], [su])

            def arr(name, dt=F32):
                return A.sbuf(name, [128, 16, nseg, tc], dt)

            bur, bui, t1, t2, gr, gi, kr, ki, hr, hi, t3, t4 = (arr(n) for n in
                                                        ("bur", "bui", "t1", "t2", "gr", "gi", "kr", "ki", "hr", "hi", "t3", "t4"))
            hrb, hib = arr("hrb", BF16), arr("hib", BF16)
            yf = A.sbuf("yf", [128, 4, nseg, tc], F32)
            c1 = A.sbuf("c1", [128, 16, nseg], F32)
            c2 = A.sbuf("c2", [128, 16, nseg], F32)
            suv = su[:].rearrange("p k (s c) -> p k s c", s=nseg)
            zv = zT[:].rearrange("p k (s c) -> p k s c", s=nseg)
            for sc in range(nsub if KSTOP >= 2 else 0):
                c0 = sc * tc
                for hf in range(2):
                    i0 = 16 * hf
                    pbr, pbi, py = self.pK[0], self.pK[1], self.pK[2]
                    for ii in sorted(range(16), key=lambda q: (((i0 + q) % 4) // 2, q)):
                        i = i0 + ii
                        kt, hb, e = i // 4, (i % 4) // 2, i % 2
                        for s in range(nseg):
                            o = (ii * nseg + s) * tc
                            rhs = suv[64 * hb:64 * hb + 64, kt, s, c0:c0 + tc]
                            self.MM(pbr[:, o:o + tc], L.BBr[64 * hb:64 * hb + 64, kt, e, :], rhs, True, True, [L.BBr, su], [pbr])
                            self.MM(pbi[:, o:o + tc], L.BBi[64 * hb:64 * hb + 64, kt, e, :], rhs, True, True, [L.BBi, su], [pbi])
                    fl = lambda b: b[:].rearrange("p a s c -> p (a s c)")
                    self.CP(ac, fl(bur), pbr[:, 0:F], [pbr], [bur])
                    self.CP(ac, fl(bui), pbi[:, 0:F], [pbi], [bui])
                    if KSTOP < 3:
                        continue
                    cs = L.cosT[:, i0:i0 + 16, 0:tc].unsqueeze(2).to_broadcast([128, 16, nseg, tc])
                    sn = L.sinT[:, i0:i0 + 16, 0:tc].unsqueeze(2).to_broadcast([128, 16, nseg, tc])
                    rz = L.rhoz[:, i0:i0 + 16, 0:tc].unsqueeze(2).to_broadcast([128, 16, nseg, tc])
                    self.TT(dv, t1[:], bur[:], cs, ALU.mult, [bur, L.cosT], [t1])
                    self.TT(dv, t2[:], bui[:], sn, ALU.mult, [bui, L.sinT], [t2])
                    self.TT(dv, gr[:], t1[:], t2[:], ALU.add, [t1, t2], [gr])
                    self.TT(dv, t1[:], bui[:], cs, ALU.mult, [bui, L.cosT], [t1])
                    self.TT(dv, t2[:], bur[:], sn, ALU.mult, [bur, L.sinT], [t2])
                    self.TT(dv, gi[:], t1[:], t2[:], ALU.subtract, [t1, t2], [gi])
                    self.TT(dv, gr[:, :, :, 0], gr[:, :, :, 0], st["car_r"][:, i0:i0 + 16, :], ALU.add, [gr, st["car_r"]], [gr])
                    self.TT(dv, gi[:, :, :, 0], gi[:, :, :, 0], st["car_i"][:, i0:i0 + 16, :], ALU.add, [gi, st["car_i"]], [gi])
                    if KSTOP < 4:
                        continue
                    if nseg == 1:
                        rzf = L.rhoz[:, i0:i0 + 16, 0:tc] if tc == TC else None
                    else:
                        rzf = None
                    if rzf is None:
                        rzm = A.sbuf("rzm", [128, 16, nseg, tc], F32)
                        self.CP(dv, rzm[:], rz, [L.rhoz], [rzm])
                        rz2 = fl(rzm)
                        rzR = [rzm]
                    else:
                        rz2 = rzf.rearrange("p a c -> p (a c)")
                        rzR = [L.rhoz]
                    P.op(dv, lambda e: e.tensor_tensor_scan(out=fl(kr), data0=rz2, data1=fl(gr), initial=0.0,
                                                            op0=ALU.mult, op1=ALU.add), rzR + [gr], [kr])
                    P.op(dv, lambda e: e.tensor_tensor_scan(out=fl(ki), data0=rz2, data1=fl(gi), initial=0.0,
                                                            op0=ALU.mult, op1=ALU.add), rzR + [gi], [ki])
                    if KSTOP < 5:
                        continue
                    self.TT(dv, t3[:], kr[:], cs, ALU.mult, [kr, L.cosT], [t3])
                    self.TT(dv, t4[:], ki[:], sn, ALU.mult, [ki, L.sinT], [t4])
                    self.TT(dv, hr[:], t3[:], t4[:], ALU.subtract, [t3, t4], [hr])
                    self.TT(dv, t3[:], kr[:], sn, ALU.mult, [kr, L.sinT], [t3])
                    self.TT(dv, t4[:], ki[:], cs, ALU.mult, [ki, L.cosT], [t4])
                    self.TT(dv, hi[:], t3[:], t4[:], ALU.add, [t3, t4], [hi])
                    self.CP(ac, hrb[:], hr[:], [hr], [hrb])
                    self.CP(ac, hib[:], hi[:], [hi], [hib])
                    hlr, hli = st["hl_r"], st["hl_i"]
                    self.CP(dv, hlr[:, i0:i0 + 16, :], hr[:, :, :, tc - 1], [hr], [hlr])
                    self.CP(dv, hli[:, i0:i0 + 16, :], hi[:, :, :, tc - 1], [hi], [hli])
                    ab_r = L.abr[:, i0:i0 + 16].unsqueeze(2).to_broadcast([128, 16, nseg])
                    ab_i = L.abi[:, i0:i0 + 16].unsqueeze(2).to_broadcast([128, 16, nseg])
                    self.TT(dv, c1[:], hlr[:, i0:i0 + 16, :], ab_r, ALU.mult, [hlr, L.abr], [c1])
                    self.TT(dv, c2[:], hli[:, i0:i0 + 16, :], ab_i, ALU.mult, [hli, L.abi], [c2])
                    self.TT(dv, st["car_r"][:, i0:i0 + 16, :], c1[:], c2[:], ALU.subtract, [c1, c2], [st["car_r"]])
                    self.TT(dv, c1[:], hli[:, i0:i0 + 16, :], ab_r, ALU.mult, [hli, L.abr], [c1])
                    self.TT(dv, c2[:], hlr[:, i0:i0 + 16, :], ab_i, ALU.mult, [hlr, L.abi], [c2])
                    self.TT(dv, st["car_i"][:, i0:i0 + 16, :], c1[:], c2[:], ALU.add, [c1, c2], [st["car_i"]])
                    if KSTOP < 6:
                        continue
                    for ko in range(4):
                        kt = 4 * hf + ko
                        for s in range(nseg):
                            o = (ko * nseg + s) * tc
                            for il in range(4):
                                i = 4 * kt + il
                                ii = i - i0
                                self.MM(py[:, o:o + tc], L.CTr[:, i, :], hrb[:, ii, s, :], il == 0, False, [L.CTr, hrb], [py])
                                self.MM(py[:, o:o + tc], L.CTi[:, i, :], hib[:, ii, s, :], False, il == 3, [L.CTi, hib], [py])
                    if KSTOP < 7:
                        continue
                    for ko in range(4):
                        kt = 4 * hf + ko
                        self.STT(dv, yf[:, ko], suv[:, kt, :, c0:c0 + tc], L.dskip[:, kt:kt + 1],
                                 py[:, ko * nseg * tc:(ko + 1) * nseg * tc].rearrange("p (s c) -> p s c", s=nseg),
                                 ALU.mult, ALU.add, [su, L.dskip, py], [yf])
                    self.AC(zv[:, 4 * hf:4 * hf + 4, :, c0:c0 + tc], yf[:], AF.Gelu_apprx_tanh, [yf], [zT])
            if self.debug and nseg == 1:
                self.dump(f"zT{l}", zT[:], [128, 8, N], [zT])
            if KSTOP < 8:
                self.MS(dv, yzs[:], 0.0, [yzs])
                return
            sgt = [A.sbuf(f"sgt{i}", [128, N], BF16) for i in range(2)]

            def glu_cons(i, ps):
                sg = sgt[i % 2]
                self.AC(sg[:], ps[:, 0:N], AF.Sigmoid, [ps], [sg])
                self.TT(dv, sg[:], sg[:], zT[:, i, :], ALU.mult, [sg, zT], [sg])
                self.TT(dv, yzs[:, i, 0:N], sg[:], ssz[:, i, :], ALU.mult, [sg, ssz], [yzs])

            self.projA([self.wG[l, t] for t in range(8)], N, lambda k: zT[:, k, 0:N], 8, glu_cons, [zT])

    def ssm_state_out(self, hl_r_ap, hl_i_ap, bufs, dst_ap):
        P = self.P
        with Arena(P) as A:
            o = A.sbuf("sso", [32, 2, 128], F32)
            ps = self.nextpA()
            self.TR(ps[0:32, 0:128], hl_r_ap, self.identf[:], bufs + [self.identf], [ps])
            self.TR(ps[0:32, 128:256], hl_i_ap, self.identf[:], bufs + [self.identf], [ps])
            self.CP(P.act, o[:].rearrange("p a n -> p (a n)"), ps[0:32, 0:256], [ps], [o])
            self.DMA(dst_ap.rearrange("a i p -> i a p"), o[:], [o], [])

    def q_to_base(self, A, N, i, ps, q0):
        P = self.P
        ac = P.act
        for hh in range(2):
            h = 2 * i + hh
            k, g = h // 4, h % 4
            slot = (k // 2) * 4 + g
            need, have = 64 * (k % 2), 64 * hh
            if need == have or os.environ.get('KQS') == '0':
                self.CP(ac, q0[have:have + 64, slot, 0:N], ps[have:have + 64, 0:N], [ps], [q0])
            else:
                if not hasattr(A, "_tq"):
                    A._tq = A.sbuf("tq", [128, N], BF16)
                tq = A._tq
                self.CP(ac, tq[have:have + 64, :], ps[have:have + 64, 0:N], [ps], [tq])
                p2 = self.pS[hh]
                if have == 64:
                    self.MM(p2[0:64, 0:N], self.ident[64:128, 64:128], tq[64:128, :], True, True, [self.ident, tq], [p2])
                    self.CP(ac, q0[0:64, slot, 0:N], p2[0:64, 0:N], [p2], [q0])
                else:
                    self.MM(p2[:, 0:N], self.shu[0:64, :], tq[0:64, :], True, True, [self.shu, tq], [p2])
                    self.CP(ac, q0[64:128, slot, 0:N], p2[64:128, 0:N], [p2], [q0])

    def kv_formB(self, l, A, hT, tiles, kvf):
        for ci, c0 in enumerate(range(0, NKVB, 128)):
            n = min(128, NKVB - c0)
            w = self.next_wt()
            self.DMA(w[:, :, 0:n], self.wB[l][:, :, c0:c0 + n], [], [w])
            for ti, (r0, nr) in enumerate(tiles):
                ps = self.pK[ti % 3]
                for k in range(16):
                    self.MM(ps[0:nr, 0:n], hT[:, k, r0:r0 + nr], w[:, k, 0:n], k == 0, k == 15, [hT, w], [ps])
                self.CP(self.P.act, kvf[ti][0:nr, c0:c0 + n], ps[0:nr, 0:n], [ps], [kvf[ti]])

    def nsa_phase(self, l, j, hT, yza, ps_):
        P, L, C, S = self.P, self.L, self.C, self.S
        dv, ac, pl = P.dve, P.act, P.pool
        NT = S // 128
        with Arena(P) as A:
            q0 = A.sbuf("q0", [128, 8, C], BF16)
            saz = A.sbuf("saz", [128, 8, C], BF16)
            self.projA([self.wA[l, t] for t in range(16, 24)], C, lambda k: hT[:, k, 0:C], 16,
                       lambda i, ps: self.q_to_base(A, C, i, ps, q0), [hT])
            self.projA([self.wA[l, t] for t in range(24, 32)], C, lambda k: hT[:, k, 0:C], 16,
                       lambda i, ps: self.AC(saz[:, i, :], ps[:, 0:C], AF.Silu, [ps], [saz]), [hT])
            if KSTOP < 2:
                self.MS(dv, yza[:], 0.0, [yza])
                return
            tiles = [(tt * 128, 128) for tt in range(C // 128)]
            kvf = [A.sbuf(f"kvf{i}", [128, NKVB], F32) for i in range(len(tiles))]
            self.kv_formB(l, A, hT, tiles, kvf)
            if KSTOP < 3:
                self.MS(dv, yza[:], 0.0, [yza])
                return
            sg = A.sbuf("sg", [128, len(tiles), 48], F32)
            xtk = A.sbuf("xtk", [128, 2, C], BF16)
            xtv = A.sbuf("xtv", [128, 2, C], BF16)
            kvb = A.sbuf("kvb", [128, 1536], BF16)
            yb = A.sbuf("yb", [128, 1024], BF16)
            for tt, (r0, nr) in enumerate(tiles):
                T = (j * C) // 128 + tt
                self.DMA(self.kvp[l][T * 128:(T + 1) * 128, :], kvf[tt][:, 0:1024], [kvf[tt]], [])
                if T >= NT - 4:
                    w0 = (T - (NT - 4)) * 128
                    self.DMA(self.winp[l][w0:w0 + 128, :], kvf[tt][:, 1024:1536], [kvf[tt]], [])
                if KSUB < 1:
                    continue
                self.CP(dv, kvb[:], kvf[tt][:, 0:1536], [kvf[tt]], [kvb])
                self.AC(sg[:, tt, :], kvf[tt][:, 1536:1584], AF.Sigmoid, [kvf[tt]], [sg])
                if KSUB < 2:
                    continue
                self.CP(ac, ps_["vsel"][:, T, :, 0:64], kvb[:, 768:1024].rearrange("p (h d) -> p h d", h=4), [kvb], [ps_["vsel"]])
                self.CP(ac, ps_["vwin"][:, T % 8, :, 0:64], kvb[:, 1280:1536].rearrange("p (h d) -> p h d", h=4), [kvb], [ps_["vwin"]])
                if KSUB < 3:
                    continue
                if os.environ.get("KBAR") == "1":
                    P.barrier()
                tb = [self.pS[0], self.pS[1]]
                for n_, c_ in enumerate((512, 640, 1024, 1152, 0, 128, 256, 384)):
                    self.MT(tb[n_ // 4][:, (n_ % 4) * 128:(n_ % 4 + 1) * 128], kvb[:, c_:c_ + 128], [kvb], [tb[n_ // 4]])
                if KSUB < 4:
                    continue
                pv = lambda a: tb[a // 2][:, (a % 2) * 256:(a % 2 + 1) * 256].rearrange("p (r t) -> p r t", r=2)
                self.CP(ac, ps_["ksT"][:, :, T * 128:(T + 1) * 128], pv(0), [tb[0]], [ps_["ksT"]])
                self.CP(ac, ps_["kwT"][:, :, (T % 8) * 128:(T % 8 + 1) * 128], pv(1), [tb[0]], [ps_["kwT"]])
                self.CP(dv, xtk[:, :, tt * 128:(tt + 1) * 128], pv(2), [tb[1]], [xtk])
                self.CP(dv, xtv[:, :, tt * 128:(tt + 1) * 128], pv(3), [tb[1]], [xtv])
            nbn = C // 64
            b0 = (j * C) // 64
            for a_, xt_ in ((0, xtk), (1, xtv)):
                v4 = xt_[:].rearrange("p r (n l) -> p r n l", l=64)
                self.TT(dv, v4, v4, L.pe2[:, a_, :].unsqueeze(1).unsqueeze(1).to_broadcast([128, 2, nbn, 64]), ALU.add, [xt_, L.pe2], [xt_])
            pkc, pvc = self.pS[0], self.pS[1]
            for k in range(4):
                base, pr = 64 * (k % 2), k // 2
                for a_, xt_, pso, ob in ((0, xtk, pkc, base), (1, xtv, pvc, 0)):
                    col = (pr * nbn) if a_ == 0 else (k * nbn)
                    xl = xt_[:].rearrange("p r (n l) -> p r l n", l=64)
                    for ll in range(64):
                        self.MM(pso[ob:ob + 64, col:col + nbn], L.wphi2[base:base + 64, a_, ll, :], xl[base:base + 64, pr, ll, :],
                                ll == 0, ll == 63, [L.wphi2, xt_], [pso])
            self.CP(ac, ps_["kcT"][:, :, b0:b0 + nbn], pkc[:, 0:2 * nbn].rearrange("p (r n) -> p r n", r=2), [pkc], [ps_["kcT"]])
            self.CP(ac, ps_["vcT"][0:64, :, b0:b0 + nbn], pvc[0:64, 0:4 * nbn].rearrange("p (k n) -> p k n", k=4), [pvc], [ps_["vcT"]])
            pvt = self.pS[0]
            for k in range(4):
                self.MT(pvt[0:64, k * 64:(k + 1) * 64], ps_["vcT"][0:64, k, :], [ps_["vcT"]], [pvt])
            self.CP(ac, ps_["vca"][0:64, :, 0:64], pvt[0:64, 0:256].rearrange("p (k e) -> p k e", k=4), [pvt], [ps_["vca"]])
            if KSTOP < 5:
                self.MS(dv, yza[:], 0.0, [yza])
                return
            T0 = (j * C) // 128
            ntt = len(tiles)
            cbq = A.sbuf("cbq", [128, ntt, 64], F32)
            tsl = A.sbuf("tsl", [128, ntt, 64], F32)
            cbTf = A.sbuf("cbTf", [64, C], F32)
            cbT = A.sbuf("cbT", [64, C], BF16)
            self.DMA(cbq[:], self.t_cbq[T0:T0 + ntt].rearrange("t p n -> p t n"), [], [cbq])
            self.DMA(tsl[:], self.t_sel[T0:T0 + ntt].rearrange("t p n -> p t n"), [], [tsl])
            self.DMA(cbTf[:], self.t_cbT[:, j * C:(j + 1) * C], [], [cbTf])
            self.CP(dv, cbT[:], cbTf[:], [cbTf], [cbT])
            BT = A.sbuf("BT", [64, 4, C], BF16)
            s1 = A.sbuf("s1", [128, 4, 64], F32)
            z4 = A.sbuf("z4", [128, 4], F32)
            sc_ = A.sbuf("sc", [128, 64], F32)
            mx = A.sbuf("mx", [128, 8], F32)
            w1 = A.sbuf("w1", [128, 64], F32)
            bq = A.sbuf("bq", [128, 64], BF16)
            for tt in range(ntt):
                for k in range(4):
                    base, pr = 64 * (k % 2), k // 2
                    psc = self.pS[(tt * 4 + k) % 2]
                    for g in range(4):
                        self.MM(psc[:, g * 64:(g + 1) * 64], q0[base:base + 64, pr * 4 + g, tt * 128:(tt + 1) * 128],
                                ps_["kcT"][base:base + 64, pr, :], True, True, [q0, ps_["kcT"]], [psc])
                    self.TT(dv, s1[:], psc[:, 0:256].rearrange("p (g n) -> p g n", g=4),
                            cbq[:, tt, :].unsqueeze(1).to_broadcast([128, 4, 64]), ALU.add, [psc, cbq], [s1])
                    self.AC(s1[:], s1[:], AF.Exp, [s1], [s1], scale=0.125)
                    P.op(dv, lambda e: e.tensor_reduce(out=z4[:], in_=s1[:], axis=AX.X, op=ALU.add), [s1], [z4])
                    self.TS(dv, z4[:], z4[:], 1e-30, None, ALU.max, None, [z4], [z4])
                    P.op(dv, lambda e: e.reciprocal(out=z4[:], in_=z4[:]), [z4], [z4])
                    self.TT(dv, s1[:], s1[:], z4[:].unsqueeze(2).to_broadcast([128, 4, 64]), ALU.mult, [s1, z4], [s1])
                    P.op(dv, lambda e: e.tensor_reduce(out=sc_[:], in_=s1[:].rearrange("p g n -> p n g"), axis=AX.X, op=ALU.add), [s1], [sc_])
                    self.TT(dv, sc_[:], sc_[:], tsl[:, tt, :], ALU.add, [sc_, tsl], [sc_])
                    P.op(dv, lambda e: e.max(out=mx[:], in_=sc_[:]), [sc_], [mx])
                    P.op(dv, lambda e: e.match_replace(out=w1[:], in_to_replace=mx[:], in_values=sc_[:], imm_value=NEG), [mx, sc_], [w1])
                    P.op(dv, lambda e: e.max(out=mx[:], in_=w1[:]), [w1], [mx])
                    P.op(dv, lambda e: e.match_replace(out=w1[:], in_to_replace=mx[:], in_values=w1[:], imm_value=NEG), [mx, w1], [w1])
                    self.TT(dv, w1[:], sc_[:], w1[:], ALU.subtract, [sc_, w1], [w1])
                    self.TS(dv, w1[:], w1[:], 1.0, -1.0, ALU.min, ALU.add, [w1], [w1])
                    self.TS(dv, bq[:], w1[:], 1e30, None, ALU.mult, None, [w1], [bq])
                    pbt = self.pA[(tt * 4 + k) % 2]
                    self.MT(pbt[0:64, 0:128], bq[:], [bq], [pbt])
                    self.CP(ac, BT[0:64, k, tt * 128:(tt + 1) * 128], pbt[0:64, 0:128], [pbt], [BT])
            if KSTOP < 6:
                self.MS(dv, yza[:], 0.0, [yza])
                return
            yat = [A.sbuf(f"yat{i}", [128, 1024], F32) for i in range(ntt)]
            zz = A.sbuf("zz", [128, 4], F32)
            ytmp = A.sbuf("ytmp", [128, 4, 64], F32)
            pts = [A.sbuf(f"pt{i}", [128, C], BF16) for i in range(3)]
            pti = [0]
            scp = [self.pS[0], self.pS[1], self.pA[0], self.pA[1]]
            sci = [0]
            Gf = self.G[:].rearrange("p j c -> p (j c)")

            def score_exp(nk, mms):
                sci[0] += 1
                psx = scp[sci[0] % 4]
                for mi, (lhsT, rhs, R) in enumerate(mms):
                    self.MM(psx[0:nk, 0:C], lhsT, rhs, mi == 0, mi == len(mms) - 1, R, [psx])
                pti[0] += 1
                pt = pts[pti[0] % 3]
                self.AC(pt[0:nk, :], psx[0:nk, 0:C], AF.Exp, [psx], [pt], scale=0.125)
                return pt

            for k in range(4):
                base, pr = 64 * (k % 2), k // 2
                po = [self.pK[tt] for tt in range(ntt)]
                pov = [po[tt][:, 0:260].rearrange("p (g e) -> p g e", g=4) for tt in range(ntt)]
                for bi in range(3):
                    for g in range(4):
                        slot = pr * 4 + g
                        qh = q0[base:base + 64, slot, :]
                        if bi == 0:
                            pt = score_exp(64, [(ps_["kcT"][base:base + 64, pr, :], qh, [ps_["kcT"], q0]),
                                                (self.ident[0:64, 0:64], cbT[:, :], [self.ident, cbT])])
                            for tt in range(ntt):
                                self.MM(pov[tt][:, g, :], pt[0:64, tt * 128:(tt + 1) * 128], ps_["vca"][0:64, k, :], True, True,
                                        [pt, ps_["vca"]], [po[tt]])
                        elif bi == 1:
                            last = T0 + ntt - 1
                            for kt in range(0, last + 1):
                                pt = score_exp(128, [(ps_["ksT"][base:base + 64, pr, kt * 128:(kt + 1) * 128], qh, [ps_["ksT"], q0]),
                                                     (Gf[:, kt * 128:(kt + 1) * 128], BT[0:64, k, :], [self.G, BT])])
                                for tt in range(ntt):
                                    T = T0 + tt
                                    if kt == T:
                                        self.TT(dv, pt[:, tt * 128:(tt + 1) * 128], pt[:, tt * 128:(tt + 1) * 128], self.tri[:], ALU.mult, [pt, self.tri], [pt])
                                for tt in range(ntt):
                                    T = T0 + tt
                                    if kt <= T:
                                        self.MM(pov[tt][:, g, :], pt[:, tt * 128:(tt + 1) * 128], ps_["vsel"][:, kt, k, :], kt == 0, kt == T,
                                                [pt, ps_["vsel"]], [po[tt]])
                        else:
                            for kt in range(max(0, T0 - 4), T0 + ntt):
                                pt = score_exp(128, [(ps_["kwT"][base:base + 64, pr, (kt % 8) * 128:(kt % 8 + 1) * 128], qh, [ps_["kwT"], q0])])
                                for tt in range(ntt):
                                    T = T0 + tt
                                    if kt == T:
                                        self.TT(dv, pt[:, tt * 128:(tt + 1) * 128], pt[:, tt * 128:(tt + 1) * 128], self.tri[:], ALU.mult, [pt, self.tri], [pt])
                                    elif kt == T - 4:
                                        self.TT(dv, pt[:, tt * 128:(tt + 1) * 128], pt[:, tt * 128:(tt + 1) * 128], self.tris[:], ALU.mult, [pt, self.tris], [pt])
                                for tt in range(ntt):
                                    T = T0 + tt
                                    if T - 4 <= kt <= T:
                                        self.MM(pov[tt][:, g, :], pt[:, tt * 128:(tt + 1) * 128], ps_["vwin"][:, kt % 8, k, :],
                                                kt == max(0, T - 4), kt == T, [pt, ps_["vwin"]], [po[tt]])
                    for tt in range(ntt):
                        z4 = zz
                        self.TS(dv, z4[:], pov[tt][:, :, 64], 1e-30, None, ALU.max, None, [po[tt]], [z4])
                        P.op(dv, lambda e: e.reciprocal(out=z4[:], in_=z4[:]), [z4], [z4])
                        gsel = sg[:, tt, 12 * k:12 * k + 12].rearrange("p (g i) -> p g i", i=3)[:, :, bi]
                        self.TT(dv, z4[:], z4[:], gsel, ALU.mult, [z4, sg], [z4])
                        yv = yat[tt][:, k * 256:(k + 1) * 256].rearrange("p (g d) -> p g d", g=4)
                        fb = z4[:].unsqueeze(2).to_broadcast([128, 4, 64])
                        if bi == 0:
                            self.TT(dv, yv, pov[tt][:, :, 0:64], fb, ALU.mult, [po[tt], z4], [yat[tt]])
                        else:
                            tmp = ytmp
                            self.TT(dv, tmp[:], pov[tt][:, :, 0:64], fb, ALU.mult, [po[tt], z4], [tmp])
                            self.TT(dv, yv, yv, tmp[:], ALU.add, [yat[tt], tmp], [yat[tt]])
            if KSTOP < 7:
                self.MS(dv, yza[:], 0.0, [yza])
                return
            for tt in range(ntt):
                if self.debug:
                    self.dump(f"yat{l}_{j}_{tt}", yat[tt][:], [128, 1024], [yat[tt]])
                self.CP(dv, yb[:], yat[tt][:], [yat[tt]], [yb])
                for hf4 in range(2):
                    pyt = self.pS[hf4]
                    for t4 in range(4):
                        t8 = 4 * hf4 + t4
                        self.MT(pyt[:, t4 * 128:(t4 + 1) * 128], yb[:, t8 * 128:(t8 + 1) * 128], [yb], [pyt])
                    self.TT(dv, yza[:, 4 * hf4:4 * hf4 + 4, tt * 128:(tt + 1) * 128], pyt[:, :].rearrange("p (t c) -> p t c", t=4),
                            saz[:, 4 * hf4:4 * hf4 + 4, tt * 128:(tt + 1) * 128], ALU.mult, [pyt, saz], [yza])

    def prompt_layer(self, l):
        P, S, C = self.P, self.S, self.C
        dv, pl = P.dve, P.pool
        NT = S // 128
        with Arena(P) as PA:
            ps_ = {
                "ksT": PA.sbuf("ksT", [128, 2, S], BF16),
                "vsel": PA.sbuf("vsel", [128, NT, 4, 65], BF16),
                "kwT": PA.sbuf("kwT", [128, 2, 1024], BF16),
                "vwin": PA.sbuf("vwin", [128, 8, 4, 65], BF16),
                "kcT": PA.sbuf("kcT", [128, 2, 64], BF16),
                "vcT": PA.sbuf("vcT", [64, 4, 64], BF16),
                "vca": PA.sbuf("vca", [64, 4, 65], BF16),
            }
            halo = PA.sbuf("halo", [128, 8, 1, 15], F32)
            st = {n: PA.sbuf(n, [128, 32, 1], F32) for n in ("car_r", "car_i", "hl_r", "hl_i")}
            self.MS(pl, ps_["vsel"][:], 1.0, [ps_["vsel"]])
            self.MS(pl, ps_["vwin"][:], 1.0, [ps_["vwin"]])
            self.MS(pl, ps_["vca"][:], 1.0, [ps_["vca"]])
            self.MS(pl, ps_["kcT"][:], 0.0, [ps_["kcT"]])
            self.MS(pl, ps_["vcT"][:], 0.0, [ps_["vcT"]])
            self.MS(dv, halo[:], 0.0, [halo])
            for b in st.values():
                self.MS(dv, b[:], 0.0, [b])
            src = self.xp if l == 0 else None
            for j in range(self.NCH):
                tiles = [(tt * 128, 128) for tt in range(C // 128)]
                T0 = (j * C) // 128

                def src_rows(r0, nr, j=j):
                    if l == 0:
                        return self.xp[j * C + r0:j * C + r0 + nr, :]
                    return self.x1_tiles[(j * C + r0) // 128][0:nr, :]

                def dst_rows(r0, nr, j=j):
                    if l == 0:
                        return self.x1_tiles[(j * C + r0) // 128][0:nr, :]
                    return self.yp[j * C + r0:j * C + r0 + nr, :]

                with Arena(P) as CA:
                    hT = CA.sbuf("hT", [128, 16, C], BF16)
                    yz = [CA.sbuf(f"yz{i}", [128, 8, C], BF16) for i in range(3)]
                    if l == 1:
                        for tt in range(C // 128):
                            xb_ = self.x1_tiles[T0 + tt]
                            for d in list(xb_.w.values()):
                                P._wait(P.sp, d)
                    self.norm_phase(src_rows, tiles, hT)
                    if "pool" in self.parts:
                        self.pool_phase(l, C, hT, yz[0], halo, first=(j == 0))
                    else:
                        self.MS(dv, yz[0][:], 0.0, [yz[0]])
                    if "nsa" in self.parts:
                        self.nsa_phase(l, j, hT, yz[1], ps_)
                    else:
                        self.MS(dv, yz[1][:], 0.0, [yz[1]])
                    if "ssm" in self.parts:
                        self.ssm_phase(l, C, hT, yz[2], st)
                    else:
                        self.MS(dv, yz[2][:], 0.0, [yz[2]])
                    if self.debug and j == 0:
                        for i in range(3):
                            self.dump(f"yz{i}_{l}", yz[i][:], [128, 8, C], [yz[i]])
                    dst_bufs = [self.x1_tiles[T0 + tt] for tt in range(C // 128)] if l == 0 else None
                    self.merge_out_phase(l, C, tiles, hT, yz, src_rows, dst_rows, dst_bufs)
            self.pool_state_out(lambda t: halo[:, t, 0, :], halo, self.poolp[l])
            self.ssm_state_out(st["hl_r"][:, :, 0], st["hl_i"][:, :, 0], [st["hl_r"], st["hl_i"]], self.ssmp[l])

    def build(self):
        P = self.P
        self.declare()
        self.x1_tiles = [Buf(self.x1.t[t * 128:(t + 1) * 128, :], f"x1_{t}") for t in range(self.S // 128)]
        self.consts()
        self.wts = [P.sbuf(f"wt{i}", [128, 16, 128], BF16) for i in range(3)]
        self.wt_i = 0
        self.prepass()
        for l in range(2):
            with Arena(P) as LA:
                self.layer_prep(l, LA)
                self.prompt_layer(l)
                if self.NS:
                    self.sample_layer(l)
        P.finish()
        P.close()
        return self.nc


def _tables():
    pos = np.arange(SEQ)
    n = np.arange(64)
    nvalid = (pos + 1) // 64
    cbq = np.where(n[None, :] < nvalid[:, None], 0.0, NEG).astype(np.float32)
    cur = pos // 64
    tsel = np.zeros((SEQ, 64), np.float32)
    tsel[n[None, :] > cur[:, None]] = NEG
    tsel[n[None, :] == cur[:, None]] = 1e4
    tsel[n[None, :] == (cur[:, None] - 1)] = 2e4
    tsel[:, 0] = 3e4
    rc = np.zeros((4, 15), np.float32)
    for gi, w in enumerate((2, 4, 8, 16)):
        rc[gi] = 1.0 / np.minimum(np.arange(15) + 1, w)
    return {
        "t_cbq": np.ascontiguousarray(cbq.reshape(SEQ // 128, 128, 64)),
        "t_sel": np.ascontiguousarray(tsel.reshape(SEQ // 128, 128, 64)),
        "t_cbT": np.ascontiguousarray(cbq.T),
        "t_rc": np.ascontiguousarray(np.tile(rc.reshape(1, 60), (128, 1))),
    }


_WKEYS = ("g_pre", "g_post", "w_in", "w_pool", "pool_scale", "pe_cmp", "w_phi", "d_skip", "w_glu",
          "w_br_pool", "w_br_nsa", "w_br_ssm", "w_out", "b_re", "b_im")


def _core_inputs(inp, b, S, NS, samples):
    f = lambda a: np.ascontiguousarray(np.asarray(a, dtype=np.float32))
    m = {k: f(inp[k]) for k in _WKEYS}
    m["lam_re"] = f(inp["lam_re"]).reshape(2, 32, 128)
    m["lam_im"] = f(inp["lam_im"]).reshape(2, 32, 128)
    m["log_step"] = f(inp["log_step"]).reshape(2, 32, 2)
    m["c_re"] = f(inp["c_re"]).reshape(2, 32, 2, 16, 64)
    m["c_im"] = f(inp["c_im"]).reshape(2, 32, 2, 16, 64)
    m["xp"] = f(inp["x_prompt"][b, :S])
    m.update(_tables())
    if NS:
        sl = list(samples)
        m["xs"] = f(inp["x_sample"][sl]).reshape(4 * NS, D)
        ck = np.asarray(inp["cache_kv"], dtype=np.float32)
        m["cache0"] = np.ascontiguousarray(ck[0]).reshape(NPOOL * 128 * 2, 512)
        m["cache1"] = np.ascontiguousarray(ck[1]).reshape(NPOOL * 128 * 2, 512)
        m["ptab"] = np.ascontiguousarray(np.asarray(inp["page_table"])[sl].astype(np.int32))
        m["swin"] = f(np.asarray(inp["state_win_kv"])[:, sl]).reshape(2, NS, 512, 512)
        m["spool"] = f(np.asarray(inp["state_pool"])[:, sl])
        m["sssm"] = f(np.asarray(inp["state_ssm"])[:, sl]).reshape(2, NS, 2, 32, 128)
        N = 4 * NS
        col = np.arange(N)
        cm = (col[None, :] // 4 == np.arange(NS)[:, None]).astype(np.float32)
        p = np.arange(128)
        wm = cm[None, :, :] * (p[:, None, None] > (col % 4)[None, None, :])
        m["t_cm"] = np.ascontiguousarray(np.broadcast_to(cm[None], (128, NS, N)).astype(np.float32))
        m["t_wm"] = np.ascontiguousarray(wm.astype(np.float32))
        m16 = ((col[:, None] // 4 == col[None, :] // 4) & (col[:, None] % 4 <= col[None, :] % 4)).astype(np.float32)
        m["t_m16"] = np.ascontiguousarray(m16)
        ts = np.zeros((N, 256), np.float32); ts[:, 0] = 3e4; ts[:, 255] = 2e4
        m["t_ssel"] = ts
    return m


_NC_CACHE = {}


def run(inp, S=SEQ, NS=4, ncores=2, debug=False, parts=("pool", "ssm", "nsa"), trace=False):
    key = (S, NS, debug, tuple(parts))
    if key not in _NC_CACHE:
        kb = KB(S=S, NS=NS, debug=debug, parts=parts)
        kb.build()
        _NC_CACHE[key] = kb
    kb = _NC_CACHE[key]
    in_maps = [_core_inputs(inp, c, S, NS, range(NS * c, NS * c + NS)) for c in range(ncores)]
    res = run_bass_kernel_spmd(kb.nc, in_maps, core_ids=list(range(ncores)), **({"trace": True} if trace else {}))
    return kb, res


def kernel(**inputs):
    kb, res = run(inputs)
    r = res.results
    B = 2
    y_prompt = np.stack([r[b]["yp"] for b in range(B)])
    y_sample = np.concatenate([r[c]["ys"].reshape(4, 4, D) for c in range(2)], axis=0)
    kv_p = np.stack([r[b]["kvp"] for b in range(B)], axis=1).reshape(2, B, SEQ, 4, 4, 64)
    kv_s = np.concatenate([r[c]["kvs"].reshape(2, 4, 4, 4, 4, 64) for c in range(2)], axis=1)
    win_p = np.stack([r[b]["winp"] for b in range(B)], axis=1).reshape(2, B, 512, 2, 4, 64)
    win_s = np.concatenate([r[c]["wins"].reshape(2, 4, 512, 2, 4, 64) for c in range(2)], axis=1)
    pool_p = np.stack([r[b]["poolp"] for b in range(B)], axis=1)
    pool_s = np.concatenate([r[c]["pools"] for c in range(2)], axis=1)
    ssm_p = np.stack([r[b]["ssmp"] for b in range(B)], axis=1).reshape(2, B, 2, 64, 64)
    ssm_s = np.concatenate([r[c]["ssms"].reshape(2, 4, 2, 64, 64) for c in range(2)], axis=1)
    outs = (y_prompt, y_sample, kv_p, kv_s, win_p, win_s, pool_p, pool_s, ssm_p, ssm_s)
    return tuple(np.ascontiguousarray(o, dtype=np.float32) for o in outs)


def _sample_layer(self, l):
    P, L, NS = self.P, self.L, self.NS
    dv, ac, pl = P.dve, P.act, P.pool
    N = 4 * NS
    tiles = [(0, N)]
    with Arena(P) as SA:
        hT = SA.sbuf("hTs", [128, 16, N], BF16)
        yz = [SA.sbuf(f"yzs{i}", [128, 8, N], BF16) for i in range(3)]

        def src_rows(r0, nr):
            return self.xs[r0:r0 + nr, :] if l == 0 else self.xs1[r0:r0 + nr, :]

        def dst_rows(r0, nr):
            return self.xs1[r0:r0 + nr, :] if l == 0 else self.ys[r0:r0 + nr, :]

        if l == 1:
            for d in list(self.xs1.w.values()):
                P._wait(P.sp, d)
        self.norm_phase(src_rows, tiles, hT)
        halo = SA.sbuf("halos", [128, 8, NS, 15], F32)
        with Arena(P) as A:
            for s in range(NS):
                sp = A.sbuf("spl", [15, 1024], F32)
                self.DMA(sp[:], self.spool[l, s], [], [sp])
                for half in range(2):
                    ps = self.nextpA()
                    for t in range(4):
                        tt = 4 * half + t
                        self.MT(ps[:, t * 16:t * 16 + 15], sp[0:15, tt * 128:(tt + 1) * 128], [sp], [ps])
                    self.CP(ac, halo[:, 4 * half:4 * half + 4, s, :], ps[:, 0:64].rearrange("p (t c) -> p t c", t=4)[:, :, 0:15], [ps], [halo])
        self.pool_phase(l, N, hT, yz[0], halo, first=False, nseg=NS)
        for s in range(NS):
            self.pool_state_out(lambda t, s=s: halo[:, t, s, :], halo, self.pools[l, s])
        st = {n: SA.sbuf(n + "s", [128, 32, NS], F32) for n in ("car_r", "car_i", "hl_r", "hl_i")}
        with Arena(P) as A:
            c1 = A.sbuf("c1s", [128, 32, NS], F32)
            c2 = A.sbuf("c2s", [128, 32, NS], F32)
            for s in range(NS):
                for ri, nm in ((0, "hl_r"), (1, "hl_i")):
                    t_ = self.load_T(A, "h0", self.sssm[l, s, ri], 32)
                    self.CP(dv, st[nm][:, :, s], t_[:, 0:32], [t_], [st[nm]])
            ab_r = L.abr[:].unsqueeze(2).to_broadcast([128, 32, NS])
            ab_i = L.abi[:].unsqueeze(2).to_broadcast([128, 32, NS])
            self.TT(dv, c1[:], st["hl_r"][:], ab_r, ALU.mult, [st["hl_r"], L.abr], [c1])
            self.TT(dv, c2[:], st["hl_i"][:], ab_i, ALU.mult, [st["hl_i"], L.abi], [c2])
            self.TT(dv, st["car_r"][:], c1[:], c2[:], ALU.subtract, [c1, c2], [st["car_r"]])
            self.TT(dv, c1[:], st["hl_i"][:], ab_r, ALU.mult, [st["hl_i"], L.abr], [c1])
            self.TT(dv, c2[:], st["hl_r"][:], ab_i, ALU.mult, [st["hl_r"], L.abi], [c2])
            self.TT(dv, st["car_i"][:], c1[:], c2[:], ALU.add, [c1, c2], [st["car_i"]])
        self.ssm_phase(l, N, hT, yz[2], st, nseg=NS)
        for s in range(NS):
            self.ssm_state_out(st["hl_r"][:, :, s], st["hl_i"][:, :, s], [st["hl_r"], st["hl_i"]], self.ssms[l, s])
        self.nsa_sample(l, hT, yz[1])
        self.merge_out_phase(l, N, tiles, hT, yz, src_rows, dst_rows, [self.xs1] if l == 0 else None)


def _nsa_sample(self, l, hT, yza):
    P, L, NS = self.P, self.L, self.NS
    dv, ac, pl = P.dve, P.act, P.pool
    N = 4 * NS
    cache_l = self.cache[l]
    with Arena(P) as A:
        q0 = A.sbuf("q0s", [128, 8, N], BF16)
        saz = A.sbuf("sazs", [128, 8, N], BF16)
        self.projA([self.wA[l, t] for t in range(16, 24)], N, lambda k: hT[:, k, 0:N], 16,
                   lambda i, ps: self.q_to_base(A, N, i, ps, q0), [hT])
        self.projA([self.wA[l, t] for t in range(24, 32)], N, lambda k: hT[:, k, 0:N], 16,
                   lambda i, ps: self.AC(saz[:, i, :], ps[:, 0:N], AF.Silu, [ps], [saz]), [hT])
        kvf = A.sbuf("kvfs", [128, NKVB], F32)
        self.kv_formB(l, A, hT, [(0, N)], [kvf])
        self.DMA(self.kvs[l][0:N, :], kvf[0:N, 0:1024], [kvf], [])
        for s in range(NS):
            self.DMA(self.wins[l, s, 0:508, :], self.swin[l, s, 4:512, :], [], [])
            self.DMA(self.wins[l, s, 508:512, :], kvf[4 * s:4 * s + 4, 1024:1536], [kvf], [])
        sg = A.sbuf("sgs", [N, 48], F32)
        self.AC(sg[:], kvf[0:N, 1536:1584], AF.Sigmoid, [kvf], [sg])
        kvb = A.sbuf("kvbs", [N, 1536], BF16)
        self.CP(dv, kvb[:], kvf[0:N, 0:1536], [kvf], [kvb])
        knT = A.sbuf("knT", [128, 4, N], BF16)
        ptn = self.pS[0]
        for n_, c_ in enumerate((512, 640, 1024, 1152)):
            self.MT(ptn[:, n_ * N:(n_ + 1) * N], kvb[0:N, c_:c_ + 128], [kvb], [ptn])
        self.CP(ac, knT[:], ptn[:, 0:4 * N].rearrange("p (a n) -> p a n", a=4), [ptn], [knT])
        vn = A.sbuf("vn", [N, 2, 4, 65], BF16)
        self.MS(pl, vn[:], 1.0, [vn])
        self.CP(ac, vn[:, 0, :, 0:64], kvb[:, 768:1024].rearrange("p (h d) -> p h d", h=4), [kvb], [vn])
        self.CP(ac, vn[:, 1, :, 0:64], kvb[:, 1280:1536].rearrange("p (h d) -> p h d", h=4), [kvb], [vn])
        cmf = A.sbuf("cmf", [128, NS, N], F32)
        wmf = A.sbuf("wmf", [128, NS, N], F32)
        m16f = A.sbuf("m16f", [N, N], F32)
        self.DMA(cmf[:], self.t_cm, [], [cmf])
        self.DMA(wmf[:], self.t_wm, [], [wmf])
        self.DMA(m16f[:], self.t_m16, [], [m16f])
        cm = A.sbuf("cm", [128, NS, N], BF16)
        wm = A.sbuf("wm", [128, NS, N], BF16)
        m16 = A.sbuf("m16", [N, N], BF16)
        self.CP(dv, cm[:], cmf[:], [cmf], [cm])
        self.CP(dv, wm[:], wmf[:], [wmf], [wm])
        self.CP(dv, m16[:], m16f[:], [m16f], [m16])
        pti = A.sbuf("pti", [128, NS * 128], I32)
        self.DMA(pti[:], self.ptab.rearrange("(o s) g -> o (s g)", o=1).to_broadcast([128, NS * 128]), [], [pti])
        iop = A.sbuf("iop", [128, 1], F32)
        P.op(pl, lambda e: e.iota(iop[:], pattern=[[0, 1]], base=0, channel_multiplier=2, allow_small_or_imprecise_dtypes=True), (), [iop])
        idxf = A.sbuf("idxf", [128, NS * 128], F32)
        self.TS(dv, idxf[:], pti[:], 256.0, iop[:, 0:1], ALU.mult, ALU.add, [pti, iop], [idxf])
        idxA = A.sbuf("idxA", [128, NS * 128], I32)
        idxB = A.sbuf("idxB", [128, NS * 128], I32)
        self.CP(dv, idxA[:], idxf[:], [idxf], [idxA])
        self.TS(dv, idxB[:], idxf[:], 1.0, None, ALU.add, None, [idxf], [idxB])
        yat = A.sbuf("yats", [N, 1024], F32)
        pgs = [A.sbuf(f"pg{i}", [128, 512], F32) for i in range(3)]
        pgb = [A.sbuf(f"pgb{i}", [128, 512], BF16) for i in range(2)]
        pts = [A.sbuf(f"pts{i}", [128, 16, N], BF16) for i in range(2)]
        cnt = {"pg": 0, "pt": 0, "sc": 0}
        Gf = self.G[:].rearrange("p j c -> p (j c)")
        po = [self.pK[0], self.pK[1], self.pK[2], self.pS[1]]
        pov = [p_[0:N, 0:260].rearrange("p (g e) -> p g e", g=4) for p_ in po]
        scps = [self.pA[0], self.pA[1]]
        zb = A.sbuf("zb", [128, 260], BF16)
        self.MS(pl, zb[:], 0.0, [zb])

        def gather_page(s, g, idx):
            cnt["pg"] += 1
            pg = pgs[cnt["pg"] % 3]
            P.gather(pg[:], cache_l, idx[:, s * 128 + g:s * 128 + g + 1], [idx], [pg])
            b = pgb[cnt["pg"] % 2]
            self.CP(dv, b[:], pg[:], [pg], [b])
            return b

        def attend(nk, kT_of, bias_of, mask_ap, mask_R, v_of, first, last):
            cnt["sc"] += 1
            psx = scps[cnt["sc"] % 2]
            for k in range(4):
                base, pr = 64 * (k % 2), k // 2
                kT, kR = kT_of(k)
                for g in range(4):
                    o = (k * 4 + g) * N
                    bz = bias_of(k) if bias_of else None
                    self.MM(psx[0:nk, o:o + N], kT, q0[base:base + 64, pr * 4 + g, :], True, bz is None, kR + [q0], [psx])
                    if bz is not None:
                        self.MM(psx[0:nk, o:o + N], bz[0], bz[1], False, True, bz[2], [psx])
            cnt["pt"] += 1
            pt = pts[cnt["pt"] % 2]
            ptf = pt[0:nk].rearrange("p a n -> p (a n)")
            self.AC(ptf, psx[0:nk, 0:16 * N], AF.Exp, [psx], [pt], scale=0.125)
            self.TT(dv, pt[0:nk], pt[0:nk], mask_ap.unsqueeze(1).to_broadcast([nk, 16, N]), ALU.mult, [pt] + mask_R, [pt])
            if first:
                for k in range(4):
                    self.MM(po[k][0:N, 0:260], zb[:, 0:N], zb[:, 0:260], True, False, [zb], [po[k]])
            for k in range(4):
                vv, vR = v_of(k)
                for g in range(4):
                    self.MM(pov[k][:, g, :], pt[0:nk, k * 4 + g, :], vv, False, last, [pt] + vR, [po[k]])

        def finish_branch(bi):
            for k in range(4):
                z4 = A.sbuf("zzs", [N, 4], F32)
                self.TS(dv, z4[:], pov[k][:, :, 64], 1e-30, None, ALU.max, None, [po[k]], [z4])
                P.op(dv, lambda e: e.reciprocal(out=z4[:], in_=z4[:]), [z4], [z4])
                gsel = sg[:, 12 * k:12 * k + 12].rearrange("p (g i) -> p g i", i=3)[:, :, bi]
                self.TT(dv, z4[:], z4[:], gsel, ALU.mult, [z4, sg], [z4])
                yv = yat[:, k * 256:(k + 1) * 256].rearrange("p (g d) -> p g d", g=4)
                fb = z4[:].unsqueeze(2).to_broadcast([N, 4, 64])
                if bi == 0:
                    self.TT(dv, yv, pov[k][:, :, 0:64], fb, ALU.mult, [po[k], z4], [yat])
                else:
                    tmp = A.sbuf("ytmps", [N, 4, 64], F32)
                    self.TT(dv, tmp[:], pov[k][:, :, 0:64], fb, ALU.mult, [po[k], z4], [tmp])
                    self.TT(dv, yv, yv, tmp[:], ALU.add, [yat, tmp], [yat])

        XTk = A.sbuf("XTk", [128, 2, 4096], BF16)
        XTv = A.sbuf("XTv", [128, 2, 4096], BF16)
        kcT = A.sbuf("kcTs", [128, 2, 256], BF16)
        vcT = A.sbuf("vcTs", [64, 4, 64], BF16)
        vca = A.sbuf("vcas", [128, 2, 4, 65], BF16)
        BT = A.sbuf("BTs", [64, NS, 4, 4, N], BF16)
        self.MS(pl, vca[:], 1.0, [vca])
        s1 = A.sbuf("s1s", [N, 4, 256], F32)
        z4a = A.sbuf("z4a", [N, 4], F32)
        sc_ = A.sbuf("scs", [N, 256], F32)
        mx = A.sbuf("mxs", [N, 8], F32)
        w1 = A.sbuf("w1s", [N, 256], F32)
        bq = A.sbuf("bqs", [N, 256], BF16)
        tsl = A.sbuf("tsls", [N, 256], F32)
        self.DMA(tsl[:], self.t_ssel, [], [tsl])
        for s in range(NS):
            for grp in range(4):
                for gg in range(32):
                    g = grp * 32 + gg
                    b = gather_page(s, g, idxA)
                    ptx = self.pS[0]
                    for n_ in range(4):
                        self.MT(ptx[:, n_ * 128:(n_ + 1) * 128], b[:, n_ * 128:(n_ + 1) * 128], [b], [ptx])
                    self.CP(ac, XTk[:, :, gg * 128:(gg + 1) * 128], ptx[:, 0:256].rearrange("p (r t) -> p r t", r=2), [ptx], [XTk])
                    self.CP(ac, XTv[:, :, gg * 128:(gg + 1) * 128], ptx[:, 256:512].rearrange("p (r t) -> p r t", r=2), [ptx], [XTv])
                for a_, xt_ in ((0, XTk), (1, XTv)):
                    v4 = xt_[:].rearrange("p r (n l) -> p r n l", l=64)
                    self.TT(dv, v4, v4, L.pe2[:, a_, :].unsqueeze(1).unsqueeze(1).to_broadcast([128, 2, 64, 64]), ALU.add, [xt_, L.pe2], [xt_])
                pkc, pvc = self.pA[0], self.pA[1]
                for k in range(4):
                    base, pr = 64 * (k % 2), k // 2
                    for a_, xt_, pso, ob in ((0, XTk, pkc, base), (1, XTv, pvc, 0)):
                        col = (pr * 64) if a_ == 0 else (k * 64)
                        xl = xt_[:].rearrange("p r (n l) -> p r l n", l=64)
                        for ll in range(64):
                            self.MM(pso[ob:ob + 64, col:col + 64], L.wphi2[base:base + 64, a_, ll, :], xl[base:base + 64, pr, ll, :],
                                    ll == 0, ll == 63, [L.wphi2, xt_], [pso])
                self.CP(ac, kcT[:, :, grp * 64:(grp + 1) * 64], pkc[:, 0:128].rearrange("p (r n) -> p r n", r=2), [pkc], [kcT])
                self.CP(ac, vcT[0:64, :, :], pvc[0:64, 0:256].rearrange("p (k n) -> p k n", k=4), [pvc], [vcT])
                pvt = self.pS[0]
                ob = 64 * (grp % 2)
                for k in range(4):
                    self.MT(pvt[ob:ob + 64, k * 64:(k + 1) * 64], vcT[0:64, k, :], [vcT], [pvt])
                self.CP(ac, vca[ob:ob + 64, grp // 2, :, 0:64], pvt[ob:ob + 64, 0:256].rearrange("p (k e) -> p k e", k=4), [pvt], [vca])
            for nt in range(2):
                attend(128, lambda k, nt=nt: (kcT[64 * (k % 2):64 * (k % 2) + 64, k // 2, nt * 128:(nt + 1) * 128], [kcT]),
                       None, cm[:, s, :], [cm], lambda k, nt=nt: (vca[:, nt, k, :], [vca]),
                       (s == 0 and nt == 0), (s == NS - 1 and nt == 1))
            for k in range(4):
                base, pr = 64 * (k % 2), k // 2
                psc = [self.pS[0], self.pS[1]] if False else [self.pA[0], self.pA[1]]
                for g in range(4):
                    pq = psc[g % 2]
                    self.MM(pq[0:N, (g // 2) * 256:(g // 2) * 256 + 256], q0[base:base + 64, pr * 4 + g, :], kcT[base:base + 64, pr, :], True, True, [q0, kcT], [pq])
                for g in range(4):
                    pq = psc[g % 2]
                    self.AC(s1[:, g, :], pq[0:N, (g // 2) * 256:(g // 2) * 256 + 256], AF.Exp, [pq], [s1], scale=0.125)
                P.op(dv, lambda e: e.tensor_reduce(out=z4a[:], in_=s1[:], axis=AX.X, op=ALU.add), [s1], [z4a])
                P.op(dv, lambda e: e.reciprocal(out=z4a[:], in_=z4a[:]), [z4a], [z4a])
                self.TT(dv, s1[:], s1[:], z4a[:].unsqueeze(2).to_broadcast([N, 4, 256]), ALU.mult, [s1, z4a], [s1])
                P.op(dv, lambda e: e.tensor_reduce(out=sc_[:], in_=s1[:].rearrange("p g n -> p n g"), axis=AX.X, op=ALU.add), [s1], [sc_])
                self.TT(dv, sc_[:], sc_[:], tsl[:], ALU.add, [sc_, tsl], [sc_])
                P.op(dv, lambda e: e.max(out=mx[:], in_=sc_[:]), [sc_], [mx])
                P.op(dv, lambda e: e.match_replace(out=w1[:], in_to_replace=mx[:], in_values=sc_[:], imm_value=NEG), [mx, sc_], [w1])
                P.op(dv, lambda e: e.max(out=mx[:], in_=w1[:]), [w1], [mx])
                self.TS(dv, w1[:], sc_[:], mx[:, 6:7], -1.0, ALU.is_ge, ALU.add, [sc_, mx], [w1])
                self.TS(dv, bq[:], w1[:], 1e30, None, ALU.mult, None, [w1], [bq])
                for t4 in range(4):
                    pbt = self.pS[0]
                    self.MT(pbt[0:64, t4 * N:(t4 + 1) * N], bq[:, t4 * 64:(t4 + 1) * 64], [bq], [pbt])
                self.CP(ac, BT[0:64, s, :, k, :], self.pS[0][0:64, 0:4 * N].rearrange("p (t n) -> p t n", t=4), [self.pS[0]], [BT])
        finish_branch(0)
        vpg = [A.sbuf(f"vpg{i}", [128, 4, 65], BF16) for i in range(2)]
        for v_ in vpg:
            self.MS(pl, v_[:], 1.0, [v_])
        kTp = [A.sbuf(f"kTp{i}", [128, 2, 128], BF16) for i in range(2)]
        it = 0
        for s in range(NS):
            for g in range(128):
                b = gather_page(s, g, idxB)
                ptx = self.pS[0]
                for n_ in range(2):
                    self.MT(ptx[:, n_ * 128:(n_ + 1) * 128], b[:, n_ * 128:(n_ + 1) * 128], [b], [ptx])
                kt_ = kTp[it % 2]
                vv_ = vpg[it % 2]
                it += 1
                self.CP(ac, kt_[:], ptx[:, 0:256].rearrange("p (r t) -> p r t", r=2), [ptx], [kt_])
                self.CP(ac, vv_[:, :, 0:64], b[:, 256:512].rearrange("p (h d) -> p h d", h=4), [b], [vv_])
                tile4, loc = g // 32, (2 * g) % 64
                attend(128, lambda k, kt_=kt_: (kt_[64 * (k % 2):64 * (k % 2) + 64, k // 2, :], [kt_]),
                       lambda k, s=s, tile4=tile4, loc=loc: (Gf[:, loc * 64:loc * 64 + 128], BT[0:64, s, tile4, k, :], [self.G, BT]),
                       cm[:, s, :], [cm], lambda k, vv_=vv_: (vv_[:, k, :], [vv_]), (s == 0 and g == 0), False)
        attend(N, lambda k: (knT[64 * (k % 2):64 * (k % 2) + 64, k // 2, :], [knT]), None, m16[:, :], [m16],
               lambda k: (vn[:, 0, k, :], [vn]), False, True)
        finish_branch(1)
        wst = A.sbuf("wst", [128, 512], F32)
        wsb = A.sbuf("wsb", [128, 512], BF16)
        for s in range(NS):
            for t4 in range(4):
                self.DMA(wst[:], self.swin[l, s, t4 * 128:(t4 + 1) * 128, :], [], [wst])
                self.CP(dv, wsb[:], wst[:], [wst], [wsb])
                ptx = self.pS[0]
                for n_ in range(2):
                    self.MT(ptx[:, n_ * 128:(n_ + 1) * 128], wsb[:, n_ * 128:(n_ + 1) * 128], [wsb], [ptx])
                kt_ = kTp[it % 2]
                vv_ = vpg[it % 2]
                it += 1
                self.CP(ac, kt_[:], ptx[:, 0:256].rearrange("p (r t) -> p r t", r=2), [ptx], [kt_])
                self.CP(ac, vv_[:, :, 0:64], wsb[:, 256:512].rearrange("p (h d) -> p h d", h=4), [wsb], [vv_])
                msk = wm if t4 == 0 else cm
                attend(128, lambda k, kt_=kt_: (kt_[64 * (k % 2):64 * (k % 2) + 64, k // 2, :], [kt_]), None,
                       msk[:, s, :], [msk], lambda k, vv_=vv_: (vv_[:, k, :], [vv_]), (s == 0 and t4 == 0), False)
        attend(N, lambda k: (knT[64 * (k % 2):64 * (k % 2) + 64, 2 + k // 2, :], [knT]), None, m16[:, :], [m16],
               lambda k: (vn[:, 1, k, :], [vn]), False, True)
        finish_branch(2)
        yb = A.sbuf("ybs", [N, 1024], BF16)
        self.CP(dv, yb[:], yat[:], [yat], [yb])
        pyt = self.pS[0]
        for t8 in range(8):
            self.MT(pyt[:, t8 * N:(t8 + 1) * N], yb[:, t8 * 128:(t8 + 1) * 128], [yb], [pyt])
        self.TT(dv, yza[:, :, 0:N], pyt[:, 0:8 * N].rearrange("p (t c) -> p t c", t=8), saz[:], ALU.mult, [pyt, saz], [yza])


KB.sample_layer = _sample_layer
KB.nsa_sample = _nsa_sample
```

```python
import math
import os
import numpy as np
KSTOP = int(os.environ.get('KSTOP', '99'))
KSUB = int(os.environ.get('KSUB', '99'))
import concourse.bass as bass
import concourse.mybir as mybir
from concourse.bass_utils import run_bass_kernel_spmd
from contextlib import ExitStack

F32 = mybir.dt.float32
BF16 = mybir.dt.bfloat16
I32 = mybir.dt.int32
ALU = mybir.AluOpType
AF = mybir.ActivationFunctionType
AX = mybir.AxisListType

EPOCH = 8192
NDMASEM = 12

D = 2048
DIN = 13872
SEQ = 4096
PAST = 16384
NPOOL = 1280
EPS = 1e-6
NEG = -1e30
O_PU, O_PZ, O_Q, O_KV, O_AG, O_AZ, O_SU, O_SZ, O_MG = 0, 1024, 2048, 3072, 4608, 4656, 5680, 6704, 7728
A_SEGS = [O_PU, O_PZ, O_Q, O_AZ, O_SU, O_SZ]
NKVB = 1584
TC = 32


class Buf:
    __slots__ = ("t", "w", "r", "name")

    def __init__(self, t, name="", carry=None):
        self.t = t
        self.w = dict(carry) if carry else {}
        self.r = {}
        self.name = name

    def __getitem__(self, idx):
        return self.t[idx]


class Eng:
    def __init__(self, P, name, obj):
        self.P = P
        self.name = name
        self.obj = obj
        self.count = 0
        self.sems = []
        self.waited = {}
        self.dsems = []
        self.dnext = 0

    def sem_for(self, seq):
        k = (seq - 1) // EPOCH
        while len(self.sems) <= k:
            self.sems.append(self.P.new_sem(f"{self.name}_e{len(self.sems)}"))
        return self.sems[k], (seq - 1) % EPOCH + 1, (self.name, k)


def _dep_rank(d):
    return d[2]


class Prog:
    def __init__(self, nc):
        self.nc = nc
        self.es = ExitStack()
        self.nsem = 0
        self.nname = 0
        self.pe = Eng(self, "pe", nc.tensor)
        self.act = Eng(self, "act", nc.scalar)
        self.dve = Eng(self, "dve", nc.vector)
        self.pool = Eng(self, "pool", nc.gpsimd)
        self.sp = Eng(self, "sp", nc.sync)
        self.engs = [self.pe, self.act, self.dve, self.pool, self.sp]
        self.carry = {}
        self.ninstr = 0

    def new_sem(self, name):
        self.nsem += 1
        return self.es.enter_context(self.nc.semaphore(f"s_{name}_{self.nsem}"))

    def uname(self, name):
        self.nname += 1
        return f"{name}_{self.nname}"

    def sbuf(self, name, shape, dtype, es=None):
        t = (es or self.es).enter_context(self.nc.sbuf_tensor(self.uname(name), list(shape), dtype))
        return Buf(t, name, self.carry)

    def psum(self, name, shape, dtype):
        t = self.es.enter_context(self.nc.psum_tensor(self.uname(name), list(shape), dtype))
        return Buf(t, name)

    def dram_in(self, name, shape, dtype):
        return self.nc.dram_tensor(name, list(shape), dtype, kind="ExternalInput").ap()

    def dram_out(self, name, shape, dtype):
        return self.nc.dram_tensor(name, list(shape), dtype, kind="ExternalOutput").ap()

    def dram_scratch(self, name, shape, dtype):
        t = self.nc.dram_tensor(name, list(shape), dtype, kind="Internal").ap()
        return Buf(t, name)

    def retire(self, bufs):
        for b in bufs:
            for dd in (b.w, b.r):
                for k, d in dd.items():
                    kk = k[:3] if k[0] == "d" else k[:2]
                    o = self.carry.get(kk)
                    if o is None or _dep_rank(o) < _dep_rank(d):
                        self.carry[kk] = d

    def _wait(self, eng, dep, force=False):
        if dep[0] == "e":
            _, src, seq = dep
            if src is eng and eng is self.pe and not force:
                return
            sem, val, key = src.sem_for(seq)
        else:
            _, sem, val, key = dep
        if eng.waited.get(key, 0) >= val:
            return
        eng.waited[key] = val
        eng.obj.wait_ge(sem, val)
        self.ninstr += 1

    def _collect(self, eng, reads, writes):
        for b in reads:
            for d in b.w.values():
                self._wait(eng, d)
        for b in writes:
            for d in b.w.values():
                self._wait(eng, d)
            for d in b.r.values():
                self._wait(eng, d)

    def _commit(self, me, mekey, reads, writes):
        for b in writes:
            b.w = {mekey: me}
            b.r = {}
        for b in reads:
            if b not in writes:
                b.r[mekey] = me

    def op(self, eng, fn, reads=(), writes=()):
        self._collect(eng, reads, writes)
        ins = fn(eng.obj)
        eng.count += 1
        sem, val, key = eng.sem_for(eng.count)
        ins.then_inc(sem, 1)
        self.ninstr += 1
        me = ("e", eng, eng.count)
        self._commit(me, ("e", eng.name), reads, writes)
        return me

    def _dma_common(self, q, reads, writes, issue):
        self._collect(q, reads, writes)
        if len(q.dsems) < NDMASEM:
            q.dsems.append([self.new_sem(f"{q.name}_d{len(q.dsems)}"), 0])
        i = q.dnext % NDMASEM
        q.dnext += 1
        ent = q.dsems[i]
        key = ("d", q.name, i)
        if ent[1] > 0:
            self._wait(q, ("d", ent[0], 16 * ent[1], key))
        ent[1] += 1
        ins = issue(q.obj)
        ins.then_inc(ent[0], 16)
        self.ninstr += 1
        me = ("d", ent[0], 16 * ent[1], key)
        self._commit(me, key, reads, writes)
        return me

    def dma(self, q, out, in_, reads=(), writes=()):
        return self._dma_common(q, reads, writes, lambda e: e.dma_start(out=out, in_=in_))

    def gather(self, out, in_, idx_ap, reads=(), writes=()):
        return self._dma_common(
            self.pool, reads, writes,
            lambda e: e.indirect_dma_start(out=out, out_offset=None, in_=in_,
                                           in_offset=bass.IndirectOffsetOnAxis(ap=idx_ap, axis=0)))

    def finish(self):
        for e in self.engs:
            if e is not self.sp and e.count > 0:
                self._wait(self.sp, ("e", e, e.count))
        for q in self.engs:
            for i, ent in enumerate(q.dsems):
                if ent[1] > 0:
                    self._wait(self.sp, ("d", ent[0], 16 * ent[1], ("d", q.name, i)))

    def barrier(self):
        for tgt in self.engs:
            for e in self.engs:
                if e is not tgt and e.count > 0:
                    self._wait(tgt, ("e", e, e.count))
            for q in self.engs:
                for i, ent in enumerate(q.dsems):
                    if ent[1] > 0:
                        self._wait(tgt, ("d", ent[0], 16 * ent[1], ("d", q.name, i)))

    def close(self):
        self.es.close()


class Arena:
    def __init__(self, P):
        self.P = P
        self.es = ExitStack()
        self.bufs = []

    def sbuf(self, name, shape, dtype):
        b = self.P.sbuf(name, shape, dtype, es=self.es)
        self.bufs.append(b)
        return b

    def close(self):
        self.P.retire(self.bufs)
        self.es.close()

    def __enter__(self):
        return self

    def __exit__(self, *a):
        self.close()


class KB:
    def __init__(self, S=SEQ, NS=4, C=256, debug=False, parts=("pool", "ssm", "nsa")):
        self.S, self.NS, self.C, self.debug, self.parts = S, NS, C, debug, set(parts)
        self.NCH = S // C
        self.nc = bass.Bass("TRN2", target_bir_lowering=False)
        self.P = Prog(self.nc)
        self.dbg_outs = []

    def TT(self, eng, out, a, b, op, R, W):
        return self.P.op(eng, lambda e: e.tensor_tensor(out=out, in0=a, in1=b, op=op), R, W)

    def TS(self, eng, out, a, s1, s2, op0, op1, R, W):
        if s2 is None:
            return self.P.op(eng, lambda e: e.tensor_scalar(out=out, in0=a, scalar1=s1, scalar2=None, op0=op0), R, W)
        return self.P.op(eng, lambda e: e.tensor_scalar(out=out, in0=a, scalar1=s1, scalar2=s2, op0=op0, op1=op1), R, W)

    def STT(self, eng, out, a, s, b, op0, op1, R, W):
        return self.P.op(eng, lambda e: e.scalar_tensor_tensor(out=out, in0=a, scalar=s, in1=b, op0=op0, op1=op1), R, W)

    def AC(self, out, in_, func, R, W, **kw):
        return self.P.op(self.P.act, lambda e: e.activation(out=out, in_=in_, func=func, **kw), R, W)

    def CP(self, eng, out, in_, R, W):
        if eng is self.P.act:
            return self.P.op(eng, lambda e: e.copy(out=out, in_=in_), R, W)
        return self.P.op(eng, lambda e: e.tensor_copy(out=out, in_=in_), R, W)

    def MS(self, eng, ap, val, W):
        return self.P.op(eng, lambda e: e.memset(ap, val), (), W)

    def _rowgroup(self, ap, out):
        rg = (ap.base_partition(), ap.shape[0], out.base_partition(), out.shape[0])
        last = getattr(self, "_last_rg", (0, 128, 0, 128))
        if rg != last and (rg[1] < 128 or last[1] < 128 or rg[3] < 128 or last[3] < 128):
            pe = self.P.pe
            if pe.count > 0:
                self.P._wait(pe, ("e", pe, pe.count), force=True)
        self._last_rg = rg

    def MM(self, out, lhsT, rhs, start, stop, R, W):
        self._rowgroup(lhsT, out)
        return self.P.op(self.P.pe, lambda e: e.matmul(out, lhsT=lhsT, rhs=rhs, start=start, stop=stop), R, W)

    def TR(self, out, in_, ident, R, W):
        self._rowgroup(in_, out)
        return self.P.op(self.P.pe, lambda e: e.transpose(out=out, in_=in_, identity=ident), R, W)

    def MT(self, out, in_, R, W):
        kp = in_.shape[0]
        idn = self.identf if in_.dtype == F32 else self.ident
        return self.MM(out, in_, idn[0:kp, 0:kp], True, True, R + [idn], W)

    def DMA(self, out, in_, R=(), W=(), q=None):
        return self.P.dma(q or self.P.sp, out, in_, R, W)

    def dump(self, name, ap, shape, R):
        if not self.debug:
            return
        o = self.P.dram_out("dbg_" + name, shape, F32)
        self.dbg_outs.append("dbg_" + name)
        if ap.dtype != F32:
            with Arena(self.P) as A:
                t = A.sbuf("dbgt", list(ap.shape), F32)
                self.CP(self.P.dve, t[:], ap, R, [t])
                self.DMA(o, t[:], [t], [])
        else:
            self.DMA(o, ap, R, [])

    def declare(self):
        P, S, NS = self.P, self.S, self.NS
        din = P.dram_in
        self.xp = din("xp", [S, D], F32)
        if NS:
            self.xs = din("xs", [4 * NS, D], F32)
            self.cache = [din(f"cache{i}", [NPOOL * 128 * 2, 512], F32) for i in range(2)]
            self.ptab = din("ptab", [NS, 128], I32)
            self.swin = din("swin", [2, NS, 512, 512], F32)
            self.spool = din("spool", [2, NS, 15, 1024], F32)
            self.sssm = din("sssm", [2, NS, 2, 32, 128], F32)
            self.t_cm = din("t_cm", [128, NS, 4 * NS], F32)
            self.t_wm = din("t_wm", [128, NS, 4 * NS], F32)
            self.t_m16 = din("t_m16", [4 * NS, 4 * NS], F32)
            self.t_ssel = din("t_ssel", [4 * NS, 256], F32)
        self.g_pre = din("g_pre", [2, D], F32)
        self.g_post = din("g_post", [2, D], F32)
        self.w_in = din("w_in", [2, D, DIN], F32)
        self.w_pool = din("w_pool", [2, 4, 256, 256], F32)
        self.pool_scale = din("pool_scale", [2, 1024], F32)
        self.pe_cmp = din("pe_cmp", [2, 2, 64, 64], F32)
        self.w_phi = din("w_phi", [2, 2, 64, 64, 64], F32)
        self.lam_re = din("lam_re", [2, 32, 128], F32)
        self.lam_im = din("lam_im", [2, 32, 128], F32)
        self.log_step = din("log_step", [2, 32, 2], F32)
        self.b_re = din("b_re", [2, 64, 64, 16], F32)
        self.b_im = din("b_im", [2, 64, 64, 16], F32)
        self.c_re = din("c_re", [2, 32, 2, 16, 64], F32)
        self.c_im = din("c_im", [2, 32, 2, 16, 64], F32)
        self.d_skip = din("d_skip", [2, 1024], F32)
        self.w_glu = din("w_glu", [2, 1024, 1024], F32)
        self.w_br = [din(n, [2, 1024, D], F32) for n in ("w_br_pool", "w_br_nsa", "w_br_ssm")]
        self.w_out = din("w_out", [2, D, D], F32)
        self.t_cbq = din("t_cbq", [SEQ // 128, 128, 64], F32)
        self.t_sel = din("t_sel", [SEQ // 128, 128, 64], F32)
        self.t_cbT = din("t_cbT", [64, SEQ], F32)
        self.t_rc = din("t_rc", [128, 60], F32)
        dout = P.dram_out
        self.yp = dout("yp", [S, D], F32)
        self.kvp = dout("kvp", [2, S, 1024], F32)
        self.winp = dout("winp", [2, 512, 512], F32)
        self.poolp = dout("poolp", [2, 15, 1024], F32)
        self.ssmp = dout("ssmp", [2, 2, 32, 128], F32)
        if NS:
            self.ys = dout("ys", [4 * NS, D], F32)
            self.kvs = dout("kvs", [2, 4 * NS, 1024], F32)
            self.wins = dout("wins", [2, NS, 512, 512], F32)
            self.pools = dout("pools", [2, NS, 15, 1024], F32)
            self.ssms = dout("ssms", [2, NS, 2, 32, 128], F32)
        sc = P.dram_scratch
        self.wA = sc("wA", [2, 96, 128, 16, 128], BF16)
        self.wB = sc("wB", [2, 128, 16, NKVB], BF16)
        self.wG = sc("wG", [2, 8, 128, 8, 128], BF16)
        self.wR = sc("wR", [2, 3, 16, 128, 8, 128], BF16)
        self.wO = sc("wO", [2, 128, 16, D], BF16)
        self.x1 = sc("x1", [S, D], F32)
        if NS:
            self.xs1 = sc("xs1", [4 * NS, D], F32)

    def consts(self):
        P = self.P
        pl, dv = P.pool, P.dve
        self.identf = P.sbuf("identf", [128, 128], F32)
        self.ident = P.sbuf("ident", [128, 128], BF16)
        self.shu = P.sbuf("shu", [128, 128], BF16)
        self.tri = P.sbuf("tri", [128, 128], BF16)
        self.tris = P.sbuf("tris", [128, 128], BF16)
        self.G = P.sbuf("G", [64, 64, 64], BF16)
        self.rc = P.sbuf("rc", [128, 4, 15], F32)
        CA_ = Arena(P)
        tmp = CA_.sbuf("ctmp", [128, 128], F32)
        P.op(pl, lambda e: e.iota(tmp[:], pattern=[[1, 128]], base=0, channel_multiplier=-1,
                                  allow_small_or_imprecise_dtypes=True), (), [tmp])
        self.TS(dv, self.identf[:], tmp[:], 0.0, None, ALU.is_equal, None, [tmp], [self.identf])
        self.CP(dv, self.ident[:], self.identf[:], [self.identf], [self.ident])
        self.TS(dv, self.tri[:], tmp[:], 0.0, None, ALU.is_ge, None, [tmp], [self.tri])
        self.TS(dv, self.tris[:], tmp[:], 0.0, None, ALU.is_lt, None, [tmp], [self.tris])
        self.TS(dv, self.shu[:], tmp[:], 64.0, None, ALU.is_equal, None, [tmp], [self.shu])
        gt = CA_.sbuf("gtmp", [64, 64, 64], F32)
        P.op(pl, lambda e: e.iota(gt[:], pattern=[[1, 64], [0, 64]], base=0, channel_multiplier=-1,
                                  allow_small_or_imprecise_dtypes=True), (), [gt])
        self.TS(dv, self.G[:], gt[:], 0.0, None, ALU.is_equal, None, [gt], [self.G])
        self.DMA(self.rc[:], self.t_rc.rearrange("p (w c) -> p w c", w=4), [], [self.rc])
        CA_.close()
        self.pA = [P.psum(f"pA{i}", [128, 512], F32) for i in range(2)]
        self.pK = [P.psum(f"pK{i}", [128, 512], F32) for i in range(3)]
        self.pS = [P.psum(f"pS{i}", [128, 512], F32) for i in range(2)]
        self.pT = P.psum("pT", [128, 1024], BF16)
        self.pa_i = 0

    def nextpA(self):
        self.pa_i += 1
        return self.pA[self.pa_i % 2]

    def prepass(self):
        P = self.P
        with Arena(P) as A:
            stg = [A.sbuf(f"stg{i}", [128, 16, 512], BF16) for i in range(2)]
            si = [0]

            def stage():
                si[0] += 1
                return stg[si[0] % 2]

            for l in range(2):
                tid = 0
                segs = [(o, 1024) for o in A_SEGS] + [(O_MG, 6144)]
                for (o, n) in segs:
                    for c0 in range(0, n, 512):
                        st = stage()
                        self.DMA(st[:], self.w_in[l][:, o + c0:o + c0 + 512].rearrange("(k p) c -> p k c", p=128),
                                 [], [st], q=P.pool)
                        for t in range(4):
                            self.DMA(self.wA[l, tid], st[:, :, t * 128:(t + 1) * 128], [st], [])
                            tid += 1
                assert tid == 96
                for c0 in range(0, NKVB, 512):
                    n = min(512, NKVB - c0)
                    st = stage()
                    self.DMA(st[:, :, :n], self.w_in[l][:, O_KV + c0:O_KV + c0 + n].rearrange("(k p) c -> p k c", p=128),
                             [], [st], q=P.pool)
                    self.DMA(self.wB[l][:, :, c0:c0 + n], st[:, :, :n], [st], [])
                for c0 in range(0, D, 512):
                    st = stage()
                    self.DMA(st[:], self.w_out[l][:, c0:c0 + 512].rearrange("(k p) c -> p k c", p=128), [], [st], q=P.pool)
                    self.DMA(self.wO[l][:, :, c0:c0 + 512], st[:], [st], [])
                for c0 in range(0, 1024, 512):
                    st = stage()
                    self.DMA(st[:, 0:8, :], self.w_glu[l][:, c0:c0 + 512].rearrange("(k p) c -> p k c", p=128), [], [st], q=P.pool)
                    for t in range(4):
                        self.DMA(self.wG[l, c0 // 128 + t], st[:, 0:8, t * 128:(t + 1) * 128], [st], [])
                for i in range(3):
                    for c0 in range(0, D, 512):
                        st = stage()
                        self.DMA(st[:, 0:8, :], self.w_br[i][l][:, c0:c0 + 512].rearrange("(k p) c -> p k c", p=128), [], [st], q=P.pool)
                        for t in range(4):
                            self.DMA(self.wR[l, i, c0 // 128 + t], st[:, 0:8, t * 128:(t + 1) * 128], [st], [])
        P.barrier()

    def load_T(self, A, name, src_ap, rows, dst=None):
        P = self.P
        t = A.sbuf(name + "_ld", [rows, 128], F32)
        self.DMA(t[:], src_ap, [], [t])
        ps = self.nextpA()
        self.TR(ps[:, 0:rows], t[:], self.identf[0:rows, 0:rows], [t, self.identf], [ps])
        o = dst if dst is not None else A.sbuf(name, [128, rows], F32)
        self.CP(P.act, o[:, 0:rows], ps[:, 0:rows], [ps], [o])
        return o

    def layer_prep(self, l, LA):
        P = self.P
        dv, pl, ac = P.dve, P.pool, P.act
        L = type("L", (), {})()
        self.L = L
        L.gpre = LA.sbuf("gpre", [128, 16], F32)
        L.pscale = LA.sbuf("pscale", [128, 8], F32)
        L.dskip = LA.sbuf("dskip", [128, 8], F32)
        L.wpool = LA.sbuf("wpool", [128, 4, 2, 256], BF16)
        L.wphi2 = LA.sbuf("wphi2", [128, 2, 64, 64], BF16)
        L.pe2 = LA.sbuf("pe2", [128, 2, 64], BF16)
        L.cosT = LA.sbuf("cosT", [128, 32, TC], F32)
        L.sinT = LA.sbuf("sinT", [128, 32, TC], F32)
        L.rhoz = LA.sbuf("rhoz", [128, 32, TC], F32)
        L.abr = LA.sbuf("abr", [128, 32], F32)
        L.abi = LA.sbuf("abi", [128, 32], F32)
        L.BBr = LA.sbuf("BBr", [128, 8, 2, 128], BF16)
        L.BBi = LA.sbuf("BBi", [128, 8, 2, 128], BF16)
        L.CTr = LA.sbuf("CTr", [128, 32, 128], BF16)
        L.CTi = LA.sbuf("CTi", [128, 32, 128], BF16)
        with Arena(P) as A:
            self.load_T(A, "gpre", self.g_pre[l].rearrange("(k p) -> k p", p=128), 16, dst=L.gpre)
            self.load_T(A, "pscale", self.pool_scale[l].rearrange("(k p) -> k p", p=128), 8, dst=L.pscale)
            self.load_T(A, "dskip", self.d_skip[l].rearrange("(k p) -> k p", p=128), 8, dst=L.dskip)
            self.DMA(L.wpool[:], self.w_pool[l].rearrange("g (k p) d -> p g k d", p=128), [], [L.wpool], q=pl)
            for h in range(2):
                self.DMA(L.wphi2[64 * h:64 * h + 64], self.w_phi[l].rearrange("a l d e -> d a l e"), [], [L.wphi2], q=pl)
            pel = A.sbuf("pel", [128, 64], F32)
            self.DMA(pel[:], self.pe_cmp[l].rearrange("a l d -> (a l) d"), [], [pel])
            pel2 = A.sbuf("pel2", [128, 2, 64], F32)
            for h in range(2):
                self.CP(dv, pel2[:, h, :], pel[:], [pel], [pel2])
            ps = self.nextpA()
            self.TR(ps[:, 0:128], pel2[:].rearrange("p a d -> p (a d)"), self.identf[:], [pel2, self.identf], [ps])
            self.CP(ac, L.pe2[:].rearrange("p a l -> p (a l)"), ps[:, 0:128], [ps], [L.pe2])

            lr = self.load_T(A, "lr", self.lam_re[l], 32)
            li = self.load_T(A, "li", self.lam_im[l], 32)
            lsl = A.sbuf("lsl", [32, 2], F32)
            self.DMA(lsl[:], self.log_step[l], [], [lsl])
            lsx = A.sbuf("lsx", [32, 2, 64], F32)
            self.CP(dv, lsx[:], lsl[:].unsqueeze(2).to_broadcast([32, 2, 64]), [lsl], [lsx])
            ps = self.nextpA()
            self.TR(ps[:, 0:32], lsx[:].rearrange("p a n -> p (a n)"), self.identf[0:32, 0:32], [lsx, self.identf], [ps])
            dt = A.sbuf("dt", [128, 32], F32)
            self.AC(dt[:], ps[:, 0:32], AF.Exp, [ps], [dt])

            def T(name):
                return A.sbuf(name, [128, 32], F32)

            def mul(o, a, b):
                self.TT(dv, o[:], a[:], b[:], ALU.mult, [a, b], [o])

            def sub(o, a, b):
                self.TT(dv, o[:], a[:], b[:], ALU.subtract, [a, b], [o])

            def add(o, a, b):
                self.TT(dv, o[:], a[:], b[:], ALU.add, [a, b], [o])

            x, mag, th, c, s_, t1, t2, t3 = T("x"), T("mag"), T("th"), T("c"), T("s"), T("t1"), T("t2"), T("t3")
            mul(x, lr, dt)
            self.AC(mag[:], x[:], AF.Exp, [x], [mag])
            mul(th, li, dt)
            hp = A.sbuf("hp", [128, 1], F32)
            self.MS(dv, hp[:], math.pi / 2, [hp])
            self.AC(s_[:], th[:], AF.Sin, [th], [s_], scale=1.0 / 16)
            self.AC(c[:], th[:], AF.Sin, [th, hp], [c], scale=1.0 / 16, bias=hp[:, 0:1])

            def csq(cc, ss):
                mul(t1, cc, cc)
                mul(t2, ss, ss)
                mul(t3, cc, ss)
                sub(cc, t1, t2)
                self.TS(dv, ss[:], t3[:], 2.0, None, ALU.mult, None, [t3], [ss])

            for _ in range(4):
                csq(c, s_)
            mul(L.abr, mag, c)
            mul(L.abi, mag, s_)
            den, rden, a1, cor, coi = T("den"), T("rden"), T("a1"), T("cor"), T("coi")
            mul(t1, lr, lr)
            mul(t2, li, li)
            add(den, t1, t2)
            self.P.op(dv, lambda e: e.reciprocal(out=rden[:], in_=den[:]), [den], [rden])
            self.TS(dv, a1[:], L.abr[:], -1.0, None, ALU.add, None, [L.abr], [a1])
            mul(t1, a1, lr)
            mul(t2, L.abi, li)
            add(t3, t1, t2)
            mul(cor, t3, rden)
            mul(t1, L.abi, lr)
            mul(t2, a1, li)
            sub(t3, t1, t2)
            mul(coi, t3, rden)
            self.MS(dv, L.cosT[:, :, 0:1], 1.0, [L.cosT])
            self.MS(dv, L.sinT[:, :, 0:1], 0.0, [L.sinT])
            self.CP(dv, L.cosT[:, :, 1:2], c[:].unsqueeze(2), [c], [L.cosT])
            self.CP(dv, L.sinT[:, :, 1:2], s_[:].unsqueeze(2), [s_], [L.sinT])
            Ck, Sk = T("Ck"), T("Sk")
            self.CP(dv, Ck[:], c[:], [c], [Ck])
            self.CP(dv, Sk[:], s_[:], [s_], [Sk])
            tb1 = A.sbuf("tb1", [128, 32, TC // 2], F32)
            tb2 = A.sbuf("tb2", [128, 32, TC // 2], F32)
            csq(Ck, Sk)
            w = 2
            while w < TC:
                Cb = Ck[:].unsqueeze(2).to_broadcast([128, 32, w])
                Sb = Sk[:].unsqueeze(2).to_broadcast([128, 32, w])
                self.TT(dv, tb1[:, :, 0:w], L.cosT[:, :, 0:w], Cb, ALU.mult, [L.cosT, Ck], [tb1])
                self.TT(dv, tb2[:, :, 0:w], L.sinT[:, :, 0:w], Sb, ALU.mult, [L.sinT, Sk], [tb2])
                self.TT(dv, L.cosT[:, :, w:2 * w], tb1[:, :, 0:w], tb2[:, :, 0:w], ALU.subtract, [tb1, tb2], [L.cosT])
                self.TT(dv, tb1[:, :, 0:w], L.sinT[:, :, 0:w], Cb, ALU.mult, [L.sinT, Ck], [tb1])
                self.TT(dv, tb2[:, :, 0:w], L.cosT[:, :, 0:w], Sb, ALU.mult, [L.cosT, Sk], [tb2])
                self.TT(dv, L.sinT[:, :, w:2 * w], tb1[:, :, 0:w], tb2[:, :, 0:w], ALU.add, [tb1, tb2], [L.sinT])
                csq(Ck, Sk)
                w *= 2
            self.MS(dv, L.rhoz[:, :, 0:1], 0.0, [L.rhoz])
            self.CP(dv, L.rhoz[:, :, 1:TC], mag[:].unsqueeze(2).to_broadcast([128, 32, TC - 1]), [mag], [L.rhoz])

            bre = A.sbuf("bre", [128, 32, 16], F32)
            bim = A.sbuf("bim", [128, 32, 16], F32)
            self.DMA(bre[:], self.b_re[l].rearrange("g n c -> (g n) c").rearrange("(i p) c -> p i c", p=128), [], [bre])
            self.DMA(bim[:], self.b_im[l].rearrange("g n c -> (g n) c").rearrange("(i p) c -> p i c", p=128), [], [bim])
            u1 = A.sbuf("u1", [128, 32, 16], F32)
            u2 = A.sbuf("u2", [128, 32, 16], F32)
            bbr = A.sbuf("bbr", [128, 32, 16], F32)
            bbi = A.sbuf("bbi", [128, 32, 16], F32)
            corb = cor[:].unsqueeze(2).to_broadcast([128, 32, 16])
            coib = coi[:].unsqueeze(2).to_broadcast([128, 32, 16])
            self.TT(dv, u1[:], bre[:], corb, ALU.mult, [bre, cor], [u1])
            self.TT(dv, u2[:], bim[:], coib, ALU.mult, [bim, coi], [u2])
            self.TT(dv, bbr[:], u1[:], u2[:], ALU.subtract, [u1, u2], [bbr])
            self.TT(dv, u1[:], bim[:], corb, ALU.mult, [bim, cor], [u1])
            self.TT(dv, u2[:], bre[:], coib, ALU.mult, [bre, coi], [u2])
            self.TT(dv, bbi[:], u1[:], u2[:], ALU.add, [u1, u2], [bbi])
            spad = A.sbuf("spad", [128, 32, 128], BF16)
            for (bb, dst) in ((bbr, L.BBr), (bbi, L.BBi)):
                self.MS(pl, spad[:], 0.0, [spad])
                sv = spad[:].rearrange("p (k j) m -> p k j m", j=4)
                bv = bb[:].rearrange("p (k j) c -> p k j c", j=4)
                for il in range(4):
                    for g2 in range(2):
                        o = 32 * il + 16 * g2
                        self.CP(dv, sv[64 * g2:64 * g2 + 64, :, il, o:o + 16], bv[64 * g2:64 * g2 + 64, :, il, :], [bb], [spad])
                for i0 in range(0, 32, 8):
                    for j in range(8):
                        self.TR(self.pT[:, j * 128:(j + 1) * 128], spad[:, i0 + j, :], self.ident[:], [spad, self.ident], [self.pT])
                    for j in range(8):
                        i = i0 + j
                        hb = (i % 4) // 2
                        self.CP(ac, dst[64 * hb:64 * hb + 64, i // 4, i % 2, :], self.pT[64 * hb:64 * hb + 64, j * 128:(j + 1) * 128], [self.pT], [dst])
            for (csrc, dst, sgn) in ((self.c_re, L.CTr, 1.0), (self.c_im, L.CTi, -1.0)):
                cn = A.sbuf("cn", [32, 2, 16, 64], F32)
                self.DMA(cn[:], csrc[l], [], [cn])
                cn2 = A.sbuf("cn2", [32, 16, 2, 64], F32)
                self.CP(dv, cn2[:].rearrange("p c a n -> p a c n"), cn[:], [cn], [cn2])
                ps = self.nextpA()
                for cc in range(16):
                    self.TR(ps[:, cc * 32:(cc + 1) * 32], cn2[:, cc, :, :].rearrange("p a n -> p (a n)"), self.identf[0:32, 0:32], [cn2, self.identf], [ps])
                cst = A.sbuf("cst", [128, 32, 16], F32)
                self.TS(dv, cst[:].rearrange("p i c -> p c i"), ps[:, 0:512].rearrange("p (c i) -> p c i", c=16), sgn, None, ALU.mult, None, [ps], [cst])
                self.MS(pl, dst[:], 0.0, [dst])
                dv4 = dst[:].rearrange("p (k j) m -> p k j m", j=4)
                cv = cst[:].rearrange("p (k j) c -> p k j c", j=4)
                for il in range(4):
                    for g2 in range(2):
                        o = 32 * il + 16 * g2
                        self.CP(dv, dv4[64 * g2:64 * g2 + 64, :, il, o:o + 16], cv[64 * g2:64 * g2 + 64, :, il, :], [cst], [dst])

    def next_wt(self):
        self.wt_i += 1
        return self.wts[self.wt_i % len(self.wts)]

    def projA(self, src, N, rhs_of_k, nk, consumer, R):
        for idx, wsrc in enumerate(src):
            wt = self.next_wt()
            self.DMA(wt[:, 0:nk, :], wsrc, [], [wt])
            ps = self.nextpA()
            for k in range(nk):
                self.MM(ps[:, 0:N], wt[:, k, :], rhs_of_k(k), k == 0, k == nk - 1, [wt] + R, [ps])
            consumer(idx, ps)

    def norm_phase(self, src_rows, tiles, hT):
        P, L = self.P, self.L
        with Arena(P) as A:
            xt = A.sbuf("xt", [128, D], F32)
            junk = A.sbuf("junk", [128, D], BF16)
            ss = A.sbuf("ss", [128, 1], F32)
            hb = A.sbuf("hb", [128, D], BF16)
            for (r0, nr) in tiles:
                self.DMA(xt[0:nr], src_rows(r0, nr), [], [xt])
                self.AC(junk[0:nr], xt[0:nr], AF.Square, [xt], [junk, ss], accum_out=ss[0:nr, 0:1])
                self.TS(P.dve, ss[0:nr], ss[0:nr], 1.0 / D, EPS, ALU.mult, ALU.add, [ss], [ss])
                self.AC(ss[0:nr], ss[0:nr], AF.Sqrt, [ss], [ss])
                P.op(P.dve, lambda e: e.reciprocal(out=ss[0:nr], in_=ss[0:nr]), [ss], [ss])
                self.TS(P.dve, hb[0:nr], xt[0:nr], ss[0:nr, 0:1], None, ALU.mult, None, [xt, ss], [hb])
                for k0 in range(0, 16, 8):
                    for kk in range(8):
                        k = k0 + kk
                        self.TR(self.pT[:, kk * 128:kk * 128 + nr], hb[0:nr, k * 128:(k + 1) * 128],
                                self.ident[0:nr, 0:nr], [hb, self.ident], [self.pT])
                    for kk in range(8):
                        k = k0 + kk
                        self.AC(hT[:, k, r0:r0 + nr], self.pT[:, kk * 128:kk * 128 + nr], AF.Copy,
                                [self.pT, L.gpre], [hT], scale=L.gpre[:, k:k + 1])

    def merge_out_phase(self, l, N, tiles, hT, yz, src_rows, dst_rows, dst_bufs):
        P, L = self.P, self.L
        dv, ac = P.dve, P.act
        with Arena(P) as A:
            mT = A.sbuf("mT", [128, 16, N], BF16)
            gs = [A.sbuf(f"gs{i}", [128, N], F32) for i in range(3)]
            tm = [A.sbuf(f"tm{i}", [128, N], F32) for i in range(3)]
            for dt in range(16):
                brp = []
                for i in range(3):
                    wt = self.next_wt()
                    self.DMA(wt[:, 0:8, :], self.wR[l, i, dt], [], [wt])
                    ps = self.pK[i]
                    for k in range(8):
                        self.MM(ps[:, 0:N], wt[:, k, :], yz[i][:, k, 0:N], k == 0, k == 7, [wt, yz[i]], [ps])
                    brp.append(ps)
                for i in range(3):
                    wt = self.next_wt()
                    self.DMA(wt[:], self.wA[l, 48 + 16 * i + dt], [], [wt])
                    ps = self.nextpA()
                    for k in range(16):
                        self.MM(ps[:, 0:N], wt[:, k, :], hT[:, k, 0:N], k == 0, k == 15, [wt, hT], [ps])
                    self.AC(gs[i][:], ps[:, 0:N], AF.Sigmoid, [ps], [gs[i]])
                for i in range(3):
                    self.TT(dv, tm[i][:], gs[i][:], brp[i][:, 0:N], ALU.mult, [gs[i], brp[i]], [tm[i]])
                self.TT(dv, tm[0][:], tm[0][:], tm[1][:], ALU.add, [tm[0], tm[1]], [tm[0]])
                self.TT(dv, mT[:, dt, :], tm[0][:], tm[2][:], ALU.add, [tm[0], tm[2]], [mT])
            if self.debug:
                self.dump(f"mT{l}", mT[:], [128, 16, N], [mT])
            gpb = A.sbuf("gpb", [128, D], F32)
            self.DMA(gpb[:], self.g_post[l:l + 1, :].to_broadcast([128, D]), [], [gpb])
            of = [A.sbuf(f"of{i}", [128, D], F32) for i in range(len(tiles))]
            for ci, c0 in enumerate(range(0, D, 128)):
                w = self.next_wt()
                self.DMA(w[:], self.wO[l][:, :, c0:c0 + 128], [], [w])
                for ti, (r0, nr) in enumerate(tiles):
                    ps = self.pK[ti % 3]
                    for k in range(16):
                        self.MM(ps[0:nr, 0:128], mT[:, k, r0:r0 + nr], w[:, k, :], k == 0, k == 15, [mT, w], [ps])
                    self.CP(ac, of[ti][0:nr, c0:c0 + 128], ps[0:nr, 0:128], [ps], [of[ti]])
            junk = A.sbuf("junk2", [128, D], BF16)
            ss = A.sbuf("ss2", [128, 1], F32)
            xt = A.sbuf("xt2", [128, D], F32)
            for ti, (r0, nr) in enumerate(tiles):
                o = of[ti]
                self.DMA(xt[0:nr], src_rows(r0, nr), [], [xt])
                self.AC(junk[0:nr], o[0:nr], AF.Square, [o], [junk, ss], accum_out=ss[0:nr, 0:1])
                self.TS(dv, ss[0:nr], ss[0:nr], 1.0 / D, EPS, ALU.mult, ALU.add, [ss], [ss])
                self.AC(ss[0:nr], ss[0:nr], AF.Sqrt, [ss], [ss])
                P.op(dv, lambda e: e.reciprocal(out=ss[0:nr], in_=ss[0:nr]), [ss], [ss])
                self.STT(dv, o[0:nr], o[0:nr], ss[0:nr, 0:1], gpb[0:nr], ALU.mult, ALU.mult, [o, ss, gpb], [o])
                self.TT(dv, o[0:nr], o[0:nr], xt[0:nr], ALU.add, [o, xt], [o])
                self.DMA(dst_rows(r0, nr), o[0:nr], [o], [dst_bufs[ti]] if dst_bufs else [])

    def pool_phase(self, l, N, hT, yzp, halo, first, nseg=1):
        P, L = self.P, self.L
        dv, ac, pl = P.dve, P.act, P.pool
        n1 = N // nseg
        W = 15 + n1
        with Arena(P) as A:
            pu = A.sbuf("pu", [128, 8, nseg, W], F32)
            spz = A.sbuf("spz", [128, 8, N], BF16)
            sA = A.sbuf("sA", [128, 8, nseg, W], F32)
            sB = A.sbuf("sB", [128, 8, nseg, W], F32)
            df = A.sbuf("df", [128, 8, N], BF16)
            self.CP(dv, pu[:, :, :, 0:15], halo[:], [halo], [pu])
            self.projA([self.wA[l, t] for t in range(0, 8)], N, lambda k: hT[:, k, 0:N], 16,
                       lambda i, ps: self.CP(ac, pu[:, i, :, 15:W], ps[:, 0:N].rearrange("p (s c) -> p s c", s=nseg), [ps], [pu]), [hT])
            self.projA([self.wA[l, t] for t in range(8, 16)], N, lambda k: hT[:, k, 0:N], 16,
                       lambda i, ps: self.AC(spz[:, i, :], ps[:, 0:N], AF.Silu, [ps], [spz]), [hT])
            if self.debug and nseg == 1:
                self.dump(f"pu{l}", pu[:, :, 0, 15:W], [128, 8, n1], [pu])
            self.TT(dv, sA[:, :, :, 1:W], pu[:, :, :, 1:W], pu[:, :, :, 0:W - 1], ALU.add, [pu], [sA])
            self.TT(dv, sB[:, 2:8, :, 3:W], sA[:, 2:8, :, 3:W], sA[:, 2:8, :, 1:W - 2], ALU.add, [sA], [sB])
            self.TT(dv, sA[:, 4:8, :, 7:W], sB[:, 4:8, :, 7:W], sB[:, 4:8, :, 3:W - 4], ALU.add, [sB], [sA])
            self.TT(dv, sB[:, 6:8, :, 15:W], sA[:, 6:8, :, 15:W], sA[:, 6:8, :, 7:W - 8], ALU.add, [sA], [sB])
            dfv = df[:].rearrange("p t (s c) -> p t s c", s=nseg)
            for gi, (src, w) in enumerate(((sA, 2), (sB, 4), (sA, 8), (sB, 16))):
                t0 = 2 * gi
                self.STT(dv, dfv[:, t0:t0 + 2], src[:, t0:t0 + 2, :, 15:W], 1.0 / w, pu[:, t0:t0 + 2, :, 15:W],
                         ALU.mult, ALU.subtract, [src, pu], [df])
                if first:
                    if not hasattr(A, "_pt"):
                        A._pt = A.sbuf("ptmp", [128, 2, 15], F32)
                    tmp = A._pt
                    self.TT(dv, tmp[:], src[:, t0:t0 + 2, 0, 15:30], self.rc[:, gi:gi + 1, :].to_broadcast([128, 2, 15]),
                            ALU.mult, [src, self.rc], [tmp])
                    self.TT(dv, df[:, t0:t0 + 2, 0:15], tmp[:], pu[:, t0:t0 + 2, 0, 15:30], ALU.subtract, [tmp, pu], [df])
            for t in range(8):
                g = t // 2
                ps = self.nextpA()
                for kc in range(2):
                    self.MM(ps[:, 0:N], L.wpool[:, g, kc, (t % 2) * 128:(t % 2) * 128 + 128], df[:, 2 * g + kc, :],
                            kc == 0, kc == 1, [L.wpool, df], [ps])
                self.STT(dv, yzp[:, t, 0:N], ps[:, 0:N], L.pscale[:, t:t + 1], spz[:, t, :], ALU.mult, ALU.mult,
                         [ps, L.pscale, spz], [yzp])
            self.CP(dv, halo[:], pu[:, :, :, n1:W], [pu], [halo])
            return None

    def pool_state_out(self, halo_seg_ap, halo_buf, dst_ap):
        P = self.P
        with Arena(P) as A:
            po = A.sbuf("po", [15, 1024], F32)
            for half in range(2):
                ps = self.nextpA()
                for t in range(4):
                    self.TR(ps[0:15, t * 128:(t + 1) * 128], halo_seg_ap(4 * half + t), self.identf[:], [halo_buf, self.identf], [ps])
                self.CP(P.act, po[0:15, half * 512:(half + 1) * 512], ps[0:15, 0:512], [ps], [po])
            self.DMA(dst_ap, po[:], [po], [])

    def ssm_phase(self, l, N, hT, yzs, st, nseg=1):
        P, L = self.P, self.L
        dv, ac, pl = P.dve, P.act, P.pool
        n1 = N // nseg
        tc = min(TC, n1)
        nsub = n1 // tc
        F = 16 * nseg * tc
        with Arena(P) as A:
            su = A.sbuf("su", [128, 8, N], BF16)
            ssz = A.sbuf("ssz", [128, 8, N], BF16)
            zT = A.sbuf("zT", [128, 8, N], BF16)
            self.projA([self.wA[l, t] for t in range(32, 40)], N, lambda k: hT[:, k, 0:N], 16,
                       lambda i, ps: self.CP(ac, su[:, i, :], ps[:, 0:N], [ps], [su]), [hT])
            self.projA([self.wA[l, t] for t in range(40, 48)], N, lambda k: hT[:, k, 0:N], 16,
                       lambda i, ps: self.AC(ssz[:, i, :], ps[:, 0:N], AF.Silu, [ps], [ssz]), [hT])
            if self.debug and nseg == 1:
                self.dump(f"su{l}", su[:], [128, 8, N], [su])

            def arr(name, dt=F32):
                return A.sbuf(name, [128, 16, nseg, tc], dt)

            bur, bui, t1, t2, gr, gi, kr, ki, hr, hi, t3, t4 = (arr(n) for n in
                                                        ("bur", "bui", "t1", "t2", "gr", "gi", "kr", "ki", "hr", "hi", "t3", "t4"))
            hrb, hib = arr("hrb", BF16), arr("hib", BF16)
            yf = A.sbuf("yf", [128, 4, nseg, tc], F32)
            c1 = A.sbuf("c1", [128, 16, nseg], F32)
            c2 = A.sbuf("c2", [128, 16, nseg], F32)
            suv = su[:].rearrange("p k (s c) -> p k s c", s=nseg)
            zv = zT[:].rearrange("p k (s c) -> p k s c", s=nseg)
            for sc in range(nsub if KSTOP >= 2 else 0):
                c0 = sc * tc
                for hf in range(2):
                    i0 = 16 * hf
                    pbr, pbi, py = self.pK[0], self.pK[1], self.pK[2]
                    for ii in sorted(range(16), key=lambda q: (((i0 + q) % 4) // 2, q)):
                        i = i0 + ii
                        kt, hb, e = i // 4, (i % 4) // 2, i % 2
                        for s in range(nseg):
                            o = (ii * nseg + s) * tc
                            rhs = suv[64 * hb:64 * hb + 64, kt, s, c0:c0 + tc]
                            self.MM(pbr[:, o:o + tc], L.BBr[64 * hb:64 * hb + 64, kt, e, :], rhs, True, True, [L.BBr, su], [pbr])
                            self.MM(pbi[:, o:o + tc], L.BBi[64 * hb:64 * hb + 64, kt, e, :], rhs, True, True, [L.BBi, su], [pbi])
                    fl = lambda b: b[:].rearrange("p a s c -> p (a s c)")
                    self.CP(ac, fl(bur), pbr[:, 0:F], [pbr], [bur])
                    self.CP(ac, fl(bui), pbi[:, 0:F], [pbi], [bui])
                    if KSTOP < 3:
                        continue
                    cs = L.cosT[:, i0:i0 + 16, 0:tc].unsqueeze(2).to_broadcast([128, 16, nseg, tc])
                    sn = L.sinT[:, i0:i0 + 16, 0:tc].unsqueeze(2).to_broadcast([128, 16, nseg, tc])
                    rz = L.rhoz[:, i0:i0 + 16, 0:tc].unsqueeze(2).to_broadcast([128, 16, nseg, tc])
                    self.TT(dv, t1[:], bur[:], cs, ALU.mult, [bur, L.cosT], [t1])
                    self.TT(dv, t2[:], bui[:], sn, ALU.mult, [bui, L.sinT], [t2])
                    self.TT(dv, gr[:], t1[:], t2[:], ALU.add, [t1, t2], [gr])
                    self.TT(dv, t1[:], bui[:], cs, ALU.mult, [bui, L.cosT], [t1])
                    self.TT(dv, t2[:], bur[:], sn, ALU.mult, [bur, L.sinT], [t2])
                    self.TT(dv, gi[:], t1[:], t2[:], ALU.subtract, [t1, t2], [gi])
                    self.TT(dv, gr[:, :, :, 0], gr[:, :, :, 0], st["car_r"][:, i0:i0 + 16, :], ALU.add, [gr, st["car_r"]], [gr])
                    self.TT(dv, gi[:, :, :, 0], gi[:, :, :, 0], st["car_i"][:, i0:i0 + 16, :], ALU.add, [gi, st["car_i"]], [gi])
                    if KSTOP < 4:
                        continue
                    if nseg == 1:
                        rzf = L.rhoz[:, i0:i0 + 16, 0:tc] if tc == TC else None
                    else:
                        rzf = None
                    if rzf is None:
                        rzm = A.sbuf("rzm", [128, 16, nseg, tc], F32)
                        self.CP(dv, rzm[:], rz, [L.rhoz], [rzm])
                        rz2 = fl(rzm)
                        rzR = [rzm]
                    else:
                        rz2 = rzf.rearrange("p a c -> p (a c)")
                        rzR = [L.rhoz]
                    P.op(dv, lambda e: e.tensor_tensor_scan(out=fl(kr), data0=rz2, data1=fl(gr), initial=0.0,
                                                            op0=ALU.mult, op1=ALU.add), rzR + [gr], [kr])
                    P.op(dv, lambda e: e.tensor_tensor_scan(out=fl(ki), data0=rz2, data1=fl(gi), initial=0.0,
                                                            op0=ALU.mult, op1=ALU.add), rzR + [gi], [ki])
                    if KSTOP < 5:
                        continue
                    self.TT(dv, t3[:], kr[:], cs, ALU.mult, [kr, L.cosT], [t3])
                    self.TT(dv, t4[:], ki[:], sn, ALU.mult, [ki, L.sinT], [t4])
                    self.TT(dv, hr[:], t3[:], t4[:], ALU.subtract, [t3, t4], [hr])
                    self.TT(dv, t3[:], kr[:], sn, ALU.mult, [kr, L.sinT], [t3])
                    self.TT(dv, t4[:], ki[:], cs, ALU.mult, [ki, L.cosT], [t4])
                    self.TT(dv, hi[:], t3[:], t4[:], ALU.add, [t3, t4], [hi])
                    self.CP(ac, hrb[:], hr[:], [hr], [hrb])
                    self.CP(ac, hib[:], hi[:], [hi], [hib])
                    hlr, hli = st["hl_r"], st["hl_i"]
                    self.CP(dv, hlr[:, i0:i0 + 16, :], hr[:, :, :, tc - 1], [hr], [hlr])
                    self.CP(dv, hli[:, i0:i0 + 16, :], hi[:, :, :, tc - 1], [hi], [hli])
                    ab_r = L.abr[:, i0:i0 + 16].unsqueeze(2).to_broadcast([128, 16, nseg])
                    ab_i = L.abi[:, i0:i0 + 16].unsqueeze(2).to_broadcast([128, 16, nseg])
                    self.TT(dv, c1[:], hlr[:, i0:i0 + 16, :], ab_r, ALU.mult, [hlr, L.abr], [c1])
                    self.TT(dv, c2[:], hli[:, i0:i0 + 16, :], ab_i, ALU.mult, [hli, L.abi], [c2])
                    self.TT(dv, st["car_r"][:, i0:i0 + 16, :], c1[:], c2[:], ALU.subtract, [c1, c2], [st["car_r"]])
                    self.TT(dv, c1[:], hli[:, i0:i0 + 16, :], ab_r, ALU.mult, [hli, L.abr], [c1])
                    self.TT(dv, c2[:], hlr[:, i0:i0 + 16, :], ab_i, ALU.mult, [hlr, L.abi], [c2])
                    self.TT(dv, st["car_i"][:, i0:i0 + 16, :], c1[:], c2[:], ALU.add, [c1, c2], [st["car_i"]])
                    if KSTOP < 6:
                        continue
                    for ko in range(4):
                        kt = 4 * hf + ko
                        for s in range(nseg):
                            o = (ko * nseg + s) * tc
                            for il in range(4):
                                i = 4 * kt + il
                                ii = i - i0
                                self.MM(py[:, o:o + tc], L.CTr[:, i, :], hrb[:, ii, s, :], il == 0, False, [L.CTr, hrb], [py])
                                self.MM(py[:, o:o + tc], L.CTi[:, i, :], hib[:, ii, s, :], False, il == 3, [L.CTi, hib], [py])
                    if KSTOP < 7:
                        continue
                    for ko in range(4):
                        kt = 4 * hf + ko
                        self.STT(dv, yf[:, ko], suv[:, kt, :, c0:c0 + tc], L.dskip[:, kt:kt + 1],
                                 py[:, ko * nseg * tc:(ko + 1) * nseg * tc].rearrange("p (s c) -> p s c", s=nseg),
                                 ALU.mult, ALU.add, [su, L.dskip, py], [yf])
                    self.AC(zv[:, 4 * hf:4 * hf + 4, :, c0:c0 + tc], yf[:], AF.Gelu_apprx_tanh, [yf], [zT])
            if self.debug and nseg == 1:
                self.dump(f"zT{l}", zT[:], [128, 8, N], [zT])
            if KSTOP < 8:
                self.MS(dv, yzs[:], 0.0, [yzs])
                return
            sgt = [A.sbuf(f"sgt{i}", [128, N], BF16) for i in range(2)]

            def glu_cons(i, ps):
                sg = sgt[i % 2]
                self.AC(sg[:], ps[:, 0:N], AF.Sigmoid, [ps], [sg])
                self.TT(dv, sg[:], sg[:], zT[:, i, :], ALU.mult, [sg, zT], [sg])
                self.TT(dv, yzs[:, i, 0:N], sg[:], ssz[:, i, :], ALU.mult, [sg, ssz], [yzs])

            self.projA([self.wG[l, t] for t in range(8)], N, lambda k: zT[:, k, 0:N], 8, glu_cons, [zT])

    def ssm_state_out(self, hl_r_ap, hl_i_ap, bufs, dst_ap):
        P = self.P
        with Arena(P) as A:
            o = A.sbuf("sso", [32, 2, 128], F32)
            ps = self.nextpA()
            self.TR(ps[0:32, 0:128], hl_r_ap, self.identf[:], bufs + [self.identf], [ps])
            self.TR(ps[0:32, 128:256], hl_i_ap, self.identf[:], bufs + [self.identf], [ps])
            self.CP(P.act, o[:].rearrange("p a n -> p (a n)"), ps[0:32, 0:256], [ps], [o])
            self.DMA(dst_ap.rearrange("a i p -> i a p"), o[:], [o], [])

    def q_to_base(self, A, N, i, ps, q0):
        P = self.P
        ac = P.act
        for hh in range(2):
            h = 2 * i + hh
            k, g = h // 4, h % 4
            slot = (k // 2) * 4 + g
            need, have = 64 * (k % 2), 64 * hh
            if need == have or os.environ.get('KQS') == '0':
                self.CP(ac, q0[have:have + 64, slot, 0:N], ps[have:have + 64, 0:N], [ps], [q0])
            else:
                if not hasattr(A, "_tq"):
                    A._tq = A.sbuf("tq", [128, N], BF16)
                tq = A._tq
                self.CP(ac, tq[have:have + 64, :], ps[have:have + 64, 0:N], [ps], [tq])
                p2 = self.pS[hh]
                if have == 64:
                    self.MM(p2[0:64, 0:N], self.ident[64:128, 64:128], tq[64:128, :], True, True, [self.ident, tq], [p2])
                    self.CP(ac, q0[0:64, slot, 0:N], p2[0:64, 0:N], [p2], [q0])
                else:
                    self.MM(p2[:, 0:N], self.shu[0:64, :], tq[0:64, :], True, True, [self.shu, tq], [p2])
                    self.CP(ac, q0[64:128, slot, 0:N], p2[64:128, 0:N], [p2], [q0])

    def kv_formB(self, l, A, hT, tiles, kvf):
        for ci, c0 in enumerate(range(0, NKVB, 128)):
            n = min(128, NKVB - c0)
            w = self.next_wt()
            self.DMA(w[:, :, 0:n], self.wB[l][:, :, c0:c0 + n], [], [w])
            for ti, (r0, nr) in enumerate(tiles):
                ps = self.pK[ti % 3]
                for k in range(16):
                    self.MM(ps[0:nr, 0:n], hT[:, k, r0:r0 + nr], w[:, k, 0:n], k == 0, k == 15, [hT, w], [ps])
                self.CP(self.P.act, kvf[ti][0:nr, c0:c0 + n], ps[0:nr, 0:n], [ps], [kvf[ti]])

    def nsa_phase(self, l, j, hT, yza, ps_):
        P, L, C, S = self.P, self.L, self.C, self.S
        dv, ac, pl = P.dve, P.act, P.pool
        NT = S // 128
        with Arena(P) as A:
            q0 = A.sbuf("q0", [128, 8, C], BF16)
            saz = A.sbuf("saz", [128, 8, C], BF16)
            self.projA([self.wA[l, t] for t in range(16, 24)], C, lambda k: hT[:, k, 0:C], 16,
                       lambda i, ps: self.q_to_base(A, C, i, ps, q0), [hT])
            self.projA([self.wA[l, t] for t in range(24, 32)], C, lambda k: hT[:, k, 0:C], 16,
                       lambda i, ps: self.AC(saz[:, i, :], ps[:, 0:C], AF.Silu, [ps], [saz]), [hT])
            if KSTOP < 2:
                self.MS(dv, yza[:], 0.0, [yza])
                return
            tiles = [(tt * 128, 128) for tt in range(C // 128)]
            kvf = [A.sbuf(f"kvf{i}", [128, NKVB], F32) for i in range(len(tiles))]
            self.kv_formB(l, A, hT, tiles, kvf)
            if KSTOP < 3:
                self.MS(dv, yza[:], 0.0, [yza])
                return
            sg = A.sbuf("sg", [128, len(tiles), 48], F32)
            xtk = A.sbuf("xtk", [128, 2, C], BF16)
            xtv = A.sbuf("xtv", [128, 2, C], BF16)
            kvb = A.sbuf("kvb", [128, 1536], BF16)
            yb = A.sbuf("yb", [128, 1024], BF16)
            for tt, (r0, nr) in enumerate(tiles):
                T = (j * C) // 128 + tt
                self.DMA(self.kvp[l][T * 128:(T + 1) * 128, :], kvf[tt][:, 0:1024], [kvf[tt]], [])
                if T >= NT - 4:
                    w0 = (T - (NT - 4)) * 128
                    self.DMA(self.winp[l][w0:w0 + 128, :], kvf[tt][:, 1024:1536], [kvf[tt]], [])
                if KSUB < 1:
                    continue
                self.CP(dv, kvb[:], kvf[tt][:, 0:1536], [kvf[tt]], [kvb])
                self.AC(sg[:, tt, :], kvf[tt][:, 1536:1584], AF.Sigmoid, [kvf[tt]], [sg])
                if KSUB < 2:
                    continue
                self.CP(ac, ps_["vsel"][:, T, :, 0:64], kvb[:, 768:1024].rearrange("p (h d) -> p h d", h=4), [kvb], [ps_["vsel"]])
                self.CP(ac, ps_["vwin"][:, T % 8, :, 0:64], kvb[:, 1280:1536].rearrange("p (h d) -> p h d", h=4), [kvb], [ps_["vwin"]])
                if KSUB < 3:
                    continue
                if os.environ.get("KBAR") == "1":
                    P.barrier()
                tb = [self.pS[0], self.pS[1]]
                for n_, c_ in enumerate((512, 640, 1024, 1152, 0, 128, 256, 384)):
                    self.MT(tb[n_ // 4][:, (n_ % 4) * 128:(n_ % 4 + 1) * 128], kvb[:, c_:c_ + 128], [kvb], [tb[n_ // 4]])
                if KSUB < 4:
                    continue
                pv = lambda a: tb[a // 2][:, (a % 2) * 256:(a % 2 + 1) * 256].rearrange("p (r t) -> p r t", r=2)
                self.CP(ac, ps_["ksT"][:, :, T * 128:(T + 1) * 128], pv(0), [tb[0]], [ps_["ksT"]])
                self.CP(ac, ps_["kwT"][:, :, (T % 8) * 128:(T % 8 + 1) * 128], pv(1), [tb[0]], [ps_["kwT"]])
                self.CP(dv, xtk[:, :, tt * 128:(tt + 1) * 128], pv(2), [tb[1]], [xtk])
                self.CP(dv, xtv[:, :, tt * 128:(tt + 1) * 128], pv(3), [tb[1]], [xtv])
            nbn = C // 64
            b0 = (j * C) // 64
            for a_, xt_ in ((0, xtk), (1, xtv)):
                v4 = xt_[:].rearrange("p r (n l) -> p r n l", l=64)
                self.TT(dv, v4, v4, L.pe2[:, a_, :].unsqueeze(1).unsqueeze(1).to_broadcast([128, 2, nbn, 64]), ALU.add, [xt_, L.pe2], [xt_])
            pkc, pvc = self.pS[0], self.pS[1]
            for k in (0, 2, 1, 3):
                base, pr = 64 * (k % 2), k // 2
                for a_, xt_, pso, ob in ((0, xtk, pkc, base), (1, xtv, pvc, 0)):
                    col = (pr * nbn) if a_ == 0 else (k * nbn)
                    xl = xt_[:].rearrange("p r (n l) -> p r l n", l=64)
                    for ll in range(64):
                        self.MM(pso[ob:ob + 64, col:col + nbn], L.wphi2[base:base + 64, a_, ll, :], xl[base:base + 64, pr, ll, :],
                                ll == 0, ll == 63, [L.wphi2, xt_], [pso])
            self.CP(ac, ps_["kcT"][:, :, b0:b0 + nbn], pkc[:, 0:2 * nbn].rearrange("p (r n) -> p r n", r=2), [pkc], [ps_["kcT"]])
            self.CP(ac, ps_["vcT"][0:64, :, b0:b0 + nbn], pvc[0:64, 0:4 * nbn].rearrange("p (k n) -> p k n", k=4), [pvc], [ps_["vcT"]])
            pvt = self.pS[0]
            for k in range(4):
                self.MT(pvt[0:64, k * 64:(k + 1) * 64], ps_["vcT"][0:64, k, :], [ps_["vcT"]], [pvt])
            self.CP(ac, ps_["vca"][0:64, :, 0:64], pvt[0:64, 0:256].rearrange("p (k e) -> p k e", k=4), [pvt], [ps_["vca"]])
            if KSTOP < 5:
                self.MS(dv, yza[:], 0.0, [yza])
                return
            T0 = (j * C) // 128
            ntt = len(tiles)
            cbq = A.sbuf("cbq", [128, ntt, 64], F32)
            tsl = A.sbuf("tsl", [128, ntt, 64], F32)
            cbTf = A.sbuf("cbTf", [64, C], F32)
            cbT = A.sbuf("cbT", [64, C], BF16)
            self.DMA(cbq[:], self.t_cbq[T0:T0 + ntt].rearrange("t p n -> p t n"), [], [cbq])
            self.DMA(tsl[:], self.t_sel[T0:T0 + ntt].rearrange("t p n -> p t n"), [], [tsl])
            self.DMA(cbTf[:], self.t_cbT[:, j * C:(j + 1) * C], [], [cbTf])
            self.CP(dv, cbT[:], cbTf[:], [cbTf], [cbT])
            BT = A.sbuf("BT", [64, 4, C], BF16)
            s1 = A.sbuf("s1", [128, 4, 64], F32)
            z4 = A.sbuf("z4", [128, 4], F32)
            sc_ = A.sbuf("sc", [128, 64], F32)
            mx = A.sbuf("mx", [128, 8], F32)
            w1 = A.sbuf("w1", [128, 64], F32)
            bq = A.sbuf("bq", [128, 64], BF16)
            for tt in range(ntt):
                for k in range(4):
                    base, pr = 64 * (k % 2), k // 2
                    psc = self.pS[(tt * 4 + k) % 2]
                    for g in range(4):
                        self.MM(psc[:, g * 64:(g + 1) * 64], q0[base:base + 64, pr * 4 + g, tt * 128:(tt + 1) * 128],
                                ps_["kcT"][base:base + 64, pr, :], True, True, [q0, ps_["kcT"]], [psc])
                    self.TT(dv, s1[:], psc[:, 0:256].rearrange("p (g n) -> p g n", g=4),
                            cbq[:, tt, :].unsqueeze(1).to_broadcast([128, 4, 64]), ALU.add, [psc, cbq], [s1])
                    self.AC(s1[:], s1[:], AF.Exp, [s1], [s1], scale=0.125)
                    P.op(dv, lambda e: e.tensor_reduce(out=z4[:], in_=s1[:], axis=AX.X, op=ALU.add), [s1], [z4])
                    self.TS(dv, z4[:], z4[:], 1e-30, None, ALU.max, None, [z4], [z4])
                    P.op(dv, lambda e: e.reciprocal(out=z4[:], in_=z4[:]), [z4], [z4])
                    self.TT(dv, s1[:], s1[:], z4[:].unsqueeze(2).to_broadcast([128, 4, 64]), ALU.mult, [s1, z4], [s1])
                    P.op(dv, lambda e: e.tensor_reduce(out=sc_[:], in_=s1[:].rearrange("p g n -> p n g"), axis=AX.X, op=ALU.add), [s1], [sc_])
                    self.TT(dv, sc_[:], sc_[:], tsl[:, tt, :], ALU.add, [sc_, tsl], [sc_])
                    P.op(dv, lambda e: e.max(out=mx[:], in_=sc_[:]), [sc_], [mx])
                    P.op(dv, lambda e: e.match_replace(out=w1[:], in_to_replace=mx[:], in_values=sc_[:], imm_value=NEG), [mx, sc_], [w1])
                    P.op(dv, lambda e: e.max(out=mx[:], in_=w1[:]), [w1], [mx])
                    P.op(dv, lambda e: e.match_replace(out=w1[:], in_to_replace=mx[:], in_values=w1[:], imm_value=NEG), [mx, w1], [w1])
                    self.TT(dv, w1[:], sc_[:], w1[:], ALU.subtract, [sc_, w1], [w1])
                    self.TS(dv, w1[:], w1[:], 1.0, -1.0, ALU.min, ALU.add, [w1], [w1])
                    self.TS(dv, bq[:], w1[:], 1e30, None, ALU.mult, None, [w1], [bq])
                    pbt = self.pA[(tt * 4 + k) % 2]
                    self.MT(pbt[0:64, 0:128], bq[:], [bq], [pbt])
                    self.CP(ac, BT[0:64, k, tt * 128:(tt + 1) * 128], pbt[0:64, 0:128], [pbt], [BT])
            if KSTOP < 6:
                self.MS(dv, yza[:], 0.0, [yza])
                return
            yat = [A.sbuf(f"yat{i}", [128, 1024], F32) for i in range(ntt)]
            zz = A.sbuf("zz", [128, 4], F32)
            ytmp = A.sbuf("ytmp", [128, 4, 64], F32)
            pts = [A.sbuf(f"pt{i}", [128, C], BF16) for i in range(3)]
            pti = [0]
            scp = [self.pS[0], self.pS[1], self.pA[0], self.pA[1]]
            sci = [0]
            Gf = self.G[:].rearrange("p j c -> p (j c)")

            def score_exp(nk, mms):
                sci[0] += 1
                psx = scp[sci[0] % 4]
                for mi, (lhsT, rhs, R) in enumerate(mms):
                    self.MM(psx[0:nk, 0:C], lhsT, rhs, mi == 0, mi == len(mms) - 1, R, [psx])
                pti[0] += 1
                pt = pts[pti[0] % 3]
                self.AC(pt[0:nk, :], psx[0:nk, 0:C], AF.Exp, [psx], [pt], scale=0.125)
                return pt

            for k in range(4):
                base, pr = 64 * (k % 2), k // 2
                po = [self.pK[tt] for tt in range(ntt)]
                pov = [po[tt][:, 0:260].rearrange("p (g e) -> p g e", g=4) for tt in range(ntt)]
                for bi in range(3):
                    for g in range(4):
                        slot = pr * 4 + g
                        qh = q0[base:base + 64, slot, :]
                        if bi == 0:
                            pt = score_exp(64, [(ps_["kcT"][base:base + 64, pr, :], qh, [ps_["kcT"], q0]),
                                                (self.ident[0:64, 0:64], cbT[:, :], [self.ident, cbT])])
                            for tt in range(ntt):
                                self.MM(pov[tt][:, g, :], pt[0:64, tt * 128:(tt + 1) * 128], ps_["vca"][0:64, k, :], True, True,
                                        [pt, ps_["vca"]], [po[tt]])
                        elif bi == 1:
                            last = T0 + ntt - 1
                            for kt in range(0, last + 1):
                                pt = score_exp(128, [(ps_["ksT"][base:base + 64, pr, kt * 128:(kt + 1) * 128], qh, [ps_["ksT"], q0]),
                                                     (Gf[:, kt * 128:(kt + 1) * 128], BT[0:64, k, :], [self.G, BT])])
                                for tt in range(ntt):
                                    T = T0 + tt
                                    if kt == T:
                                        self.TT(dv, pt[:, tt * 128:(tt + 1) * 128], pt[:, tt * 128:(tt + 1) * 128], self.tri[:], ALU.mult, [pt, self.tri], [pt])
                                for tt in range(ntt):
                                    T = T0 + tt
                                    if kt <= T:
                                        self.MM(pov[tt][:, g, :], pt[:, tt * 128:(tt + 1) * 128], ps_["vsel"][:, kt, k, :], kt == 0, kt == T,
                                                [pt, ps_["vsel"]], [po[tt]])
                        else:
                            for kt in range(max(0, T0 - 4), T0 + ntt):
                                pt = score_exp(128, [(ps_["kwT"][base:base + 64, pr, (kt % 8) * 128:(kt % 8 + 1) * 128], qh, [ps_["kwT"], q0])])
                                for tt in range(ntt):
                                    T = T0 + tt
                                    if kt == T:
                                        self.TT(dv, pt[:, tt * 128:(tt + 1) * 128], pt[:, tt * 128:(tt + 1) * 128], self.tri[:], ALU.mult, [pt, self.tri], [pt])
                                    elif kt == T - 4:
                                        self.TT(dv, pt[:, tt * 128:(tt + 1) * 128], pt[:, tt * 128:(tt + 1) * 128], self.tris[:], ALU.mult, [pt, self.tris], [pt])
                                for tt in range(ntt):
                                    T = T0 + tt
                                    if T - 4 <= kt <= T:
                                        self.MM(pov[tt][:, g, :], pt[:, tt * 128:(tt + 1) * 128], ps_["vwin"][:, kt % 8, k, :],
                                                kt == max(0, T - 4), kt == T, [pt, ps_["vwin"]], [po[tt]])
                    for tt in range(ntt):
                        z4 = zz
                        self.TS(dv, z4[:], pov[tt][:, :, 64], 1e-30, None, ALU.max, None, [po[tt]], [z4])
                        P.op(dv, lambda e: e.reciprocal(out=z4[:], in_=z4[:]), [z4], [z4])
                        gsel = sg[:, tt, 12 * k:12 * k + 12].rearrange("p (g i) -> p g i", i=3)[:, :, bi]
                        self.TT(dv, z4[:], z4[:], gsel, ALU.mult, [z4, sg], [z4])
                        yv = yat[tt][:, k * 256:(k + 1) * 256].rearrange("p (g d) -> p g d", g=4)
                        fb = z4[:].unsqueeze(2).to_broadcast([128, 4, 64])
                        if bi == 0:
                            self.TT(dv, yv, pov[tt][:, :, 0:64], fb, ALU.mult, [po[tt], z4], [yat[tt]])
                        else:
                            tmp = ytmp
                            self.TT(dv, tmp[:], pov[tt][:, :, 0:64], fb, ALU.mult, [po[tt], z4], [tmp])
                            self.TT(dv, yv, yv, tmp[:], ALU.add, [yat[tt], tmp], [yat[tt]])
            if KSTOP < 7:
                self.MS(dv, yza[:], 0.0, [yza])
                return
            for tt in range(ntt):
                if self.debug:
                    self.dump(f"yat{l}_{j}_{tt}", yat[tt][:], [128, 1024], [yat[tt]])
                self.CP(dv, yb[:], yat[tt][:], [yat[tt]], [yb])
                for hf4 in range(2):
                    pyt = self.pS[hf4]
                    for t4 in range(4):
                        t8 = 4 * hf4 + t4
                        self.MT(pyt[:, t4 * 128:(t4 + 1) * 128], yb[:, t8 * 128:(t8 + 1) * 128], [yb], [pyt])
                    self.TT(dv, yza[:, 4 * hf4:4 * hf4 + 4, tt * 128:(tt + 1) * 128], pyt[:, :].rearrange("p (t c) -> p t c", t=4),
                            saz[:, 4 * hf4:4 * hf4 + 4, tt * 128:(tt + 1) * 128], ALU.mult, [pyt, saz], [yza])

    def prompt_layer(self, l):
        P, S, C = self.P, self.S, self.C
        dv, pl = P.dve, P.pool
        NT = S // 128
        with Arena(P) as PA:
            ps_ = {
                "ksT": PA.sbuf("ksT", [128, 2, S], BF16),
                "vsel": PA.sbuf("vsel", [128, NT, 4, 65], BF16),
                "kwT": PA.sbuf("kwT", [128, 2, 1024], BF16),
                "vwin": PA.sbuf("vwin", [128, 8, 4, 65], BF16),
                "kcT": PA.sbuf("kcT", [128, 2, 64], BF16),
                "vcT": PA.sbuf("vcT", [64, 4, 64], BF16),
                "vca": PA.sbuf("vca", [64, 4, 65], BF16),
            }
            halo = PA.sbuf("halo", [128, 8, 1, 15], F32)
            st = {n: PA.sbuf(n, [128, 32, 1], F32) for n in ("car_r", "car_i", "hl_r", "hl_i")}
            self.MS(pl, ps_["vsel"][:], 1.0, [ps_["vsel"]])
            self.MS(pl, ps_["vwin"][:], 1.0, [ps_["vwin"]])
            self.MS(pl, ps_["vca"][:], 1.0, [ps_["vca"]])
            self.MS(pl, ps_["kcT"][:], 0.0, [ps_["kcT"]])
            self.MS(pl, ps_["vcT"][:], 0.0, [ps_["vcT"]])
            self.MS(dv, halo[:], 0.0, [halo])
            for b in st.values():
                self.MS(dv, b[:], 0.0, [b])
            src = self.xp if l == 0 else None
            for j in range(self.NCH):
                tiles = [(tt * 128, 128) for tt in range(C // 128)]
                T0 = (j * C) // 128

                def src_rows(r0, nr, j=j):
                    if l == 0:
                        return self.xp[j * C + r0:j * C + r0 + nr, :]
                    return self.x1_tiles[(j * C + r0) // 128][0:nr, :]

                def dst_rows(r0, nr, j=j):
                    if l == 0:
                        return self.x1_tiles[(j * C + r0) // 128][0:nr, :]
                    return self.yp[j * C + r0:j * C + r0 + nr, :]

                with Arena(P) as CA:
                    hT = CA.sbuf("hT", [128, 16, C], BF16)
                    yz = [CA.sbuf(f"yz{i}", [128, 8, C], BF16) for i in range(3)]
                    if l == 1:
                        for tt in range(C // 128):
                            xb_ = self.x1_tiles[T0 + tt]
                            for d in list(xb_.w.values()):
                                P._wait(P.sp, d)
                    self.norm_phase(src_rows, tiles, hT)
                    if "pool" in self.parts:
                        self.pool_phase(l, C, hT, yz[0], halo, first=(j == 0))
                    else:
                        self.MS(dv, yz[0][:], 0.0, [yz[0]])
                    if "nsa" in self.parts:
                        self.nsa_phase(l, j, hT, yz[1], ps_)
                    else:
                        self.MS(dv, yz[1][:], 0.0, [yz[1]])
                    if "ssm" in self.parts:
                        self.ssm_phase(l, C, hT, yz[2], st)
                    else:
                        self.MS(dv, yz[2][:], 0.0, [yz[2]])
                    if self.debug and j == 0:
                        for i in range(3):
                            self.dump(f"yz{i}_{l}", yz[i][:], [128, 8, C], [yz[i]])
                    dst_bufs = [self.x1_tiles[T0 + tt] for tt in range(C // 128)] if l == 0 else None
                    self.merge_out_phase(l, C, tiles, hT, yz, src_rows, dst_rows, dst_bufs)
            self.pool_state_out(lambda t: halo[:, t, 0, :], halo, self.poolp[l])
            self.ssm_state_out(st["hl_r"][:, :, 0], st["hl_i"][:, :, 0], [st["hl_r"], st["hl_i"]], self.ssmp[l])

    def build(self):
        P = self.P
        self.declare()
        self.x1_tiles = [Buf(self.x1.t[t * 128:(t + 1) * 128, :], f"x1_{t}") for t in range(self.S // 128)]
        self.consts()
        self.wts = [P.sbuf(f"wt{i}", [128, 16, 128], BF16) for i in range(6)]
        self.wt_i = 0
        self.prepass()
        for l in range(2):
            with Arena(P) as LA:
                self.layer_prep(l, LA)
                self.prompt_layer(l)
                if self.NS:
                    self.sample_layer(l)
        P.finish()
        P.close()
        return self.nc


def _tables():
    pos = np.arange(SEQ)
    n = np.arange(64)
    nvalid = (pos + 1) // 64
    cbq = np.where(n[None, :] < nvalid[:, None], 0.0, NEG).astype(np.float32)
    cur = pos // 64
    tsel = np.zeros((SEQ, 64), np.float32)
    tsel[n[None, :] > cur[:, None]] = NEG
    tsel[n[None, :] == cur[:, None]] = 1e4
    tsel[n[None, :] == (cur[:, None] - 1)] = 2e4
    tsel[:, 0] = 3e4
    rc = np.zeros((4, 15), np.float32)
    for gi, w in enumerate((2, 4, 8, 16)):
        rc[gi] = 1.0 / np.minimum(np.arange(15) + 1, w)
    return {
        "t_cbq": np.ascontiguousarray(cbq.reshape(SEQ // 128, 128, 64)),
        "t_sel": np.ascontiguousarray(tsel.reshape(SEQ // 128, 128, 64)),
        "t_cbT": np.ascontiguousarray(cbq.T),
        "t_rc": np.ascontiguousarray(np.tile(rc.reshape(1, 60), (128, 1))),
    }


_WKEYS = ("g_pre", "g_post", "w_in", "w_pool", "pool_scale", "pe_cmp", "w_phi", "d_skip", "w_glu",
          "w_br_pool", "w_br_nsa", "w_br_ssm", "w_out", "b_re", "b_im")


def _core_inputs(inp, b, S, NS, samples):
    f = lambda a: np.ascontiguousarray(np.asarray(a, dtype=np.float32))
    m = {k: f(inp[k]) for k in _WKEYS}
    m["lam_re"] = f(inp["lam_re"]).reshape(2, 32, 128)
    m["lam_im"] = f(inp["lam_im"]).reshape(2, 32, 128)
    m["log_step"] = f(inp["log_step"]).reshape(2, 32, 2)
    m["c_re"] = f(inp["c_re"]).reshape(2, 32, 2, 16, 64)
    m["c_im"] = f(inp["c_im"]).reshape(2, 32, 2, 16, 64)
    m["xp"] = f(inp["x_prompt"][b, :S])
    m.update(_tables())
    if NS:
        sl = list(samples)
        m["xs"] = f(inp["x_sample"][sl]).reshape(4 * NS, D)
        ck = np.asarray(inp["cache_kv"], dtype=np.float32)
        m["cache0"] = np.ascontiguousarray(ck[0]).reshape(NPOOL * 128 * 2, 512)
        m["cache1"] = np.ascontiguousarray(ck[1]).reshape(NPOOL * 128 * 2, 512)
        m["ptab"] = np.ascontiguousarray(np.asarray(inp["page_table"])[sl].astype(np.int32))
        m["swin"] = f(np.asarray(inp["state_win_kv"])[:, sl]).reshape(2, NS, 512, 512)
        m["spool"] = f(np.asarray(inp["state_pool"])[:, sl])
        m["sssm"] = f(np.asarray(inp["state_ssm"])[:, sl]).reshape(2, NS, 2, 32, 128)
        N = 4 * NS
        col = np.arange(N)
        cm = (col[None, :] // 4 == np.arange(NS)[:, None]).astype(np.float32)
        p = np.arange(128)
        wm = cm[None, :, :] * (p[:, None, None] > (col % 4)[None, None, :])
        m["t_cm"] = np.ascontiguousarray(np.broadcast_to(cm[None], (128, NS, N)).astype(np.float32))
        m["t_wm"] = np.ascontiguousarray(wm.astype(np.float32))
        m16 = ((col[:, None] // 4 == col[None, :] // 4) & (col[:, None] % 4 <= col[None, :] % 4)).astype(np.float32)
        m["t_m16"] = np.ascontiguousarray(m16)
        ts = np.zeros((N, 256), np.float32); ts[:, 0] = 3e4; ts[:, 255] = 2e4
        m["t_ssel"] = ts
    return m


_NC_CACHE = {}


def run(inp, S=SEQ, NS=4, ncores=2, debug=False, parts=("pool", "ssm", "nsa"), trace=False):
    key = (S, NS, debug, tuple(parts))
    if key not in _NC_CACHE:
        kb = KB(S=S, NS=NS, debug=debug, parts=parts)
        kb.build()
        _NC_CACHE[key] = kb
    kb = _NC_CACHE[key]
    in_maps = [_core_inputs(inp, c, S, NS, range(NS * c, NS * c + NS)) for c in range(ncores)]
    res = run_bass_kernel_spmd(kb.nc, in_maps, core_ids=list(range(ncores)), **({"trace": True} if trace else {}))
    return kb, res


def kernel(**inputs):
    kb, res = run(inputs)
    r = res.results
    B = 2
    y_prompt = np.stack([r[b]["yp"] for b in range(B)])
    y_sample = np.concatenate([r[c]["ys"].reshape(4, 4, D) for c in range(2)], axis=0)
    kv_p = np.stack([r[b]["kvp"] for b in range(B)], axis=1).reshape(2, B, SEQ, 4, 4, 64)
    kv_s = np.concatenate([r[c]["kvs"].reshape(2, 4, 4, 4, 4, 64) for c in range(2)], axis=1)
    win_p = np.stack([r[b]["winp"] for b in range(B)], axis=1).reshape(2, B, 512, 2, 4, 64)
    win_s = np.concatenate([r[c]["wins"].reshape(2, 4, 512, 2, 4, 64) for c in range(2)], axis=1)
    pool_p = np.stack([r[b]["poolp"] for b in range(B)], axis=1)
    pool_s = np.concatenate([r[c]["pools"] for c in range(2)], axis=1)
    ssm_p = np.stack([r[b]["ssmp"] for b in range(B)], axis=1).reshape(2, B, 2, 64, 64)
    ssm_s = np.concatenate([r[c]["ssms"].reshape(2, 4, 2, 64, 64) for c in range(2)], axis=1)
    outs = (y_prompt, y_sample, kv_p, kv_s, win_p, win_s, pool_p, pool_s, ssm_p, ssm_s)
    return tuple(np.ascontiguousarray(o, dtype=np.float32) for o in outs)


def _sample_layer(self, l):
    P, L, NS = self.P, self.L, self.NS
    dv, ac, pl = P.dve, P.act, P.pool
    N = 4 * NS
    tiles = [(0, N)]
    with Arena(P) as SA:
        hT = SA.sbuf("hTs", [128, 16, N], BF16)
        yz = [SA.sbuf(f"yzs{i}", [128, 8, N], BF16) for i in range(3)]

        def src_rows(r0, nr):
            return self.xs[r0:r0 + nr, :] if l == 0 else self.xs1[r0:r0 + nr, :]

        def dst_rows(r0, nr):
            return self.xs1[r0:r0 + nr, :] if l == 0 else self.ys[r0:r0 + nr, :]

        if l == 1:
            for d in list(self.xs1.w.values()):
                P._wait(P.sp, d)
        self.norm_phase(src_rows, tiles, hT)
        halo = SA.sbuf("halos", [128, 8, NS, 15], F32)
        with Arena(P) as A:
            for s in range(NS):
                sp = A.sbuf("spl", [15, 1024], F32)
                self.DMA(sp[:], self.spool[l, s], [], [sp])
                for half in range(2):
                    ps = self.nextpA()
                    for t in range(4):
                        tt = 4 * half + t
                        self.MT(ps[:, t * 16:t * 16 + 15], sp[0:15, tt * 128:(tt + 1) * 128], [sp], [ps])
                    self.CP(ac, halo[:, 4 * half:4 * half + 4, s, :], ps[:, 0:64].rearrange("p (t c) -> p t c", t=4)[:, :, 0:15], [ps], [halo])
        self.pool_phase(l, N, hT, yz[0], halo, first=False, nseg=NS)
        for s in range(NS):
            self.pool_state_out(lambda t, s=s: halo[:, t, s, :], halo, self.pools[l, s])
        st = {n: SA.sbuf(n + "s", [128, 32, NS], F32) for n in ("car_r", "car_i", "hl_r", "hl_i")}
        with Arena(P) as A:
            c1 = A.sbuf("c1s", [128, 32, NS], F32)
            c2 = A.sbuf("c2s", [128, 32, NS], F32)
            for s in range(NS):
                for ri, nm in ((0, "hl_r"), (1, "hl_i")):
                    t_ = self.load_T(A, "h0", self.sssm[l, s, ri], 32)
                    self.CP(dv, st[nm][:, :, s], t_[:, 0:32], [t_], [st[nm]])
            ab_r = L.abr[:].unsqueeze(2).to_broadcast([128, 32, NS])
            ab_i = L.abi[:].unsqueeze(2).to_broadcast([128, 32, NS])
            self.TT(dv, c1[:], st["hl_r"][:], ab_r, ALU.mult, [st["hl_r"], L.abr], [c1])
            self.TT(dv, c2[:], st["hl_i"][:], ab_i, ALU.mult, [st["hl_i"], L.abi], [c2])
            self.TT(dv, st["car_r"][:], c1[:], c2[:], ALU.subtract, [c1, c2], [st["car_r"]])
            self.TT(dv, c1[:], st["hl_i"][:], ab_r, ALU.mult, [st["hl_i"], L.abr], [c1])
            self.TT(dv, c2[:], st["hl_r"][:], ab_i, ALU.mult, [st["hl_r"], L.abi], [c2])
            self.TT(dv, st["car_i"][:], c1[:], c2[:], ALU.add, [c1, c2], [st["car_i"]])
        self.ssm_phase(l, N, hT, yz[2], st, nseg=NS)
        for s in range(NS):
            self.ssm_state_out(st["hl_r"][:, :, s], st["hl_i"][:, :, s], [st["hl_r"], st["hl_i"]], self.ssms[l, s])
        self.nsa_sample(l, hT, yz[1])
        self.merge_out_phase(l, N, tiles, hT, yz, src_rows, dst_rows, [self.xs1] if l == 0 else None)


def _nsa_sample(self, l, hT, yza):
    P, L, NS = self.P, self.L, self.NS
    dv, ac, pl = P.dve, P.act, P.pool
    N = 4 * NS
    cache_l = self.cache[l]
    with Arena(P) as A:
        q0 = A.sbuf("q0s", [128, 8, N], BF16)
        saz = A.sbuf("sazs", [128, 8, N], BF16)
        self.projA([self.wA[l, t] for t in range(16, 24)], N, lambda k: hT[:, k, 0:N], 16,
                   lambda i, ps: self.q_to_base(A, N, i, ps, q0), [hT])
        self.projA([self.wA[l, t] for t in range(24, 32)], N, lambda k: hT[:, k, 0:N], 16,
                   lambda i, ps: self.AC(saz[:, i, :], ps[:, 0:N], AF.Silu, [ps], [saz]), [hT])
        kvf = A.sbuf("kvfs", [128, NKVB], F32)
        self.kv_formB(l, A, hT, [(0, N)], [kvf])
        self.DMA(self.kvs[l][0:N, :], kvf[0:N, 0:1024], [kvf], [])
        for s in range(NS):
            self.DMA(self.wins[l, s, 0:508, :], self.swin[l, s, 4:512, :], [], [])
            self.DMA(self.wins[l, s, 508:512, :], kvf[4 * s:4 * s + 4, 1024:1536], [kvf], [])
        sg = A.sbuf("sgs", [N, 48], F32)
        self.AC(sg[:], kvf[0:N, 1536:1584], AF.Sigmoid, [kvf], [sg])
        kvb = A.sbuf("kvbs", [N, 1536], BF16)
        self.CP(dv, kvb[:], kvf[0:N, 0:1536], [kvf], [kvb])
        knT = A.sbuf("knT", [128, 4, N], BF16)
        ptn = self.pS[0]
        for n_, c_ in enumerate((512, 640, 1024, 1152)):
            self.MT(ptn[:, n_ * N:(n_ + 1) * N], kvb[0:N, c_:c_ + 128], [kvb], [ptn])
        self.CP(ac, knT[:], ptn[:, 0:4 * N].rearrange("p (a n) -> p a n", a=4), [ptn], [knT])
        vn = A.sbuf("vn", [N, 2, 4, 65], BF16)
        self.MS(pl, vn[:], 1.0, [vn])
        self.CP(ac, vn[:, 0, :, 0:64], kvb[:, 768:1024].rearrange("p (h d) -> p h d", h=4), [kvb], [vn])
        self.CP(ac, vn[:, 1, :, 0:64], kvb[:, 1280:1536].rearrange("p (h d) -> p h d", h=4), [kvb], [vn])
        cmf = A.sbuf("cmf", [128, NS, N], F32)
        wmf = A.sbuf("wmf", [128, NS, N], F32)
        m16f = A.sbuf("m16f", [N, N], F32)
        self.DMA(cmf[:], self.t_cm, [], [cmf])
        self.DMA(wmf[:], self.t_wm, [], [wmf])
        self.DMA(m16f[:], self.t_m16, [], [m16f])
        cm = A.sbuf("cm", [128, NS, N], BF16)
        wm = A.sbuf("wm", [128, NS, N], BF16)
        m16 = A.sbuf("m16", [N, N], BF16)
        self.CP(dv, cm[:], cmf[:], [cmf], [cm])
        self.CP(dv, wm[:], wmf[:], [wmf], [wm])
        self.CP(dv, m16[:], m16f[:], [m16f], [m16])
        pti = A.sbuf("pti", [128, NS * 128], I32)
        self.DMA(pti[:], self.ptab.rearrange("(o s) g -> o (s g)", o=1).to_broadcast([128, NS * 128]), [], [pti])
        iop = A.sbuf("iop", [128, 1], F32)
        P.op(pl, lambda e: e.iota(iop[:], pattern=[[0, 1]], base=0, channel_multiplier=2, allow_small_or_imprecise_dtypes=True), (), [iop])
        idxf = A.sbuf("idxf", [128, NS * 128], F32)
        self.TS(dv, idxf[:], pti[:], 256.0, iop[:, 0:1], ALU.mult, ALU.add, [pti, iop], [idxf])
        idxA = A.sbuf("idxA", [128, NS * 128], I32)
        idxB = A.sbuf("idxB", [128, NS * 128], I32)
        self.CP(dv, idxA[:], idxf[:], [idxf], [idxA])
        self.TS(dv, idxB[:], idxf[:], 1.0, None, ALU.add, None, [idxf], [idxB])
        yat = A.sbuf("yats", [N, 1024], F32)
        pgs = [A.sbuf(f"pg{i}", [128, 512], F32) for i in range(3)]
        pgb = [A.sbuf(f"pgb{i}", [128, 512], BF16) for i in range(2)]
        pts = [A.sbuf(f"pts{i}", [128, 16, N], BF16) for i in range(2)]
        cnt = {"pg": 0, "pt": 0, "sc": 0}
        Gf = self.G[:].rearrange("p j c -> p (j c)")
        po = [self.pK[0], self.pK[1], self.pK[2], self.pS[1]]
        pov = [p_[0:N, 0:260].rearrange("p (g e) -> p g e", g=4) for p_ in po]
        scps = [self.pA[0], self.pA[1]]
        zb = A.sbuf("zb", [128, 260], BF16)
        self.MS(pl, zb[:], 0.0, [zb])

        def gather_page(s, g, idx):
            cnt["pg"] += 1
            pg = pgs[cnt["pg"] % 3]
            P.gather(pg[:], cache_l, idx[:, s * 128 + g:s * 128 + g + 1], [idx], [pg])
            b = pgb[cnt["pg"] % 2]
            self.CP(dv, b[:], pg[:], [pg], [b])
            return b

        def attend(nk, kT_of, bias_of, mask_ap, mask_R, v_of, first, last):
            cnt["sc"] += 1
            psx = scps[cnt["sc"] % 2]
            if bias_of:
                bz = bias_of(None)
                self.MM(psx[0:nk, 0:16 * N], bz[0], bz[1], True, False, bz[2], [psx])
            for k in (0, 2, 1, 3):
                base, pr = 64 * (k % 2), k // 2
                kT, kR = kT_of(k)
                for g in range(4):
                    o = (k * 4 + g) * N
                    self.MM(psx[0:nk, o:o + N], kT, q0[base:base + 64, pr * 4 + g, :], bias_of is None, True, kR + [q0], [psx])
            cnt["pt"] += 1
            pt = pts[cnt["pt"] % 2]
            ptf = pt[0:nk].rearrange("p a n -> p (a n)")
            self.AC(ptf, psx[0:nk, 0:16 * N], AF.Exp, [psx], [pt], scale=0.125)
            self.TT(dv, pt[0:nk], pt[0:nk], mask_ap.unsqueeze(1).to_broadcast([nk, 16, N]), ALU.mult, [pt] + mask_R, [pt])
            if first:
                for k in range(4):
                    self.MM(po[k][0:N, 0:260], zb[:, 0:N], zb[:, 0:260], True, False, [zb], [po[k]])
            for k in range(4):
                vv, vR = v_of(k)
                for g in range(4):
                    self.MM(pov[k][:, g, :], pt[0:nk, k * 4 + g, :], vv, False, last, [pt] + vR, [po[k]])

        def finish_branch(bi):
            for k in range(4):
                z4 = A.sbuf("zzs", [N, 4], F32)
                self.TS(dv, z4[:], pov[k][:, :, 64], 1e-30, None, ALU.max, None, [po[k]], [z4])
                P.op(dv, lambda e: e.reciprocal(out=z4[:], in_=z4[:]), [z4], [z4])
                gsel = sg[:, 12 * k:12 * k + 12].rearrange("p (g i) -> p g i", i=3)[:, :, bi]
                self.TT(dv, z4[:], z4[:], gsel, ALU.mult, [z4, sg], [z4])
                yv = yat[:, k * 256:(k + 1) * 256].rearrange("p (g d) -> p g d", g=4)
                fb = z4[:].unsqueeze(2).to_broadcast([N, 4, 64])
                if bi == 0:
                    self.TT(dv, yv, pov[k][:, :, 0:64], fb, ALU.mult, [po[k], z4], [yat])
                else:
                    tmp = A.sbuf("ytmps", [N, 4, 64], F32)
                    self.TT(dv, tmp[:], pov[k][:, :, 0:64], fb, ALU.mult, [po[k], z4], [tmp])
                    self.TT(dv, yv, yv, tmp[:], ALU.add, [yat, tmp], [yat])

        XTk = A.sbuf("XTk", [128, 2, 4096], BF16)
        XTv = A.sbuf("XTv", [128, 2, 4096], BF16)
        kcT = A.sbuf("kcTs", [128, 2, 256], BF16)
        vcT = A.sbuf("vcTs", [64, 4, 64], BF16)
        vca = A.sbuf("vcas", [128, 2, 4, 65], BF16)
        BT = A.sbuf("BTs", [64, NS, 4, 4, 4 * N], BF16)
        self.MS(pl, vca[:], 1.0, [vca])
        s1 = A.sbuf("s1s", [N, 4, 256], F32)
        z4a = A.sbuf("z4a", [N, 4], F32)
        sc_ = A.sbuf("scs", [N, 256], F32)
        mx = A.sbuf("mxs", [N, 8], F32)
        w1 = A.sbuf("w1s", [N, 256], F32)
        bq = A.sbuf("bqs", [N, 256], BF16)
        tsl = A.sbuf("tsls", [N, 256], F32)
        self.DMA(tsl[:], self.t_ssel, [], [tsl])
        for s in range(NS):
            for grp in range(4):
                for gg in range(32):
                    g = grp * 32 + gg
                    b = gather_page(s, g, idxA)
                    ptx = self.pS[0]
                    for n_ in range(4):
                        self.MT(ptx[:, n_ * 128:(n_ + 1) * 128], b[:, n_ * 128:(n_ + 1) * 128], [b], [ptx])
                    self.CP(ac, XTk[:, :, gg * 128:(gg + 1) * 128], ptx[:, 0:256].rearrange("p (r t) -> p r t", r=2), [ptx], [XTk])
                    self.CP(ac, XTv[:, :, gg * 128:(gg + 1) * 128], ptx[:, 256:512].rearrange("p (r t) -> p r t", r=2), [ptx], [XTv])
                for a_, xt_ in ((0, XTk), (1, XTv)):
                    v4 = xt_[:].rearrange("p r (n l) -> p r n l", l=64)
                    self.TT(dv, v4, v4, L.pe2[:, a_, :].unsqueeze(1).unsqueeze(1).to_broadcast([128, 2, 64, 64]), ALU.add, [xt_, L.pe2], [xt_])
                pkc, pvc = self.pA[0], self.pA[1]
                for k in (0, 2, 1, 3):
                    base, pr = 64 * (k % 2), k // 2
                    for a_, xt_, pso, ob in ((0, XTk, pkc, base), (1, XTv, pvc, 0)):
                        col = (pr * 64) if a_ == 0 else (k * 64)
                        xl = xt_[:].rearrange("p r (n l) -> p r l n", l=64)
                        for ll in range(64):
                            self.MM(pso[ob:ob + 64, col:col + 64], L.wphi2[base:base + 64, a_, ll, :], xl[base:base + 64, pr, ll, :],
                                    ll == 0, ll == 63, [L.wphi2, xt_], [pso])
                self.CP(ac, kcT[:, :, grp * 64:(grp + 1) * 64], pkc[:, 0:128].rearrange("p (r n) -> p r n", r=2), [pkc], [kcT])
                self.CP(ac, vcT[0:64, :, :], pvc[0:64, 0:256].rearrange("p (k n) -> p k n", k=4), [pvc], [vcT])
                pvt = self.pS[0]
                ob = 64 * (grp % 2)
                for k in range(4):
                    self.MT(pvt[ob:ob + 64, k * 64:(k + 1) * 64], vcT[0:64, k, :], [vcT], [pvt])
                self.CP(ac, vca[ob:ob + 64, grp // 2, :, 0:64], pvt[ob:ob + 64, 0:256].rearrange("p (k e) -> p k e", k=4), [pvt], [vca])
            for nt in range(2):
                attend(128, lambda k, nt=nt: (kcT[64 * (k % 2):64 * (k % 2) + 64, k // 2, nt * 128:(nt + 1) * 128], [kcT]),
                       None, cm[:, s, :], [cm], lambda k, nt=nt: (vca[:, nt, k, :], [vca]),
                       (s == 0 and nt == 0), (s == NS - 1 and nt == 1))
            for k in range(4):
                base, pr = 64 * (k % 2), k // 2
                psc = [self.pS[0], self.pS[1]] if False else [self.pA[0], self.pA[1]]
                for g in range(4):
                    pq = psc[g % 2]
                    self.MM(pq[0:N, (g // 2) * 256:(g // 2) * 256 + 256], q0[base:base + 64, pr * 4 + g, :], kcT[base:base + 64, pr, :], True, True, [q0, kcT], [pq])
                for g in range(4):
                    pq = psc[g % 2]
                    self.AC(s1[:, g, :], pq[0:N, (g // 2) * 256:(g // 2) * 256 + 256], AF.Exp, [pq], [s1], scale=0.125)
                P.op(dv, lambda e: e.tensor_reduce(out=z4a[:], in_=s1[:], axis=AX.X, op=ALU.add), [s1], [z4a])
                P.op(dv, lambda e: e.reciprocal(out=z4a[:], in_=z4a[:]), [z4a], [z4a])
                self.TT(dv, s1[:], s1[:], z4a[:].unsqueeze(2).to_broadcast([N, 4, 256]), ALU.mult, [s1, z4a], [s1])
                P.op(dv, lambda e: e.tensor_reduce(out=sc_[:], in_=s1[:].rearrange("p g n -> p n g"), axis=AX.X, op=ALU.add), [s1], [sc_])
                self.TT(dv, sc_[:], sc_[:], tsl[:], ALU.add, [sc_, tsl], [sc_])
                P.op(dv, lambda e: e.max(out=mx[:], in_=sc_[:]), [sc_], [mx])
                P.op(dv, lambda e: e.match_replace(out=w1[:], in_to_replace=mx[:], in_values=sc_[:], imm_value=NEG), [mx, sc_], [w1])
                P.op(dv, lambda e: e.max(out=mx[:], in_=w1[:]), [w1], [mx])
                self.TS(dv, w1[:], sc_[:], mx[:, 6:7], -1.0, ALU.is_ge, ALU.add, [sc_, mx], [w1])
                self.TS(dv, bq[:], w1[:], 1e30, None, ALU.mult, None, [w1], [bq])
                for t4 in range(4):
                    pbt = self.pS[0]
                    self.MT(pbt[0:64, t4 * N:(t4 + 1) * N], bq[:, t4 * 64:(t4 + 1) * 64], [bq], [pbt])
                for g_ in range(4):
                    self.CP(ac, BT[0:64, s, :, k, g_ * N:(g_ + 1) * N], self.pS[0][0:64, 0:4 * N].rearrange("p (t n) -> p t n", t=4), [self.pS[0]], [BT])
        finish_branch(0)
        vpg = [A.sbuf(f"vpg{i}", [128, 4, 65], BF16) for i in range(2)]
        for v_ in vpg:
            self.MS(pl, v_[:], 1.0, [v_])
        kTp = [A.sbuf(f"kTp{i}", [128, 2, 128], BF16) for i in range(2)]
        it = 0
        for s in range(NS):
            for g in range(128):
                b = gather_page(s, g, idxB)
                ptx = self.pS[0]
                for n_ in range(2):
                    self.MT(ptx[:, n_ * 128:(n_ + 1) * 128], b[:, n_ * 128:(n_ + 1) * 128], [b], [ptx])
                kt_ = kTp[it % 2]
                vv_ = vpg[it % 2]
                it += 1
                self.CP(ac, kt_[:], ptx[:, 0:256].rearrange("p (r t) -> p r t", r=2), [ptx], [kt_])
                self.CP(ac, vv_[:, :, 0:64], b[:, 256:512].rearrange("p (h d) -> p h d", h=4), [b], [vv_])
                tile4, loc = g // 32, (2 * g) % 64
                attend(128, lambda k, kt_=kt_: (kt_[64 * (k % 2):64 * (k % 2) + 64, k // 2, :], [kt_]),
                       lambda k, s=s, tile4=tile4, loc=loc: (Gf[:, loc * 64:loc * 64 + 128], BT[0:64, s, tile4, :, :].rearrange("p k c -> p (k c)"), [self.G, BT]),
                       cm[:, s, :], [cm], lambda k, vv_=vv_: (vv_[:, k, :], [vv_]), (s == 0 and g == 0), False)
        attend(N, lambda k: (knT[64 * (k % 2):64 * (k % 2) + 64, k // 2, :], [knT]), None, m16[:, :], [m16],
               lambda k: (vn[:, 0, k, :], [vn]), False, True)
        finish_branch(1)
        wst = A.sbuf("wst", [128, 512], F32)
        wsb = A.sbuf("wsb", [128, 512], BF16)
        for s in range(NS):
            for t4 in range(4):
                self.DMA(wst[:], self.swin[l, s, t4 * 128:(t4 + 1) * 128, :], [], [wst])
                self.CP(dv, wsb[:], wst[:], [wst], [wsb])
                ptx = self.pS[0]
                for n_ in range(2):
                    self.MT(ptx[:, n_ * 128:(n_ + 1) * 128], wsb[:, n_ * 128:(n_ + 1) * 128], [wsb], [ptx])
                kt_ = kTp[it % 2]
                vv_ = vpg[it % 2]
                it += 1
                self.CP(ac, kt_[:], ptx[:, 0:256].rearrange("p (r t) -> p r t", r=2), [ptx], [kt_])
                self.CP(ac, vv_[:, :, 0:64], wsb[:, 256:512].rearrange("p (h d) -> p h d", h=4), [wsb], [vv_])
                msk = wm if t4 == 0 else cm
                attend(128, lambda k, kt_=kt_: (kt_[64 * (k % 2):64 * (k % 2) + 64, k // 2, :], [kt_]), None,
                       msk[:, s, :], [msk], lambda k, vv_=vv_: (vv_[:, k, :], [vv_]), (s == 0 and t4 == 0), False)
        attend(N, lambda k: (knT[64 * (k % 2):64 * (k % 2) + 64, 2 + k // 2, :], [knT]), None, m16[:, :], [m16],
               lambda k: (vn[:, 1, k, :], [vn]), False, True)
        finish_branch(2)
        yb = A.sbuf("ybs", [N, 1024], BF16)
        self.CP(dv, yb[:], yat[:], [yat], [yb])
        pyt = self.pS[0]
        for t8 in range(8):
            self.MT(pyt[:, t8 * N:(t8 + 1) * N], yb[:, t8 * 128:(t8 + 1) * 128], [yb], [pyt])
        self.TT(dv, yza[:, :, 0:N], pyt[:, 0:8 * N].rearrange("p (t c) -> p t c", t=8), saz[:], ALU.mult, [pyt, saz], [yza])


KB.sample_layer = _sample_layer
KB.nsa_sample = _nsa_sample
```

```python
import math
import os
import numpy as np
KSTOP = int(os.environ.get('KSTOP', '99'))
KSUB = int(os.environ.get('KSUB', '99'))
import concourse.bass as bass
import concourse.mybir as mybir
from concourse.bass_utils import run_bass_kernel_spmd
from contextlib import ExitStack

F32 = mybir.dt.float32
BF16 = mybir.dt.bfloat16
I32 = mybir.dt.int32
ALU = mybir.AluOpType
AF = mybir.ActivationFunctionType
AX = mybir.AxisListType

EPOCH = 8192
NDMASEM = 24

D = 2048
DIN = 13872
SEQ = 4096
PAST = 16384
NPOOL = 1280
EPS = 1e-6
NEG = -1e30
O_PU, O_PZ, O_Q, O_KV, O_AG, O_AZ, O_SU, O_SZ, O_MG = 0, 1024, 2048, 3072, 4608, 4656, 5680, 6704, 7728
A_SEGS = [O_PU, O_PZ, O_Q, O_AZ, O_SU, O_SZ]
NKVB = 1584
TC = 32


class Buf:
    __slots__ = ("t", "w", "r", "name")

    def __init__(self, t, name="", carry=None):
        self.t = t
        self.w = dict(carry) if carry else {}
        self.r = {}
        self.name = name

    def __getitem__(self, idx):
        return self.t[idx]


class Eng:
    def __init__(self, P, name, obj):
        self.P = P
        self.name = name
        self.obj = obj
        self.count = 0
        self.sems = []
        self.waited = {}
        self.dsems = []
        self.dnext = 0

    def sem_for(self, seq):
        k = (seq - 1) // EPOCH
        while len(self.sems) <= k:
            self.sems.append(self.P.new_sem(f"{self.name}_e{len(self.sems)}"))
        return self.sems[k], (seq - 1) % EPOCH + 1, (self.name, k)


def _dep_rank(d):
    return d[2]


class Prog:
    def __init__(self, nc):
        self.nc = nc
        self.es = ExitStack()
        self.nsem = 0
        self.nname = 0
        self.pe = Eng(self, "pe", nc.tensor)
        self.act = Eng(self, "act", nc.scalar)
        self.dve = Eng(self, "dve", nc.vector)
        self.pool = Eng(self, "pool", nc.gpsimd)
        self.sp = Eng(self, "sp", nc.sync)
        self.engs = [self.pe, self.act, self.dve, self.pool, self.sp]
        self.carry = {}
        self.ninstr = 0

    def new_sem(self, name):
        self.nsem += 1
        return self.es.enter_context(self.nc.semaphore(f"s_{name}_{self.nsem}"))

    def uname(self, name):
        self.nname += 1
        return f"{name}_{self.nname}"

    def sbuf(self, name, shape, dtype, es=None):
        t = (es or self.es).enter_context(self.nc.sbuf_tensor(self.uname(name), list(shape), dtype))
        return Buf(t, name, self.carry)

    def psum(self, name, shape, dtype):
        t = self.es.enter_context(self.nc.psum_tensor(self.uname(name), list(shape), dtype))
        return Buf(t, name)

    def dram_in(self, name, shape, dtype):
        return self.nc.dram_tensor(name, list(shape), dtype, kind="ExternalInput").ap()

    def dram_out(self, name, shape, dtype):
        return self.nc.dram_tensor(name, list(shape), dtype, kind="ExternalOutput").ap()

    def dram_scratch(self, name, shape, dtype):
        t = self.nc.dram_tensor(name, list(shape), dtype, kind="Internal").ap()
        return Buf(t, name)

    def retire(self, bufs):
        for b in bufs:
            for dd in (b.w, b.r):
                for k, d in dd.items():
                    kk = k[:3] if k[0] == "d" else k[:2]
                    o = self.carry.get(kk)
                    if o is None or _dep_rank(o) < _dep_rank(d):
                        self.carry[kk] = d

    def _wait(self, eng, dep, force=False):
        if dep[0] == "e":
            _, src, seq = dep
            if src is eng and eng is self.pe and not force:
                return
            sem, val, key = src.sem_for(seq)
        else:
            _, sem, val, key = dep
        if eng.waited.get(key, 0) >= val:
            return
        eng.waited[key] = val
        eng.obj.wait_ge(sem, val)
        self.ninstr += 1

    def _collect(self, eng, reads, writes):
        for b in reads:
            for d in b.w.values():
                self._wait(eng, d)
        for b in writes:
            for d in b.w.values():
                self._wait(eng, d)
            for d in b.r.values():
                self._wait(eng, d)

    def _commit(self, me, mekey, reads, writes):
        for b in writes:
            b.w = {mekey: me}
            b.r = {}
        for b in reads:
            if b not in writes:
                b.r[mekey] = me

    def op(self, eng, fn, reads=(), writes=()):
        self._collect(eng, reads, writes)
        ins = fn(eng.obj)
        eng.count += 1
        sem, val, key = eng.sem_for(eng.count)
        ins.then_inc(sem, 1)
        self.ninstr += 1
        me = ("e", eng, eng.count)
        self._commit(me, ("e", eng.name), reads, writes)
        return me

    def _dma_common(self, q, reads, writes, issue):
        self._collect(q, reads, writes)
        if len(q.dsems) < NDMASEM:
            q.dsems.append([self.new_sem(f"{q.name}_d{len(q.dsems)}"), 0])
        i = q.dnext % NDMASEM
        q.dnext += 1
        ent = q.dsems[i]
        key = ("d", q.name, i)
        if ent[1] > 0:
            self._wait(q, ("d", ent[0], 16 * ent[1], key))
        ent[1] += 1
        ins = issue(q.obj)
        ins.then_inc(ent[0], 16)
        self.ninstr += 1
        me = ("d", ent[0], 16 * ent[1], key)
        self._commit(me, key, reads, writes)
        return me

    def dma(self, q, out, in_, reads=(), writes=()):
        return self._dma_common(q, reads, writes, lambda e: e.dma_start(out=out, in_=in_))

    def gather(self, out, in_, idx_ap, reads=(), writes=()):
        return self._dma_common(
            self.pool, reads, writes,
            lambda e: e.indirect_dma_start(out=out, out_offset=None, in_=in_,
                                           in_offset=bass.IndirectOffsetOnAxis(ap=idx_ap, axis=0)))

    def finish(self):
        for e in self.engs:
            if e is not self.sp and e.count > 0:
                self._wait(self.sp, ("e", e, e.count))
        for q in self.engs:
            for i, ent in enumerate(q.dsems):
                if ent[1] > 0:
                    self._wait(self.sp, ("d", ent[0], 16 * ent[1], ("d", q.name, i)))

    def barrier(self):
        for tgt in self.engs:
            for e in self.engs:
                if e is not tgt and e.count > 0:
                    self._wait(tgt, ("e", e, e.count))
            for q in self.engs:
                for i, ent in enumerate(q.dsems):
                    if ent[1] > 0:
                        self._wait(tgt, ("d", ent[0], 16 * ent[1], ("d", q.name, i)))

    def close(self):
        self.es.close()


class Arena:
    def __init__(self, P):
        self.P = P
        self.es = ExitStack()
        self.bufs = []

    def sbuf(self, name, shape, dtype):
        b = self.P.sbuf(name, shape, dtype, es=self.es)
        self.bufs.append(b)
        return b

    def close(self):
        self.P.retire(self.bufs)
        self.es.close()

    def __enter__(self):
        return self

    def __exit__(self, *a):
        self.close()


class KB:
    def __init__(self, S=SEQ, NS=4, C=256, debug=False, parts=("pool", "ssm", "nsa")):
        self.S, self.NS, self.C, self.debug, self.parts = S, NS, C, debug, set(parts)
        self.NCH = S // C
        self.nc = bass.Bass("TRN2", target_bir_lowering=False)
        self.P = Prog(self.nc)
        self.dbg_outs = []

    def TT(self, eng, out, a, b, op, R, W):
        return self.P.op(eng, lambda e: e.tensor_tensor(out=out, in0=a, in1=b, op=op), R, W)

    def TS(self, eng, out, a, s1, s2, op0, op1, R, W):
        if s2 is None:
            return self.P.op(eng, lambda e: e.tensor_scalar(out=out, in0=a, scalar1=s1, scalar2=None, op0=op0), R, W)
        return self.P.op(eng, lambda e: e.tensor_scalar(out=out, in0=a, scalar1=s1, scalar2=s2, op0=op0, op1=op1), R, W)

    def STT(self, eng, out, a, s, b, op0, op1, R, W):
        return self.P.op(eng, lambda e: e.scalar_tensor_tensor(out=out, in0=a, scalar=s, in1=b, op0=op0, op1=op1), R, W)

    def AC(self, out, in_, func, R, W, **kw):
        return self.P.op(self.P.act, lambda e: e.activation(out=out, in_=in_, func=func, **kw), R, W)

    def CP(self, eng, out, in_, R, W):
        if eng is self.P.act:
            return self.P.op(eng, lambda e: e.copy(out=out, in_=in_), R, W)
        return self.P.op(eng, lambda e: e.tensor_copy(out=out, in_=in_), R, W)

    def MS(self, eng, ap, val, W):
        return self.P.op(eng, lambda e: e.memset(ap, val), (), W)

    def _rowgroup(self, ap, out):
        rg = (ap.base_partition(), ap.shape[0], out.base_partition(), out.shape[0])
        last = getattr(self, "_last_rg", (0, 128, 0, 128))
        if rg != last and (rg[1] < 128 or last[1] < 128 or rg[3] < 128 or last[3] < 128):
            pe = self.P.pe
            if pe.count > 0:
                self.P._wait(pe, ("e", pe, pe.count), force=True)
        self._last_rg = rg

    def MM(self, out, lhsT, rhs, start, stop, R, W):
        self._rowgroup(lhsT, out)
        return self.P.op(self.P.pe, lambda e: e.matmul(out, lhsT=lhsT, rhs=rhs, start=start, stop=stop), R, W)

    def TR(self, out, in_, ident, R, W):
        self._rowgroup(in_, out)
        return self.P.op(self.P.pe, lambda e: e.transpose(out=out, in_=in_, identity=ident), R, W)

    def MT(self, out, in_, R, W):
        kp = in_.shape[0]
        idn = self.identf if in_.dtype == F32 else self.ident
        return self.MM(out, in_, idn[0:kp, 0:kp], True, True, R + [idn], W)

    def DMA(self, out, in_, R=(), W=(), q=None):
        return self.P.dma(q or self.P.sp, out, in_, R, W)

    def dump(self, name, ap, shape, R):
        if not self.debug:
            return
        o = self.P.dram_out("dbg_" + name, shape, F32)
        self.dbg_outs.append("dbg_" + name)
        if ap.dtype != F32:
            with Arena(self.P) as A:
                t = A.sbuf("dbgt", list(ap.shape), F32)
                self.CP(self.P.dve, t[:], ap, R, [t])
                self.DMA(o, t[:], [t], [])
        else:
            self.DMA(o, ap, R, [])

    def declare(self):
        P, S, NS = self.P, self.S, self.NS
        din = P.dram_in
        self.xp = din("xp", [S, D], F32)
        if NS:
            self.xs = din("xs", [4 * NS, D], F32)
            self.cache = [din(f"cache{i}", [NPOOL * 128 * 2, 512], F32) for i in range(2)]
            self.ptab = din("ptab", [NS, 128], I32)
            self.swin = din("swin", [2, NS, 512, 512], F32)
            self.spool = din("spool", [2, NS, 15, 1024], F32)
            self.sssm = din("sssm", [2, NS, 2, 32, 128], F32)
            self.t_cm = din("t_cm", [128, NS, 4 * NS], F32)
            self.t_wm = din("t_wm", [128, NS, 4 * NS], F32)
            self.t_m16 = din("t_m16", [4 * NS, 4 * NS], F32)
            self.t_ssel = din("t_ssel", [4 * NS, 256], F32)
        self.g_pre = din("g_pre", [2, D], F32)
        self.g_post = din("g_post", [2, D], F32)
        self.w_in = din("w_in", [2, D, DIN], F32)
        self.w_pool = din("w_pool", [2, 4, 256, 256], F32)
        self.pool_scale = din("pool_scale", [2, 1024], F32)
        self.pe_cmp = din("pe_cmp", [2, 2, 64, 64], F32)
        self.w_phi = din("w_phi", [2, 2, 64, 64, 64], F32)
        self.lam_re = din("lam_re", [2, 32, 128], F32)
        self.lam_im = din("lam_im", [2, 32, 128], F32)
        self.log_step = din("log_step", [2, 32, 2], F32)
        self.b_re = din("b_re", [2, 64, 64, 16], F32)
        self.b_im = din("b_im", [2, 64, 64, 16], F32)
        self.c_re = din("c_re", [2, 32, 2, 16, 64], F32)
        self.c_im = din("c_im", [2, 32, 2, 16, 64], F32)
        self.d_skip = din("d_skip", [2, 1024], F32)
        self.w_glu = din("w_glu", [2, 1024, 1024], F32)
        self.w_br = [din(n, [2, 1024, D], F32) for n in ("w_br_pool", "w_br_nsa", "w_br_ssm")]
        self.w_out = din("w_out", [2, D, D], F32)
        self.t_cbq = din("t_cbq", [SEQ // 128, 128, 64], F32)
        self.t_sel = din("t_sel", [SEQ // 128, 128, 64], F32)
        self.t_cbT = din("t_cbT", [64, SEQ], F32)
        self.t_rc = din("t_rc", [128, 60], F32)
        dout = P.dram_out
        self.yp = dout("yp", [S, D], F32)
        self.kvp = dout("kvp", [2, S, 1024], F32)
        self.winp = dout("winp", [2, 512, 512], F32)
        self.poolp = dout("poolp", [2, 15, 1024], F32)
        self.ssmp = dout("ssmp", [2, 2, 32, 128], F32)
        if NS:
            self.ys = dout("ys", [4 * NS, D], F32)
            self.kvs = dout("kvs", [2, 4 * NS, 1024], F32)
            self.wins = dout("wins", [2, NS, 512, 512], F32)
            self.pools = dout("pools", [2, NS, 15, 1024], F32)
            self.ssms = dout("ssms", [2, NS, 2, 32, 128], F32)
        sc = P.dram_scratch
        self.wA = sc("wA", [2, 96, 128, 16, 128], BF16)
        self.wB = sc("wB", [2, 128, 16, NKVB], BF16)
        self.wG = sc("wG", [2, 8, 128, 8, 128], BF16)
        self.wR = sc("wR", [2, 3, 16, 128, 8, 128], BF16)
        self.wO = sc("wO", [2, 128, 16, D], BF16)
        self.x1 = sc("x1", [S, D], F32)
        if NS:
            self.xs1 = sc("xs1", [4 * NS, D], F32)

    def consts(self):
        P = self.P
        pl, dv = P.pool, P.dve
        self.identf = P.sbuf("identf", [128, 128], F32)
        self.ident = P.sbuf("ident", [128, 128], BF16)
        self.shu = P.sbuf("shu", [128, 128], BF16)
        self.tri = P.sbuf("tri", [128, 128], BF16)
        self.tris = P.sbuf("tris", [128, 128], BF16)
        self.G = P.sbuf("G", [64, 64, 64], BF16)
        self.rc = P.sbuf("rc", [128, 4, 15], F32)
        CA_ = Arena(P)
        tmp = CA_.sbuf("ctmp", [128, 128], F32)
        P.op(pl, lambda e: e.iota(tmp[:], pattern=[[1, 128]], base=0, channel_multiplier=-1,
                                  allow_small_or_imprecise_dtypes=True), (), [tmp])
        self.TS(dv, self.identf[:], tmp[:], 0.0, None, ALU.is_equal, None, [tmp], [self.identf])
        self.CP(dv, self.ident[:], self.identf[:], [self.identf], [self.ident])
        self.TS(dv, self.tri[:], tmp[:], 0.0, None, ALU.is_ge, None, [tmp], [self.tri])
        self.TS(dv, self.tris[:], tmp[:], 0.0, None, ALU.is_lt, None, [tmp], [self.tris])
        self.TS(dv, self.shu[:], tmp[:], 64.0, None, ALU.is_equal, None, [tmp], [self.shu])
        gt = CA_.sbuf("gtmp", [64, 64, 64], F32)
        P.op(pl, lambda e: e.iota(gt[:], pattern=[[1, 64], [0, 64]], base=0, channel_multiplier=-1,
                                  allow_small_or_imprecise_dtypes=True), (), [gt])
        self.TS(dv, self.G[:], gt[:], 0.0, None, ALU.is_equal, None, [gt], [self.G])
        self.DMA(self.rc[:], self.t_rc.rearrange("p (w c) -> p w c", w=4), [], [self.rc])
        CA_.close()
        self.pA = [P.psum(f"pA{i}", [128, 512], F32) for i in range(2)]
        self.pK = [P.psum(f"pK{i}", [128, 512], F32) for i in range(3)]
        self.pS = [P.psum(f"pS{i}", [128, 512], F32) for i in range(2)]
        self.pT = P.psum("pT", [128, 1024], BF16)
        self.pa_i = 0

    def nextpA(self):
        self.pa_i += 1
        return self.pA[self.pa_i % 2]

    def prepass(self):
        P = self.P
        with Arena(P) as A:
            stg = [A.sbuf(f"stg{i}", [128, 16, 512], BF16) for i in range(2)]
            si = [0]

            def stage():
                si[0] += 1
                return stg[si[0] % 2]

            for l in range(2):
                tid = 0
                segs = [(o, 1024) for o in A_SEGS] + [(O_MG, 6144)]
                for (o, n) in segs:
                    for c0 in range(0, n, 512):
                        st = stage()
                        self.DMA(st[:], self.w_in[l][:, o + c0:o + c0 + 512].rearrange("(k p) c -> p k c", p=128),
                                 [], [st], q=P.pool)
                        for t in range(4):
                            self.DMA(self.wA[l, tid], st[:, :, t * 128:(t + 1) * 128], [st], [])
                            tid += 1
                assert tid == 96
                for c0 in range(0, NKVB, 512):
                    n = min(512, NKVB - c0)
                    st = stage()
                    self.DMA(st[:, :, :n], self.w_in[l][:, O_KV + c0:O_KV + c0 + n].rearrange("(k p) c -> p k c", p=128),
                             [], [st], q=P.pool)
                    self.DMA(self.wB[l][:, :, c0:c0 + n], st[:, :, :n], [st], [])
                for c0 in range(0, D, 512):
                    st = stage()
                    self.DMA(st[:], self.w_out[l][:, c0:c0 + 512].rearrange("(k p) c -> p k c", p=128), [], [st], q=P.pool)
                    self.DMA(self.wO[l][:, :, c0:c0 + 512], st[:], [st], [])
                for c0 in range(0, 1024, 512):
                    st = stage()
                    self.DMA(st[:, 0:8, :], self.w_glu[l][:, c0:c0 + 512].rearrange("(k p) c -> p k c", p=128), [], [st], q=P.pool)
                    for t in range(4):
                        self.DMA(self.wG[l, c0 // 128 + t], st[:, 0:8, t * 128:(t + 1) * 128], [st], [])
                for i in range(3):
                    for c0 in range(0, D, 512):
                        st = stage()
                        self.DMA(st[:, 0:8, :], self.w_br[i][l][:, c0:c0 + 512].rearrange("(k p) c -> p k c", p=128), [], [st], q=P.pool)
                        for t in range(4):
                            self.DMA(self.wR[l, i, c0 // 128 + t], st[:, 0:8, t * 128:(t + 1) * 128], [st], [])
        P.barrier()

    def load_T(self, A, name, src_ap, rows, dst=None):
        P = self.P
        t = A.sbuf(name + "_ld", [rows, 128], F32)
        self.DMA(t[:], src_ap, [], [t])
        ps = self.nextpA()
        self.TR(ps[:, 0:rows], t[:], self.identf[0:rows, 0:rows], [t, self.identf], [ps])
        o = dst if dst is not None else A.sbuf(name, [128, rows], F32)
        self.CP(P.act, o[:, 0:rows], ps[:, 0:rows], [ps], [o])
        return o

    def layer_prep(self, l, LA):
        P = self.P
        dv, pl, ac = P.dve, P.pool, P.act
        L = type("L", (), {})()
        self.L = L
        L.gpre = LA.sbuf("gpre", [128, 16], F32)
        L.pscale = LA.sbuf("pscale", [128, 8], F32)
        L.dskip = LA.sbuf("dskip", [128, 8], F32)
        L.wpool = LA.sbuf("wpool", [128, 4, 2, 256], BF16)
        L.wphi2 = LA.sbuf("wphi2", [128, 2, 64, 64], BF16)
        L.pe2 = LA.sbuf("pe2", [128, 2, 64], BF16)
        L.cosT = LA.sbuf("cosT", [128, 32, TC], F32)
        L.sinT = LA.sbuf("sinT", [128, 32, TC], F32)
        L.rhoz = LA.sbuf("rhoz", [128, 32, TC], F32)
        L.abr = LA.sbuf("abr", [128, 32], F32)
        L.abi = LA.sbuf("abi", [128, 32], F32)
        L.BBr = LA.sbuf("BBr", [128, 8, 2, 128], BF16)
        L.BBi = LA.sbuf("BBi", [128, 8, 2, 128], BF16)
        L.CTr = LA.sbuf("CTr", [128, 32, 128], BF16)
        L.CTi = LA.sbuf("CTi", [128, 32, 128], BF16)
        with Arena(P) as A:
            self.load_T(A, "gpre", self.g_pre[l].rearrange("(k p) -> k p", p=128), 16, dst=L.gpre)
            self.load_T(A, "pscale", self.pool_scale[l].rearrange("(k p) -> k p", p=128), 8, dst=L.pscale)
            self.load_T(A, "dskip", self.d_skip[l].rearrange("(k p) -> k p", p=128), 8, dst=L.dskip)
            self.DMA(L.wpool[:], self.w_pool[l].rearrange("g (k p) d -> p g k d", p=128), [], [L.wpool], q=pl)
            for h in range(2):
                self.DMA(L.wphi2[64 * h:64 * h + 64], self.w_phi[l].rearrange("a l d e -> d a l e"), [], [L.wphi2], q=pl)
            pel = A.sbuf("pel", [128, 64], F32)
            self.DMA(pel[:], self.pe_cmp[l].rearrange("a l d -> (a l) d"), [], [pel])
            pel2 = A.sbuf("pel2", [128, 2, 64], F32)
            for h in range(2):
                self.CP(dv, pel2[:, h, :], pel[:], [pel], [pel2])
            ps = self.nextpA()
            self.TR(ps[:, 0:128], pel2[:].rearrange("p a d -> p (a d)"), self.identf[:], [pel2, self.identf], [ps])
            self.CP(ac, L.pe2[:].rearrange("p a l -> p (a l)"), ps[:, 0:128], [ps], [L.pe2])

            lr = self.load_T(A, "lr", self.lam_re[l], 32)
            li = self.load_T(A, "li", self.lam_im[l], 32)
            lsl = A.sbuf("lsl", [32, 2], F32)
            self.DMA(lsl[:], self.log_step[l], [], [lsl])
            lsx = A.sbuf("lsx", [32, 2, 64], F32)
            self.CP(dv, lsx[:], lsl[:].unsqueeze(2).to_broadcast([32, 2, 64]), [lsl], [lsx])
            ps = self.nextpA()
            self.TR(ps[:, 0:32], lsx[:].rearrange("p a n -> p (a n)"), self.identf[0:32, 0:32], [lsx, self.identf], [ps])
            dt = A.sbuf("dt", [128, 32], F32)
            self.AC(dt[:], ps[:, 0:32], AF.Exp, [ps], [dt])

            def T(name):
                return A.sbuf(name, [128, 32], F32)

            def mul(o, a, b):
                self.TT(dv, o[:], a[:], b[:], ALU.mult, [a, b], [o])

            def sub(o, a, b):
                self.TT(dv, o[:], a[:], b[:], ALU.subtract, [a, b], [o])

            def add(o, a, b):
                self.TT(dv, o[:], a[:], b[:], ALU.add, [a, b], [o])

            x, mag, th, c, s_, t1, t2, t3 = T("x"), T("mag"), T("th"), T("c"), T("s"), T("t1"), T("t2"), T("t3")
            mul(x, lr, dt)
            self.AC(mag[:], x[:], AF.Exp, [x], [mag])
            mul(th, li, dt)
            hp = A.sbuf("hp", [128, 1], F32)
            self.MS(dv, hp[:], math.pi / 2, [hp])
            self.AC(s_[:], th[:], AF.Sin, [th], [s_], scale=1.0 / 16)
            self.AC(c[:], th[:], AF.Sin, [th, hp], [c], scale=1.0 / 16, bias=hp[:, 0:1])

            def csq(cc, ss):
                mul(t1, cc, cc)
                mul(t2, ss, ss)
                mul(t3, cc, ss)
                sub(cc, t1, t2)
                self.TS(dv, ss[:], t3[:], 2.0, None, ALU.mult, None, [t3], [ss])

            for _ in range(4):
                csq(c, s_)
            mul(L.abr, mag, c)
            mul(L.abi, mag, s_)
            den, rden, a1, cor, coi = T("den"), T("rden"), T("a1"), T("cor"), T("coi")
            mul(t1, lr, lr)
            mul(t2, li, li)
            add(den, t1, t2)
            self.P.op(dv, lambda e: e.reciprocal(out=rden[:], in_=den[:]), [den], [rden])
            self.TS(dv, a1[:], L.abr[:], -1.0, None, ALU.add, None, [L.abr], [a1])
            mul(t1, a1, lr)
            mul(t2, L.abi, li)
            add(t3, t1, t2)
            mul(cor, t3, rden)
            mul(t1, L.abi, lr)
            mul(t2, a1, li)
            sub(t3, t1, t2)
            mul(coi, t3, rden)
            self.MS(dv, L.cosT[:, :, 0:1], 1.0, [L.cosT])
            self.MS(dv, L.sinT[:, :, 0:1], 0.0, [L.sinT])
            self.CP(dv, L.cosT[:, :, 1:2], c[:].unsqueeze(2), [c], [L.cosT])
            self.CP(dv, L.sinT[:, :, 1:2], s_[:].unsqueeze(2), [s_], [L.sinT])
            Ck, Sk = T("Ck"), T("Sk")
            self.CP(dv, Ck[:], c[:], [c], [Ck])
            self.CP(dv, Sk[:], s_[:], [s_], [Sk])
            tb1 = A.sbuf("tb1", [128, 32, TC // 2], F32)
            tb2 = A.sbuf("tb2", [128, 32, TC // 2], F32)
            csq(Ck, Sk)
            w = 2
            while w < TC:
                Cb = Ck[:].unsqueeze(2).to_broadcast([128, 32, w])
                Sb = Sk[:].unsqueeze(2).to_broadcast([128, 32, w])
                self.TT(dv, tb1[:, :, 0:w], L.cosT[:, :, 0:w], Cb, ALU.mult, [L.cosT, Ck], [tb1])
                self.TT(dv, tb2[:, :, 0:w], L.sinT[:, :, 0:w], Sb, ALU.mult, [L.sinT, Sk], [tb2])
                self.TT(dv, L.cosT[:, :, w:2 * w], tb1[:, :, 0:w], tb2[:, :, 0:w], ALU.subtract, [tb1, tb2], [L.cosT])
                self.TT(dv, tb1[:, :, 0:w], L.sinT[:, :, 0:w], Cb, ALU.mult, [L.sinT, Ck], [tb1])
                self.TT(dv, tb2[:, :, 0:w], L.cosT[:, :, 0:w], Sb, ALU.mult, [L.cosT, Sk], [tb2])
                self.TT(dv, L.sinT[:, :, w:2 * w], tb1[:, :, 0:w], tb2[:, :, 0:w], ALU.add, [tb1, tb2], [L.sinT])
                csq(Ck, Sk)
                w *= 2
            self.MS(dv, L.rhoz[:, :, 0:1], 0.0, [L.rhoz])
            self.CP(dv, L.rhoz[:, :, 1:TC], mag[:].unsqueeze(2).to_broadcast([128, 32, TC - 1]), [mag], [L.rhoz])

            bre = A.sbuf("bre", [128, 32, 16], F32)
            bim = A.sbuf("bim", [128, 32, 16], F32)
            self.DMA(bre[:], self.b_re[l].rearrange("g n c -> (g n) c").rearrange("(i p) c -> p i c", p=128), [], [bre])
            self.DMA(bim[:], self.b_im[l].rearrange("g n c -> (g n) c").rearrange("(i p) c -> p i c", p=128), [], [bim])
            u1 = A.sbuf("u1", [128, 32, 16], F32)
            u2 = A.sbuf("u2", [128, 32, 16], F32)
            bbr = A.sbuf("bbr", [128, 32, 16], F32)
            bbi = A.sbuf("bbi", [128, 32, 16], F32)
            corb = cor[:].unsqueeze(2).to_broadcast([128, 32, 16])
            coib = coi[:].unsqueeze(2).to_broadcast([128, 32, 16])
            self.TT(dv, u1[:], bre[:], corb, ALU.mult, [bre, cor], [u1])
            self.TT(dv, u2[:], bim[:], coib, ALU.mult, [bim, coi], [u2])
            self.TT(dv, bbr[:], u1[:], u2[:], ALU.subtract, [u1, u2], [bbr])
            self.TT(dv, u1[:], bim[:], corb, ALU.mult, [bim, cor], [u1])
            self.TT(dv, u2[:], bre[:], coib, ALU.mult, [bre, coi], [u2])
            self.TT(dv, bbi[:], u1[:], u2[:], ALU.add, [u1, u2], [bbi])
            spad = A.sbuf("spad", [128, 32, 128], BF16)
            for (bb, dst) in ((bbr, L.BBr), (bbi, L.BBi)):
                self.MS(pl, spad[:], 0.0, [spad])
                sv = spad[:].rearrange("p (k j) m -> p k j m", j=4)
                bv = bb[:].rearrange("p (k j) c -> p k j c", j=4)
                for il in range(4):
                    for g2 in range(2):
                        o = 32 * il + 16 * g2
                        self.CP(dv, sv[64 * g2:64 * g2 + 64, :, il, o:o + 16], bv[64 * g2:64 * g2 + 64, :, il, :], [bb], [spad])
                for i0 in range(0, 32, 8):
                    for j in range(8):
                        self.TR(self.pT[:, j * 128:(j + 1) * 128], spad[:, i0 + j, :], self.ident[:], [spad, self.ident], [self.pT])
                    for j in range(8):
                        i = i0 + j
                        hb = (i % 4) // 2
                        self.CP(ac, dst[64 * hb:64 * hb + 64, i // 4, i % 2, :], self.pT[64 * hb:64 * hb + 64, j * 128:(j + 1) * 128], [self.pT], [dst])
            for (csrc, dst, sgn) in ((self.c_re, L.CTr, 1.0), (self.c_im, L.CTi, -1.0)):
                cn = A.sbuf("cn", [32, 2, 16, 64], F32)
                self.DMA(cn[:], csrc[l], [], [cn])
                cn2 = A.sbuf("cn2", [32, 16, 2, 64], F32)
                self.CP(dv, cn2[:].rearrange("p c a n -> p a c n"), cn[:], [cn], [cn2])
                ps = self.nextpA()
                for cc in range(16):
                    self.TR(ps[:, cc * 32:(cc + 1) * 32], cn2[:, cc, :, :].rearrange("p a n -> p (a n)"), self.identf[0:32, 0:32], [cn2, self.identf], [ps])
                cst = A.sbuf("cst", [128, 32, 16], F32)
                self.TS(dv, cst[:].rearrange("p i c -> p c i"), ps[:, 0:512].rearrange("p (c i) -> p c i", c=16), sgn, None, ALU.mult, None, [ps], [cst])
                self.MS(pl, dst[:], 0.0, [dst])
                dv4 = dst[:].rearrange("p (k j) m -> p k j m", j=4)
                cv = cst[:].rearrange("p (k j) c -> p k j c", j=4)
                for il in range(4):
                    for g2 in range(2):
                        o = 32 * il + 16 * g2
                        self.CP(dv, dv4[64 * g2:64 * g2 + 64, :, il, o:o + 16], cv[64 * g2:64 * g2 + 64, :, il, :], [cst], [dst])

    def next_wt(self):
        self.wt_i += 1
        return self.wts[self.wt_i % len(self.wts)]

    def projA(self, src, N, rhs_of_k, nk, consumer, R):
        for idx, wsrc in enumerate(src):
            wt = self.next_wt()
            self.DMA(wt[:, 0:nk, :], wsrc, [], [wt])
            ps = self.nextpA()
            for k in range(nk):
                self.MM(ps[:, 0:N], wt[:, k, :], rhs_of_k(k), k == 0, k == nk - 1, [wt] + R, [ps])
            consumer(idx, ps)

    def norm_phase(self, src_rows, tiles, hT):
        P, L = self.P, self.L
        with Arena(P) as A:
            xt = A.sbuf("xt", [128, D], F32)
            junk = A.sbuf("junk", [128, D], BF16)
            ss = A.sbuf("ss", [128, 1], F32)
            hb = A.sbuf("hb", [128, D], BF16)
            for (r0, nr) in tiles:
                self.DMA(xt[0:nr], src_rows(r0, nr), [], [xt])
                self.AC(junk[0:nr], xt[0:nr], AF.Square, [xt], [junk, ss], accum_out=ss[0:nr, 0:1])
                self.TS(P.dve, ss[0:nr], ss[0:nr], 1.0 / D, EPS, ALU.mult, ALU.add, [ss], [ss])
                self.AC(ss[0:nr], ss[0:nr], AF.Sqrt, [ss], [ss])
                P.op(P.dve, lambda e: e.reciprocal(out=ss[0:nr], in_=ss[0:nr]), [ss], [ss])
                self.TS(P.dve, hb[0:nr], xt[0:nr], ss[0:nr, 0:1], None, ALU.mult, None, [xt, ss], [hb])
                for k0 in range(0, 16, 8):
                    for kk in range(8):
                        k = k0 + kk
                        self.TR(self.pT[:, kk * 128:kk * 128 + nr], hb[0:nr, k * 128:(k + 1) * 128],
                                self.ident[0:nr, 0:nr], [hb, self.ident], [self.pT])
                    for kk in range(8):
                        k = k0 + kk
                        self.AC(hT[:, k, r0:r0 + nr], self.pT[:, kk * 128:kk * 128 + nr], AF.Copy,
                                [self.pT, L.gpre], [hT], scale=L.gpre[:, k:k + 1])

    def merge_out_phase(self, l, N, tiles, hT, yz, src_rows, dst_rows, dst_bufs):
        P, L = self.P, self.L
        dv, ac = P.dve, P.act
        with Arena(P) as A:
            mT = A.sbuf("mT", [128, 16, N], BF16)
            gs = [A.sbuf(f"gs{i}", [128, N], F32) for i in range(3)]
            tm = [A.sbuf(f"tm{i}", [128, N], F32) for i in range(3)]
            for dt in range(16):
                brp = []
                for i in range(3):
                    wt = self.next_wt()
                    self.DMA(wt[:, 0:8, :], self.wR[l, i, dt], [], [wt])
                    ps = self.pK[i]
                    for k in range(8):
                        self.MM(ps[:, 0:N], wt[:, k, :], yz[i][:, k, 0:N], k == 0, k == 7, [wt, yz[i]], [ps])
                    brp.append(ps)
                for i in range(3):
                    wt = self.next_wt()
                    self.DMA(wt[:], self.wA[l, 48 + 16 * i + dt], [], [wt])
                    ps = self.nextpA()
                    for k in range(16):
                        self.MM(ps[:, 0:N], wt[:, k, :], hT[:, k, 0:N], k == 0, k == 15, [wt, hT], [ps])
                    self.AC(gs[i][:], ps[:, 0:N], AF.Sigmoid, [ps], [gs[i]])
                for i in range(3):
                    self.TT(dv, tm[i][:], gs[i][:], brp[i][:, 0:N], ALU.mult, [gs[i], brp[i]], [tm[i]])
                self.TT(dv, tm[0][:], tm[0][:], tm[1][:], ALU.add, [tm[0], tm[1]], [tm[0]])
                self.TT(dv, mT[:, dt, :], tm[0][:], tm[2][:], ALU.add, [tm[0], tm[2]], [mT])
            if self.debug:
                self.dump(f"mT{l}", mT[:], [128, 16, N], [mT])
            gpb = A.sbuf("gpb", [128, D], F32)
            self.DMA(gpb[:], self.g_post[l:l + 1, :].to_broadcast([128, D]), [], [gpb])
            of = [A.sbuf(f"of{i}", [128, D], F32) for i in range(len(tiles))]
            for ci, c0 in enumerate(range(0, D, 128)):
                w = self.next_wt()
                self.DMA(w[:], self.wO[l][:, :, c0:c0 + 128], [], [w])
                for ti, (r0, nr) in enumerate(tiles):
                    ps = self.pK[ti % 3]
                    for k in range(16):
                        self.MM(ps[0:nr, 0:128], mT[:, k, r0:r0 + nr], w[:, k, :], k == 0, k == 15, [mT, w], [ps])
                    self.CP(ac, of[ti][0:nr, c0:c0 + 128], ps[0:nr, 0:128], [ps], [of[ti]])
            junk = A.sbuf("junk2", [128, D], BF16)
            ss = A.sbuf("ss2", [128, 1], F32)
            xt = A.sbuf("xt2", [128, D], F32)
            for ti, (r0, nr) in enumerate(tiles):
                o = of[ti]
                self.DMA(xt[0:nr], src_rows(r0, nr), [], [xt])
                self.AC(junk[0:nr], o[0:nr], AF.Square, [o], [junk, ss], accum_out=ss[0:nr, 0:1])
                self.TS(dv, ss[0:nr], ss[0:nr], 1.0 / D, EPS, ALU.mult, ALU.add, [ss], [ss])
                self.AC(ss[0:nr], ss[0:nr], AF.Sqrt, [ss], [ss])
                P.op(dv, lambda e: e.reciprocal(out=ss[0:nr], in_=ss[0:nr]), [ss], [ss])
                self.STT(dv, o[0:nr], o[0:nr], ss[0:nr, 0:1], gpb[0:nr], ALU.mult, ALU.mult, [o, ss, gpb], [o])
                self.TT(dv, o[0:nr], o[0:nr], xt[0:nr], ALU.add, [o, xt], [o])
                self.DMA(dst_rows(r0, nr), o[0:nr], [o], [dst_bufs[ti]] if dst_bufs else [])

    def pool_phase(self, l, N, hT, yzp, halo, first, nseg=1):
        P, L = self.P, self.L
        dv, ac, pl = P.dve, P.act, P.pool
        n1 = N // nseg
        W = 15 + n1
        with Arena(P) as A:
            pu = A.sbuf("pu", [128, 8, nseg, W], F32)
            spz = A.sbuf("spz", [128, 8, N], BF16)
            sA = A.sbuf("sA", [128, 8, nseg, W], F32)
            sB = A.sbuf("sB", [128, 8, nseg, W], F32)
            df = A.sbuf("df", [128, 8, N], BF16)
            self.CP(dv, pu[:, :, :, 0:15], halo[:], [halo], [pu])
            self.projA([self.wA[l, t] for t in range(0, 8)], N, lambda k: hT[:, k, 0:N], 16,
                       lambda i, ps: self.CP(ac, pu[:, i, :, 15:W], ps[:, 0:N].rearrange("p (s c) -> p s c", s=nseg), [ps], [pu]), [hT])
            self.projA([self.wA[l, t] for t in range(8, 16)], N, lambda k: hT[:, k, 0:N], 16,
                       lambda i, ps: self.AC(spz[:, i, :], ps[:, 0:N], AF.Silu, [ps], [spz]), [hT])
            if self.debug and nseg == 1:
                self.dump(f"pu{l}", pu[:, :, 0, 15:W], [128, 8, n1], [pu])
            self.TT(dv, sA[:, :, :, 1:W], pu[:, :, :, 1:W], pu[:, :, :, 0:W - 1], ALU.add, [pu], [sA])
            self.TT(dv, sB[:, 2:8, :, 3:W], sA[:, 2:8, :, 3:W], sA[:, 2:8, :, 1:W - 2], ALU.add, [sA], [sB])
            self.TT(dv, sA[:, 4:8, :, 7:W], sB[:, 4:8, :, 7:W], sB[:, 4:8, :, 3:W - 4], ALU.add, [sB], [sA])
            self.TT(dv, sB[:, 6:8, :, 15:W], sA[:, 6:8, :, 15:W], sA[:, 6:8, :, 7:W - 8], ALU.add, [sA], [sB])
            dfv = df[:].rearrange("p t (s c) -> p t s c", s=nseg)
            for gi, (src, w) in enumerate(((sA, 2), (sB, 4), (sA, 8), (sB, 16))):
                t0 = 2 * gi
                self.STT(dv, dfv[:, t0:t0 + 2], src[:, t0:t0 + 2, :, 15:W], 1.0 / w, pu[:, t0:t0 + 2, :, 15:W],
                         ALU.mult, ALU.subtract, [src, pu], [df])
                if first:
                    if not hasattr(A, "_pt"):
                        A._pt = A.sbuf("ptmp", [128, 2, 15], F32)
                    tmp = A._pt
                    self.TT(dv, tmp[:], src[:, t0:t0 + 2, 0, 15:30], self.rc[:, gi:gi + 1, :].to_broadcast([128, 2, 15]),
                            ALU.mult, [src, self.rc], [tmp])
                    self.TT(dv, df[:, t0:t0 + 2, 0:15], tmp[:], pu[:, t0:t0 + 2, 0, 15:30], ALU.subtract, [tmp, pu], [df])
            for t in range(8):
                g = t // 2
                ps = self.nextpA()
                for kc in range(2):
                    self.MM(ps[:, 0:N], L.wpool[:, g, kc, (t % 2) * 128:(t % 2) * 128 + 128], df[:, 2 * g + kc, :],
                            kc == 0, kc == 1, [L.wpool, df], [ps])
                self.STT(dv, yzp[:, t, 0:N], ps[:, 0:N], L.pscale[:, t:t + 1], spz[:, t, :], ALU.mult, ALU.mult,
                         [ps, L.pscale, spz], [yzp])
            self.CP(dv, halo[:], pu[:, :, :, n1:W], [pu], [halo])
            return None

    def pool_state_out(self, halo_seg_ap, halo_buf, dst_ap):
        P = self.P
        with Arena(P) as A:
            po = A.sbuf("po", [15, 1024], F32)
            for half in range(2):
                ps = self.nextpA()
                for t in range(4):
                    self.TR(ps[0:15, t * 128:(t + 1) * 128], halo_seg_ap(4 * half + t), self.identf[:], [halo_buf, self.identf], [ps])
                self.CP(P.act, po[0:15, half * 512:(half + 1) * 512], ps[0:15, 0:512], [ps], [po])
            self.DMA(dst_ap, po[:], [po], [])

    def ssm_phase(self, l, N, hT, yzs, st, nseg=1):
        P, L = self.P, self.L
        dv, ac, pl = P.dve, P.act, P.pool
        n1 = N // nseg
        tc = min(TC, n1)
        nsub = n1 // tc
        F = 16 * nseg * tc
        with Arena(P) as A:
            su = A.sbuf("su", [128, 8, N], BF16)
            ssz = A.sbuf("ssz", [128, 8, N], BF16)
            zT = A.sbuf("zT", [128, 8, N], BF16)
            self.projA([self.wA[l, t] for t in range(32, 40)], N, lambda k: hT[:, k, 0:N], 16,
                       lambda i, ps: self.CP(ac, su[:, i, :], ps[:, 0:N], [ps], [su]), [hT])
            self.projA([self.wA[l, t] for t in range(40, 48)], N, lambda k: hT[:, k, 0:N], 16,
                       lambda i, ps: self.AC(ssz[:, i, :], ps[:, 0:N], AF.Silu, [ps], [ssz]), [hT])
            if self.debug and nseg == 1:
                self.dump(f"su{l}", su[:], [128, 8, N], [su])

            def arr(name, dt=F32):
                return A.sbuf(name, [128, 16, nseg, tc], dt)

            bur, bui, t1, t2, gr, gi, kr, ki, hr, hi, t3, t4 = (arr(n) for n in
                                                        ("bur", "bui", "t1", "t2", "gr", "gi", "kr", "ki", "hr", "hi", "t3", "t4"))
            hrb, hib = arr("hrb", BF16), arr("hib", BF16)
            yf = A.sbuf("yf", [128, 4, nseg, tc], F32)
            c1 = A.sbuf("c1", [128, 16, nseg], F32)
            c2 = A.sbuf("c2", [128, 16, nseg], F32)
            suv = su[:].rearrange("p k (s c) -> p k s c", s=nseg)
            zv = zT[:].rearrange("p k (s c) -> p k s c", s=nseg)
            for sc in range(nsub if KSTOP >= 2 else 0):
                c0 = sc * tc
                for hf in range(2):
                    i0 = 16 * hf
                    pbr, pbi, py = self.pK[0], self.pK[1], self.pK[2]
                    for ii in sorted(range(16), key=lambda q: (((i0 + q) % 4) // 2, q)):
                        i = i0 + ii
                        kt, hb, e = i // 4, (i % 4) // 2, i % 2
                        for s in range(nseg):
                            o = (ii * nseg + s) * tc
                            rhs = suv[64 * hb:64 * hb + 64, kt, s, c0:c0 + tc]
                            self.MM(pbr[:, o:o + tc], L.BBr[64 * hb:64 * hb + 64, kt, e, :], rhs, True, True, [L.BBr, su], [pbr])
                            self.MM(pbi[:, o:o + tc], L.BBi[64 * hb:64 * hb + 64, kt, e, :], rhs, True, True, [L.BBi, su], [pbi])
                    fl = lambda b: b[:].rearrange("p a s c -> p (a s c)")
                    self.CP(ac, fl(bur), pbr[:, 0:F], [pbr], [bur])
                    self.CP(ac, fl(bui), pbi[:, 0:F], [pbi], [bui])
                    if KSTOP < 3:
                        continue
                    cs = L.cosT[:, i0:i0 + 16, 0:tc].unsqueeze(2).to_broadcast([128, 16, nseg, tc])
                    sn = L.sinT[:, i0:i0 + 16, 0:tc].unsqueeze(2).to_broadcast([128, 16, nseg, tc])
                    rz = L.rhoz[:, i0:i0 + 16, 0:tc].unsqueeze(2).to_broadcast([128, 16, nseg, tc])
                    self.TT(dv, t1[:], bur[:], cs, ALU.mult, [bur, L.cosT], [t1])
                    self.TT(dv, t2[:], bui[:], sn, ALU.mult, [bui, L.sinT], [t2])
                    self.TT(dv, gr[:], t1[:], t2[:], ALU.add, [t1, t2], [gr])
                    self.TT(dv, t1[:], bui[:], cs, ALU.mult, [bui, L.cosT], [t1])
                    self.TT(dv, t2[:], bur[:], sn, ALU.mult, [bur, L.sinT], [t2])
                    self.TT(dv, gi[:], t1[:], t2[:], ALU.subtract, [t1, t2], [gi])
                    self.TT(dv, gr[:, :, :, 0], gr[:, :, :, 0], st["car_r"][:, i0:i0 + 16, :], ALU.add, [gr, st["car_r"]], [gr])
                    self.TT(dv, gi[:, :, :, 0], gi[:, :, :, 0], st["car_i"][:, i0:i0 + 16, :], ALU.add, [gi, st["car_i"]], [gi])
                    if KSTOP < 4:
                        continue
                    if nseg == 1:
                        rzf = L.rhoz[:, i0:i0 + 16, 0:tc] if tc == TC else None
                    else:
                        rzf = None
                    if rzf is None:
                        rzm = A.sbuf("rzm", [128, 16, nseg, tc], F32)
                        self.CP(dv, rzm[:], rz, [L.rhoz], [rzm])
                        rz2 = fl(rzm)
                        rzR = [rzm]
                    else:
                        rz2 = rzf.rearrange("p a c -> p (a c)")
                        rzR = [L.rhoz]
                    P.op(dv, lambda e: e.tensor_tensor_scan(out=fl(kr), data0=rz2, data1=fl(gr), initial=0.0,
                                                            op0=ALU.mult, op1=ALU.add), rzR + [gr], [kr])
                    P.op(dv, lambda e: e.tensor_tensor_scan(out=fl(ki), data0=rz2, data1=fl(gi), initial=0.0,
                                                            op0=ALU.mult, op1=ALU.add), rzR + [gi], [ki])
                    if KSTOP < 5:
                        continue
                    self.TT(dv, t3[:], kr[:], cs, ALU.mult, [kr, L.cosT], [t3])
                    self.TT(dv, t4[:], ki[:], sn, ALU.mult, [ki, L.sinT], [t4])
                    self.TT(dv, hr[:], t3[:], t4[:], ALU.subtract, [t3, t4], [hr])
                    self.TT(dv, t3[:], kr[:], sn, ALU.mult, [kr, L.sinT], [t3])
                    self.TT(dv, t4[:], ki[:], cs, ALU.mult, [ki, L.cosT], [t4])
                    self.TT(dv, hi[:], t3[:], t4[:], ALU.add, [t3, t4], [hi])
                    self.CP(ac, hrb[:], hr[:], [hr], [hrb])
                    self.CP(ac, hib[:], hi[:], [hi], [hib])
                    hlr, hli = st["hl_r"], st["hl_i"]
                    self.CP(dv, hlr[:, i0:i0 + 16, :], hr[:, :, :, tc - 1], [hr], [hlr])
                    self.CP(dv, hli[:, i0:i0 + 16, :], hi[:, :, :, tc - 1], [hi], [hli])
                    ab_r = L.abr[:, i0:i0 + 16].unsqueeze(2).to_broadcast([128, 16, nseg])
                    ab_i = L.abi[:, i0:i0 + 16].unsqueeze(2).to_broadcast([128, 16, nseg])
                    self.TT(dv, c1[:], hlr[:, i0:i0 + 16, :], ab_r, ALU.mult, [hlr, L.abr], [c1])
                    self.TT(dv, c2[:], hli[:, i0:i0 + 16, :], ab_i, ALU.mult, [hli, L.abi], [c2])
                    self.TT(dv, st["car_r"][:, i0:i0 + 16, :], c1[:], c2[:], ALU.subtract, [c1, c2], [st["car_r"]])
                    self.TT(dv, c1[:], hli[:, i0:i0 + 16, :], ab_r, ALU.mult, [hli, L.abr], [c1])
                    self.TT(dv, c2[:], hlr[:, i0:i0 + 16, :], ab_i, ALU.mult, [hlr, L.abi], [c2])
                    self.TT(dv, st["car_i"][:, i0:i0 + 16, :], c1[:], c2[:], ALU.add, [c1, c2], [st["car_i"]])
                    if KSTOP < 6:
                        continue
                    for ko in range(4):
                        kt = 4 * hf + ko
                        for s in range(nseg):
                            o = (ko * nseg + s) * tc
                            for il in range(4):
                                i = 4 * kt + il
                                ii = i - i0
                                self.MM(py[:, o:o + tc], L.CTr[:, i, :], hrb[:, ii, s, :], il == 0, False, [L.CTr, hrb], [py])
                                self.MM(py[:, o:o + tc], L.CTi[:, i, :], hib[:, ii, s, :], False, il == 3, [L.CTi, hib], [py])
                    if KSTOP < 7:
                        continue
                    for ko in range(4):
                        kt = 4 * hf + ko
                        self.STT(dv, yf[:, ko], suv[:, kt, :, c0:c0 + tc], L.dskip[:, kt:kt + 1],
                                 py[:, ko * nseg * tc:(ko + 1) * nseg * tc].rearrange("p (s c) -> p s c", s=nseg),
                                 ALU.mult, ALU.add, [su, L.dskip, py], [yf])
                    self.AC(zv[:, 4 * hf:4 * hf + 4, :, c0:c0 + tc], yf[:], AF.Gelu_apprx_tanh, [yf], [zT])
            if self.debug and nseg == 1:
                self.dump(f"zT{l}", zT[:], [128, 8, N], [zT])
            if KSTOP < 8:
                self.MS(dv, yzs[:], 0.0, [yzs])
                return
            sgt = [A.sbuf(f"sgt{i}", [128, N], BF16) for i in range(2)]

            def glu_cons(i, ps):
                sg = sgt[i % 2]
                self.AC(sg[:], ps[:, 0:N], AF.Sigmoid, [ps], [sg])
                self.TT(dv, sg[:], sg[:], zT[:, i, :], ALU.mult, [sg, zT], [sg])
                self.TT(dv, yzs[:, i, 0:N], sg[:], ssz[:, i, :], ALU.mult, [sg, ssz], [yzs])

            self.projA([self.wG[l, t] for t in range(8)], N, lambda k: zT[:, k, 0:N], 8, glu_cons, [zT])

    def ssm_state_out(self, hl_r_ap, hl_i_ap, bufs, dst_ap):
        P = self.P
        with Arena(P) as A:
            o = A.sbuf("sso", [32, 2, 128], F32)
            ps = self.nextpA()
            self.TR(ps[0:32, 0:128], hl_r_ap, self.identf[:], bufs + [self.identf], [ps])
            self.TR(ps[0:32, 128:256], hl_i_ap, self.identf[:], bufs + [self.identf], [ps])
            self.CP(P.act, o[:].rearrange("p a n -> p (a n)"), ps[0:32, 0:256], [ps], [o])
            self.DMA(dst_ap.rearrange("a i p -> i a p"), o[:], [o], [])

    def q_to_base(self, A, N, i, ps, q0):
        P = self.P
        ac = P.act
        for hh in range(2):
            h = 2 * i + hh
            k, g = h // 4, h % 4
            slot = (k // 2) * 4 + g
            need, have = 64 * (k % 2), 64 * hh
            if need == have or os.environ.get('KQS') == '0':
                self.CP(ac, q0[have:have + 64, slot, 0:N], ps[have:have + 64, 0:N], [ps], [q0])
            else:
                if not hasattr(A, "_tq"):
                    A._tq = A.sbuf("tq", [128, N], BF16)
                tq = A._tq
                self.CP(ac, tq[have:have + 64, :], ps[have:have + 64, 0:N], [ps], [tq])
                p2 = self.pS[hh]
                if have == 64:
                    self.MM(p2[0:64, 0:N], self.ident[64:128, 64:128], tq[64:128, :], True, True, [self.ident, tq], [p2])
                    self.CP(ac, q0[0:64, slot, 0:N], p2[0:64, 0:N], [p2], [q0])
                else:
                    self.MM(p2[:, 0:N], self.shu[0:64, :], tq[0:64, :], True, True, [self.shu, tq], [p2])
                    self.CP(ac, q0[64:128, slot, 0:N], p2[64:128, 0:N], [p2], [q0])

    def kv_formB(self, l, A, hT, tiles, kvf):
        for ci, c0 in enumerate(range(0, NKVB, 128)):
            n = min(128, NKVB - c0)
            w = self.next_wt()
            self.DMA(w[:, :, 0:n], self.wB[l][:, :, c0:c0 + n], [], [w])
            for ti, (r0, nr) in enumerate(tiles):
                ps = self.pK[ti % 3]
                for k in range(16):
                    self.MM(ps[0:nr, 0:n], hT[:, k, r0:r0 + nr], w[:, k, 0:n], k == 0, k == 15, [hT, w], [ps])
                self.CP(self.P.act, kvf[ti][0:nr, c0:c0 + n], ps[0:nr, 0:n], [ps], [kvf[ti]])

    def nsa_phase(self, l, j, hT, yza, ps_):
        P, L, C, S = self.P, self.L, self.C, self.S
        dv, ac, pl = P.dve, P.act, P.pool
        NT = S // 128
        with Arena(P) as A:
            q0 = A.sbuf("q0", [128, 8, C], BF16)
            saz = A.sbuf("saz", [128, 8, C], BF16)
            self.projA([self.wA[l, t] for t in range(16, 24)], C, lambda k: hT[:, k, 0:C], 16,
                       lambda i, ps: self.q_to_base(A, C, i, ps, q0), [hT])
            self.projA([self.wA[l, t] for t in range(24, 32)], C, lambda k: hT[:, k, 0:C], 16,
                       lambda i, ps: self.AC(saz[:, i, :], ps[:, 0:C], AF.Silu, [ps], [saz]), [hT])
            if KSTOP < 2:
                self.MS(dv, yza[:], 0.0, [yza])
                return
            tiles = [(tt * 128, 128) for tt in range(C // 128)]
            kvf = [A.sbuf(f"kvf{i}", [128, NKVB], F32) for i in range(len(tiles))]
            self.kv_formB(l, A, hT, tiles, kvf)
            if KSTOP < 3:
                self.MS(dv, yza[:], 0.0, [yza])
                return
            sg = A.sbuf("sg", [128, len(tiles), 48], F32)
            xtk = A.sbuf("xtk", [128, 2, C], BF16)
            xtv = A.sbuf("xtv", [128, 2, C], BF16)
            kvb = A.sbuf("kvb", [128, 1536], BF16)
            yb = A.sbuf("yb", [128, 1024], BF16)
            for tt, (r0, nr) in enumerate(tiles):
                T = (j * C) // 128 + tt
                self.DMA(self.kvp[l][T * 128:(T + 1) * 128, :], kvf[tt][:, 0:1024], [kvf[tt]], [])
                if T >= NT - 4:
                    w0 = (T - (NT - 4)) * 128
                    self.DMA(self.winp[l][w0:w0 + 128, :], kvf[tt][:, 1024:1536], [kvf[tt]], [])
                if KSUB < 1:
                    continue
                self.CP(dv, kvb[:], kvf[tt][:, 0:1536], [kvf[tt]], [kvb])
                self.AC(sg[:, tt, :], kvf[tt][:, 1536:1584], AF.Sigmoid, [kvf[tt]], [sg])
                if KSUB < 2:
                    continue
                self.CP(ac, ps_["vsel"][:, T, :, 0:64], kvb[:, 768:1024].rearrange("p (h d) -> p h d", h=4), [kvb], [ps_["vsel"]])
                self.CP(ac, ps_["vwin"][:, T % 8, :, 0:64], kvb[:, 1280:1536].rearrange("p (h d) -> p h d", h=4), [kvb], [ps_["vwin"]])
                if KSUB < 3:
                    continue
                if os.environ.get("KBAR") == "1":
                    P.barrier()
                tb = [self.pS[0], self.pS[1]]
                for n_, c_ in enumerate((512, 640, 1024, 1152, 0, 128, 256, 384)):
                    self.MT(tb[n_ // 4][:, (n_ % 4) * 128:(n_ % 4 + 1) * 128], kvb[:, c_:c_ + 128], [kvb], [tb[n_ // 4]])
                if KSUB < 4:
                    continue
                pv = lambda a: tb[a // 2][:, (a % 2) * 256:(a % 2 + 1) * 256].rearrange("p (r t) -> p r t", r=2)
                self.CP(ac, ps_["ksT"][:, :, T * 128:(T + 1) * 128], pv(0), [tb[0]], [ps_["ksT"]])
                self.CP(ac, ps_["kwT"][:, :, (T % 8) * 128:(T % 8 + 1) * 128], pv(1), [tb[0]], [ps_["kwT"]])
                self.CP(dv, xtk[:, :, tt * 128:(tt + 1) * 128], pv(2), [tb[1]], [xtk])
                self.CP(dv, xtv[:, :, tt * 128:(tt + 1) * 128], pv(3), [tb[1]], [xtv])
            nbn = C // 64
            b0 = (j * C) // 64
            for a_, xt_ in ((0, xtk), (1, xtv)):
                v4 = xt_[:].rearrange("p r (n l) -> p r n l", l=64)
                self.TT(dv, v4, v4, L.pe2[:, a_, :].unsqueeze(1).unsqueeze(1).to_broadcast([128, 2, nbn, 64]), ALU.add, [xt_, L.pe2], [xt_])
            pkc, pvc = self.pS[0], self.pS[1]
            for k in (0, 2, 1, 3):
                base, pr = 64 * (k % 2), k // 2
                for a_, xt_, pso, ob in ((0, xtk, pkc, base), (1, xtv, pvc, 0)):
                    col = (pr * nbn) if a_ == 0 else (k * nbn)
                    xl = xt_[:].rearrange("p r (n l) -> p r l n", l=64)
                    for ll in range(64):
                        self.MM(pso[ob:ob + 64, col:col + nbn], L.wphi2[base:base + 64, a_, ll, :], xl[base:base + 64, pr, ll, :],
                                ll == 0, ll == 63, [L.wphi2, xt_], [pso])
            self.CP(ac, ps_["kcT"][:, :, b0:b0 + nbn], pkc[:, 0:2 * nbn].rearrange("p (r n) -> p r n", r=2), [pkc], [ps_["kcT"]])
            self.CP(ac, ps_["vcT"][0:64, :, b0:b0 + nbn], pvc[0:64, 0:4 * nbn].rearrange("p (k n) -> p k n", k=4), [pvc], [ps_["vcT"]])
            pvt = self.pS[0]
            for k in range(4):
                self.MT(pvt[0:64, k * 64:(k + 1) * 64], ps_["vcT"][0:64, k, :], [ps_["vcT"]], [pvt])
            self.CP(ac, ps_["vca"][0:64, :, 0:64], pvt[0:64, 0:256].rearrange("p (k e) -> p k e", k=4), [pvt], [ps_["vca"]])
            if KSTOP < 5:
                self.MS(dv, yza[:], 0.0, [yza])
                return
            T0 = (j * C) // 128
            ntt = len(tiles)
            cbq = A.sbuf("cbq", [128, ntt, 64], F32)
            tsl = A.sbuf("tsl", [128, ntt, 64], F32)
            cbTf = A.sbuf("cbTf", [64, C], F32)
            cbT = A.sbuf("cbT", [64, C], BF16)
            self.DMA(cbq[:], self.t_cbq[T0:T0 + ntt].rearrange("t p n -> p t n"), [], [cbq])
            self.DMA(tsl[:], self.t_sel[T0:T0 + ntt].rearrange("t p n -> p t n"), [], [tsl])
            self.DMA(cbTf[:], self.t_cbT[:, j * C:(j + 1) * C], [], [cbTf])
            self.CP(dv, cbT[:], cbTf[:], [cbTf], [cbT])
            BT = A.sbuf("BT", [64, 4, C], BF16)
            s1 = A.sbuf("s1", [128, 4, 64], F32)
            z4 = A.sbuf("z4", [128, 4], F32)
            sc_ = A.sbuf("sc", [128, 64], F32)
            mx = A.sbuf("mx", [128, 8], F32)
            w1 = A.sbuf("w1", [128, 64], F32)
            bq = A.sbuf("bq", [128, 64], BF16)
            for tt in range(ntt):
                for k in range(4):
                    base, pr = 64 * (k % 2), k // 2
                    psc = self.pS[(tt * 4 + k) % 2]
                    for g in range(4):
                        self.MM(psc[:, g * 64:(g + 1) * 64], q0[base:base + 64, pr * 4 + g, tt * 128:(tt + 1) * 128],
                                ps_["kcT"][base:base + 64, pr, :], True, True, [q0, ps_["kcT"]], [psc])
                    self.TT(dv, s1[:], psc[:, 0:256].rearrange("p (g n) -> p g n", g=4),
                            cbq[:, tt, :].unsqueeze(1).to_broadcast([128, 4, 64]), ALU.add, [psc, cbq], [s1])
                    self.AC(s1[:], s1[:], AF.Exp, [s1], [s1], scale=0.125)
                    P.op(dv, lambda e: e.tensor_reduce(out=z4[:], in_=s1[:], axis=AX.X, op=ALU.add), [s1], [z4])
                    self.TS(dv, z4[:], z4[:], 1e-30, None, ALU.max, None, [z4], [z4])
                    P.op(dv, lambda e: e.reciprocal(out=z4[:], in_=z4[:]), [z4], [z4])
                    self.TT(dv, s1[:], s1[:], z4[:].unsqueeze(2).to_broadcast([128, 4, 64]), ALU.mult, [s1, z4], [s1])
                    P.op(dv, lambda e: e.tensor_reduce(out=sc_[:], in_=s1[:].rearrange("p g n -> p n g"), axis=AX.X, op=ALU.add), [s1], [sc_])
                    self.TT(dv, sc_[:], sc_[:], tsl[:, tt, :], ALU.add, [sc_, tsl], [sc_])
                    P.op(dv, lambda e: e.max(out=mx[:], in_=sc_[:]), [sc_], [mx])
                    P.op(dv, lambda e: e.match_replace(out=w1[:], in_to_replace=mx[:], in_values=sc_[:], imm_value=NEG), [mx, sc_], [w1])
                    P.op(dv, lambda e: e.max(out=mx[:], in_=w1[:]), [w1], [mx])
                    P.op(dv, lambda e: e.match_replace(out=w1[:], in_to_replace=mx[:], in_values=w1[:], imm_value=NEG), [mx, w1], [w1])
                    self.TT(dv, w1[:], sc_[:], w1[:], ALU.subtract, [sc_, w1], [w1])
                    self.TS(dv, w1[:], w1[:], 1.0, -1.0, ALU.min, ALU.add, [w1], [w1])
                    self.TS(dv, bq[:], w1[:], 1e30, None, ALU.mult, None, [w1], [bq])
                    pbt = self.pA[(tt * 4 + k) % 2]
                    self.MT(pbt[0:64, 0:128], bq[:], [bq], [pbt])
                    self.CP(ac, BT[0:64, k, tt * 128:(tt + 1) * 128], pbt[0:64, 0:128], [pbt], [BT])
            if KSTOP < 6:
                self.MS(dv, yza[:], 0.0, [yza])
                return
            yat = [A.sbuf(f"yat{i}", [128, 1024], F32) for i in range(ntt)]
            zz = A.sbuf("zz", [128, 4], F32)
            ytmp = A.sbuf("ytmp", [128, 4, 64], F32)
            pts = [A.sbuf(f"pt{i}", [128, C], BF16) for i in range(3)]
            pti = [0]
            scp = [self.pS[0], self.pS[1], self.pA[0], self.pA[1]]
            sci = [0]
            Gf = self.G[:].rearrange("p j c -> p (j c)")

            def score_exp(nk, mms):
                sci[0] += 1
                psx = scp[sci[0] % 4]
                for mi, (lhsT, rhs, R) in enumerate(mms):
                    self.MM(psx[0:nk, 0:C], lhsT, rhs, mi == 0, mi == len(mms) - 1, R, [psx])
                pti[0] += 1
                pt = pts[pti[0] % 3]
                self.AC(pt[0:nk, :], psx[0:nk, 0:C], AF.Exp, [psx], [pt], scale=0.125)
                return pt

            for k in range(4):
                base, pr = 64 * (k % 2), k // 2
                po = [self.pK[tt] for tt in range(ntt)]
                pov = [po[tt][:, 0:260].rearrange("p (g e) -> p g e", g=4) for tt in range(ntt)]
                for bi in range(3):
                    for g in range(4):
                        slot = pr * 4 + g
                        qh = q0[base:base + 64, slot, :]
                        if bi == 0:
                            pt = score_exp(64, [(ps_["kcT"][base:base + 64, pr, :], qh, [ps_["kcT"], q0]),
                                                (self.ident[0:64, 0:64], cbT[:, :], [self.ident, cbT])])
                            for tt in range(ntt):
                                self.MM(pov[tt][:, g, :], pt[0:64, tt * 128:(tt + 1) * 128], ps_["vca"][0:64, k, :], True, True,
                                        [pt, ps_["vca"]], [po[tt]])
                        elif bi == 1:
                            last = T0 + ntt - 1
                            for kt in range(0, last + 1):
                                pt = score_exp(128, [(ps_["ksT"][base:base + 64, pr, kt * 128:(kt + 1) * 128], qh, [ps_["ksT"], q0]),
                                                     (Gf[:, kt * 128:(kt + 1) * 128], BT[0:64, k, :], [self.G, BT])])
                                for tt in range(ntt):
                                    T = T0 + tt
                                    if kt == T:
                                        self.TT(dv, pt[:, tt * 128:(tt + 1) * 128], pt[:, tt * 128:(tt + 1) * 128], self.tri[:], ALU.mult, [pt, self.tri], [pt])
                                for tt in range(ntt):
                                    T = T0 + tt
                                    if kt <= T:
                                        self.MM(pov[tt][:, g, :], pt[:, tt * 128:(tt + 1) * 128], ps_["vsel"][:, kt, k, :], kt == 0, kt == T,
                                                [pt, ps_["vsel"]], [po[tt]])
                        else:
                            for kt in range(max(0, T0 - 4), T0 + ntt):
                                pt = score_exp(128, [(ps_["kwT"][base:base + 64, pr, (kt % 8) * 128:(kt % 8 + 1) * 128], qh, [ps_["kwT"], q0])])
                                for tt in range(ntt):
                                    T = T0 + tt
                                    if kt == T:
                                        self.TT(dv, pt[:, tt * 128:(tt + 1) * 128], pt[:, tt * 128:(tt + 1) * 128], self.tri[:], ALU.mult, [pt, self.tri], [pt])
                                    elif kt == T - 4:
                                        self.TT(dv, pt[:, tt * 128:(tt + 1) * 128], pt[:, tt * 128:(tt + 1) * 128], self.tris[:], ALU.mult, [pt, self.tris], [pt])
                                for tt in range(ntt):
                                    T = T0 + tt
                                    if T - 4 <= kt <= T:
                                        self.MM(pov[tt][:, g, :], pt[:, tt * 128:(tt + 1) * 128], ps_["vwin"][:, kt % 8, k, :],
                                                kt == max(0, T - 4), kt == T, [pt, ps_["vwin"]], [po[tt]])
                    for tt in range(ntt):
                        z4 = zz
                        self.TS(dv, z4[:], pov[tt][:, :, 64], 1e-30, None, ALU.max, None, [po[tt]], [z4])
                        P.op(dv, lambda e: e.reciprocal(out=z4[:], in_=z4[:]), [z4], [z4])
                        gsel = sg[:, tt, 12 * k:12 * k + 12].rearrange("p (g i) -> p g i", i=3)[:, :, bi]
                        self.TT(dv, z4[:], z4[:], gsel, ALU.mult, [z4, sg], [z4])
                        yv = yat[tt][:, k * 256:(k + 1) * 256].rearrange("p (g d) -> p g d", g=4)
                        fb = z4[:].unsqueeze(2).to_broadcast([128, 4, 64])
                        if bi == 0:
                            self.TT(dv, yv, pov[tt][:, :, 0:64], fb, ALU.mult, [po[tt], z4], [yat[tt]])
                        else:
                            tmp = ytmp
                            self.TT(dv, tmp[:], pov[tt][:, :, 0:64], fb, ALU.mult, [po[tt], z4], [tmp])
                            self.TT(dv, yv, yv, tmp[:], ALU.add, [yat[tt], tmp], [yat[tt]])
            if KSTOP < 7:
                self.MS(dv, yza[:], 0.0, [yza])
                return
            for tt in range(ntt):
                if self.debug:
                    self.dump(f"yat{l}_{j}_{tt}", yat[tt][:], [128, 1024], [yat[tt]])
                self.CP(dv, yb[:], yat[tt][:], [yat[tt]], [yb])
                for hf4 in range(2):
                    pyt = self.pS[hf4]
                    for t4 in range(4):
                        t8 = 4 * hf4 + t4
                        self.MT(pyt[:, t4 * 128:(t4 + 1) * 128], yb[:, t8 * 128:(t8 + 1) * 128], [yb], [pyt])
                    self.TT(dv, yza[:, 4 * hf4:4 * hf4 + 4, tt * 128:(tt + 1) * 128], pyt[:, :].rearrange("p (t c) -> p t c", t=4),
                            saz[:, 4 * hf4:4 * hf4 + 4, tt * 128:(tt + 1) * 128], ALU.mult, [pyt, saz], [yza])

    def prompt_layer(self, l):
        P, S, C = self.P, self.S, self.C
        dv, pl = P.dve, P.pool
        NT = S // 128
        with Arena(P) as PA:
            ps_ = {
                "ksT": PA.sbuf("ksT", [128, 2, S], BF16),
                "vsel": PA.sbuf("vsel", [128, NT, 4, 65], BF16),
                "kwT": PA.sbuf("kwT", [128, 2, 1024], BF16),
                "vwin": PA.sbuf("vwin", [128, 8, 4, 65], BF16),
                "kcT": PA.sbuf("kcT", [128, 2, 64], BF16),
                "vcT": PA.sbuf("vcT", [64, 4, 64], BF16),
                "vca": PA.sbuf("vca", [64, 4, 65], BF16),
            }
            halo = PA.sbuf("halo", [128, 8, 1, 15], F32)
            st = {n: PA.sbuf(n, [128, 32, 1], F32) for n in ("car_r", "car_i", "hl_r", "hl_i")}
            self.MS(pl, ps_["vsel"][:], 1.0, [ps_["vsel"]])
            self.MS(pl, ps_["vwin"][:], 1.0, [ps_["vwin"]])
            self.MS(pl, ps_["vca"][:], 1.0, [ps_["vca"]])
            self.MS(pl, ps_["kcT"][:], 0.0, [ps_["kcT"]])
            self.MS(pl, ps_["vcT"][:], 0.0, [ps_["vcT"]])
            self.MS(dv, halo[:], 0.0, [halo])
            for b in st.values():
                self.MS(dv, b[:], 0.0, [b])
            src = self.xp if l == 0 else None
            for j in range(self.NCH):
                tiles = [(tt * 128, 128) for tt in range(C // 128)]
                T0 = (j * C) // 128

                def src_rows(r0, nr, j=j):
                    if l == 0:
                        return self.xp[j * C + r0:j * C + r0 + nr, :]
                    return self.x1_tiles[(j * C + r0) // 128][0:nr, :]

                def dst_rows(r0, nr, j=j):
                    if l == 0:
                        return self.x1_tiles[(j * C + r0) // 128][0:nr, :]
                    return self.yp[j * C + r0:j * C + r0 + nr, :]

                with Arena(P) as CA:
                    hT = CA.sbuf("hT", [128, 16, C], BF16)
                    yz = [CA.sbuf(f"yz{i}", [128, 8, C], BF16) for i in range(3)]
                    if l == 1:
                        for tt in range(C // 128):
                            xb_ = self.x1_tiles[T0 + tt]
                            for d in list(xb_.w.values()):
                                P._wait(P.sp, d)
                    self.norm_phase(src_rows, tiles, hT)
                    if "pool" in self.parts:
                        self.pool_phase(l, C, hT, yz[0], halo, first=(j == 0))
                    else:
                        self.MS(dv, yz[0][:], 0.0, [yz[0]])
                    if "nsa" in self.parts:
                        self.nsa_phase(l, j, hT, yz[1], ps_)
                    else:
                        self.MS(dv, yz[1][:], 0.0, [yz[1]])
                    if "ssm" in self.parts:
                        self.ssm_phase(l, C, hT, yz[2], st)
                    else:
                        self.MS(dv, yz[2][:], 0.0, [yz[2]])
                    if self.debug and j == 0:
                        for i in range(3):
                            self.dump(f"yz{i}_{l}", yz[i][:], [128, 8, C], [yz[i]])
                    dst_bufs = [self.x1_tiles[T0 + tt] for tt in range(C // 128)] if l == 0 else None
                    self.merge_out_phase(l, C, tiles, hT, yz, src_rows, dst_rows, dst_bufs)
            self.pool_state_out(lambda t: halo[:, t, 0, :], halo, self.poolp[l])
            self.ssm_state_out(st["hl_r"][:, :, 0], st["hl_i"][:, :, 0], [st["hl_r"], st["hl_i"]], self.ssmp[l])

    def build(self):
        P = self.P
        self.declare()
        self.x1_tiles = [Buf(self.x1.t[t * 128:(t + 1) * 128, :], f"x1_{t}") for t in range(self.S // 128)]
        self.consts()
        self.wts = [P.sbuf(f"wt{i}", [128, 16, 128], BF16) for i in range(6)]
        self.wt_i = 0
        self.prepass()
        for l in range(2):
            with Arena(P) as LA:
                self.layer_prep(l, LA)
                self.prompt_layer(l)
                if self.NS:
                    self.sample_layer(l)
        P.finish()
        P.close()
        return self.nc


def _tables():
    pos = np.arange(SEQ)
    n = np.arange(64)
    nvalid = (pos + 1) // 64
    cbq = np.where(n[None, :] < nvalid[:, None], 0.0, NEG).astype(np.float32)
    cur = pos // 64
    tsel = np.zeros((SEQ, 64), np.float32)
    tsel[n[None, :] > cur[:, None]] = NEG
    tsel[n[None, :] == cur[:, None]] = 1e4
    tsel[n[None, :] == (cur[:, None] - 1)] = 2e4
    tsel[:, 0] = 3e4
    rc = np.zeros((4, 15), np.float32)
    for gi, w in enumerate((2, 4, 8, 16)):
        rc[gi] = 1.0 / np.minimum(np.arange(15) + 1, w)
    return {
        "t_cbq": np.ascontiguousarray(cbq.reshape(SEQ // 128, 128, 64)),
        "t_sel": np.ascontiguousarray(tsel.reshape(SEQ // 128, 128, 64)),
        "t_cbT": np.ascontiguousarray(cbq.T),
        "t_rc": np.ascontiguousarray(np.tile(rc.reshape(1, 60), (128, 1))),
    }


_WKEYS = ("g_pre", "g_post", "w_in", "w_pool", "pool_scale", "pe_cmp", "w_phi", "d_skip", "w_glu",
          "w_br_pool", "w_br_nsa", "w_br_ssm", "w_out", "b_re", "b_im")


def _core_inputs(inp, b, S, NS, samples):
    f = lambda a: np.ascontiguousarray(np.asarray(a, dtype=np.float32))
    m = {k: f(inp[k]) for k in _WKEYS}
    m["lam_re"] = f(inp["lam_re"]).reshape(2, 32, 128)
    m["lam_im"] = f(inp["lam_im"]).reshape(2, 32, 128)
    m["log_step"] = f(inp["log_step"]).reshape(2, 32, 2)
    m["c_re"] = f(inp["c_re"]).reshape(2, 32, 2, 16, 64)
    m["c_im"] = f(inp["c_im"]).reshape(2, 32, 2, 16, 64)
    m["xp"] = f(inp["x_prompt"][b, :S])
    m.update(_tables())
    if NS:
        sl = list(samples)
        m["xs"] = f(inp["x_sample"][sl]).reshape(4 * NS, D)
        ck = np.asarray(inp["cache_kv"], dtype=np.float32)
        m["cache0"] = np.ascontiguousarray(ck[0]).reshape(NPOOL * 128 * 2, 512)
        m["cache1"] = np.ascontiguousarray(ck[1]).reshape(NPOOL * 128 * 2, 512)
        m["ptab"] = np.ascontiguousarray(np.asarray(inp["page_table"])[sl].astype(np.int32))
        m["swin"] = f(np.asarray(inp["state_win_kv"])[:, sl]).reshape(2, NS, 512, 512)
        m["spool"] = f(np.asarray(inp["state_pool"])[:, sl])
        m["sssm"] = f(np.asarray(inp["state_ssm"])[:, sl]).reshape(2, NS, 2, 32, 128)
        N = 4 * NS
        col = np.arange(N)
        cm = (col[None, :] // 4 == np.arange(NS)[:, None]).astype(np.float32)
        p = np.arange(128)
        wm = cm[None, :, :] * (p[:, None, None] > (col % 4)[None, None, :])
        m["t_cm"] = np.ascontiguousarray(np.broadcast_to(cm[None], (128, NS, N)).astype(np.float32))
        m["t_wm"] = np.ascontiguousarray(wm.astype(np.float32))
        m16 = ((col[:, None] // 4 == col[None, :] // 4) & (col[:, None] % 4 <= col[None, :] % 4)).astype(np.float32)
        m["t_m16"] = np.ascontiguousarray(m16)
        ts = np.zeros((N, 256), np.float32); ts[:, 0] = 3e4; ts[:, 255] = 2e4
        m["t_ssel"] = ts
    return m


_NC_CACHE = {}


def run(inp, S=SEQ, NS=4, ncores=2, debug=False, parts=("pool", "ssm", "nsa"), trace=False):
    key = (S, NS, debug, tuple(parts))
    if key not in _NC_CACHE:
        kb = KB(S=S, NS=NS, debug=debug, parts=parts)
        kb.build()
        _NC_CACHE[key] = kb
    kb = _NC_CACHE[key]
    in_maps = [_core_inputs(inp, c, S, NS, range(NS * c, NS * c + NS)) for c in range(ncores)]
    res = run_bass_kernel_spmd(kb.nc, in_maps, core_ids=list(range(ncores)), **({"trace": True} if trace else {}))
    return kb, res


def kernel(**inputs):
    kb, res = run(inputs)
    r = res.results
    B = 2
    y_prompt = np.stack([r[b]["yp"] for b in range(B)])
    y_sample = np.concatenate([r[c]["ys"].reshape(4, 4, D) for c in range(2)], axis=0)
    kv_p = np.stack([r[b]["kvp"] for b in range(B)], axis=1).reshape(2, B, SEQ, 4, 4, 64)
    kv_s = np.concatenate([r[c]["kvs"].reshape(2, 4, 4, 4, 4, 64) for c in range(2)], axis=1)
    win_p = np.stack([r[b]["winp"] for b in range(B)], axis=1).reshape(2, B, 512, 2, 4, 64)
    win_s = np.concatenate([r[c]["wins"].reshape(2, 4, 512, 2, 4, 64) for c in range(2)], axis=1)
    pool_p = np.stack([r[b]["poolp"] for b in range(B)], axis=1)
    pool_s = np.concatenate([r[c]["pools"] for c in range(2)], axis=1)
    ssm_p = np.stack([r[b]["ssmp"] for b in range(B)], axis=1).reshape(2, B, 2, 64, 64)
    ssm_s = np.concatenate([r[c]["ssms"].reshape(2, 4, 2, 64, 64) for c in range(2)], axis=1)
    outs = (y_prompt, y_sample, kv_p, kv_s, win_p, win_s, pool_p, pool_s, ssm_p, ssm_s)
    return tuple(np.ascontiguousarray(o, dtype=np.float32) for o in outs)


def _sample_layer(self, l):
    P, L, NS = self.P, self.L, self.NS
    dv, ac, pl = P.dve, P.act, P.pool
    N = 4 * NS
    tiles = [(0, N)]
    with Arena(P) as SA:
        hT = SA.sbuf("hTs", [128, 16, N], BF16)
        yz = [SA.sbuf(f"yzs{i}", [128, 8, N], BF16) for i in range(3)]

        def src_rows(r0, nr):
            return self.xs[r0:r0 + nr, :] if l == 0 else self.xs1[r0:r0 + nr, :]

        def dst_rows(r0, nr):
            return self.xs1[r0:r0 + nr, :] if l == 0 else self.ys[r0:r0 + nr, :]

        if l == 1:
            for d in list(self.xs1.w.values()):
                P._wait(P.sp, d)
        self.norm_phase(src_rows, tiles, hT)
        halo = SA.sbuf("halos", [128, 8, NS, 15], F32)
        with Arena(P) as A:
            for s in range(NS):
                sp = A.sbuf("spl", [15, 1024], F32)
                self.DMA(sp[:], self.spool[l, s], [], [sp])
                for half in range(2):
                    ps = self.nextpA()
                    for t in range(4):
                        tt = 4 * half + t
                        self.MT(ps[:, t * 16:t * 16 + 15], sp[0:15, tt * 128:(tt + 1) * 128], [sp], [ps])
                    self.CP(ac, halo[:, 4 * half:4 * half + 4, s, :], ps[:, 0:64].rearrange("p (t c) -> p t c", t=4)[:, :, 0:15], [ps], [halo])
        self.pool_phase(l, N, hT, yz[0], halo, first=False, nseg=NS)
        for s in range(NS):
            self.pool_state_out(lambda t, s=s: halo[:, t, s, :], halo, self.pools[l, s])
        st = {n: SA.sbuf(n + "s", [128, 32, NS], F32) for n in ("car_r", "car_i", "hl_r", "hl_i")}
        with Arena(P) as A:
            c1 = A.sbuf("c1s", [128, 32, NS], F32)
            c2 = A.sbuf("c2s", [128, 32, NS], F32)
            for s in range(NS):
                for ri, nm in ((0, "hl_r"), (1, "hl_i")):
                    t_ = self.load_T(A, "h0", self.sssm[l, s, ri], 32)
                    self.CP(dv, st[nm][:, :, s], t_[:, 0:32], [t_], [st[nm]])
            ab_r = L.abr[:].unsqueeze(2).to_broadcast([128, 32, NS])
            ab_i = L.abi[:].unsqueeze(2).to_broadcast([128, 32, NS])
            self.TT(dv, c1[:], st["hl_r"][:], ab_r, ALU.mult, [st["hl_r"], L.abr], [c1])
            self.TT(dv, c2[:], st["hl_i"][:], ab_i, ALU.mult, [st["hl_i"], L.abi], [c2])
            self.TT(dv, st["car_r"][:], c1[:], c2[:], ALU.subtract, [c1, c2], [st["car_r"]])
            self.TT(dv, c1[:], st["hl_i"][:], ab_r, ALU.mult, [st["hl_i"], L.abr], [c1])
            self.TT(dv, c2[:], st["hl_r"][:], ab_i, ALU.mult, [st["hl_r"], L.abi], [c2])
            self.TT(dv, st["car_i"][:], c1[:], c2[:], ALU.add, [c1, c2], [st["car_i"]])
        self.ssm_phase(l, N, hT, yz[2], st, nseg=NS)
        for s in range(NS):
            self.ssm_state_out(st["hl_r"][:, :, s], st["hl_i"][:, :, s], [st["hl_r"], st["hl_i"]], self.ssms[l, s])
        self.nsa_sample(l, hT, yz[1])
        self.merge_out_phase(l, N, tiles, hT, yz, src_rows, dst_rows, [self.xs1] if l == 0 else None)


def _nsa_sample(self, l, hT, yza):
    P, L, NS = self.P, self.L, self.NS
    dv, ac, pl = P.dve, P.act, P.pool
    N = 4 * NS
    cache_l = self.cache[l]
    with Arena(P) as A:
        q0 = A.sbuf("q0s", [128, 8, N], BF16)
        saz = A.sbuf("sazs", [128, 8, N], BF16)
        self.projA([self.wA[l, t] for t in range(16, 24)], N, lambda k: hT[:, k, 0:N], 16,
                   lambda i, ps: self.q_to_base(A, N, i, ps, q0), [hT])
        self.projA([self.wA[l, t] for t in range(24, 32)], N, lambda k: hT[:, k, 0:N], 16,
                   lambda i, ps: self.AC(saz[:, i, :], ps[:, 0:N], AF.Silu, [ps], [saz]), [hT])
        kvf = A.sbuf("kvfs", [128, NKVB], F32)
        self.kv_formB(l, A, hT, [(0, N)], [kvf])
        self.DMA(self.kvs[l][0:N, :], kvf[0:N, 0:1024], [kvf], [])
        for s in range(NS):
            self.DMA(self.wins[l, s, 0:508, :], self.swin[l, s, 4:512, :], [], [])
            self.DMA(self.wins[l, s, 508:512, :], kvf[4 * s:4 * s + 4, 1024:1536], [kvf], [])
        sg = A.sbuf("sgs", [N, 48], F32)
        self.AC(sg[:], kvf[0:N, 1536:1584], AF.Sigmoid, [kvf], [sg])
        kvb = A.sbuf("kvbs", [N, 1536], BF16)
        self.CP(dv, kvb[:], kvf[0:N, 0:1536], [kvf], [kvb])
        knT = A.sbuf("knT", [128, 4, N], BF16)
        ptn = self.pS[0]
        for n_, c_ in enumerate((512, 640, 1024, 1152)):
            self.MT(ptn[:, n_ * N:(n_ + 1) * N], kvb[0:N, c_:c_ + 128], [kvb], [ptn])
        self.CP(ac, knT[:], ptn[:, 0:4 * N].rearrange("p (a n) -> p a n", a=4), [ptn], [knT])
        vn = A.sbuf("vn", [N, 2, 4, 65], BF16)
        self.MS(pl, vn[:], 1.0, [vn])
        self.CP(ac, vn[:, 0, :, 0:64], kvb[:, 768:1024].rearrange("p (h d) -> p h d", h=4), [kvb], [vn])
        self.CP(ac, vn[:, 1, :, 0:64], kvb[:, 1280:1536].rearrange("p (h d) -> p h d", h=4), [kvb], [vn])
        cmf = A.sbuf("cmf", [128, NS, N], F32)
        wmf = A.sbuf("wmf", [128, NS, N], F32)
        m16f = A.sbuf("m16f", [N, N], F32)
        self.DMA(cmf[:], self.t_cm, [], [cmf])
        self.DMA(wmf[:], self.t_wm, [], [wmf])
        self.DMA(m16f[:], self.t_m16, [], [m16f])
        cm = A.sbuf("cm", [128, NS, N], BF16)
        wm = A.sbuf("wm", [128, NS, N], BF16)
        m16 = A.sbuf("m16", [N, N], BF16)
        self.CP(dv, cm[:], cmf[:], [cmf], [cm])
        self.CP(dv, wm[:], wmf[:], [wmf], [wm])
        self.CP(dv, m16[:], m16f[:], [m16f], [m16])
        pti = A.sbuf("pti", [128, NS * 128], I32)
        self.DMA(pti[:], self.ptab.rearrange("(o s) g -> o (s g)", o=1).to_broadcast([128, NS * 128]), [], [pti])
        iop = A.sbuf("iop", [128, 1], F32)
        P.op(pl, lambda e: e.iota(iop[:], pattern=[[0, 1]], base=0, channel_multiplier=2, allow_small_or_imprecise_dtypes=True), (), [iop])
        idxf = A.sbuf("idxf", [128, NS * 128], F32)
        self.TS(dv, idxf[:], pti[:], 256.0, iop[:, 0:1], ALU.mult, ALU.add, [pti, iop], [idxf])
        idxA = A.sbuf("idxA", [128, NS * 128], I32)
        idxB = A.sbuf("idxB", [128, NS * 128], I32)
        self.CP(dv, idxA[:], idxf[:], [idxf], [idxA])
        self.TS(dv, idxB[:], idxf[:], 1.0, None, ALU.add, None, [idxf], [idxB])
        yat = A.sbuf("yats", [N, 1024], F32)
        pgs = [A.sbuf(f"pg{i}", [128, 512], F32) for i in range(3)]
        pgb = [A.sbuf(f"pgb{i}", [128, 512], BF16) for i in range(2)]
        pts = [A.sbuf(f"pts{i}", [128, 16, N], BF16) for i in range(2)]
        cnt = {"pg": 0, "pt": 0, "sc": 0}
        Gf = self.G[:].rearrange("p j c -> p (j c)")
        po = [self.pK[0], self.pK[1], self.pK[2], self.pS[1]]
        pov = [p_[0:N, 0:260].rearrange("p (g e) -> p g e", g=4) for p_ in po]
        scps = [self.pA[0], self.pA[1]]
        zb = A.sbuf("zb", [128, 260], BF16)
        self.MS(pl, zb[:], 0.0, [zb])

        def gather_page(s, g, idx):
            cnt["pg"] += 1
            pg = pgs[cnt["pg"] % 3]
            P.gather(pg[:], cache_l, idx[:, s * 128 + g:s * 128 + g + 1], [idx], [pg])
            b = pgb[cnt["pg"] % 2]
            self.CP(dv, b[:], pg[:], [pg], [b])
            return b

        def attend(nk, kT_of, bias_of, mask_ap, mask_R, v_of, first, last):
            cnt["sc"] += 1
            psx = scps[cnt["sc"] % 2]
            if bias_of:
                bz = bias_of(None)
                self.MM(psx[0:nk, 0:16 * N], bz[0], bz[1], True, False, bz[2], [psx])
            for k in (0, 2, 1, 3):
                base, pr = 64 * (k % 2), k // 2
                kT, kR = kT_of(k)
                self.MM(psx[0:nk, k * 4 * N:(k + 1) * 4 * N], kT,
                        q0[base:base + 64, pr * 4:pr * 4 + 4, :].rearrange("p g n -> p (g n)"), bias_of is None, True, kR + [q0], [psx])
            cnt["pt"] += 1
            pt = pts[cnt["pt"] % 2]
            ptf = pt[0:nk].rearrange("p a n -> p (a n)")
            self.AC(ptf, psx[0:nk, 0:16 * N], AF.Exp, [psx], [pt], scale=0.125)
            self.TT(dv, pt[0:nk], pt[0:nk], mask_ap.unsqueeze(1).to_broadcast([nk, 16, N]), ALU.mult, [pt] + mask_R, [pt])
            if first:
                for k in range(4):
                    self.MM(po[k][0:N, 0:260], zb[:, 0:N], zb[:, 0:260], True, False, [zb], [po[k]])
            for k in range(4):
                vv, vR = v_of(k)
                for g in range(4):
                    self.MM(pov[k][:, g, :], pt[0:nk, k * 4 + g, :], vv, False, last, [pt] + vR, [po[k]])

        def finish_branch(bi):
            for k in range(4):
                z4 = A.sbuf("zzs", [N, 4], F32)
                self.TS(dv, z4[:], pov[k][:, :, 64], 1e-30, None, ALU.max, None, [po[k]], [z4])
                P.op(dv, lambda e: e.reciprocal(out=z4[:], in_=z4[:]), [z4], [z4])
                gsel = sg[:, 12 * k:12 * k + 12].rearrange("p (g i) -> p g i", i=3)[:, :, bi]
                self.TT(dv, z4[:], z4[:], gsel, ALU.mult, [z4, sg], [z4])
                yv = yat[:, k * 256:(k + 1) * 256].rearrange("p (g d) -> p g d", g=4)
                fb = z4[:].unsqueeze(2).to_broadcast([N, 4, 64])
                if bi == 0:
                    self.TT(dv, yv, pov[k][:, :, 0:64], fb, ALU.mult, [po[k], z4], [yat])
                else:
                    tmp = A.sbuf("ytmps", [N, 4, 64], F32)
                    self.TT(dv, tmp[:], pov[k][:, :, 0:64], fb, ALU.mult, [po[k], z4], [tmp])
                    self.TT(dv, yv, yv, tmp[:], ALU.add, [yat, tmp], [yat])

        XTk = A.sbuf("XTk", [128, 2, 4096], BF16)
        XTv = A.sbuf("XTv", [128, 2, 4096], BF16)
        kcT = A.sbuf("kcTs", [128, 2, 256], BF16)
        vcT = A.sbuf("vcTs", [64, 4, 64], BF16)
        vca = A.sbuf("vcas", [128, 2, 4, 65], BF16)
        BT = A.sbuf("BTs", [64, NS, 4, 4, 4 * N], BF16)
        self.MS(pl, vca[:], 1.0, [vca])
        s1 = A.sbuf("s1s", [N, 4, 256], F32)
        z4a = A.sbuf("z4a", [N, 4], F32)
        sc_ = A.sbuf("scs", [N, 256], F32)
        mx = A.sbuf("mxs", [N, 8], F32)
        w1 = A.sbuf("w1s", [N, 256], F32)
        bq = A.sbuf("bqs", [N, 256], BF16)
        tsl = A.sbuf("tsls", [N, 256], F32)
        self.DMA(tsl[:], self.t_ssel, [], [tsl])
        for s in range(NS):
            for grp in range(4):
                for gg in range(32):
                    g = grp * 32 + gg
                    b = gather_page(s, g, idxA)
                    ptx = self.pS[0]
                    for n_ in range(4):
                        self.MT(ptx[:, n_ * 128:(n_ + 1) * 128], b[:, n_ * 128:(n_ + 1) * 128], [b], [ptx])
                    self.CP(ac, XTk[:, :, gg * 128:(gg + 1) * 128], ptx[:, 0:256].rearrange("p (r t) -> p r t", r=2), [ptx], [XTk])
                    self.CP(ac, XTv[:, :, gg * 128:(gg + 1) * 128], ptx[:, 256:512].rearrange("p (r t) -> p r t", r=2), [ptx], [XTv])
                for a_, xt_ in ((0, XTk), (1, XTv)):
                    v4 = xt_[:].rearrange("p r (n l) -> p r n l", l=64)
                    self.TT(dv, v4, v4, L.pe2[:, a_, :].unsqueeze(1).unsqueeze(1).to_broadcast([128, 2, 64, 64]), ALU.add, [xt_, L.pe2], [xt_])
                pkc, pvc = self.pA[0], self.pA[1]
                for k in (0, 2, 1, 3):
                    base, pr = 64 * (k % 2), k // 2
                    for a_, xt_, pso, ob in ((0, XTk, pkc, base), (1, XTv, pvc, 0)):
                        col = (pr * 64) if a_ == 0 else (k * 64)
                        xl = xt_[:].rearrange("p r (n l) -> p r l n", l=64)
                        for ll in range(64):
                            self.MM(pso[ob:ob + 64, col:col + 64], L.wphi2[base:base + 64, a_, ll, :], xl[base:base + 64, pr, ll, :],
                                    ll == 0, ll == 63, [L.wphi2, xt_], [pso])
                self.CP(ac, kcT[:, :, grp * 64:(grp + 1) * 64], pkc[:, 0:128].rearrange("p (r n) -> p r n", r=2), [pkc], [kcT])
                self.CP(ac, vcT[0:64, :, :], pvc[0:64, 0:256].rearrange("p (k n) -> p k n", k=4), [pvc], [vcT])
                pvt = self.pS[0]
                ob = 64 * (grp % 2)
                for k in range(4):
                    self.MT(pvt[ob:ob + 64, k * 64:(k + 1) * 64], vcT[0:64, k, :], [vcT], [pvt])
                self.CP(ac, vca[ob:ob + 64, grp // 2, :, 0:64], pvt[ob:ob + 64, 0:256].rearrange("p (k e) -> p k e", k=4), [pvt], [vca])
            for nt in range(2):
                attend(128, lambda k, nt=nt: (kcT[64 * (k % 2):64 * (k % 2) + 64, k // 2, nt * 128:(nt + 1) * 128], [kcT]),
                       None, cm[:, s, :], [cm], lambda k, nt=nt: (vca[:, nt, k, :], [vca]),
                       (s == 0 and nt == 0), (s == NS - 1 and nt == 1))
            for k in range(4):
                base, pr = 64 * (k % 2), k // 2
                psc = [self.pS[0], self.pS[1]] if False else [self.pA[0], self.pA[1]]
                for g in range(4):
                    pq = psc[g % 2]
                    self.MM(pq[0:N, (g // 2) * 256:(g // 2) * 256 + 256], q0[base:base + 64, pr * 4 + g, :], kcT[base:base + 64, pr, :], True, True, [q0, kcT], [pq])
                for g in range(4):
                    pq = psc[g % 2]
                    self.AC(s1[:, g, :], pq[0:N, (g // 2) * 256:(g // 2) * 256 + 256], AF.Exp, [pq], [s1], scale=0.125)
                P.op(dv, lambda e: e.tensor_reduce(out=z4a[:], in_=s1[:], axis=AX.X, op=ALU.add), [s1], [z4a])
                P.op(dv, lambda e: e.reciprocal(out=z4a[:], in_=z4a[:]), [z4a], [z4a])
                self.TT(dv, s1[:], s1[:], z4a[:].unsqueeze(2).to_broadcast([N, 4, 256]), ALU.mult, [s1, z4a], [s1])
                P.op(dv, lambda e: e.tensor_reduce(out=sc_[:], in_=s1[:].rearrange("p g n -> p n g"), axis=AX.X, op=ALU.add), [s1], [sc_])
                self.TT(dv, sc_[:], sc_[:], tsl[:], ALU.add, [sc_, tsl], [sc_])
                P.op(dv, lambda e: e.max(out=mx[:], in_=sc_[:]), [sc_], [mx])
                P.op(dv, lambda e: e.match_replace(out=w1[:], in_to_replace=mx[:], in_values=sc_[:], imm_value=NEG), [mx, sc_], [w1])
                P.op(dv, lambda e: e.max(out=mx[:], in_=w1[:]), [w1], [mx])
                self.TS(dv, w1[:], sc_[:], mx[:, 6:7], -1.0, ALU.is_ge, ALU.add, [sc_, mx], [w1])
                self.TS(dv, bq[:], w1[:], 1e30, None, ALU.mult, None, [w1], [bq])
                for t4 in range(4):
                    pbt = self.pS[0]
                    self.MT(pbt[0:64, t4 * N:(t4 + 1) * N], bq[:, t4 * 64:(t4 + 1) * 64], [bq], [pbt])
                for g_ in range(4):
                    self.CP(ac, BT[0:64, s, :, k, g_ * N:(g_ + 1) * N], self.pS[0][0:64, 0:4 * N].rearrange("p (t n) -> p t n", t=4), [self.pS[0]], [BT])
        finish_branch(0)
        vpg = [A.sbuf(f"vpg{i}", [128, 4, 65], BF16) for i in range(2)]
        for v_ in vpg:
            self.MS(pl, v_[:], 1.0, [v_])
        kTp = [A.sbuf(f"kTp{i}", [128, 2, 128], BF16) for i in range(2)]
        it = 0
        for s in range(NS):
            for g in range(128):
                b = gather_page(s, g, idxB)
                ptx = self.pS[0]
                for n_ in range(2):
                    self.MT(ptx[:, n_ * 128:(n_ + 1) * 128], b[:, n_ * 128:(n_ + 1) * 128], [b], [ptx])
                kt_ = kTp[it % 2]
                vv_ = vpg[it % 2]
                it += 1
                self.CP(ac, kt_[:], ptx[:, 0:256].rearrange("p (r t) -> p r t", r=2), [ptx], [kt_])
                self.CP(ac, vv_[:, :, 0:64], b[:, 256:512].rearrange("p (h d) -> p h d", h=4), [b], [vv_])
                tile4, loc = g // 32, (2 * g) % 64
                attend(128, lambda k, kt_=kt_: (kt_[64 * (k % 2):64 * (k % 2) + 64, k // 2, :], [kt_]),
                       lambda k, s=s, tile4=tile4, loc=loc: (Gf[:, loc * 64:loc * 64 + 128], BT[0:64, s, tile4, :, :].rearrange("p k c -> p (k c)"), [self.G, BT]),
                       cm[:, s, :], [cm], lambda k, vv_=vv_: (vv_[:, k, :], [vv_]), (s == 0 and g == 0), False)
        attend(N, lambda k: (knT[64 * (k % 2):64 * (k % 2) + 64, k // 2, :], [knT]), None, m16[:, :], [m16],
               lambda k: (vn[:, 0, k, :], [vn]), False, True)
        finish_branch(1)
        wst = A.sbuf("wst", [128, 512], F32)
        wsb = A.sbuf("wsb", [128, 512], BF16)
        for s in range(NS):
            for t4 in range(4):
                self.DMA(wst[:], self.swin[l, s, t4 * 128:(t4 + 1) * 128, :], [], [wst])
                self.CP(dv, wsb[:], wst[:], [wst], [wsb])
                ptx = self.pS[0]
                for n_ in range(2):
                    self.MT(ptx[:, n_ * 128:(n_ + 1) * 128], wsb[:, n_ * 128:(n_ + 1) * 128], [wsb], [ptx])
                kt_ = kTp[it % 2]
                vv_ = vpg[it % 2]
                it += 1
                self.CP(ac, kt_[:], ptx[:, 0:256].rearrange("p (r t) -> p r t", r=2), [ptx], [kt_])
                self.CP(ac, vv_[:, :, 0:64], wsb[:, 256:512].rearrange("p (h d) -> p h d", h=4), [wsb], [vv_])
                msk = wm if t4 == 0 else cm
                attend(128, lambda k, kt_=kt_: (kt_[64 * (k % 2):64 * (k % 2) + 64, k // 2, :], [kt_]), None,
                       msk[:, s, :], [msk], lambda k, vv_=vv_: (vv_[:, k, :], [vv_]), (s == 0 and t4 == 0), False)
        attend(N, lambda k: (knT[64 * (k % 2):64 * (k % 2) + 64, 2 + k // 2, :], [knT]), None, m16[:, :], [m16],
               lambda k: (vn[:, 1, k, :], [vn]), False, True)
        finish_branch(2)
        yb = A.sbuf("ybs", [N, 1024], BF16)
        self.CP(dv, yb[:], yat[:], [yat], [yb])
        pyt = self.pS[0]
        for t8 in range(8):
            self.MT(pyt[:, t8 * N:(t8 + 1) * N], yb[:, t8 * 128:(t8 + 1) * 128], [yb], [pyt])
        self.TT(dv, yza[:, :, 0:N], pyt[:, 0:8 * N].rearrange("p (t c) -> p t c", t=8), saz[:], ALU.mult, [pyt, saz], [yza])


KB.sample_layer = _sample_layer
KB.nsa_sample = _nsa_sample
```

```python
import math
import os
import numpy as np
KSTOP = int(os.environ.get('KSTOP', '99'))
KSUB = int(os.environ.get('KSUB', '99'))
import concourse.bass as bass
import concourse.mybir as mybir
from concourse.bass_utils import run_bass_kernel_spmd
from contextlib import ExitStack

F32 = mybir.dt.float32
BF16 = mybir.dt.bfloat16
I32 = mybir.dt.int32
ALU = mybir.AluOpType
AF = mybir.ActivationFunctionType
AX = mybir.AxisListType

EPOCH = 8192
NDMASEM = 24

D = 2048
DIN = 13872
SEQ = 4096
PAST = 16384
NPOOL = 1280
EPS = 1e-6
NEG = -1e30
O_PU, O_PZ, O_Q, O_KV, O_AG, O_AZ, O_SU, O_SZ, O_MG = 0, 1024, 2048, 3072, 4608, 4656, 5680, 6704, 7728
A_SEGS = [O_PU, O_PZ, O_Q, O_AZ, O_SU, O_SZ]
NKVB = 1584
TC = 32


class Buf:
    __slots__ = ("t", "w", "r", "name")

    def __init__(self, t, name="", carry=None):
        self.t = t
        self.w = dict(carry) if carry else {}
        self.r = {}
        self.name = name

    def __getitem__(self, idx):
        return self.t[idx]


class Eng:
    def __init__(self, P, name, obj):
        self.P = P
        self.name = name
        self.obj = obj
        self.count = 0
        self.sems = []
        self.waited = {}
        self.dsems = []
        self.dnext = 0

    def sem_for(self, seq):
        k = (seq - 1) // EPOCH
        while len(self.sems) <= k:
            self.sems.append(self.P.new_sem(f"{self.name}_e{len(self.sems)}"))
        return self.sems[k], (seq - 1) % EPOCH + 1, (self.name, k)


def _dep_rank(d):
    return d[2]


class Prog:
    def __init__(self, nc):
        self.nc = nc
        self.es = ExitStack()
        self.nsem = 0
        self.nname = 0
        self.pe = Eng(self, "pe", nc.tensor)
        self.act = Eng(self, "act", nc.scalar)
        self.dve = Eng(self, "dve", nc.vector)
        self.pool = Eng(self, "pool", nc.gpsimd)
        self.sp = Eng(self, "sp", nc.sync)
        self.engs = [self.pe, self.act, self.dve, self.pool, self.sp]
        self.carry = {}
        self.ninstr = 0

    def new_sem(self, name):
        self.nsem += 1
        return self.es.enter_context(self.nc.semaphore(f"s_{name}_{self.nsem}"))

    def uname(self, name):
        self.nname += 1
        return f"{name}_{self.nname}"

    def sbuf(self, name, shape, dtype, es=None):
        t = (es or self.es).enter_context(self.nc.sbuf_tensor(self.uname(name), list(shape), dtype))
        return Buf(t, name, self.carry)

    def psum(self, name, shape, dtype):
        t = self.es.enter_context(self.nc.psum_tensor(self.uname(name), list(shape), dtype))
        return Buf(t, name)

    def dram_in(self, name, shape, dtype):
        return self.nc.dram_tensor(name, list(shape), dtype, kind="ExternalInput").ap()

    def dram_out(self, name, shape, dtype):
        return self.nc.dram_tensor(name, list(shape), dtype, kind="ExternalOutput").ap()

    def dram_scratch(self, name, shape, dtype):
        t = self.nc.dram_tensor(name, list(shape), dtype, kind="Internal").ap()
        return Buf(t, name)

    def retire(self, bufs):
        for b in bufs:
            for dd in (b.w, b.r):
                for k, d in dd.items():
                    kk = k[:3] if k[0] == "d" else k[:2]
                    o = self.carry.get(kk)
                    if o is None or _dep_rank(o) < _dep_rank(d):
                        self.carry[kk] = d

    def _wait(self, eng, dep, force=False):
        if dep[0] == "e":
            _, src, seq = dep
            if src is eng and eng is self.pe and not force:
                return
            sem, val, key = src.sem_for(seq)
        else:
            _, sem, val, key = dep
        if eng.waited.get(key, 0) >= val:
            return
        eng.waited[key] = val
        eng.obj.wait_ge(sem, val)
        self.ninstr += 1

    def _collect(self, eng, reads, writes):
        for b in reads:
            for d in b.w.values():
                self._wait(eng, d)
        for b in writes:
            for d in b.w.values():
                self._wait(eng, d)
            for d in b.r.values():
                self._wait(eng, d)

    def _commit(self, me, mekey, reads, writes):
        for b in writes:
            b.w = {mekey: me}
            b.r = {}
        for b in reads:
            if b not in writes:
                b.r[mekey] = me

    def op(self, eng, fn, reads=(), writes=()):
        self._collect(eng, reads, writes)
        ins = fn(eng.obj)
        eng.count += 1
        sem, val, key = eng.sem_for(eng.count)
        ins.then_inc(sem, 1)
        self.ninstr += 1
        me = ("e", eng, eng.count)
        self._commit(me, ("e", eng.name), reads, writes)
        return me

    def _dma_common(self, q, reads, writes, issue):
        self._collect(q, reads, writes)
        if len(q.dsems) < NDMASEM:
            q.dsems.append([self.new_sem(f"{q.name}_d{len(q.dsems)}"), 0])
        i = q.dnext % NDMASEM
        q.dnext += 1
        ent = q.dsems[i]
        key = ("d", q.name, i)
        if ent[1] > 0:
            self._wait(q, ("d", ent[0], 16 * ent[1], key))
        ent[1] += 1
        ins = issue(q.obj)
        ins.then_inc(ent[0], 16)
        self.ninstr += 1
        me = ("d", ent[0], 16 * ent[1], key)
        self._commit(me, key, reads, writes)
        return me

    def dma(self, q, out, in_, reads=(), writes=()):
        return self._dma_common(q, reads, writes, lambda e: e.dma_start(out=out, in_=in_))

    def gather(self, out, in_, idx_ap, reads=(), writes=()):
        return self._dma_common(
            self.pool, reads, writes,
            lambda e: e.indirect_dma_start(out=out, out_offset=None, in_=in_,
                                           in_offset=bass.IndirectOffsetOnAxis(ap=idx_ap, axis=0)))

    def finish(self):
        for e in self.engs:
            if e is not self.sp and e.count > 0:
                self._wait(self.sp, ("e", e, e.count))
        for q in self.engs:
            for i, ent in enumerate(q.dsems):
                if ent[1] > 0:
                    self._wait(self.sp, ("d", ent[0], 16 * ent[1], ("d", q.name, i)))

    def barrier(self):
        for tgt in self.engs:
            for e in self.engs:
                if e is not tgt and e.count > 0:
                    self._wait(tgt, ("e", e, e.count))
            for q in self.engs:
                for i, ent in enumerate(q.dsems):
                    if ent[1] > 0:
                        self._wait(tgt, ("d", ent[0], 16 * ent[1], ("d", q.name, i)))

    def close(self):
        self.es.close()


class Arena:
    def __init__(self, P):
        self.P = P
        self.es = ExitStack()
        self.bufs = []

    def sbuf(self, name, shape, dtype):
        b = self.P.sbuf(name, shape, dtype, es=self.es)
        self.bufs.append(b)
        return b

    def close(self):
        self.P.retire(self.bufs)
        self.es.close()

    def __enter__(self):
        return self

    def __exit__(self, *a):
        self.close()


class KB:
    def __init__(self, S=SEQ, NS=4, C=256, debug=False, parts=("pool", "ssm", "nsa")):
        self.S, self.NS, self.C, self.debug, self.parts = S, NS, C, debug, set(parts)
        self.NCH = S // C
        self.nc = bass.Bass("TRN2", target_bir_lowering=False)
        self.P = Prog(self.nc)
        self.dbg_outs = []

    def TT(self, eng, out, a, b, op, R, W):
        return self.P.op(eng, lambda e: e.tensor_tensor(out=out, in0=a, in1=b, op=op), R, W)

    def TS(self, eng, out, a, s1, s2, op0, op1, R, W):
        if s2 is None:
            return self.P.op(eng, lambda e: e.tensor_scalar(out=out, in0=a, scalar1=s1, scalar2=None, op0=op0), R, W)
        return self.P.op(eng, lambda e: e.tensor_scalar(out=out, in0=a, scalar1=s1, scalar2=s2, op0=op0, op1=op1), R, W)

    def STT(self, eng, out, a, s, b, op0, op1, R, W):
        return self.P.op(eng, lambda e: e.scalar_tensor_tensor(out=out, in0=a, scalar=s, in1=b, op0=op0, op1=op1), R, W)

    def AC(self, out, in_, func, R, W, **kw):
        return self.P.op(self.P.act, lambda e: e.activation(out=out, in_=in_, func=func, **kw), R, W)

    def CP(self, eng, out, in_, R, W):
        if eng is self.P.act:
            return self.P.op(eng, lambda e: e.copy(out=out, in_=in_), R, W)
        return self.P.op(eng, lambda e: e.tensor_copy(out=out, in_=in_), R, W)

    def MS(self, eng, ap, val, W):
        return self.P.op(eng, lambda e: e.memset(ap, val), (), W)

    def _rowgroup(self, ap, out):
        rg = (ap.base_partition(), ap.shape[0], out.base_partition(), out.shape[0])
        last = getattr(self, "_last_rg", (0, 128, 0, 128))
        if rg[:2] != last[:2] and rg[1] < 128 and last[1] < 128:
            pe = self.P.pe
            if pe.count > 0:
                self.P._wait(pe, ("e", pe, pe.count), force=True)
        self._last_rg = rg

    def MM(self, out, lhsT, rhs, start, stop, R, W):
        self._rowgroup(lhsT, out)
        return self.P.op(self.P.pe, lambda e: e.matmul(out, lhsT=lhsT, rhs=rhs, start=start, stop=stop), R, W)

    def TR(self, out, in_, ident, R, W):
        self._rowgroup(in_, out)
        return self.P.op(self.P.pe, lambda e: e.transpose(out=out, in_=in_, identity=ident), R, W)

    def MT(self, out, in_, R, W):
        kp = in_.shape[0]
        idn = self.identf if in_.dtype == F32 else self.ident
        return self.MM(out, in_, idn[0:kp, 0:kp], True, True, R + [idn], W)

    def DMA(self, out, in_, R=(), W=(), q=None):
        return self.P.dma(q or self.P.sp, out, in_, R, W)

    def dump(self, name, ap, shape, R):
        if not self.debug:
            return
        o = self.P.dram_out("dbg_" + name, shape, F32)
        self.dbg_outs.append("dbg_" + name)
        if ap.dtype != F32:
            with Arena(self.P) as A:
                t = A.sbuf("dbgt", list(ap.shape), F32)
                self.CP(self.P.dve, t[:], ap, R, [t])
                self.DMA(o, t[:], [t], [])
        else:
            self.DMA(o, ap, R, [])

    def declare(self):
        P, S, NS = self.P, self.S, self.NS
        din = P.dram_in
        self.xp = din("xp", [S, D], F32)
        if NS:
            self.xs = din("xs", [4 * NS, D], F32)
            self.cache = [din(f"cache{i}", [NPOOL * 128 * 2, 512], F32) for i in range(2)]
            self.ptab = din("ptab", [NS, 128], I32)
            self.swin = din("swin", [2, NS, 512, 512], F32)
            self.spool = din("spool", [2, NS, 15, 1024], F32)
            self.sssm = din("sssm", [2, NS, 2, 32, 128], F32)
            self.t_cm = din("t_cm", [128, NS, 4 * NS], F32)
            self.t_wm = din("t_wm", [128, NS, 4 * NS], F32)
            self.t_m16 = din("t_m16", [4 * NS, 4 * NS], F32)
            self.t_ssel = din("t_ssel", [4 * NS, 256], F32)
        self.g_pre = din("g_pre", [2, D], F32)
        self.g_post = din("g_post", [2, D], F32)
        self.w_in = din("w_in", [2, D, DIN], F32)
        self.w_pool = din("w_pool", [2, 4, 256, 256], F32)
        self.pool_scale = din("pool_scale", [2, 1024], F32)
        self.pe_cmp = din("pe_cmp", [2, 2, 64, 64], F32)
        self.w_phi = din("w_phi", [2, 2, 64, 64, 64], F32)
        self.lam_re = din("lam_re", [2, 32, 128], F32)
        self.lam_im = din("lam_im", [2, 32, 128], F32)
        self.log_step = din("log_step", [2, 32, 2], F32)
        self.b_re = din("b_re", [2, 64, 64, 16], F32)
        self.b_im = din("b_im", [2, 64, 64, 16], F32)
        self.c_re = din("c_re", [2, 32, 2, 16, 64], F32)
        self.c_im = din("c_im", [2, 32, 2, 16, 64], F32)
        self.d_skip = din("d_skip", [2, 1024], F32)
        self.w_glu = din("w_glu", [2, 1024, 1024], F32)
        self.w_br = [din(n, [2, 1024, D], F32) for n in ("w_br_pool", "w_br_nsa", "w_br_ssm")]
        self.w_out = din("w_out", [2, D, D], F32)
        self.t_cbq = din("t_cbq", [SEQ // 128, 128, 64], F32)
        self.t_sel = din("t_sel", [SEQ // 128, 128, 64], F32)
        self.t_cbT = din("t_cbT", [64, SEQ], F32)
        self.t_rc = din("t_rc", [128, 60], F32)
        dout = P.dram_out
        self.yp = dout("yp", [S, D], F32)
        self.kvp = dout("kvp", [2, S, 1024], F32)
        self.winp = dout("winp", [2, 512, 512], F32)
        self.poolp = dout("poolp", [2, 15, 1024], F32)
        self.ssmp = dout("ssmp", [2, 2, 32, 128], F32)
        if NS:
            self.ys = dout("ys", [4 * NS, D], F32)
            self.kvs = dout("kvs", [2, 4 * NS, 1024], F32)
            self.wins = dout("wins", [2, NS, 512, 512], F32)
            self.pools = dout("pools", [2, NS, 15, 1024], F32)
            self.ssms = dout("ssms", [2, NS, 2, 32, 128], F32)
        sc = P.dram_scratch
        self.wA = sc("wA", [2, 96, 128, 16, 128], BF16)
        self.wB = sc("wB", [2, 128, 16, NKVB], BF16)
        self.wG = sc("wG", [2, 8, 128, 8, 128], BF16)
        self.wR = sc("wR", [2, 3, 16, 128, 8, 128], BF16)
        self.wO = sc("wO", [2, 128, 16, D], BF16)
        self.x1 = sc("x1", [S, D], F32)
        if NS:
            self.xs1 = sc("xs1", [4 * NS, D], F32)

    def consts(self):
        P = self.P
        pl, dv = P.pool, P.dve
        self.identf = P.sbuf("identf", [128, 128], F32)
        self.ident = P.sbuf("ident", [128, 128], BF16)
        self.shu = P.sbuf("shu", [128, 128], BF16)
        self.tri = P.sbuf("tri", [128, 128], BF16)
        self.tris = P.sbuf("tris", [128, 128], BF16)
        self.G = P.sbuf("G", [64, 64, 64], BF16)
        self.rc = P.sbuf("rc", [128, 4, 15], F32)
        CA_ = Arena(P)
        tmp = CA_.sbuf("ctmp", [128, 128], F32)
        P.op(pl, lambda e: e.iota(tmp[:], pattern=[[1, 128]], base=0, channel_multiplier=-1,
                                  allow_small_or_imprecise_dtypes=True), (), [tmp])
        self.TS(dv, self.identf[:], tmp[:], 0.0, None, ALU.is_equal, None, [tmp], [self.identf])
        self.CP(dv, self.ident[:], self.identf[:], [self.identf], [self.ident])
        self.TS(dv, self.tri[:], tmp[:], 0.0, None, ALU.is_ge, None, [tmp], [self.tri])
        self.TS(dv, self.tris[:], tmp[:], 0.0, None, ALU.is_lt, None, [tmp], [self.tris])
        self.TS(dv, self.shu[:], tmp[:], 64.0, None, ALU.is_equal, None, [tmp], [self.shu])
        gt = CA_.sbuf("gtmp", [64, 64, 64], F32)
        P.op(pl, lambda e: e.iota(gt[:], pattern=[[1, 64], [0, 64]], base=0, channel_multiplier=-1,
                                  allow_small_or_imprecise_dtypes=True), (), [gt])
        self.TS(dv, self.G[:], gt[:], 0.0, None, ALU.is_equal, None, [gt], [self.G])
        self.DMA(self.rc[:], self.t_rc.rearrange("p (w c) -> p w c", w=4), [], [self.rc])
        CA_.close()
        self.pA = [P.psum(f"pA{i}", [128, 512], F32) for i in range(2)]
        self.pK = [P.psum(f"pK{i}", [128, 512], F32) for i in range(3)]
        self.pS = [P.psum(f"pS{i}", [128, 512], F32) for i in range(2)]
        self.pT = P.psum("pT", [128, 1024], BF16)
        self.pa_i = 0

    def nextpA(self):
        self.pa_i += 1
        return self.pA[self.pa_i % 2]

    def prepass(self):
        P = self.P
        with Arena(P) as A:
            stg = [A.sbuf(f"stg{i}", [128, 16, 512], BF16) for i in range(2)]
            si = [0]

            def stage():
                si[0] += 1
                return stg[si[0] % 2]

            for l in range(2):
                tid = 0
                segs = [(o, 1024) for o in A_SEGS] + [(O_MG, 6144)]
                for (o, n) in segs:
                    for c0 in range(0, n, 512):
                        st = stage()
                        self.DMA(st[:], self.w_in[l][:, o + c0:o + c0 + 512].rearrange("(k p) c -> p k c", p=128),
                                 [], [st], q=P.pool)
                        for t in range(4):
                            self.DMA(self.wA[l, tid], st[:, :, t * 128:(t + 1) * 128], [st], [])
                            tid += 1
                assert tid == 96
                for c0 in range(0, NKVB, 512):
                    n = min(512, NKVB - c0)
                    st = stage()
                    self.DMA(st[:, :, :n], self.w_in[l][:, O_KV + c0:O_KV + c0 + n].rearrange("(k p) c -> p k c", p=128),
                             [], [st], q=P.pool)
                    self.DMA(self.wB[l][:, :, c0:c0 + n], st[:, :, :n], [st], [])
                for c0 in range(0, D, 512):
                    st = stage()
                    self.DMA(st[:], self.w_out[l][:, c0:c0 + 512].rearrange("(k p) c -> p k c", p=128), [], [st], q=P.pool)
                    self.DMA(self.wO[l][:, :, c0:c0 + 512], st[:], [st], [])
                for c0 in range(0, 1024, 512):
                    st = stage()
                    self.DMA(st[:, 0:8, :], self.w_glu[l][:, c0:c0 + 512].rearrange("(k p) c -> p k c", p=128), [], [st], q=P.pool)
                    for t in range(4):
                        self.DMA(self.wG[l, c0 // 128 + t], st[:, 0:8, t * 128:(t + 1) * 128], [st], [])
                for i in range(3):
                    for c0 in range(0, D, 512):
                        st = stage()
                        self.DMA(st[:, 0:8, :], self.w_br[i][l][:, c0:c0 + 512].rearrange("(k p) c -> p k c", p=128), [], [st], q=P.pool)
                        for t in range(4):
                            self.DMA(self.wR[l, i, c0 // 128 + t], st[:, 0:8, t * 128:(t + 1) * 128], [st], [])
        P.barrier()

    def load_T(self, A, name, src_ap, rows, dst=None):
        P = self.P
        t = A.sbuf(name + "_ld", [rows, 128], F32)
        self.DMA(t[:], src_ap, [], [t])
        ps = self.nextpA()
        self.TR(ps[:, 0:rows], t[:], self.identf[0:rows, 0:rows], [t, self.identf], [ps])
        o = dst if dst is not None else A.sbuf(name, [128, rows], F32)
        self.CP(P.act, o[:, 0:rows], ps[:, 0:rows], [ps], [o])
        return o

    def layer_prep(self, l, LA):
        P = self.P
        dv, pl, ac = P.dve, P.pool, P.act
        L = type("L", (), {})()
        self.L = L
        L.gpre = LA.sbuf("gpre", [128, 16], F32)
        L.pscale = LA.sbuf("pscale", [128, 8], F32)
        L.dskip = LA.sbuf("dskip", [128, 8], F32)
        L.wpool = LA.sbuf("wpool", [128, 4, 2, 256], BF16)
        L.wphi2 = LA.sbuf("wphi2", [128, 2, 64, 64], BF16)
        L.pe2 = LA.sbuf("pe2", [128, 2, 64], BF16)
        L.cosT = LA.sbuf("cosT", [128, 32, TC], F32)
        L.sinT = LA.sbuf("sinT", [128, 32, TC], F32)
        L.rhoz = LA.sbuf("rhoz", [128, 32, TC], F32)
        L.abr = LA.sbuf("abr", [128, 32], F32)
        L.abi = LA.sbuf("abi", [128, 32], F32)
        L.BBr = LA.sbuf("BBr", [128, 8, 2, 128], BF16)
        L.BBi = LA.sbuf("BBi", [128, 8, 2, 128], BF16)
        L.CTr = LA.sbuf("CTr", [128, 32, 128], BF16)
        L.CTi = LA.sbuf("CTi", [128, 32, 128], BF16)
        with Arena(P) as A:
            self.load_T(A, "gpre", self.g_pre[l].rearrange("(k p) -> k p", p=128), 16, dst=L.gpre)
            self.load_T(A, "pscale", self.pool_scale[l].rearrange("(k p) -> k p", p=128), 8, dst=L.pscale)
            self.load_T(A, "dskip", self.d_skip[l].rearrange("(k p) -> k p", p=128), 8, dst=L.dskip)
            self.DMA(L.wpool[:], self.w_pool[l].rearrange("g (k p) d -> p g k d", p=128), [], [L.wpool], q=pl)
            for h in range(2):
                self.DMA(L.wphi2[64 * h:64 * h + 64], self.w_phi[l].rearrange("a l d e -> d a l e"), [], [L.wphi2], q=pl)
            pel = A.sbuf("pel", [128, 64], F32)
            self.DMA(pel[:], self.pe_cmp[l].rearrange("a l d -> (a l) d"), [], [pel])
            pel2 = A.sbuf("pel2", [128, 2, 64], F32)
            for h in range(2):
                self.CP(dv, pel2[:, h, :], pel[:], [pel], [pel2])
            ps = self.nextpA()
            self.TR(ps[:, 0:128], pel2[:].rearrange("p a d -> p (a d)"), self.identf[:], [pel2, self.identf], [ps])
            self.CP(ac, L.pe2[:].rearrange("p a l -> p (a l)"), ps[:, 0:128], [ps], [L.pe2])

            lr = self.load_T(A, "lr", self.lam_re[l], 32)
            li = self.load_T(A, "li", self.lam_im[l], 32)
            lsl = A.sbuf("lsl", [32, 2], F32)
            self.DMA(lsl[:], self.log_step[l], [], [lsl])
            lsx = A.sbuf("lsx", [32, 2, 64], F32)
            self.CP(dv, lsx[:], lsl[:].unsqueeze(2).to_broadcast([32, 2, 64]), [lsl], [lsx])
            ps = self.nextpA()
            self.TR(ps[:, 0:32], lsx[:].rearrange("p a n -> p (a n)"), self.identf[0:32, 0:32], [lsx, self.identf], [ps])
            dt = A.sbuf("dt", [128, 32], F32)
            self.AC(dt[:], ps[:, 0:32], AF.Exp, [ps], [dt])

            def T(name):
                return A.sbuf(name, [128, 32], F32)

            def mul(o, a, b):
                self.TT(dv, o[:], a[:], b[:], ALU.mult, [a, b], [o])

            def sub(o, a, b):
                self.TT(dv, o[:], a[:], b[:], ALU.subtract, [a, b], [o])

            def add(o, a, b):
                self.TT(dv, o[:], a[:], b[:], ALU.add, [a, b], [o])

            x, mag, th, c, s_, t1, t2, t3 = T("x"), T("mag"), T("th"), T("c"), T("s"), T("t1"), T("t2"), T("t3")
            mul(x, lr, dt)
            self.AC(mag[:], x[:], AF.Exp, [x], [mag])
            mul(th, li, dt)
            hp = A.sbuf("hp", [128, 1], F32)
            self.MS(dv, hp[:], math.pi / 2, [hp])
            self.AC(s_[:], th[:], AF.Sin, [th], [s_], scale=1.0 / 16)
            self.AC(c[:], th[:], AF.Sin, [th, hp], [c], scale=1.0 / 16, bias=hp[:, 0:1])

            def csq(cc, ss):
                mul(t1, cc, cc)
                mul(t2, ss, ss)
                mul(t3, cc, ss)
                sub(cc, t1, t2)
                self.TS(dv, ss[:], t3[:], 2.0, None, ALU.mult, None, [t3], [ss])

            for _ in range(4):
                csq(c, s_)
            mul(L.abr, mag, c)
            mul(L.abi, mag, s_)
            den, rden, a1, cor, coi = T("den"), T("rden"), T("a1"), T("cor"), T("coi")
            mul(t1, lr, lr)
            mul(t2, li, li)
            add(den, t1, t2)
            self.P.op(dv, lambda e: e.reciprocal(out=rden[:], in_=den[:]), [den], [rden])
            self.TS(dv, a1[:], L.abr[:], -1.0, None, ALU.add, None, [L.abr], [a1])
            mul(t1, a1, lr)
            mul(t2, L.abi, li)
            add(t3, t1, t2)
            mul(cor, t3, rden)
            mul(t1, L.abi, lr)
            mul(t2, a1, li)
            sub(t3, t1, t2)
            mul(coi, t3, rden)
            self.MS(dv, L.cosT[:, :, 0:1], 1.0, [L.cosT])
            self.MS(dv, L.sinT[:, :, 0:1], 0.0, [L.sinT])
            self.CP(dv, L.cosT[:, :, 1:2], c[:].unsqueeze(2), [c], [L.cosT])
            self.CP(dv, L.sinT[:, :, 1:2], s_[:].unsqueeze(2), [s_], [L.sinT])
            Ck, Sk = T("Ck"), T("Sk")
            self.CP(dv, Ck[:], c[:], [c], [Ck])
            self.CP(dv, Sk[:], s_[:], [s_], [Sk])
            tb1 = A.sbuf("tb1", [128, 32, TC // 2], F32)
            tb2 = A.sbuf("tb2", [128, 32, TC // 2], F32)
            csq(Ck, Sk)
            w = 2
            while w < TC:
                Cb = Ck[:].unsqueeze(2).to_broadcast([128, 32, w])
                Sb = Sk[:].unsqueeze(2).to_broadcast([128, 32, w])
                self.TT(dv, tb1[:, :, 0:w], L.cosT[:, :, 0:w], Cb, ALU.mult, [L.cosT, Ck], [tb1])
                self.TT(dv, tb2[:, :, 0:w], L.sinT[:, :, 0:w], Sb, ALU.mult, [L.sinT, Sk], [tb2])
                self.TT(dv, L.cosT[:, :, w:2 * w], tb1[:, :, 0:w], tb2[:, :, 0:w], ALU.subtract, [tb1, tb2], [L.cosT])
                self.TT(dv, tb1[:, :, 0:w], L.sinT[:, :, 0:w], Cb, ALU.mult, [L.sinT, Ck], [tb1])
                self.TT(dv, tb2[:, :, 0:w], L.cosT[:, :, 0:w], Sb, ALU.mult, [L.cosT, Sk], [tb2])
                self.TT(dv, L.sinT[:, :, w:2 * w], tb1[:, :, 0:w], tb2[:, :, 0:w], ALU.add, [tb1, tb2], [L.sinT])
                csq(Ck, Sk)
                w *= 2
            self.MS(dv, L.rhoz[:, :, 0:1], 0.0, [L.rhoz])
            self.CP(dv, L.rhoz[:, :, 1:TC], mag[:].unsqueeze(2).to_broadcast([128, 32, TC - 1]), [mag], [L.rhoz])

            bre = A.sbuf("bre", [128, 32, 16], F32)
            bim = A.sbuf("bim", [128, 32, 16], F32)
            self.DMA(bre[:], self.b_re[l].rearrange("g n c -> (g n) c").rearrange("(i p) c -> p i c", p=128), [], [bre])
            self.DMA(bim[:], self.b_im[l].rearrange("g n c -> (g n) c").rearrange("(i p) c -> p i c", p=128), [], [bim])
            u1 = A.sbuf("u1", [128, 32, 16], F32)
            u2 = A.sbuf("u2", [128, 32, 16], F32)
            bbr = A.sbuf("bbr", [128, 32, 16], F32)
            bbi = A.sbuf("bbi", [128, 32, 16], F32)
            corb = cor[:].unsqueeze(2).to_broadcast([128, 32, 16])
            coib = coi[:].unsqueeze(2).to_broadcast([128, 32, 16])
            self.TT(dv, u1[:], bre[:], corb, ALU.mult, [bre, cor], [u1])
            self.TT(dv, u2[:], bim[:], coib, ALU.mult, [bim, coi], [u2])
            self.TT(dv, bbr[:], u1[:], u2[:], ALU.subtract, [u1, u2], [bbr])
            self.TT(dv, u1[:], bim[:], corb, ALU.mult, [bim, cor], [u1])
            self.TT(dv, u2[:], bre[:], coib, ALU.mult, [bre, coi], [u2])
            self.TT(dv, bbi[:], u1[:], u2[:], ALU.add, [u1, u2], [bbi])
            spad = A.sbuf("spad", [128, 32, 128], BF16)
            for (bb, dst) in ((bbr, L.BBr), (bbi, L.BBi)):
                self.MS(pl, spad[:], 0.0, [spad])
                sv = spad[:].rearrange("p (k j) m -> p k j m", j=4)
                bv = bb[:].rearrange("p (k j) c -> p k j c", j=4)
                for il in range(4):
                    for g2 in range(2):
                        o = 32 * il + 16 * g2
                        self.CP(dv, sv[64 * g2:64 * g2 + 64, :, il, o:o + 16], bv[64 * g2:64 * g2 + 64, :, il, :], [bb], [spad])
                for i0 in range(0, 32, 8):
                    for j in range(8):
                        self.TR(self.pT[:, j * 128:(j + 1) * 128], spad[:, i0 + j, :], self.ident[:], [spad, self.ident], [self.pT])
                    for j in range(8):
                        i = i0 + j
                        hb = (i % 4) // 2
                        self.CP(ac, dst[64 * hb:64 * hb + 64, i // 4, i % 2, :], self.pT[64 * hb:64 * hb + 64, j * 128:(j + 1) * 128], [self.pT], [dst])
            for (csrc, dst, sgn) in ((self.c_re, L.CTr, 1.0), (self.c_im, L.CTi, -1.0)):
                cn = A.sbuf("cn", [32, 2, 16, 64], F32)
                self.DMA(cn[:], csrc[l], [], [cn])
                cn2 = A.sbuf("cn2", [32, 16, 2, 64], F32)
                self.CP(dv, cn2[:].rearrange("p c a n -> p a c n"), cn[:], [cn], [cn2])
                ps = self.nextpA()
                for cc in range(16):
                    self.TR(ps[:, cc * 32:(cc + 1) * 32], cn2[:, cc, :, :].rearrange("p a n -> p (a n)"), self.identf[0:32, 0:32], [cn2, self.identf], [ps])
                cst = A.sbuf("cst", [128, 32, 16], F32)
                self.TS(dv, cst[:].rearrange("p i c -> p c i"), ps[:, 0:512].rearrange("p (c i) -> p c i", c=16), sgn, None, ALU.mult, None, [ps], [cst])
                self.MS(pl, dst[:], 0.0, [dst])
                dv4 = dst[:].rearrange("p (k j) m -> p k j m", j=4)
                cv = cst[:].rearrange("p (k j) c -> p k j c", j=4)
                for il in range(4):
                    for g2 in range(2):
                        o = 32 * il + 16 * g2
                        self.CP(dv, dv4[64 * g2:64 * g2 + 64, :, il, o:o + 16], cv[64 * g2:64 * g2 + 64, :, il, :], [cst], [dst])

    def next_wt(self):
        self.wt_i += 1
        return self.wts[self.wt_i % len(self.wts)]

    def projA(self, src, N, rhs_of_k, nk, consumer, R):
        for idx, wsrc in enumerate(src):
            wt = self.next_wt()
            self.DMA(wt[:, 0:nk, :], wsrc, [], [wt])
            ps = self.nextpA()
            for k in range(nk):
                self.MM(ps[:, 0:N], wt[:, k, :], rhs_of_k(k), k == 0, k == nk - 1, [wt] + R, [ps])
            consumer(idx, ps)

    def norm_phase(self, src_rows, tiles, hT):
        P, L = self.P, self.L
        with Arena(P) as A:
            xt = A.sbuf("xt", [128, D], F32)
            junk = A.sbuf("junk", [128, D], BF16)
            ss = A.sbuf("ss", [128, 1], F32)
            hb = A.sbuf("hb", [128, D], BF16)
            for (r0, nr) in tiles:
                self.DMA(xt[0:nr], src_rows(r0, nr), [], [xt])
                self.AC(junk[0:nr], xt[0:nr], AF.Square, [xt], [junk, ss], accum_out=ss[0:nr, 0:1])
                self.TS(P.dve, ss[0:nr], ss[0:nr], 1.0 / D, EPS, ALU.mult, ALU.add, [ss], [ss])
                self.AC(ss[0:nr], ss[0:nr], AF.Sqrt, [ss], [ss])
                P.op(P.dve, lambda e: e.reciprocal(out=ss[0:nr], in_=ss[0:nr]), [ss], [ss])
                self.TS(P.dve, hb[0:nr], xt[0:nr], ss[0:nr, 0:1], None, ALU.mult, None, [xt, ss], [hb])
                for k0 in range(0, 16, 8):
                    for kk in range(8):
                        k = k0 + kk
                        self.TR(self.pT[:, kk * 128:kk * 128 + nr], hb[0:nr, k * 128:(k + 1) * 128],
                                self.ident[0:nr, 0:nr], [hb, self.ident], [self.pT])
                    for kk in range(8):
                        k = k0 + kk
                        self.AC(hT[:, k, r0:r0 + nr], self.pT[:, kk * 128:kk * 128 + nr], AF.Copy,
                                [self.pT, L.gpre], [hT], scale=L.gpre[:, k:k + 1])

    def merge_out_phase(self, l, N, tiles, hT, yz, src_rows, dst_rows, dst_bufs):
        P, L = self.P, self.L
        dv, ac = P.dve, P.act
        with Arena(P) as A:
            mT = A.sbuf("mT", [128, 16, N], BF16)
            gs = [A.sbuf(f"gs{i}", [128, N], F32) for i in range(3)]
            tm = [A.sbuf(f"tm{i}", [128, N], F32) for i in range(3)]
            for dt in range(16):
                brp = []
                for i in range(3):
                    wt = self.next_wt()
                    self.DMA(wt[:, 0:8, :], self.wR[l, i, dt], [], [wt])
                    ps = self.pK[i]
                    for k in range(8):
                        self.MM(ps[:, 0:N], wt[:, k, :], yz[i][:, k, 0:N], k == 0, k == 7, [wt, yz[i]], [ps])
                    brp.append(ps)
                for i in range(3):
                    wt = self.next_wt()
                    self.DMA(wt[:], self.wA[l, 48 + 16 * i + dt], [], [wt])
                    ps = self.nextpA()
                    for k in range(16):
                        self.MM(ps[:, 0:N], wt[:, k, :], hT[:, k, 0:N], k == 0, k == 15, [wt, hT], [ps])
                    self.AC(gs[i][:], ps[:, 0:N], AF.Sigmoid, [ps], [gs[i]])
                for i in range(3):
                    self.TT(dv, tm[i][:], gs[i][:], brp[i][:, 0:N], ALU.mult, [gs[i], brp[i]], [tm[i]])
                self.TT(dv, tm[0][:], tm[0][:], tm[1][:], ALU.add, [tm[0], tm[1]], [tm[0]])
                self.TT(dv, mT[:, dt, :], tm[0][:], tm[2][:], ALU.add, [tm[0], tm[2]], [mT])
            if self.debug:
                self.dump(f"mT{l}", mT[:], [128, 16, N], [mT])
            gpb = A.sbuf("gpb", [128, D], F32)
            self.DMA(gpb[:], self.g_post[l:l + 1, :].to_broadcast([128, D]), [], [gpb])
            of = [A.sbuf(f"of{i}", [128, D], F32) for i in range(len(tiles))]
            for ci, c0 in enumerate(range(0, D, 128)):
                w = self.next_wt()
                self.DMA(w[:], self.wO[l][:, :, c0:c0 + 128], [], [w])
                for ti, (r0, nr) in enumerate(tiles):
                    ps = self.pK[ti % 3]
                    for k in range(16):
                        self.MM(ps[0:nr, 0:128], mT[:, k, r0:r0 + nr], w[:, k, :], k == 0, k == 15, [mT, w], [ps])
                    self.CP(ac, of[ti][0:nr, c0:c0 + 128], ps[0:nr, 0:128], [ps], [of[ti]])
            junk = A.sbuf("junk2", [128, D], BF16)
            ss = A.sbuf("ss2", [128, 1], F32)
            xt = A.sbuf("xt2", [128, D], F32)
            for ti, (r0, nr) in enumerate(tiles):
                o = of[ti]
                self.DMA(xt[0:nr], src_rows(r0, nr), [], [xt])
                self.AC(junk[0:nr], o[0:nr], AF.Square, [o], [junk, ss], accum_out=ss[0:nr, 0:1])
                self.TS(dv, ss[0:nr], ss[0:nr], 1.0 / D, EPS, ALU.mult, ALU.add, [ss], [ss])
                self.AC(ss[0:nr], ss[0:nr], AF.Sqrt, [ss], [ss])
                P.op(dv, lambda e: e.reciprocal(out=ss[0:nr], in_=ss[0:nr]), [ss], [ss])
                self.STT(dv, o[0:nr], o[0:nr], ss[0:nr, 0:1], gpb[0:nr], ALU.mult, ALU.mult, [o, ss, gpb], [o])
                self.TT(dv, o[0:nr], o[0:nr], xt[0:nr], ALU.add, [o, xt], [o])
                self.DMA(dst_rows(r0, nr), o[0:nr], [o], [dst_bufs[ti]] if dst_bufs else [])

    def pool_phase(self, l, N, hT, yzp, halo, first, nseg=1):
        P, L = self.P, self.L
        dv, ac, pl = P.dve, P.act, P.pool
        n1 = N // nseg
        W = 15 + n1
        with Arena(P) as A:
            pu = A.sbuf("pu", [128, 8, nseg, W], F32)
            spz = A.sbuf("spz", [128, 8, N], BF16)
            sA = A.sbuf("sA", [128, 8, nseg, W], F32)
            sB = A.sbuf("sB", [128, 8, nseg, W], F32)
            df = A.sbuf("df", [128, 8, N], BF16)
            self.CP(dv, pu[:, :, :, 0:15], halo[:], [halo], [pu])
            self.projA([self.wA[l, t] for t in range(0, 8)], N, lambda k: hT[:, k, 0:N], 16,
                       lambda i, ps: self.CP(ac, pu[:, i, :, 15:W], ps[:, 0:N].rearrange("p (s c) -> p s c", s=nseg), [ps], [pu]), [hT])
            self.projA([self.wA[l, t] for t in range(8, 16)], N, lambda k: hT[:, k, 0:N], 16,
                       lambda i, ps: self.AC(spz[:, i, :], ps[:, 0:N], AF.Silu, [ps], [spz]), [hT])
            if self.debug and nseg == 1:
                self.dump(f"pu{l}", pu[:, :, 0, 15:W], [128, 8, n1], [pu])
            self.TT(dv, sA[:, :, :, 1:W], pu[:, :, :, 1:W], pu[:, :, :, 0:W - 1], ALU.add, [pu], [sA])
            self.TT(dv, sB[:, 2:8, :, 3:W], sA[:, 2:8, :, 3:W], sA[:, 2:8, :, 1:W - 2], ALU.add, [sA], [sB])
            self.TT(dv, sA[:, 4:8, :, 7:W], sB[:, 4:8, :, 7:W], sB[:, 4:8, :, 3:W - 4], ALU.add, [sB], [sA])
            self.TT(dv, sB[:, 6:8, :, 15:W], sA[:, 6:8, :, 15:W], sA[:, 6:8, :, 7:W - 8], ALU.add, [sA], [sB])
            dfv = df[:].rearrange("p t (s c) -> p t s c", s=nseg)
            for gi, (src, w) in enumerate(((sA, 2), (sB, 4), (sA, 8), (sB, 16))):
                t0 = 2 * gi
                self.STT(dv, dfv[:, t0:t0 + 2], src[:, t0:t0 + 2, :, 15:W], 1.0 / w, pu[:, t0:t0 + 2, :, 15:W],
                         ALU.mult, ALU.subtract, [src, pu], [df])
                if first:
                    if not hasattr(A, "_pt"):
                        A._pt = A.sbuf("ptmp", [128, 2, 15], F32)
                    tmp = A._pt
                    self.TT(dv, tmp[:], src[:, t0:t0 + 2, 0, 15:30], self.rc[:, gi:gi + 1, :].to_broadcast([128, 2, 15]),
                            ALU.mult, [src, self.rc], [tmp])
                    self.TT(dv, df[:, t0:t0 + 2, 0:15], tmp[:], pu[:, t0:t0 + 2, 0, 15:30], ALU.subtract, [tmp, pu], [df])
            for t in range(8):
                g = t // 2
                ps = self.nextpA()
                for kc in range(2):
                    self.MM(ps[:, 0:N], L.wpool[:, g, kc, (t % 2) * 128:(t % 2) * 128 + 128], df[:, 2 * g + kc, :],
                            kc == 0, kc == 1, [L.wpool, df], [ps])
                self.STT(dv, yzp[:, t, 0:N], ps[:, 0:N], L.pscale[:, t:t + 1], spz[:, t, :], ALU.mult, ALU.mult,
                         [ps, L.pscale, spz], [yzp])
            self.CP(dv, halo[:], pu[:, :, :, n1:W], [pu], [halo])
            return None

    def pool_state_out(self, halo_seg_ap, halo_buf, dst_ap):
        P = self.P
        with Arena(P) as A:
            po = A.sbuf("po", [15, 1024], F32)
            for half in range(2):
                ps = self.nextpA()
                for t in range(4):
                    self.TR(ps[0:15, t * 128:(t + 1) * 128], halo_seg_ap(4 * half + t), self.identf[:], [halo_buf, self.identf], [ps])
                self.CP(P.act, po[0:15, half * 512:(half + 1) * 512], ps[0:15, 0:512], [ps], [po])
            self.DMA(dst_ap, po[:], [po], [])

    def ssm_phase(self, l, N, hT, yzs, st, nseg=1):
        P, L = self.P, self.L
        dv, ac, pl = P.dve, P.act, P.pool
        n1 = N // nseg
        tc = min(TC, n1)
        nsub = n1 // tc
        F = 16 * nseg * tc
        with Arena(P) as A:
            su = A.sbuf("su", [128, 8, N], BF16)
            ssz = A.sbuf("ssz", [128, 8, N], BF16)
            zT = A.sbuf("zT", [128, 8, N], BF16)
            self.projA([self.wA[l, t] for t in range(32, 40)], N, lambda k: hT[:, k, 0:N], 16,
                       lambda i, ps: self.CP(ac, su[:, i, :], ps[:, 0:N], [ps], [su]), [hT])
            self.projA([self.wA[l, t] for t in range(40, 48)], N, lambda k: hT[:, k, 0:N], 16,
                       lambda i, ps: self.AC(ssz[:, i, :], ps[:, 0:N], AF.Silu, [ps], [ssz]), [hT])
            if self.debug and nseg == 1:
                self.dump(f"su{l}", su[:], [128, 8, N], [su])

            def arr(name, dt=F32):
                return A.sbuf(name, [128, 16, nseg, tc], dt)

            bur, bui, t1, t2, gr, gi, kr, ki, hr, hi, t3, t4 = (arr(n) for n in
                                                        ("bur", "bui", "t1", "t2", "gr", "gi", "kr", "ki", "hr", "hi", "t3", "t4"))
            hrb, hib = arr("hrb", BF16), arr("hib", BF16)
            yf = A.sbuf("yf", [128, 4, nseg, tc], F32)
            c1 = A.sbuf("c1", [128, 16, nseg], F32)
            c2 = A.sbuf("c2", [128, 16, nseg], F32)
            suv = su[:].rearrange("p k (s c) -> p k s c", s=nseg)
            zv = zT[:].rearrange("p k (s c) -> p k s c", s=nseg)
            for sc in range(nsub if KSTOP >= 2 else 0):
                c0 = sc * tc
                for hf in range(2):
                    i0 = 16 * hf
                    pbr, pbi, py = self.pK[0], self.pK[1], self.pK[2]
                    for ii in sorted(range(16), key=lambda q: (((i0 + q) % 4) // 2, q)):
                        i = i0 + ii
                        kt, hb, e = i // 4, (i % 4) // 2, i % 2
                        for s in range(nseg):
                            o = (ii * nseg + s) * tc
                            rhs = suv[64 * hb:64 * hb + 64, kt, s, c0:c0 + tc]
                            self.MM(pbr[:, o:o + tc], L.BBr[64 * hb:64 * hb + 64, kt, e, :], rhs, True, True, [L.BBr, su], [pbr])
                            self.MM(pbi[:, o:o + tc], L.BBi[64 * hb:64 * hb + 64, kt, e, :], rhs, True, True, [L.BBi, su], [pbi])
                    fl = lambda b: b[:].rearrange("p a s c -> p (a s c)")
                    self.CP(ac, fl(bur), pbr[:, 0:F], [pbr], [bur])
                    self.CP(ac, fl(bui), pbi[:, 0:F], [pbi], [bui])
                    if KSTOP < 3:
                        continue
                    cs = L.cosT[:, i0:i0 + 16, 0:tc].unsqueeze(2).to_broadcast([128, 16, nseg, tc])
                    sn = L.sinT[:, i0:i0 + 16, 0:tc].unsqueeze(2).to_broadcast([128, 16, nseg, tc])
                    rz = L.rhoz[:, i0:i0 + 16, 0:tc].unsqueeze(2).to_broadcast([128, 16, nseg, tc])
                    self.TT(dv, t1[:], bur[:], cs, ALU.mult, [bur, L.cosT], [t1])
                    self.TT(dv, t2[:], bui[:], sn, ALU.mult, [bui, L.sinT], [t2])
                    self.TT(dv, gr[:], t1[:], t2[:], ALU.add, [t1, t2], [gr])
                    self.TT(dv, t1[:], bui[:], cs, ALU.mult, [bui, L.cosT], [t1])
                    self.TT(dv, t2[:], bur[:], sn, ALU.mult, [bur, L.sinT], [t2])
                    self.TT(dv, gi[:], t1[:], t2[:], ALU.subtract, [t1, t2], [gi])
                    self.TT(dv, gr[:, :, :, 0], gr[:, :, :, 0], st["car_r"][:, i0:i0 + 16, :], ALU.add, [gr, st["car_r"]], [gr])
                    self.TT(dv, gi[:, :, :, 0], gi[:, :, :, 0], st["car_i"][:, i0:i0 + 16, :], ALU.add, [gi, st["car_i"]], [gi])
                    if KSTOP < 4:
                        continue
                    if nseg == 1:
                        rzf = L.rhoz[:, i0:i0 + 16, 0:tc] if tc == TC else None
                    else:
                        rzf = None
                    if rzf is None:
                        rzm = A.sbuf("rzm", [128, 16, nseg, tc], F32)
                        self.CP(dv, rzm[:], rz, [L.rhoz], [rzm])
                        rz2 = fl(rzm)
                        rzR = [rzm]
                    else:
                        rz2 = rzf.rearrange("p a c -> p (a c)")
                        rzR = [L.rhoz]
                    P.op(dv, lambda e: e.tensor_tensor_scan(out=fl(kr), data0=rz2, data1=fl(gr), initial=0.0,
                                                            op0=ALU.mult, op1=ALU.add), rzR + [gr], [kr])
                    P.op(dv, lambda e: e.tensor_tensor_scan(out=fl(ki), data0=rz2, data1=fl(gi), initial=0.0,
                                                            op0=ALU.mult, op1=ALU.add), rzR + [gi], [ki])
                    if KSTOP < 5:
                        continue
                    self.TT(dv, t3[:], kr[:], cs, ALU.mult, [kr, L.cosT], [t3])
                    self.TT(dv, t4[:], ki[:], sn, ALU.mult, [ki, L.sinT], [t4])
                    self.TT(dv, hr[:], t3[:], t4[:], ALU.subtract, [t3, t4], [hr])
                    self.TT(dv, t3[:], kr[:], sn, ALU.mult, [kr, L.sinT], [t3])
                    self.TT(dv, t4[:], ki[:], cs, ALU.mult, [ki, L.cosT], [t4])
                    self.TT(dv, hi[:], t3[:], t4[:], ALU.add, [t3, t4], [hi])
                    self.CP(ac, hrb[:], hr[:], [hr], [hrb])
                    self.CP(ac, hib[:], hi[:], [hi], [hib])
                    hlr, hli = st["hl_r"], st["hl_i"]
                    self.CP(dv, hlr[:, i0:i0 + 16, :], hr[:, :, :, tc - 1], [hr], [hlr])
                    self.CP(dv, hli[:, i0:i0 + 16, :], hi[:, :, :, tc - 1], [hi], [hli])
                    ab_r = L.abr[:, i0:i0 + 16].unsqueeze(2).to_broadcast([128, 16, nseg])
                    ab_i = L.abi[:, i0:i0 + 16].unsqueeze(2).to_broadcast([128, 16, nseg])
                    self.TT(dv, c1[:], hlr[:, i0:i0 + 16, :], ab_r, ALU.mult, [hlr, L.abr], [c1])
                    self.TT(dv, c2[:], hli[:, i0:i0 + 16, :], ab_i, ALU.mult, [hli, L.abi], [c2])
                    self.TT(dv, st["car_r"][:, i0:i0 + 16, :], c1[:], c2[:], ALU.subtract, [c1, c2], [st["car_r"]])
                    self.TT(dv, c1[:], hli[:, i0:i0 + 16, :], ab_r, ALU.mult, [hli, L.abr], [c1])
                    self.TT(dv, c2[:], hlr[:, i0:i0 + 16, :], ab_i, ALU.mult, [hlr, L.abi], [c2])
                    self.TT(dv, st["car_i"][:, i0:i0 + 16, :], c1[:], c2[:], ALU.add, [c1, c2], [st["car_i"]])
                    if KSTOP < 6:
                        continue
                    for ko in range(4):
                        kt = 4 * hf + ko
                        for s in range(nseg):
                            o = (ko * nseg + s) * tc
                            for il in range(4):
                                i = 4 * kt + il
                                ii = i - i0
                                self.MM(py[:, o:o + tc], L.CTr[:, i, :], hrb[:, ii, s, :], il == 0, False, [L.CTr, hrb], [py])
                                self.MM(py[:, o:o + tc], L.CTi[:, i, :], hib[:, ii, s, :], False, il == 3, [L.CTi, hib], [py])
                    if KSTOP < 7:
                        continue
                    for ko in range(4):
                        kt = 4 * hf + ko
                        self.STT(dv, yf[:, ko], suv[:, kt, :, c0:c0 + tc], L.dskip[:, kt:kt + 1],
                                 py[:, ko * nseg * tc:(ko + 1) * nseg * tc].rearrange("p (s c) -> p s c", s=nseg),
                                 ALU.mult, ALU.add, [su, L.dskip, py], [yf])
                    self.AC(zv[:, 4 * hf:4 * hf + 4, :, c0:c0 + tc], yf[:], AF.Gelu_apprx_tanh, [yf], [zT])
            if self.debug and nseg == 1:
                self.dump(f"zT{l}", zT[:], [128, 8, N], [zT])
            if KSTOP < 8:
                self.MS(dv, yzs[:], 0.0, [yzs])
                return
            sgt = [A.sbuf(f"sgt{i}", [128, N], BF16) for i in range(2)]

            def glu_cons(i, ps):
                sg = sgt[i % 2]
                self.AC(sg[:], ps[:, 0:N], AF.Sigmoid, [ps], [sg])
                self.TT(dv, sg[:], sg[:], zT[:, i, :], ALU.mult, [sg, zT], [sg])
                self.TT(dv, yzs[:, i, 0:N], sg[:], ssz[:, i, :], ALU.mult, [sg, ssz], [yzs])

            self.projA([self.wG[l, t] for t in range(8)], N, lambda k: zT[:, k, 0:N], 8, glu_cons, [zT])

    def ssm_state_out(self, hl_r_ap, hl_i_ap, bufs, dst_ap):
        P = self.P
        with Arena(P) as A:
            o = A.sbuf("sso", [32, 2, 128], F32)
            ps = self.nextpA()
            self.TR(ps[0:32, 0:128], hl_r_ap, self.identf[:], bufs + [self.identf], [ps])
            self.TR(ps[0:32, 128:256], hl_i_ap, self.identf[:], bufs + [self.identf], [ps])
            self.CP(P.act, o[:].rearrange("p a n -> p (a n)"), ps[0:32, 0:256], [ps], [o])
            self.DMA(dst_ap.rearrange("a i p -> i a p"), o[:], [o], [])

    def q_to_base(self, A, N, i, ps, q0):
        P = self.P
        ac = P.act
        for hh in range(2):
            h = 2 * i + hh
            k, g = h // 4, h % 4
            slot = (k // 2) * 4 + g
            need, have = 64 * (k % 2), 64 * hh
            if need == have or os.environ.get('KQS') == '0':
                self.CP(ac, q0[have:have + 64, slot, 0:N], ps[have:have + 64, 0:N], [ps], [q0])
            else:
                if not hasattr(A, "_tq"):
                    A._tq = A.sbuf("tq", [128, N], BF16)
                tq = A._tq
                self.CP(ac, tq[have:have + 64, :], ps[have:have + 64, 0:N], [ps], [tq])
                p2 = self.pS[hh]
                if have == 64:
                    self.MM(p2[0:64, 0:N], self.ident[64:128, 64:128], tq[64:128, :], True, True, [self.ident, tq], [p2])
                    self.CP(ac, q0[0:64, slot, 0:N], p2[0:64, 0:N], [p2], [q0])
                else:
                    self.MM(p2[:, 0:N], self.shu[0:64, :], tq[0:64, :], True, True, [self.shu, tq], [p2])
                    self.CP(ac, q0[64:128, slot, 0:N], p2[64:128, 0:N], [p2], [q0])

    def kv_formB(self, l, A, hT, tiles, kvf):
        for ci, c0 in enumerate(range(0, NKVB, 128)):
            n = min(128, NKVB - c0)
            w = self.next_wt()
            self.DMA(w[:, :, 0:n], self.wB[l][:, :, c0:c0 + n], [], [w])
            for ti, (r0, nr) in enumerate(tiles):
                ps = self.pK[ti % 3]
                for k in range(16):
                    self.MM(ps[0:nr, 0:n], hT[:, k, r0:r0 + nr], w[:, k, 0:n], k == 0, k == 15, [hT, w], [ps])
                self.CP(self.P.act, kvf[ti][0:nr, c0:c0 + n], ps[0:nr, 0:n], [ps], [kvf[ti]])

    def nsa_phase(self, l, j, hT, yza, ps_):
        P, L, C, S = self.P, self.L, self.C, self.S
        dv, ac, pl = P.dve, P.act, P.pool
        NT = S // 128
        with Arena(P) as A:
            q0 = A.sbuf("q0", [128, 8, C], BF16)
            saz = A.sbuf("saz", [128, 8, C], BF16)
            self.projA([self.wA[l, t] for t in range(16, 24)], C, lambda k: hT[:, k, 0:C], 16,
                       lambda i, ps: self.q_to_base(A, C, i, ps, q0), [hT])
            self.projA([self.wA[l, t] for t in range(24, 32)], C, lambda k: hT[:, k, 0:C], 16,
                       lambda i, ps: self.AC(saz[:, i, :], ps[:, 0:C], AF.Silu, [ps], [saz]), [hT])
            if KSTOP < 2:
                self.MS(dv, yza[:], 0.0, [yza])
                return
            tiles = [(tt * 128, 128) for tt in range(C // 128)]
            kvf = [A.sbuf(f"kvf{i}", [128, NKVB], F32) for i in range(len(tiles))]
            self.kv_formB(l, A, hT, tiles, kvf)
            if KSTOP < 3:
                self.MS(dv, yza[:], 0.0, [yza])
                return
            sg = A.sbuf("sg", [128, len(tiles), 48], F32)
            xtk = A.sbuf("xtk", [128, 2, C], BF16)
            xtv = A.sbuf("xtv", [128, 2, C], BF16)
            kvb = A.sbuf("kvb", [128, 1536], BF16)
            yb = A.sbuf("yb", [128, 1024], BF16)
            for tt, (r0, nr) in enumerate(tiles):
                T = (j * C) // 128 + tt
                self.DMA(self.kvp[l][T * 128:(T + 1) * 128, :], kvf[tt][:, 0:1024], [kvf[tt]], [])
                if T >= NT - 4:
                    w0 = (T - (NT - 4)) * 128
                    self.DMA(self.winp[l][w0:w0 + 128, :], kvf[tt][:, 1024:1536], [kvf[tt]], [])
                if KSUB < 1:
                    continue
                self.CP(dv, kvb[:], kvf[tt][:, 0:1536], [kvf[tt]], [kvb])
                self.AC(sg[:, tt, :], kvf[tt][:, 1536:1584], AF.Sigmoid, [kvf[tt]], [sg])
                if KSUB < 2:
                    continue
                self.CP(ac, ps_["vsel"][:, T, :, 0:64], kvb[:, 768:1024].rearrange("p (h d) -> p h d", h=4), [kvb], [ps_["vsel"]])
                self.CP(ac, ps_["vwin"][:, T % 8, :, 0:64], kvb[:, 1280:1536].rearrange("p (h d) -> p h d", h=4), [kvb], [ps_["vwin"]])
                if KSUB < 3:
                    continue
                if os.environ.get("KBAR") == "1":
                    P.barrier()
                tb = [self.pS[0], self.pS[1]]
                for n_, c_ in enumerate((512, 640, 1024, 1152, 0, 128, 256, 384)):
                    self.MT(tb[n_ // 4][:, (n_ % 4) * 128:(n_ % 4 + 1) * 128], kvb[:, c_:c_ + 128], [kvb], [tb[n_ // 4]])
                if KSUB < 4:
                    continue
                pv = lambda a: tb[a // 2][:, (a % 2) * 256:(a % 2 + 1) * 256].rearrange("p (r t) -> p r t", r=2)
                self.CP(ac, ps_["ksT"][:, :, T * 128:(T + 1) * 128], pv(0), [tb[0]], [ps_["ksT"]])
                self.CP(ac, ps_["kwT"][:, :, (T % 8) * 128:(T % 8 + 1) * 128], pv(1), [tb[0]], [ps_["kwT"]])
                self.CP(dv, xtk[:, :, tt * 128:(tt + 1) * 128], pv(2), [tb[1]], [xtk])
                self.CP(dv, xtv[:, :, tt * 128:(tt + 1) * 128], pv(3), [tb[1]], [xtv])
            nbn = C // 64
            b0 = (j * C) // 64
            for a_, xt_ in ((0, xtk), (1, xtv)):
                v4 = xt_[:].rearrange("p r (n l) -> p r n l", l=64)
                self.TT(dv, v4, v4, L.pe2[:, a_, :].unsqueeze(1).unsqueeze(1).to_broadcast([128, 2, nbn, 64]), ALU.add, [xt_, L.pe2], [xt_])
            pkc, pvc = self.pS[0], self.pS[1]
            for k in (0, 2, 1, 3):
                base, pr = 64 * (k % 2), k // 2
                for a_, xt_, pso, ob in ((0, xtk, pkc, base), (1, xtv, pvc, 0)):
                    col = (pr * nbn) if a_ == 0 else (k * nbn)
                    xl = xt_[:].rearrange("p r (n l) -> p r l n", l=64)
                    for ll in range(64):
                        self.MM(pso[ob:ob + 64, col:col + nbn], L.wphi2[base:base + 64, a_, ll, :], xl[base:base + 64, pr, ll, :],
                                ll == 0, ll == 63, [L.wphi2, xt_], [pso])
            self.CP(ac, ps_["kcT"][:, :, b0:b0 + nbn], pkc[:, 0:2 * nbn].rearrange("p (r n) -> p r n", r=2), [pkc], [ps_["kcT"]])
            self.CP(ac, ps_["vcT"][0:64, :, b0:b0 + nbn], pvc[0:64, 0:4 * nbn].rearrange("p (k n) -> p k n", k=4), [pvc], [ps_["vcT"]])
            pvt = self.pS[0]
            for k in range(4):
                self.MT(pvt[0:64, k * 64:(k + 1) * 64], ps_["vcT"][0:64, k, :], [ps_["vcT"]], [pvt])
            self.CP(ac, ps_["vca"][0:64, :, 0:64], pvt[0:64, 0:256].rearrange("p (k e) -> p k e", k=4), [pvt], [ps_["vca"]])
            if KSTOP < 5:
                self.MS(dv, yza[:], 0.0, [yza])
                return
            T0 = (j * C) // 128
            ntt = len(tiles)
            cbq = A.sbuf("cbq", [128, ntt, 64], F32)
            tsl = A.sbuf("tsl", [128, ntt, 64], F32)
            cbTf = A.sbuf("cbTf", [64, C], F32)
            cbT = A.sbuf("cbT", [64, C], BF16)
            self.DMA(cbq[:], self.t_cbq[T0:T0 + ntt].rearrange("t p n -> p t n"), [], [cbq])
            self.DMA(tsl[:], self.t_sel[T0:T0 + ntt].rearrange("t p n -> p t n"), [], [tsl])
            self.DMA(cbTf[:], self.t_cbT[:, j * C:(j + 1) * C], [], [cbTf])
            self.CP(dv, cbT[:], cbTf[:], [cbTf], [cbT])
            BT = A.sbuf("BT", [64, 4, C], BF16)
            s1 = A.sbuf("s1", [128, 4, 64], F32)
            z4 = A.sbuf("z4", [128, 4], F32)
            sc_ = A.sbuf("sc", [128, 64], F32)
            mx = A.sbuf("mx", [128, 8], F32)
            w1 = A.sbuf("w1", [128, 64], F32)
            bq = A.sbuf("bq", [128, 64], BF16)
            for tt in range(ntt):
                for k in range(4):
                    base, pr = 64 * (k % 2), k // 2
                    psc = self.pS[(tt * 4 + k) % 2]
                    for g in range(4):
                        self.MM(psc[:, g * 64:(g + 1) * 64], q0[base:base + 64, pr * 4 + g, tt * 128:(tt + 1) * 128],
                                ps_["kcT"][base:base + 64, pr, :], True, True, [q0, ps_["kcT"]], [psc])
                    self.TT(dv, s1[:], psc[:, 0:256].rearrange("p (g n) -> p g n", g=4),
                            cbq[:, tt, :].unsqueeze(1).to_broadcast([128, 4, 64]), ALU.add, [psc, cbq], [s1])
                    self.AC(s1[:], s1[:], AF.Exp, [s1], [s1], scale=0.125)
                    P.op(dv, lambda e: e.tensor_reduce(out=z4[:], in_=s1[:], axis=AX.X, op=ALU.add), [s1], [z4])
                    self.TS(dv, z4[:], z4[:], 1e-30, None, ALU.max, None, [z4], [z4])
                    P.op(dv, lambda e: e.reciprocal(out=z4[:], in_=z4[:]), [z4], [z4])
                    self.TT(dv, s1[:], s1[:], z4[:].unsqueeze(2).to_broadcast([128, 4, 64]), ALU.mult, [s1, z4], [s1])
                    P.op(dv, lambda e: e.tensor_reduce(out=sc_[:], in_=s1[:].rearrange("p g n -> p n g"), axis=AX.X, op=ALU.add), [s1], [sc_])
                    self.TT(dv, sc_[:], sc_[:], tsl[:, tt, :], ALU.add, [sc_, tsl], [sc_])
                    P.op(dv, lambda e: e.max(out=mx[:], in_=sc_[:]), [sc_], [mx])
                    P.op(dv, lambda e: e.match_replace(out=w1[:], in_to_replace=mx[:], in_values=sc_[:], imm_value=NEG), [mx, sc_], [w1])
                    P.op(dv, lambda e: e.max(out=mx[:], in_=w1[:]), [w1], [mx])
                    P.op(dv, lambda e: e.match_replace(out=w1[:], in_to_replace=mx[:], in_values=w1[:], imm_value=NEG), [mx, w1], [w1])
                    self.TT(dv, w1[:], sc_[:], w1[:], ALU.subtract, [sc_, w1], [w1])
                    self.TS(dv, w1[:], w1[:], 1.0, -1.0, ALU.min, ALU.add, [w1], [w1])
                    self.TS(dv, bq[:], w1[:], 1e30, None, ALU.mult, None, [w1], [bq])
                    pbt = self.pA[(tt * 4 + k) % 2]
                    self.MT(pbt[0:64, 0:128], bq[:], [bq], [pbt])
                    self.CP(ac, BT[0:64, k, tt * 128:(tt + 1) * 128], pbt[0:64, 0:128], [pbt], [BT])
            if KSTOP < 6:
                self.MS(dv, yza[:], 0.0, [yza])
                return
            yat = [A.sbuf(f"yat{i}", [128, 1024], F32) for i in range(ntt)]
            zz = A.sbuf("zz", [128, 4], F32)
            ytmp = A.sbuf("ytmp", [128, 4, 64], F32)
            pts = [A.sbuf(f"pt{i}", [128, C], BF16) for i in range(3)]
            pti = [0]
            scp = [self.pS[0], self.pS[1], self.pA[0], self.pA[1]]
            sci = [0]
            Gf = self.G[:].rearrange("p j c -> p (j c)")

            def score_exp(nk, mms):
                sci[0] += 1
                psx = scp[sci[0] % 4]
                for mi, (lhsT, rhs, R) in enumerate(mms):
                    self.MM(psx[0:nk, 0:C], lhsT, rhs, mi == 0, mi == len(mms) - 1, R, [psx])
                pti[0] += 1
                pt = pts[pti[0] % 3]
                self.AC(pt[0:nk, :], psx[0:nk, 0:C], AF.Exp, [psx], [pt], scale=0.125)
                return pt

            for k in range(4):
                base, pr = 64 * (k % 2), k // 2
                po = [self.pK[tt] for tt in range(ntt)]
                pov = [po[tt][:, 0:260].rearrange("p (g e) -> p g e", g=4) for tt in range(ntt)]
                for bi in range(3):
                    for g in range(4):
                        slot = pr * 4 + g
                        qh = q0[base:base + 64, slot, :]
                        if bi == 0:
                            pt = score_exp(64, [(ps_["kcT"][base:base + 64, pr, :], qh, [ps_["kcT"], q0]),
                                                (self.ident[0:64, 0:64], cbT[:, :], [self.ident, cbT])])
                            for tt in range(ntt):
                                self.MM(pov[tt][:, g, :], pt[0:64, tt * 128:(tt + 1) * 128], ps_["vca"][0:64, k, :], True, True,
                                        [pt, ps_["vca"]], [po[tt]])
                        elif bi == 1:
                            last = T0 + ntt - 1
                            for kt in range(0, last + 1):
                                pt = score_exp(128, [(ps_["ksT"][base:base + 64, pr, kt * 128:(kt + 1) * 128], qh, [ps_["ksT"], q0]),
                                                     (Gf[:, kt * 128:(kt + 1) * 128], BT[0:64, k, :], [self.G, BT])])
                                for tt in range(ntt):
                                    T = T0 + tt
                                    if kt == T:
                                        self.TT(dv, pt[:, tt * 128:(tt + 1) * 128], pt[:, tt * 128:(tt + 1) * 128], self.tri[:], ALU.mult, [pt, self.tri], [pt])
                                for tt in range(ntt):
                                    T = T0 + tt
                                    if kt <= T:
                                        self.MM(pov[tt][:, g, :], pt[:, tt * 128:(tt + 1) * 128], ps_["vsel"][:, kt, k, :], kt == 0, kt == T,
                                                [pt, ps_["vsel"]], [po[tt]])
                        else:
                            for kt in range(max(0, T0 - 4), T0 + ntt):
                                pt = score_exp(128, [(ps_["kwT"][base:base + 64, pr, (kt % 8) * 128:(kt % 8 + 1) * 128], qh, [ps_["kwT"], q0])])
                                for tt in range(ntt):
                                    T = T0 + tt
                                    if kt == T:
                                        self.TT(dv, pt[:, tt * 128:(tt + 1) * 128], pt[:, tt * 128:(tt + 1) * 128], self.tri[:], ALU.mult, [pt, self.tri], [pt])
                                    elif kt == T - 4:
                                        self.TT(dv, pt[:, tt * 128:(tt + 1) * 128], pt[:, tt * 128:(tt + 1) * 128], self.tris[:], ALU.mult, [pt, self.tris], [pt])
                                for tt in range(ntt):
                                    T = T0 + tt
                                    if T - 4 <= kt <= T:
                                        self.MM(pov[tt][:, g, :], pt[:, tt * 128:(tt + 1) * 128], ps_["vwin"][:, kt % 8, k, :],
                                                kt == max(0, T - 4), kt == T, [pt, ps_["vwin"]], [po[tt]])
                    for tt in range(ntt):
                        z4 = zz
                        self.TS(dv, z4[:], pov[tt][:, :, 64], 1e-30, None, ALU.max, None, [po[tt]], [z4])
                        P.op(dv, lambda e: e.reciprocal(out=z4[:], in_=z4[:]), [z4], [z4])
                        gsel = sg[:, tt, 12 * k:12 * k + 12].rearrange("p (g i) -> p g i", i=3)[:, :, bi]
                        self.TT(dv, z4[:], z4[:], gsel, ALU.mult, [z4, sg], [z4])
                        yv = yat[tt][:, k * 256:(k + 1) * 256].rearrange("p (g d) -> p g d", g=4)
                        fb = z4[:].unsqueeze(2).to_broadcast([128, 4, 64])
                        if bi == 0:
                            self.TT(dv, yv, pov[tt][:, :, 0:64], fb, ALU.mult, [po[tt], z4], [yat[tt]])
                        else:
                            tmp = ytmp
                            self.TT(dv, tmp[:], pov[tt][:, :, 0:64], fb, ALU.mult, [po[tt], z4], [tmp])
                            self.TT(dv, yv, yv, tmp[:], ALU.add, [yat[tt], tmp], [yat[tt]])
            if KSTOP < 7:
                self.MS(dv, yza[:], 0.0, [yza])
                return
            for tt in range(ntt):
                if self.debug:
                    self.dump(f"yat{l}_{j}_{tt}", yat[tt][:], [128, 1024], [yat[tt]])
                self.CP(dv, yb[:], yat[tt][:], [yat[tt]], [yb])
                for hf4 in range(2):
                    pyt = self.pS[hf4]
                    for t4 in range(4):
                        t8 = 4 * hf4 + t4
                        self.MT(pyt[:, t4 * 128:(t4 + 1) * 128], yb[:, t8 * 128:(t8 + 1) * 128], [yb], [pyt])
                    self.TT(dv, yza[:, 4 * hf4:4 * hf4 + 4, tt * 128:(tt + 1) * 128], pyt[:, :].rearrange("p (t c) -> p t c", t=4),
                            saz[:, 4 * hf4:4 * hf4 + 4, tt * 128:(tt + 1) * 128], ALU.mult, [pyt, saz], [yza])

    def prompt_layer(self, l):
        P, S, C = self.P, self.S, self.C
        dv, pl = P.dve, P.pool
        NT = S // 128
        with Arena(P) as PA:
            ps_ = {
                "ksT": PA.sbuf("ksT", [128, 2, S], BF16),
                "vsel": PA.sbuf("vsel", [128, NT, 4, 65], BF16),
                "kwT": PA.sbuf("kwT", [128, 2, 1024], BF16),
                "vwin": PA.sbuf("vwin", [128, 8, 4, 65], BF16),
                "kcT": PA.sbuf("kcT", [128, 2, 64], BF16),
                "vcT": PA.sbuf("vcT", [64, 4, 64], BF16),
                "vca": PA.sbuf("vca", [64, 4, 65], BF16),
            }
            halo = PA.sbuf("halo", [128, 8, 1, 15], F32)
            st = {n: PA.sbuf(n, [128, 32, 1], F32) for n in ("car_r", "car_i", "hl_r", "hl_i")}
            self.MS(pl, ps_["vsel"][:], 1.0, [ps_["vsel"]])
            self.MS(pl, ps_["vwin"][:], 1.0, [ps_["vwin"]])
            self.MS(pl, ps_["vca"][:], 1.0, [ps_["vca"]])
            self.MS(pl, ps_["kcT"][:], 0.0, [ps_["kcT"]])
            self.MS(pl, ps_["vcT"][:], 0.0, [ps_["vcT"]])
            self.MS(dv, halo[:], 0.0, [halo])
            for b in st.values():
                self.MS(dv, b[:], 0.0, [b])
            src = self.xp if l == 0 else None
            for j in range(self.NCH):
                tiles = [(tt * 128, 128) for tt in range(C // 128)]
                T0 = (j * C) // 128

                def src_rows(r0, nr, j=j):
                    if l == 0:
                        return self.xp[j * C + r0:j * C + r0 + nr, :]
                    return self.x1_tiles[(j * C + r0) // 128][0:nr, :]

                def dst_rows(r0, nr, j=j):
                    if l == 0:
                        return self.x1_tiles[(j * C + r0) // 128][0:nr, :]
                    return self.yp[j * C + r0:j * C + r0 + nr, :]

                with Arena(P) as CA:
                    hT = CA.sbuf("hT", [128, 16, C], BF16)
                    yz = [CA.sbuf(f"yz{i}", [128, 8, C], BF16) for i in range(3)]
                    if l == 1:
                        for tt in range(C // 128):
                            xb_ = self.x1_tiles[T0 + tt]
                            for d in list(xb_.w.values()):
                                P._wait(P.sp, d)
                    self.norm_phase(src_rows, tiles, hT)
                    if "pool" in self.parts:
                        self.pool_phase(l, C, hT, yz[0], halo, first=(j == 0))
                    else:
                        self.MS(dv, yz[0][:], 0.0, [yz[0]])
                    if "nsa" in self.parts:
                        self.nsa_phase(l, j, hT, yz[1], ps_)
                    else:
                        self.MS(dv, yz[1][:], 0.0, [yz[1]])
                    if "ssm" in self.parts:
                        self.ssm_phase(l, C, hT, yz[2], st)
                    else:
                        self.MS(dv, yz[2][:], 0.0, [yz[2]])
                    if self.debug and j == 0:
                        for i in range(3):
                            self.dump(f"yz{i}_{l}", yz[i][:], [128, 8, C], [yz[i]])
                    dst_bufs = [self.x1_tiles[T0 + tt] for tt in range(C // 128)] if l == 0 else None
                    self.merge_out_phase(l, C, tiles, hT, yz, src_rows, dst_rows, dst_bufs)
            self.pool_state_out(lambda t: halo[:, t, 0, :], halo, self.poolp[l])
            self.ssm_state_out(st["hl_r"][:, :, 0], st["hl_i"][:, :, 0], [st["hl_r"], st["hl_i"]], self.ssmp[l])

    def build(self):
        P = self.P
        self.declare()
        self.x1_tiles = [Buf(self.x1.t[t * 128:(t + 1) * 128, :], f"x1_{t}") for t in range(self.S // 128)]
        self.consts()
        self.wts = [P.sbuf(f"wt{i}", [128, 16, 128], BF16) for i in range(6)]
        self.wt_i = 0
        self.prepass()
        for l in range(2):
            with Arena(P) as LA:
                self.layer_prep(l, LA)
                self.prompt_layer(l)
                if self.NS:
                    self.sample_layer(l)
        P.finish()
        P.close()
        return self.nc


def _tables():
    pos = np.arange(SEQ)
    n = np.arange(64)
    nvalid = (pos + 1) // 64
    cbq = np.where(n[None, :] < nvalid[:, None], 0.0, NEG).astype(np.float32)
    cur = pos // 64
    tsel = np.zeros((SEQ, 64), np.float32)
    tsel[n[None, :] > cur[:, None]] = NEG
    tsel[n[None, :] == cur[:, None]] = 1e4
    tsel[n[None, :] == (cur[:, None] - 1)] = 2e4
    tsel[:, 0] = 3e4
    rc = np.zeros((4, 15), np.float32)
    for gi, w in enumerate((2, 4, 8, 16)):
        rc[gi] = 1.0 / np.minimum(np.arange(15) + 1, w)
    return {
        "t_cbq": np.ascontiguousarray(cbq.reshape(SEQ // 128, 128, 64)),
        "t_sel": np.ascontiguousarray(tsel.reshape(SEQ // 128, 128, 64)),
        "t_cbT": np.ascontiguousarray(cbq.T),
        "t_rc": np.ascontiguousarray(np.tile(rc.reshape(1, 60), (128, 1))),
    }


_WKEYS = ("g_pre", "g_post", "w_in", "w_pool", "pool_scale", "pe_cmp", "w_phi", "d_skip", "w_glu",
          "w_br_pool", "w_br_nsa", "w_br_ssm", "w_out", "b_re", "b_im")


def _core_inputs(inp, b, S, NS, samples):
    f = lambda a: np.ascontiguousarray(np.asarray(a, dtype=np.float32))
    m = {k: f(inp[k]) for k in _WKEYS}
    m["lam_re"] = f(inp["lam_re"]).reshape(2, 32, 128)
    m["lam_im"] = f(inp["lam_im"]).reshape(2, 32, 128)
    m["log_step"] = f(inp["log_step"]).reshape(2, 32, 2)
    m["c_re"] = f(inp["c_re"]).reshape(2, 32, 2, 16, 64)
    m["c_im"] = f(inp["c_im"]).reshape(2, 32, 2, 16, 64)
    m["xp"] = f(inp["x_prompt"][b, :S])
    m.update(_tables())
    if NS:
        sl = list(samples)
        m["xs"] = f(inp["x_sample"][sl]).reshape(4 * NS, D)
        ck = np.asarray(inp["cache_kv"], dtype=np.float32)
        m["cache0"] = np.ascontiguousarray(ck[0]).reshape(NPOOL * 128 * 2, 512)
        m["cache1"] = np.ascontiguousarray(ck[1]).reshape(NPOOL * 128 * 2, 512)
        m["ptab"] = np.ascontiguousarray(np.asarray(inp["page_table"])[sl].astype(np.int32))
        m["swin"] = f(np.asarray(inp["state_win_kv"])[:, sl]).reshape(2, NS, 512, 512)
        m["spool"] = f(np.asarray(inp["state_pool"])[:, sl])
        m["sssm"] = f(np.asarray(inp["state_ssm"])[:, sl]).reshape(2, NS, 2, 32, 128)
        N = 4 * NS
        col = np.arange(N)
        cm = (col[None, :] // 4 == np.arange(NS)[:, None]).astype(np.float32)
        p = np.arange(128)
        wm = cm[None, :, :] * (p[:, None, None] > (col % 4)[None, None, :])
        m["t_cm"] = np.ascontiguousarray(np.broadcast_to(cm[None], (128, NS, N)).astype(np.float32))
        m["t_wm"] = np.ascontiguousarray(wm.astype(np.float32))
        m16 = ((col[:, None] // 4 == col[None, :] // 4) & (col[:, None] % 4 <= col[None, :] % 4)).astype(np.float32)
        m["t_m16"] = np.ascontiguousarray(m16)
        ts = np.zeros((N, 256), np.float32); ts[:, 0] = 3e4; ts[:, 255] = 2e4
        m["t_ssel"] = ts
    return m


_NC_CACHE = {}


def run(inp, S=SEQ, NS=4, ncores=2, debug=False, parts=("pool", "ssm", "nsa"), trace=False):
    key = (S, NS, debug, tuple(parts))
    if key not in _NC_CACHE:
        kb = KB(S=S, NS=NS, debug=debug, parts=parts)
        kb.build()
        _NC_CACHE[key] = kb
    kb = _NC_CACHE[key]
    in_maps = [_core_inputs(inp, c, S, NS, range(NS * c, NS * c + NS)) for c in range(ncores)]
    res = run_bass_kernel_spmd(kb.nc, in_maps, core_ids=list(range(ncores)), **({"trace": True} if trace else {}))
    return kb, res


def kernel(**inputs):
    kb, res = run(inputs)
    r = res.results
    B = 2
    y_prompt = np.stack([r[b]["yp"] for b in range(B)])
    y_sample = np.concatenate([r[c]["ys"].reshape(4, 4, D) for c in range(2)], axis=0)
    kv_p = np.stack([r[b]["kvp"] for b in range(B)], axis=1).reshape(2, B, SEQ, 4, 4, 64)
    kv_s = np.concatenate([r[c]["kvs"].reshape(2, 4, 4, 4, 4, 64) for c in range(2)], axis=1)
    win_p = np.stack([r[b]["winp"] for b in range(B)], axis=1).reshape(2, B, 512, 2, 4, 64)
    win_s = np.concatenate([r[c]["wins"].reshape(2, 4, 512, 2, 4, 64) for c in range(2)], axis=1)
    pool_p = np.stack([r[b]["poolp"] for b in range(B)], axis=1)
    pool_s = np.concatenate([r[c]["pools"] for c in range(2)], axis=1)
    ssm_p = np.stack([r[b]["ssmp"] for b in range(B)], axis=1).reshape(2, B, 2, 64, 64)
    ssm_s = np.concatenate([r[c]["ssms"].reshape(2, 4, 2, 64, 64) for c in range(2)], axis=1)
    outs = (y_prompt, y_sample, kv_p, kv_s, win_p, win_s, pool_p, pool_s, ssm_p, ssm_s)
    return tuple(np.ascontiguousarray(o, dtype=np.float32) for o in outs)


def _sample_layer(self, l):
    P, L, NS = self.P, self.L, self.NS
    dv, ac, pl = P.dve, P.act, P.pool
    N = 4 * NS
    tiles = [(0, N)]
    with Arena(P) as SA:
        hT = SA.sbuf("hTs", [128, 16, N], BF16)
        yz = [SA.sbuf(f"yzs{i}", [128, 8, N], BF16) for i in range(3)]

        def src_rows(r0, nr):
            return self.xs[r0:r0 + nr, :] if l == 0 else self.xs1[r0:r0 + nr, :]

        def dst_rows(r0, nr):
            return self.xs1[r0:r0 + nr, :] if l == 0 else self.ys[r0:r0 + nr, :]

        if l == 1:
            for d in list(self.xs1.w.values()):
                P._wait(P.sp, d)
        self.norm_phase(src_rows, tiles, hT)
        halo = SA.sbuf("halos", [128, 8, NS, 15], F32)
        with Arena(P) as A:
            for s in range(NS):
                sp = A.sbuf("spl", [15, 1024], F32)
                self.DMA(sp[:], self.spool[l, s], [], [sp])
                for half in range(2):
                    ps = self.nextpA()
                    for t in range(4):
                        tt = 4 * half + t
                        self.MT(ps[:, t * 16:t * 16 + 15], sp[0:15, tt * 128:(tt + 1) * 128], [sp], [ps])
                    self.CP(ac, halo[:, 4 * half:4 * half + 4, s, :], ps[:, 0:64].rearrange("p (t c) -> p t c", t=4)[:, :, 0:15], [ps], [halo])
        self.pool_phase(l, N, hT, yz[0], halo, first=False, nseg=NS)
        for s in range(NS):
            self.pool_state_out(lambda t, s=s: halo[:, t, s, :], halo, self.pools[l, s])
        st = {n: SA.sbuf(n + "s", [128, 32, NS], F32) for n in ("car_r", "car_i", "hl_r", "hl_i")}
        with Arena(P) as A:
            c1 = A.sbuf("c1s", [128, 32, NS], F32)
            c2 = A.sbuf("c2s", [128, 32, NS], F32)
            for s in range(NS):
                for ri, nm in ((0, "hl_r"), (1, "hl_i")):
                    t_ = self.load_T(A, "h0", self.sssm[l, s, ri], 32)
                    self.CP(dv, st[nm][:, :, s], t_[:, 0:32], [t_], [st[nm]])
            ab_r = L.abr[:].unsqueeze(2).to_broadcast([128, 32, NS])
            ab_i = L.abi[:].unsqueeze(2).to_broadcast([128, 32, NS])
            self.TT(dv, c1[:], st["hl_r"][:], ab_r, ALU.mult, [st["hl_r"], L.abr], [c1])
            self.TT(dv, c2[:], st["hl_i"][:], ab_i, ALU.mult, [st["hl_i"], L.abi], [c2])
            self.TT(dv, st["car_r"][:], c1[:], c2[:], ALU.subtract, [c1, c2], [st["car_r"]])
            self.TT(dv, c1[:], st["hl_i"][:], ab_r, ALU.mult, [st["hl_i"], L.abr], [c1])
            self.TT(dv, c2[:], st["hl_r"][:], ab_i, ALU.mult, [st["hl_r"], L.abi], [c2])
            self.TT(dv, st["car_i"][:], c1[:], c2[:], ALU.add, [c1, c2], [st["car_i"]])
        self.ssm_phase(l, N, hT, yz[2], st, nseg=NS)
        for s in range(NS):
            self.ssm_state_out(st["hl_r"][:, :, s], st["hl_i"][:, :, s], [st["hl_r"], st["hl_i"]], self.ssms[l, s])
        self.nsa_sample(l, hT, yz[1])
        self.merge_out_phase(l, N, tiles, hT, yz, src_rows, dst_rows, [self.xs1] if l == 0 else None)


def _nsa_sample(self, l, hT, yza):
    P, L, NS = self.P, self.L, self.NS
    dv, ac, pl = P.dve, P.act, P.pool
    N = 4 * NS
    cache_l = self.cache[l]
    with Arena(P) as A:
        q0 = A.sbuf("q0s", [128, 8, N], BF16)
        saz = A.sbuf("sazs", [128, 8, N], BF16)
        self.projA([self.wA[l, t] for t in range(16, 24)], N, lambda k: hT[:, k, 0:N], 16,
                   lambda i, ps: self.q_to_base(A, N, i, ps, q0), [hT])
        self.projA([self.wA[l, t] for t in range(24, 32)], N, lambda k: hT[:, k, 0:N], 16,
                   lambda i, ps: self.AC(saz[:, i, :], ps[:, 0:N], AF.Silu, [ps], [saz]), [hT])
        kvf = A.sbuf("kvfs", [128, NKVB], F32)
        self.kv_formB(l, A, hT, [(0, N)], [kvf])
        self.DMA(self.kvs[l][0:N, :], kvf[0:N, 0:1024], [kvf], [])
        for s in range(NS):
            self.DMA(self.wins[l, s, 0:508, :], self.swin[l, s, 4:512, :], [], [])
            self.DMA(self.wins[l, s, 508:512, :], kvf[4 * s:4 * s + 4, 1024:1536], [kvf], [])
        sg = A.sbuf("sgs", [N, 48], F32)
        self.AC(sg[:], kvf[0:N, 1536:1584], AF.Sigmoid, [kvf], [sg])
        kvb = A.sbuf("kvbs", [N, 1536], BF16)
        self.CP(dv, kvb[:], kvf[0:N, 0:1536], [kvf], [kvb])
        knT = A.sbuf("knT", [128, 4, N], BF16)
        ptn = self.pS[0]
        for n_, c_ in enumerate((512, 640, 1024, 1152)):
            self.MT(ptn[:, n_ * N:(n_ + 1) * N], kvb[0:N, c_:c_ + 128], [kvb], [ptn])
        self.CP(ac, knT[:], ptn[:, 0:4 * N].rearrange("p (a n) -> p a n", a=4), [ptn], [knT])
        vn = A.sbuf("vn", [N, 2, 4, 65], BF16)
        self.MS(pl, vn[:], 1.0, [vn])
        self.CP(ac, vn[:, 0, :, 0:64], kvb[:, 768:1024].rearrange("p (h d) -> p h d", h=4), [kvb], [vn])
        self.CP(ac, vn[:, 1, :, 0:64], kvb[:, 1280:1536].rearrange("p (h d) -> p h d", h=4), [kvb], [vn])
        cmf = A.sbuf("cmf", [128, NS, N], F32)
        wmf = A.sbuf("wmf", [128, NS, N], F32)
        m16f = A.sbuf("m16f", [N, N], F32)
        self.DMA(cmf[:], self.t_cm, [], [cmf])
        self.DMA(wmf[:], self.t_wm, [], [wmf])
        self.DMA(m16f[:], self.t_m16, [], [m16f])
        cm = A.sbuf("cm", [128, NS, N], BF16)
        wm = A.sbuf("wm", [128, NS, N], BF16)
        m16 = A.sbuf("m16", [N, N], BF16)
        self.CP(dv, cm[:], cmf[:], [cmf], [cm])
        self.CP(dv, wm[:], wmf[:], [wmf], [wm])
        self.CP(dv, m16[:], m16f[:], [m16f], [m16])
        pti = A.sbuf("pti", [128, NS * 128], I32)
        self.DMA(pti[:], self.ptab.rearrange("(o s) g -> o (s g)", o=1).to_broadcast([128, NS * 128]), [], [pti])
        iop = A.sbuf("iop", [128, 1], F32)
        P.op(pl, lambda e: e.iota(iop[:], pattern=[[0, 1]], base=0, channel_multiplier=2, allow_small_or_imprecise_dtypes=True), (), [iop])
        idxf = A.sbuf("idxf", [128, NS * 128], F32)
        self.TS(dv, idxf[:], pti[:], 256.0, iop[:, 0:1], ALU.mult, ALU.add, [pti, iop], [idxf])
        idxA = A.sbuf("idxA", [128, NS * 128], I32)
        idxB = A.sbuf("idxB", [128, NS * 128], I32)
        self.CP(dv, idxA[:], idxf[:], [idxf], [idxA])
        self.TS(dv, idxB[:], idxf[:], 1.0, None, ALU.add, None, [idxf], [idxB])
        yat = A.sbuf("yats", [N, 1024], F32)
        pgs = [A.sbuf(f"pg{i}", [128, 512], F32) for i in range(3)]
        pgb = [A.sbuf(f"pgb{i}", [128, 512], BF16) for i in range(2)]
        pts = [A.sbuf(f"pts{i}", [128, 16, N], BF16) for i in range(2)]
        cnt = {"pg": 0, "pt": 0, "sc": 0}
        Gf = self.G[:].rearrange("p j c -> p (j c)")
        po = [self.pK[0], self.pK[1], self.pK[2], self.pS[1]]
        pov = [p_[0:N, 0:260].rearrange("p (g e) -> p g e", g=4) for p_ in po]
        scps = [self.pA[0], self.pA[1]]
        zb = A.sbuf("zb", [128, 260], BF16)
        self.MS(pl, zb[:], 0.0, [zb])

        def gather_page(s, g, idx):
            cnt["pg"] += 1
            pg = pgs[cnt["pg"] % 3]
            P.gather(pg[:], cache_l, idx[:, s * 128 + g:s * 128 + g + 1], [idx], [pg])
            b = pgb[cnt["pg"] % 2]
            self.CP(dv, b[:], pg[:], [pg], [b])
            return b

        def attend(nk, kT_of, bias_of, mask_ap, mask_R, v_of, first, last):
            cnt["sc"] += 1
            psx = scps[cnt["sc"] % 2]
            if bias_of:
                bz = bias_of(None)
                self.MM(psx[0:nk, 0:16 * N], bz[0], bz[1], True, False, bz[2], [psx])
            for k in (0, 2, 1, 3):
                base, pr = 64 * (k % 2), k // 2
                kT, kR = kT_of(k)
                self.MM(psx[0:nk, k * 4 * N:(k + 1) * 4 * N], kT,
                        q0[base:base + 64, pr * 4:pr * 4 + 4, :].rearrange("p g n -> p (g n)"), bias_of is None, True, kR + [q0], [psx])
            cnt["pt"] += 1
            pt = pts[cnt["pt"] % 2]
            ptf = pt[0:nk].rearrange("p a n -> p (a n)")
            self.AC(ptf, psx[0:nk, 0:16 * N], AF.Exp, [psx], [pt], scale=0.125)
            self.TT(dv, pt[0:nk], pt[0:nk], mask_ap.unsqueeze(1).to_broadcast([nk, 16, N]), ALU.mult, [pt] + mask_R, [pt])
            if first:
                for k in range(4):
                    self.MM(po[k][0:N, 0:260], zb[:, 0:N], zb[:, 0:260], True, False, [zb], [po[k]])
            for k in range(4):
                vv, vR = v_of(k)
                for g in range(4):
                    self.MM(pov[k][:, g, :], pt[0:nk, k * 4 + g, :], vv, False, last, [pt] + vR, [po[k]])

        def finish_branch(bi):
            for k in range(4):
                z4 = A.sbuf("zzs", [N, 4], F32)
                self.TS(dv, z4[:], pov[k][:, :, 64], 1e-30, None, ALU.max, None, [po[k]], [z4])
                P.op(dv, lambda e: e.reciprocal(out=z4[:], in_=z4[:]), [z4], [z4])
                gsel = sg[:, 12 * k:12 * k + 12].rearrange("p (g i) -> p g i", i=3)[:, :, bi]
                self.TT(dv, z4[:], z4[:], gsel, ALU.mult, [z4, sg], [z4])
                yv = yat[:, k * 256:(k + 1) * 256].rearrange("p (g d) -> p g d", g=4)
                fb = z4[:].unsqueeze(2).to_broadcast([N, 4, 64])
                if bi == 0:
                    self.TT(dv, yv, pov[k][:, :, 0:64], fb, ALU.mult, [po[k], z4], [yat])
                else:
                    tmp = A.sbuf("ytmps", [N, 4, 64], F32)
                    self.TT(dv, tmp[:], pov[k][:, :, 0:64], fb, ALU.mult, [po[k], z4], [tmp])
                    self.TT(dv, yv, yv, tmp[:], ALU.add, [yat, tmp], [yat])

        XTk = A.sbuf("XTk", [128, 2, 4096], BF16)
        XTv = A.sbuf("XTv", [128, 2, 4096], BF16)
        kcT = A.sbuf("kcTs", [128, 2, 256], BF16)
        vcT = A.sbuf("vcTs", [64, 4, 64], BF16)
        vca = A.sbuf("vcas", [128, 2, 4, 65], BF16)
        BT = A.sbuf("BTs", [64, NS, 4, 4, 4 * N], BF16)
        self.MS(pl, vca[:], 1.0, [vca])
        s1 = A.sbuf("s1s", [N, 4, 256], F32)
        z4a = A.sbuf("z4a", [N, 4], F32)
        sc_ = A.sbuf("scs", [N, 256], F32)
        mx = A.sbuf("mxs", [N, 8], F32)
        w1 = A.sbuf("w1s", [N, 256], F32)
        bq = A.sbuf("bqs", [N, 256], BF16)
        tsl = A.sbuf("tsls", [N, 256], F32)
        self.DMA(tsl[:], self.t_ssel, [], [tsl])
        for s in range(NS):
            for grp in range(4):
                for gg in range(32):
                    g = grp * 32 + gg
                    b = gather_page(s, g, idxA)
                    ptx = self.pS[0]
                    for n_ in range(4):
                        self.MT(ptx[:, n_ * 128:(n_ + 1) * 128], b[:, n_ * 128:(n_ + 1) * 128], [b], [ptx])
                    self.CP(ac, XTk[:, :, gg * 128:(gg + 1) * 128], ptx[:, 0:256].rearrange("p (r t) -> p r t", r=2), [ptx], [XTk])
                    self.CP(ac, XTv[:, :, gg * 128:(gg + 1) * 128], ptx[:, 256:512].rearrange("p (r t) -> p r t", r=2), [ptx], [XTv])
                for a_, xt_ in ((0, XTk), (1, XTv)):
                    v4 = xt_[:].rearrange("p r (n l) -> p r n l", l=64)
                    self.TT(dv, v4, v4, L.pe2[:, a_, :].unsqueeze(1).unsqueeze(1).to_broadcast([128, 2, 64, 64]), ALU.add, [xt_, L.pe2], [xt_])
                pkc, pvc = self.pA[0], self.pA[1]
                for k in (0, 2, 1, 3):
                    base, pr = 64 * (k % 2), k // 2
                    for a_, xt_, pso, ob in ((0, XTk, pkc, base), (1, XTv, pvc, 0)):
                        col = (pr * 64) if a_ == 0 else (k * 64)
                        xl = xt_[:].rearrange("p r (n l) -> p r l n", l=64)
                        for ll in range(64):
                            self.MM(pso[ob:ob + 64, col:col + 64], L.wphi2[base:base + 64, a_, ll, :], xl[base:base + 64, pr, ll, :],
                                    ll == 0, ll == 63, [L.wphi2, xt_], [pso])
                self.CP(ac, kcT[:, :, grp * 64:(grp + 1) * 64], pkc[:, 0:128].rearrange("p (r n) -> p r n", r=2), [pkc], [kcT])
                self.CP(ac, vcT[0:64, :, :], pvc[0:64, 0:256].rearrange("p (k n) -> p k n", k=4), [pvc], [vcT])
                pvt = self.pS[0]
                ob = 64 * (grp % 2)
                for k in range(4):
                    self.MT(pvt[ob:ob + 64, k * 64:(k + 1) * 64], vcT[0:64, k, :], [vcT], [pvt])
                self.CP(ac, vca[ob:ob + 64, grp // 2, :, 0:64], pvt[ob:ob + 64, 0:256].rearrange("p (k e) -> p k e", k=4), [pvt], [vca])
            for nt in range(2):
                attend(128, lambda k, nt=nt: (kcT[64 * (k % 2):64 * (k % 2) + 64, k // 2, nt * 128:(nt + 1) * 128], [kcT]),
                       None, cm[:, s, :], [cm], lambda k, nt=nt: (vca[:, nt, k, :], [vca]),
                       (s == 0 and nt == 0), (s == NS - 1 and nt == 1))
            for k in range(4):
                base, pr = 64 * (k % 2), k // 2
                psc = [self.pS[0], self.pS[1]] if False else [self.pA[0], self.pA[1]]
                for g in range(4):
                    pq = psc[g % 2]
                    self.MM(pq[0:N, (g // 2) * 256:(g // 2) * 256 + 256], q0[base:base + 64, pr * 4 + g, :], kcT[base:base + 64, pr, :], True, True, [q0, kcT], [pq])
                for g in range(4):
                    pq = psc[g % 2]
                    self.AC(s1[:, g, :], pq[0:N, (g // 2) * 256:(g // 2) * 256 + 256], AF.Exp, [pq], [s1], scale=0.125)
                P.op(dv, lambda e: e.tensor_reduce(out=z4a[:], in_=s1[:], axis=AX.X, op=ALU.add), [s1], [z4a])
                P.op(dv, lambda e: e.reciprocal(out=z4a[:], in_=z4a[:]), [z4a], [z4a])
                self.TT(dv, s1[:], s1[:], z4a[:].unsqueeze(2).to_broadcast([N, 4, 256]), ALU.mult, [s1, z4a], [s1])
                P.op(dv, lambda e: e.tensor_reduce(out=sc_[:], in_=s1[:].rearrange("p g n -> p n g"), axis=AX.X, op=ALU.add), [s1], [sc_])
                self.TT(dv, sc_[:], sc_[:], tsl[:], ALU.add, [sc_, tsl], [sc_])
                P.op(dv, lambda e: e.max(out=mx[:], in_=sc_[:]), [sc_], [mx])
                P.op(dv, lambda e: e.match_replace(out=w1[:], in_to_replace=mx[:], in_values=sc_[:], imm_value=NEG), [mx, sc_], [w1])
                P.op(dv, lambda e: e.max(out=mx[:], in_=w1[:]), [w1], [mx])
                self.TS(dv, w1[:], sc_[:], mx[:, 6:7], -1.0, ALU.is_ge, ALU.add, [sc_, mx], [w1])
                self.TS(dv, bq[:], w1[:], 1e30, None, ALU.mult, None, [w1], [bq])
                for t4 in range(4):
                    pbt = self.pS[0]
                    self.MT(pbt[0:64, t4 * N:(t4 + 1) * N], bq[:, t4 * 64:(t4 + 1) * 64], [bq], [pbt])
                for g_ in range(4):
                    self.CP(ac, BT[0:64, s, :, k, g_ * N:(g_ + 1) * N], self.pS[0][0:64, 0:4 * N].rearrange("p (t n) -> p t n", t=4), [self.pS[0]], [BT])
        finish_branch(0)
        vpg = [A.sbuf(f"vpg{i}", [128, 4, 65], BF16) for i in range(2)]
        for v_ in vpg:
            self.MS(pl, v_[:], 1.0, [v_])
        kTp = [A.sbuf(f"kTp{i}", [128, 2, 128], BF16) for i in range(2)]
        it = 0
        for s in range(NS):
            for g in range(128):
                b = gather_page(s, g, idxB)
                ptx = self.pS[0]
                for n_ in range(2):
                    self.MT(ptx[:, n_ * 128:(n_ + 1) * 128], b[:, n_ * 128:(n_ + 1) * 128], [b], [ptx])
                kt_ = kTp[it % 2]
                vv_ = vpg[it % 2]
                it += 1
                self.CP(ac, kt_[:], ptx[:, 0:256].rearrange("p (r t) -> p r t", r=2), [ptx], [kt_])
                self.CP(ac, vv_[:, :, 0:64], b[:, 256:512].rearrange("p (h d) -> p h d", h=4), [b], [vv_])
                tile4, loc = g // 32, (2 * g) % 64
                attend(128, lambda k, kt_=kt_: (kt_[64 * (k % 2):64 * (k % 2) + 64, k // 2, :], [kt_]),
                       lambda k, s=s, tile4=tile4, loc=loc: (Gf[:, loc * 64:loc * 64 + 128], BT[0:64, s, tile4, :, :].rearrange("p k c -> p (k c)"), [self.G, BT]),
                       cm[:, s, :], [cm], lambda k, vv_=vv_: (vv_[:, k, :], [vv_]), (s == 0 and g == 0), False)
        attend(N, lambda k: (knT[64 * (k % 2):64 * (k % 2) + 64, k // 2, :], [knT]), None, m16[:, :], [m16],
               lambda k: (vn[:, 0, k, :], [vn]), False, True)
        finish_branch(1)
        wst = A.sbuf("wst", [128, 512], F32)
        wsb = A.sbuf("wsb", [128, 512], BF16)
        for s in range(NS):
            for t4 in range(4):
                self.DMA(wst[:], self.swin[l, s, t4 * 128:(t4 + 1) * 128, :], [], [wst])
                self.CP(dv, wsb[:], wst[:], [wst], [wsb])
                ptx = self.pS[0]
                for n_ in range(2):
                    self.MT(ptx[:, n_ * 128:(n_ + 1) * 128], wsb[:, n_ * 128:(n_ + 1) * 128], [wsb], [ptx])
                kt_ = kTp[it % 2]
                vv_ = vpg[it % 2]
                it += 1
                self.CP(ac, kt_[:], ptx[:, 0:256].rearrange("p (r t) -> p r t", r=2), [ptx], [kt_])
                self.CP(ac, vv_[:, :, 0:64], wsb[:, 256:512].rearrange("p (h d) -> p h d", h=4), [wsb], [vv_])
                msk = wm if t4 == 0 else cm
                attend(128, lambda k, kt_=kt_: (kt_[64 * (k % 2):64 * (k % 2) + 64, k // 2, :], [kt_]), None,
                       msk[:, s, :], [msk], lambda k, vv_=vv_: (vv_[:, k, :], [vv_]), (s == 0 and t4 == 0), False)
        attend(N, lambda k: (knT[64 * (k % 2):64 * (k % 2) + 64, 2 + k // 2, :], [knT]), None, m16[:, :], [m16],
               lambda k: (vn[:, 1, k, :], [vn]), False, True)
        finish_branch(2)
        yb = A.sbuf("ybs", [N, 1024], BF16)
        self.CP(dv, yb[:], yat[:], [yat], [yb])
        pyt = self.pS[0]
        for t8 in range(8):
            self.MT(pyt[:, t8 * N:(t8 + 1) * N], yb[:, t8 * 128:(t8 + 1) * 128], [yb], [pyt])
        self.TT(dv, yza[:, :, 0:N], pyt[:, 0:8 * N].rearrange("p (t c) -> p t c", t=8), saz[:], ALU.mult, [pyt, saz], [yza])


KB.sample_layer = _sample_layer
KB.nsa_sample = _nsa_sample
```
